# Optimizing a Trainium2 kernel written in Bass

```python
import math
import jax, jax.numpy as jnp
from jax import lax
import numpy as np

D_MODEL = 1024
BATCH = 16
SEQ = 256
DEPTH = 2
DEC_BATCH = 4
DEC_SEQ = 4096
PAST_LEN = 256

GRID_W = 64
HEAD_DIM = 64
N_HEADS = 4
ATTN_W = N_HEADS * 2 * HEAD_DIM
LRU_W = D_MODEL // 4
LRU_BLOCKS = 4
LRU_BW = LRU_W // LRU_BLOCKS
CONV_W = 4
FOURIER_W = D_MODEL // 4
FOURIER_GROUPS = 4
FOURIER_GW = FOURIER_W // FOURIER_GROUPS
MIX_W = ATTN_W + LRU_W + FOURIER_W
IN_W = 3 * ATTN_W + 2 * LRU_W + FOURIER_W
FFN_HIDDEN = -(-8 * D_MODEL // (3 * 256)) * 256
ROPE_BASE = 10000.0
RG_C = 8.0
Q_BLOCK = 128
EPS = 1e-6

kernel_name = "hybrid_diff_lru_fourier_prefix_dit_step"


def rmsnorm(x, g):
    xf = x.astype(jnp.float32)
    y = xf * lax.rsqrt(jnp.mean(xf * xf, axis=-1, keepdims=True) + EPS)
    return (y * g.astype(jnp.float32)).astype(x.dtype)


def lambda_init(l):
    return 0.8 - 0.6 * math.exp(-0.3 * l)


def grid_angles(n_tokens):
    rows = n_tokens // GRID_W
    row = jnp.repeat(jnp.arange(rows, dtype=jnp.float32), GRID_W)
    col = jnp.tile(jnp.arange(GRID_W, dtype=jnp.float32), rows)
    n = HEAD_DIM // 4
    inv = ROPE_BASE ** (-jnp.arange(n, dtype=jnp.float32) / n)
    return row[:, None] * inv, col[:, None] * inv


def rotate(x, ang):
    cos = jnp.cos(ang)[None, :, None, None, :].astype(x.dtype)
    sin = jnp.sin(ang)[None, :, None, None, :].astype(x.dtype)
    x1, x2 = jnp.split(x, 2, axis=-1)
    return jnp.concatenate([x1 * cos - x2 * sin, x2 * cos + x1 * sin], axis=-1)


def rope2d(x, ang_r, ang_c):
    h = HEAD_DIM // 2
    return jnp.concatenate([rotate(x[..., :h], ang_r), rotate(x[..., h:], ang_c)], axis=-1)


def diff_attention(q, k, v, lam):
    B, Lq = q.shape[0], q.shape[1]
    nblk = Lq // Q_BLOCK
    qb = jnp.moveaxis(q.reshape(B, nblk, Q_BLOCK, N_HEADS, 2, HEAD_DIM), 1, 0)
    scale = HEAD_DIM ** -0.5

    def block(qi):
        s = jnp.einsum('bqhmd,bkhmd->bhmqk', qi, k).astype(jnp.float32) * scale
        p = jax.nn.softmax(s, axis=-1)
        w = p[:, :, 0] - lam * p[:, :, 1]
        return jnp.einsum('bhqk,bkhe->bqhe', w.astype(v.dtype), v)

    o = lax.map(block, qb)
    return jnp.moveaxis(o, 0, 1).reshape(B, Lq, N_HEADS, 2 * HEAD_DIM)


def dw_conv(x, w, b):
    L = x.shape[1]
    lp = CONV_W // 2
    xp = jnp.pad(x, ((0, 0), (lp, CONV_W - 1 - lp), (0, 0)))
    out = xp[:, 0:L] * w[0]
    for j in range(1, CONV_W):
        out = out + xp[:, j:j + L] * w[j]
    return out + b


def rg_lru_dir(x, wa, ba, wx, bx, lam_p, h0, reverse):
    B, L, _ = x.shape
    xb = x.reshape(B, L, LRU_BLOCKS, LRU_BW)
    r = jax.nn.sigmoid(jnp.einsum('blni,nij->blnj', xb, wa).reshape(B, L, LRU_W) + ba)
    i = jax.nn.sigmoid(jnp.einsum('blni,nij->blnj', xb, wx).reshape(B, L, LRU_W) + bx)
    log_a = -RG_C * r.astype(jnp.float32) * jax.nn.softplus(-lam_p.astype(jnp.float32))
    a = jnp.exp(log_a)
    b = jnp.sqrt(-jnp.expm1(2.0 * log_a)) * (i * x).astype(jnp.float32)
    idx = -1 if reverse else 0
    b = b.at[:, idx].add(a[:, idx] * h0.astype(jnp.float32))

    def comb(e1, e2):
        a1, b1 = e1
        a2, b2 = e2
        return a1 * a2, a2 * b1 + b2

    _, h = lax.associative_scan(comb, (a, b), axis=1, reverse=reverse)
    h_fin = h[:, 0] if reverse else h[:, -1]
    return h.astype(x.dtype), h_fin.astype(x.dtype)


def mixer(P, l, u, ang, ctx_k, ctx_v, h0):
    B, L, _ = u.shape
    proj = u @ P['w_in'][l]
    q, k, v, xr, gr, xf = jnp.split(
        proj, [ATTN_W, 2 * ATTN_W, 3 * ATTN_W, 3 * ATTN_W + LRU_W, 3 * ATTN_W + 2 * LRU_W], axis=-1)
    q = q.reshape(B, L, N_HEADS, 2, HEAD_DIM)
    k = k.reshape(B, L, N_HEADS, 2, HEAD_DIM)
    v = v.reshape(B, L, N_HEADS, 2 * HEAD_DIM)
    if ang is not None:
        q = rope2d(q, ang[0], ang[1])
        k = rope2d(k, ang[0], ang[1])
    if ctx_k is None:
        k_all, v_all = k, v
    else:
        k_all = jnp.concatenate([k, ctx_k.astype(k.dtype)], axis=1)
        v_all = jnp.concatenate([v, ctx_v.astype(v.dtype)], axis=1)
    li = lambda_init(l)
    lp = P['w_lambda'][l].astype(jnp.float32)
    lam = jnp.exp(jnp.sum(lp[0] * lp[1])) - jnp.exp(jnp.sum(lp[2] * lp[3])) + li
    o = diff_attention(q, k_all, v_all, lam)
    attn_out = (rmsnorm(o, P['g_subln'][l]) * (1.0 - li)).reshape(B, L, ATTN_W)
    xc = dw_conv(xr, P['conv_w'][l], P['conv_b'][l])
    hf, hf_fin = rg_lru_dir(xc, P['lru_wa'][l, 0], P['lru_ba'][l, 0], P['lru_wx'][l, 0],
                            P['lru_bx'][l, 0], P['lru_lambda'][l, 0], h0[:, 0], False)
    hb, hb_fin = rg_lru_dir(xc, P['lru_wa'][l, 1], P['lru_ba'][l, 1], P['lru_wx'][l, 1],
                            P['lru_bx'][l, 1], P['lru_lambda'][l, 1], h0[:, 1], True)
    lru_out = (hf + hb) * jax.nn.gelu(gr)
    xg = xf.reshape(B, L, FOURIER_GROUPS, FOURIER_GW).astype(jnp.float32)
    four_out = jnp.fft.fft2(xg, axes=(1, 3), norm='ortho').real.astype(u.dtype).reshape(B, L, FOURIER_W)
    out = jnp.concatenate([attn_out, lru_out, four_out], axis=-1) @ P['w_out'][l]
    return out, k, v, jnp.stack([hf_fin, hb_fin], axis=1)


def layer(P, l, x, cond, ang, ctx_k, ctx_v, h0):
    mod = (jax.nn.silu(cond) @ P['w_mod'][l] + P['b_mod'][l])[:, None, :]
    sh1, sc1, g1, sh2, sc2, g2 = jnp.split(mod, 6, axis=-1)
    h = rmsnorm(x, P['g_pre_mix'][l]) * (1.0 + sc1) + sh1
    o, k, v, hfin = mixer(P, l, h, ang, ctx_k, ctx_v, h0)
    x = x + g1 * rmsnorm(o, P['g_post_mix'][l])
    h = rmsnorm(x, P['g_pre_ffn'][l]) * (1.0 + sc2) + sh2
    gt, up = jnp.split(h @ P['w_gate_up'][l], 2, axis=-1)
    f = (jax.nn.silu(gt) * up) @ P['w_down'][l]
    x = x + g2 * rmsnorm(f, P['g_post_ffn'][l])
    return x, k, v, hfin


def setup_inputs(seed: int = 0) -> dict:
    key = jax.random.key(seed)
    ks = jax.random.split(key, 26)
    f32 = jnp.float32
    D = D_MODEL

    def nrm(k, shape, s=1.0):
        return jax.random.normal(k, shape, f32) * s

    a_c = jax.random.uniform(ks[23], (DEPTH, 2, LRU_W), f32, minval=0.9, maxval=0.999)
    s = a_c ** (1.0 / RG_C)
    lru_lambda = jnp.log(s) - jnp.log1p(-s)
    return {
        'x_prompt': nrm(ks[0], (BATCH, SEQ, D)),
        'x_sample': nrm(ks[1], (DEC_BATCH, DEC_SEQ, D)),
        'cache_k': nrm(ks[2], (DEC_BATCH, DEPTH, PAST_LEN, N_HEADS, 2, HEAD_DIM)),
        'cache_v': nrm(ks[3], (DEC_BATCH, DEPTH, PAST_LEN, N_HEADS, 2 * HEAD_DIM)),
        'state_lru': nrm(ks[4], (DEC_BATCH, DEPTH, 2, LRU_W), 0.5),
        'c': nrm(ks[5], (DEC_BATCH, D)),
        'c_ctx': nrm(ks[6], (D,)),
        'w_mod': nrm(ks[7], (DEPTH, D, 6 * D), 0.5 * D ** -0.5),
        'b_mod': nrm(ks[8], (DEPTH, 6 * D), 0.01),
        'g_pre_mix': 1.0 + nrm(ks[9], (DEPTH, D), 0.05),
        'g_post_mix': 1.0 + nrm(ks[10], (DEPTH, D), 0.05),
        'g_pre_ffn': 1.0 + nrm(ks[11], (DEPTH, D), 0.05),
        'g_post_ffn': 1.0 + nrm(ks[12], (DEPTH, D), 0.05),
        'w_in': nrm(ks[13], (DEPTH, D, IN_W), D ** -0.5),
        'w_out': nrm(ks[14], (DEPTH, MIX_W, D), MIX_W ** -0.5),
        'w_lambda': nrm(ks[15], (DEPTH, 4, HEAD_DIM), 0.1),
        'g_subln': 1.0 + nrm(ks[16], (DEPTH, 2 * HEAD_DIM), 0.05),
        'conv_w': nrm(ks[17], (DEPTH, CONV_W, LRU_W), CONV_W ** -0.5),
        'conv_b': nrm(ks[18], (DEPTH, LRU_W), 0.01),
        'lru_wa': nrm(ks[19], (DEPTH, 2, LRU_BLOCKS, LRU_BW, LRU_BW), LRU_BW ** -0.5),
        'lru_ba': nrm(ks[20], (DEPTH, 2, LRU_W), 0.01),
        'lru_wx': nrm(ks[21], (DEPTH, 2, LRU_BLOCKS, LRU_BW, LRU_BW), LRU_BW ** -0.5),
        'lru_bx': nrm(ks[22], (DEPTH, 2, LRU_W), 0.01),
        'lru_lambda': lru_lambda,
        'w_gate_up': nrm(ks[24], (DEPTH, D, 2 * FFN_HIDDEN), D ** -0.5),
        'w_down': nrm(ks[25], (DEPTH, FFN_HIDDEN, D), FFN_HIDDEN ** -0.5),
    }


def reference(x_prompt, x_sample, cache_k, cache_v, state_lru, c, c_ctx,
              w_mod, b_mod, g_pre_mix, g_post_mix, g_pre_ffn, g_post_ffn,
              w_in, w_out, w_lambda, g_subln, conv_w, conv_b,
              lru_wa, lru_ba, lru_wx, lru_bx, lru_lambda, w_gate_up, w_down):
    P = {'w_mod': w_mod, 'b_mod': b_mod, 'g_pre_mix': g_pre_mix, 'g_post_mix': g_post_mix,
         'g_pre_ffn': g_pre_ffn, 'g_post_ffn': g_post_ffn, 'w_in': w_in, 'w_out': w_out,
         'w_lambda': w_lambda, 'g_subln': g_subln, 'conv_w': conv_w, 'conv_b': conv_b,
         'lru_wa': lru_wa, 'lru_ba': lru_ba, 'lru_wx': lru_wx, 'lru_bx': lru_bx,
         'lru_lambda': lru_lambda, 'w_gate_up': w_gate_up, 'w_down': w_down}
    ang = grid_angles(x_sample.shape[1])
    xp, xs = x_prompt, x_sample
    h0_ctx = jnp.zeros((xp.shape[0], 2, LRU_W), xp.dtype)
    ks_, vs_, hs_ = [], [], []
    for l in range(DEPTH):
        xp, k_l, v_l, h_l = layer(P, l, xp, c_ctx[None, :], None, None, None, h0_ctx)
        ks_.append(k_l)
        vs_.append(v_l)
        hs_.append(h_l)
        xs, _, _, _ = layer(P, l, xs, c, ang, cache_k[:, l], cache_v[:, l], state_lru[:, l])
    new_k = jnp.stack(ks_, axis=1)
    new_v = jnp.stack(vs_, axis=1)
    new_h = jnp.stack(hs_, axis=1)
    return (xp, xs, new_k, new_v, new_h)
```

```python
import math
from contextlib import ExitStack
import numpy as np
import ml_dtypes
import concourse.bass as bass
import concourse.mybir as mybir
from concourse.bass_utils import run_bass_kernel_spmd

F32 = mybir.dt.float32
BF16 = mybir.dt.bfloat16
ALU = mybir.AluOpType
AF = mybir.ActivationFunctionType

D = 1024
TS = 2048
TP = 512
T = 2560
EPS = 1e-6
WA_W = 8448
WB_W = 9472
OFF_WOUT, OFF_GU, OFF_DN = 0, 1024, 6656

SVL = {}
_o = 0
for _n, _w in (("gpm", 8), ("gqm", 8), ("gpf", 8), ("gqf", 8), ("bmod", 48), ("gsub", 1), ("convw", 8),
               ("convb", 2), ("ba", 4), ("bx", 4), ("lam", 4), ("wl", 256), ("h0", 4)):
    SVL[_n] = (_o, _w)
    _o += _w
SV_PER_L = _o
SV_MISC = 2 * SV_PER_L
NSV = SV_MISC + 8 + 16


def lambda_init(l):
    return 0.8 - 0.6 * math.exp(-0.3 * l)


class Buf:
    __slots__ = ("name", "w", "r")

    def __init__(self, name):
        self.name = name
        self.w = None
        self.r = []


class Sched:
    ENGS = ("pe", "act", "dve", "pool", "sp")
    NDMA = 16

    def __init__(self, nc, es):
        self.nc = nc
        self.q = {e: [] for e in self.ENGS}
        self.sem = {e: es.enter_context(nc.semaphore("c_" + e)) for e in self.ENGS}
        self.cnt = {e: 0 for e in self.ENGS}
        self.known = {e: {} for e in self.ENGS}
        self.dsem, self.dcnt, self.dnext, self.dlast = {}, {}, {}, {}
        for e in ("sp", "pool", "act"):
            self.dsem[e] = [es.enter_context(nc.semaphore("d_%s%d" % (e, i))) for i in range(self.NDMA)]
            self.dcnt[e] = [0] * self.NDMA
            self.dlast[e] = [None] * self.NDMA
            self.dnext[e] = 0

    def _deps(self, reads, writes):
        deps = []
        for b in reads:
            if b.w is not None:
                deps.append(b.w)
        for b in writes:
            if b.w is not None:
                deps.append(b.w)
            deps.extend(b.r)
        return deps

    def _emit_waits(self, eng, deps, skip_own=False):
        kn = self.known[eng]
        best = {}
        for (s, v, key) in deps:
            if skip_own and key == ("c", eng):
                continue
            if kn.get(key, 0) >= v:
                continue
            if best.get(key, (None, 0))[1] < v:
                best[key] = (s, v)
        for key, (s, v) in best.items():
            kn[key] = v
            self.q[eng].append(lambda e, s=s, v=v: e.wait_ge(s, v))

    def _commit(self, ev, reads, writes):
        for b in reads:
            b.r.append(ev)
            if len(b.r) > 64:
                last = {}
                for x in b.r:
                    if last.get(x[2], (None, 0, None))[1] < x[1]:
                        last[x[2]] = x
                b.r = list(last.values())
        for b in writes:
            b.w = ev
            b.r = []

    def op(self, eng, fn, reads=(), writes=()):
        writes = list(writes) + [b for b in reads if b.name.startswith("ps") and b not in writes]
        deps = self._deps(reads, writes)
        self._emit_waits(eng, deps, skip_own=(eng == "pe"))
        self.cnt[eng] += 1
        s = self.sem[eng]
        ev = (s, self.cnt[eng], ("c", eng))
        self.q[eng].append(lambda e, fn=fn, s=s: fn(e).then_inc(s, 1))
        self._commit(ev, reads, writes)
        return ev

    def dma(self, eng, fn, reads=(), writes=()):
        i = self.dnext[eng]
        self.dnext[eng] = (i + 1) % self.NDMA
        deps = self._deps(reads, writes)
        if self.dlast[eng][i] is not None:
            deps.append(self.dlast[eng][i])
        self._emit_waits(eng, deps)
        self.dcnt[eng][i] += 16
        s = self.dsem[eng][i]
        ev = (s, self.dcnt[eng][i], ("d", eng, i))
        self.dlast[eng][i] = ev
        self.q[eng].append(lambda e, fn=fn, s=s: fn(e).then_inc(s, 16))
        self._commit(ev, reads, writes)
        return ev

    def _all_events(self):
        evs = []
        for e in self.ENGS:
            if self.cnt[e] > 0:
                evs.append((self.sem[e], self.cnt[e], ("c", e)))
        for e in self.dsem:
            for i in range(self.NDMA):
                if self.dlast[e][i] is not None:
                    evs.append(self.dlast[e][i])
        ex = getattr(self, "extra_events", [])
        if ex:
            evs.append(ex[-1])
        return evs

    def barrier(self):
        evs = self._all_events()
        for e in self.ENGS:
            self._emit_waits(e, evs)

    def final_wait(self, eng="sp"):
        self._emit_waits(eng, self._all_events())

    def emit(self, block):
        q = self.q

        @block.tensor
        def _(e):
            for t in q["pe"]:
                t(e)

        @block.scalar
        def _(e):
            for t in q["act"]:
                t(e)

        @block.vector
        def _(e):
            for t in q["dve"]:
                t(e)

        @block.gpsimd
        def _(e):
            for t in q["pool"]:
                t(e)

        @block.sync
        def _(e):
            for t in q["sp"]:
                t(e)


def build_program(stop=None):
    nc = bass.Bass("TRN2", target_bir_lowering=False)
    TAPS = []

    def din(name, shape, dt=F32):
        return nc.dram_tensor(name, shape, dt, kind="ExternalInput").ap()

    def dout(name, shape, dt=F32):
        return nc.dram_tensor(name, shape, dt, kind="ExternalOutput").ap()

    def dint(name, shape, dt):
        return nc.dram_tensor(name, shape, dt).ap()

    xs_d = din("xs", [2, TS, D])
    xp_d = din("xp", [TP, D])
    ck_d = din("ck", [2, 256, 512])
    cv_d = din("cv", [2, 256, 512])
    sv_d = din("sv", [128, NSV])
    lw_d = din("lw", [128, 2048])
    wsl_d = din("wsl", [2, 8, 128, WA_W + WB_W])
    rope_d = din("rope", [2, 128, 2, TS])
    csign_d = din("csign", [128, 512])
    rmat_d = din("rmat", [128, 128])
    cs64_d = din("cs64", [128, 256])
    dft_d = din("dft", [2, 4096, 2048], BF16)
    dftp_d = din("dftp", [2, 256, 256], BF16)
    ys_d = dout("ys", [TS, D])
    yp_d = dout("yp", [TP, D])
    nk_d = dout("nk", [2, 2, 256, 512])
    nv_d = dout("nv", [2, 2, 256, 512])
    nh_d = dout("nh", [2, 2, 2, 256])

    WGA = [dint("WGA%d" % l, [1024, WA_W], BF16) for l in range(2)]
    WGB = [dint("WGB%d" % l, [1024, WB_W], BF16) for l in range(2)]
    WGU = [dint("WGU%d" % l, [11, 128, 8, 512], BF16) for l in range(2)]
    EXG = [dint("EXG%d" % l, [3072, 2048], BF16) for l in range(2)]
    QS = [[dint("QS%d_%d" % (l, k), [128, 4, TS], BF16) for k in range(2)] for l in range(2)]
    GS = [[dint("GS%d_%d" % (l, k), [128, 2, TS], BF16) for k in range(2)] for l in range(2)]
    XS = [[dint("XS%d_%d" % (l, k), [128, 8, T], F32) for k in range(2)] for l in range(2)]

    es = ExitStack()
    with es:
        S = Sched(nc, es)

        def OP(eng, method, *args, rd=(), wr=(), **kw):
            return S.op(eng, lambda e: getattr(e, method)(*args, **kw), rd, wr)

        def DMA(eng, out, in_, rd=(), wr=()):
            return S.dma(eng, lambda e: e.dma_start(out=out, in_=in_), rd, wr)

        def MM(out, lhsT, rhs, start, stop, rd=(), wr=()):
            return S.op("pe", lambda e: e.matmul(out, lhsT, rhs, start=start, stop=stop), rd, wr)

        class _Stop(Exception):
            pass

        def tap(name, src):
            o = nc.dram_tensor("dbg_" + name, list(src.shape), src.dtype, kind="ExternalOutput").ap()
            S.dma("sp", lambda e: e.dma_start(out=o, in_=src), (), ())
            TAPS.append("dbg_" + name)

        def check_stop(tag, taps=()):
            if stop == tag:
                S.barrier()
                for (n_, a_) in taps:
                    tap(n_, a_)
                raise _Stop()

        cc_sem = es.enter_context(nc.semaphore("cc_sem"))
        cc_state = {"n": 0}

        def CC(groups, src, dst, rd, wr):
            deps = S._deps(rd, wr)
            S._emit_waits("pool", deps)
            cc_state["n"] += 1
            v = cc_state["n"]
            ev = (cc_sem, v, ("cc",))
            S.q["pool"].append(lambda e: e.collective_compute(
                "AllGather", ALU.bypass, replica_groups=groups, ins=[src], outs=[dst]).then_inc(cc_sem, 1))
            S._commit(ev, rd, wr)
            S.extra_events = getattr(S, "extra_events", [])
            S.extra_events.append(ev)

        NBIG = 188 * 1024 // 4
        BIG = es.enter_context(nc.sbuf_tensor("big", [128, NBIG], F32))
        PSP = [es.enter_context(nc.psum_tensor("psp%d" % i, [128, 1024], F32))[:, :] for i in range(2)]
        PS = [PSP[0][:, 0:512], PSP[0][:, 512:1024], PSP[1][:, 0:512], PSP[1][:, 512:1024]] + \
             [es.enter_context(nc.psum_tensor("ps%d" % i, [128, 512], F32))[:, :] for i in range(4, 8)]
        PB = [Buf("ps%d" % i) for i in range(8)]

        class Arena:
            def __init__(self, start, end):
                self.start, self.cur, self.end = start, start, end

            def alloc(self, free_shape, dt):
                n = 1
                for s_ in free_shape:
                    n *= s_
                nbytes = n * (4 if dt == F32 else 2)
                nbytes = (nbytes + 63) // 64 * 64
                off = self.cur
                self.cur += nbytes
                assert self.cur <= self.end, ("arena overflow", self.cur, self.end)
                ap = BIG[:, off // 4:(off + nbytes) // 4]
                if dt != F32:
                    ap = ap.bitcast(dt)
                ap = ap[:, 0:n]
                if len(free_shape) == 2:
                    ap = ap.rearrange("p (a b) -> p a b", a=free_shape[0], b=free_shape[1])
                elif len(free_shape) == 3:
                    ap = ap.rearrange("p (a b c) -> p a b c", a=free_shape[0], b=free_shape[1], c=free_shape[2])
                return ap

            def reset(self, to=None):
                self.cur = self.start if to is None else to

        TOTAL = NBIG * 4
        AR = Arena(0, TOTAL)
        sv = AR.alloc([NSV], F32)
        ident = AR.alloc([128], F32)
        ones_b = AR.alloc([128], BF16)
        rmat_b = AR.alloc([128], BF16)
        cs64_b = AR.alloc([256], BF16)
        lw_b = AR.alloc([16, 128], BF16)
        rope_c = AR.alloc([TS], F32)
        rope_s = AR.alloc([TS], F32)
        csign = AR.alloc([512], F32)
        cst = [[AR.alloc([6, 8], F32) for w in range(2)] for l in range(2)]
        scond = AR.alloc([8, 2], BF16)
        small = AR.alloc([64], F32)
        nhst = AR.alloc([16], F32)
        dftp_b = AR.alloc([2, 2, 256], BF16)
        P_MLRU = AR.cur
        mix_lru = AR.alloc([2, T], BF16)
        P_MFOUR = AR.cur
        mix_four = AR.alloc([2, T], BF16)
        P_ATTN = AR.cur
        mix_attn = AR.alloc([4, T], BF16)
        P_Q = AR.cur
        kTp = AR.alloc([4, TP], BF16)
        Vp = AR.alloc([4, 512], BF16)
        xr_p = AR.alloc([2, TP], BF16)
        xf_p = AR.alloc([2, TP], BF16)
        q_all = AR.alloc([4, T], BF16)
        gr_all = AR.alloc([2, T], BF16)
        P_SCR = AR.cur
        print("persistent bytes", P_MLRU, P_Q, P_SCR, TOTAL)
        b_const = Buf("const")
        b_cst = Buf("cst")
        b_q = [Buf("q%d" % i) for i in range(5)]
        b_gr = [Buf("gr%d" % i) for i in range(5)]
        b_prm = Buf("prm")
        b_mixa = [Buf("mixa%d" % i) for i in range(5)]
        b_mixl = Buf("mixl")
        b_mixf = Buf("mixf")
        b_exg = [Buf("exg0"), Buf("exg1")]
        b_wga = [Buf("wga0"), Buf("wga1")]
        b_wgb = [Buf("wgb0"), Buf("wgb1")]
        b_xs = [[[Buf("xs%d_%d_%d" % (l, k, i)) for i in range(5)] for k in range(2)] for l in range(2)]
        b_qs = [[Buf("qs%d_%d" % (l, k)) for k in range(2)] for l in range(2)]
        b_rope = Buf("rope")
        b_nh = Buf("nh")

        def svc(l, name, i=0, n=1):
            o, w = SVL[name]
            return sv[:, l * SV_PER_L + o + i: l * SV_PER_L + o + i + n]

        try:
            DMA("sp", sv, sv_d, wr=[b_const])
            DMA("sp", csign, csign_d, wr=[b_const])
            DMA("sp", dftp_b, dftp_d.rearrange("c (t p) j -> p c t j", p=128), wr=[b_const])
            OP("pool", "memset", ident, 0.0, wr=[b_const])
            OP("pool", "affine_select", ident, ident, [[-1, 128]], ALU.not_equal, 1.0, base=0, channel_multiplier=1,
               rd=[b_const], wr=[b_const])
            OP("pool", "memset", ones_b, 1.0, wr=[b_const])
            OP("pool", "memset", nhst, 0.0, wr=[b_nh])
            AR.reset(P_SCR)
            st_a = AR.alloc([2048], F32)
            st_b = AR.alloc([128], F32)
            st_c = AR.alloc([256], F32)
            b_st = Buf("st")
            DMA("sp", st_a, lw_d, wr=[b_st])
            DMA("sp", st_b, rmat_d, wr=[b_st])
            DMA("sp", st_c, cs64_d, wr=[b_st])
            OP("dve", "tensor_copy", lw_b.rearrange("p a b -> p (a b)"), st_a, rd=[b_st], wr=[b_const])
            OP("dve", "tensor_copy", rmat_b, st_b, rd=[b_st], wr=[b_const])
            OP("dve", "tensor_copy", cs64_b, st_c, rd=[b_st], wr=[b_const])
            OP("act", "activation", scond.rearrange("p a b -> p (a b)"), sv[:, SV_MISC + 8: SV_MISC + 24], AF.Silu,
               rd=[b_const], wr=[b_const])

            def convert_weights(l, cast_engs, pwmax=2368, reset_to=None):
                if reset_to is not None:
                    AR.reset(reset_to)
                stf = [AR.alloc([pwmax], F32) for _ in range(2)]
                stb = [AR.alloc([pwmax], BF16) for _ in range(2)]
                bf = [Buf("stf0"), Buf("stf1")]
                bb = [Buf("stb0"), Buf("stb1")]
                jobs = []
                def split(c0, wtot, kind, base, align=1):
                    step = (pwmax // align) * align
                    o = 0
                    while o < wtot:
                        w_ = min(step, wtot - o)
                        jobs.append((c0 + o, w_, kind, base + o))
                        o += w_
                split(0, WA_W, "A", 0)
                split(WA_W + OFF_WOUT, 1024, "B", OFF_WOUT)
                split(WA_W + OFF_GU, 2816, "G0", 0, 256)
                split(WA_W + OFF_GU + 2816, 2816, "G1", 0, 256)
                split(WA_W + OFF_DN, 2816, "B", OFF_DN)
                k_ = 0
                for r in range(8):
                    for (c0, w_, kind, base) in jobs:
                        i = k_ % 2
                        eng = cast_engs[k_ % len(cast_engs)]
                        k_ += 1
                        DMA("sp", stf[i][:, 0:w_], wsl_d[l, r, :, c0:c0 + w_], wr=[bf[i]])
                        OP(eng, "tensor_copy", stb[i][:, 0:w_], stf[i][:, 0:w_], rd=[bf[i]], wr=[bb[i]])
                        if kind == "A":
                            DMA("pool", WGA[l][r * 128:(r + 1) * 128, base:base + w_], stb[i][:, 0:w_], rd=[bb[i]], wr=[b_wga[l]])
                        elif kind == "B":
                            DMA("pool", WGB[l][r * 128:(r + 1) * 128, base:base + w_], stb[i][:, 0:w_], rd=[bb[i]], wr=[b_wgb[l]])
                        else:
                            part = int(kind[1])
                            ns_ = w_ // 256
                            js0 = base // 256
                            DMA("pool", WGU[l][js0:js0 + ns_, :, r, part * 256:(part + 1) * 256].rearrange("s p n -> p s n"),
                                stb[i][:, 0:w_].rearrange("p (s n) -> p s n", n=256), rd=[bb[i]], wr=[b_wgb[l]])

            convert_weights(0, ["dve", "pool"], reset_to=P_SCR + 12 * 1024)
            S.barrier()
            check_stop("setup", [("wga", WGA[0][:, 0:2304]), ("wgb", WGB[0][:, OFF_DN:OFF_DN + 2816])])

            def rstd_from(ps_ap, pbuf, out_ap, obuf, scale, tmp_ap, tbuf):
                OP("act", "activation", tmp_ap, ps_ap, AF.Ln, bias=small[:, 0:1], scale=scale, rd=[pbuf, b_const], wr=[tbuf])
                OP("act", "activation", out_ap, tmp_ap, AF.Exp, scale=-0.5, rd=[tbuf], wr=[obuf])

            OP("pool", "memset", small[:, 0:1], EPS, wr=[b_const])
            OP("pool", "memset", small[:, 1:2], 1.0, wr=[b_const])

            m0 = sv[:, SV_MISC + 0:SV_MISC + 1]
            m1 = sv[:, SV_MISC + 1:SV_MISC + 2]

            def blend(dst, other, rd_dst, rd_other, wr_dst):
                OP("act", "activation", dst, dst, AF.Identity, scale=m0, rd=list(rd_dst) + [b_const], wr=wr_dst)
                OP("dve", "scalar_tensor_tensor", dst, other, m1, dst, ALU.mult, ALU.add,
                   rd=list(rd_other) + [b_const] + list(wr_dst), wr=wr_dst)

            for l in range(2):
                li = lambda_init(l)
                AR.reset(P_SCR)
                wm = [AR.alloc([8, 1536], BF16) for _ in range(2)]
                bwm = [Buf("wm0"), Buf("wm1")]
                modsb = AR.alloc([48, 2], F32)
                bmod_ = Buf("modsb")
                for sl in range(4):
                    i = sl % 2
                    DMA("sp", wm[i], WGA[l][:, 2304 + sl * 1536: 2304 + (sl + 1) * 1536].rearrange("(k p) n -> p k n", p=128),
                        rd=[b_wga[l]], wr=[bwm[i]])
                    for jj in range(12):
                        j = sl * 12 + jj
                        for kc in range(8):
                            MM(PS[0][:, 2 * j:2 * j + 2], wm[i][:, kc, jj * 128:(jj + 1) * 128], scond[:, kc, :],
                               kc == 0, kc == 7, rd=[bwm[i], b_const], wr=[PB[0]])
                check_stop("Ma%d" % l, [("wm0", wm[0]), ("wm1", wm[1])])
                psm = PS[0][:, 0:96].rearrange("p (a b) -> p a b", b=2)
                for w in range(2):
                    OP("dve", "tensor_tensor", modsb[:, :, w], psm[:, :, w], svc(l, "bmod", 0, 48), ALU.add,
                       rd=[PB[0], b_const], wr=[bmod_])
                for w in range(2):
                    c_ = cst[l][w]
                    OP("dve", "scalar_tensor_tensor", c_[:, 0, :], modsb[:, 8:16, w], 1.0, svc(l, "gpm", 0, 8), ALU.add, ALU.mult,
                       rd=[bmod_, b_const], wr=[b_cst])
                    OP("dve", "tensor_copy", c_[:, 1, :], modsb[:, 0:8, w], rd=[bmod_], wr=[b_cst])
                    OP("dve", "tensor_tensor", c_[:, 2, :], modsb[:, 16:24, w], svc(l, "gqm", 0, 8), ALU.mult,
                       rd=[bmod_, b_const], wr=[b_cst])
                    OP("dve", "scalar_tensor_tensor", c_[:, 3, :], modsb[:, 32:40, w], 1.0, svc(l, "gpf", 0, 8), ALU.add, ALU.mult,
                       rd=[bmod_, b_const], wr=[b_cst])
                    OP("dve", "tensor_copy", c_[:, 4, :], modsb[:, 24:32, w], rd=[bmod_], wr=[b_cst])
                    OP("dve", "tensor_tensor", c_[:, 5, :], modsb[:, 40:48, w], svc(l, "gqf", 0, 8), ALU.mult,
                       rd=[bmod_, b_const], wr=[b_cst])
                S.barrier()
                check_stop("M%d" % l, [("cst0", cst[l][0]), ("cst1", cst[l][1]), ("modsb", modsb)])

                def phaseA(k):
                    AR.reset(P_MLRU)
                    win = AR.alloc([8, 2304], BF16)
                    rc_t = AR.alloc([512], F32)
                    rs_t = AR.alloc([512], F32)
                    assert AR.cur <= P_Q
                    AR.reset(P_SCR)
                    b_win = Buf("win")
                    xblk = AR.alloc([8, 512], F32)
                    b_x = Buf("xblk")
                    xtm = [AR.alloc([1024], F32) for _ in range(4)]
                    b_xtm = [Buf("xtm%d" % i) for i in range(4)]
                    sq = AR.alloc([8, 512], BF16)
                    b_sq = Buf("sq")
                    u = AR.alloc([8, 512], BF16)
                    b_u = Buf("u")
                    rstd = AR.alloc([512], F32)
                    b_rstd = Buf("rstd")
                    tl = AR.alloc([512], F32)
                    b_tl = Buf("tl")
                    tmpf = [AR.alloc([512], F32) for _ in range(2)]
                    b_tmpf = [Buf("tmpf0"), Buf("tmpf1")]
                    qb_ = [AR.alloc([512], BF16) for _ in range(2)]
                    b_qb = [Buf("qb0"), Buf("qb1")]
                    t1_ = [AR.alloc([512], F32) for _ in range(2)]
                    t2_ = [AR.alloc([512], F32) for _ in range(2)]
                    b_t1 = [Buf("t1_0"), Buf("t1_1")]
                    b_t2 = [Buf("t2_0"), Buf("t2_1")]
                    stg = [AR.alloc([512], BF16) for _ in range(3)]
                    b_stg = [Buf("stg%d" % i) for i in range(3)]
                    stgf = [AR.alloc([512], F32) for _ in range(2)]
                    b_stgf = [Buf("stgf0"), Buf("stgf1")]
                    b_rt = Buf("rt")
                    for h_ in range(2):
                        DMA("sp", win[:, h_ * 4:(h_ + 1) * 4, :],
                            WGA[l][h_ * 512:(h_ + 1) * 512, 0:2304].rearrange("(k p) n -> p k n", p=128),
                            rd=[b_wga[l]], wr=[b_win])
                    cnt = {"pj": 0, "rot": 0, "tm": 0, "stg": 0, "stgf": 0, "tmpf": 0, "qb": 0}
                    DMA("sp", rope_c, rope_d[k, :, 0, :], wr=[b_rope])
                    DMA("sp", rope_s, rope_d[k, :, 1, :], wr=[b_rope])
                    b_exb_k = Buf("exbk")
                    exbk = EXG[l][k * 1536:(k + 1) * 1536, :]
                    for tb in range(5 if k == 0 else 4):
                        w = 0 if tb < 4 else 1
                        c_ = cst[l][w]
                        tok = slice(tb * 512, (tb + 1) * 512)
                        if l == 0:
                            src = xs_d[k] if tb < 4 else xp_d
                            r0 = tb * 512 if tb < 4 else 0
                            for tt in range(4):
                                DMA("sp", xtm[tt], src[r0 + tt * 128: r0 + (tt + 1) * 128, :], wr=[b_xtm[tt]])
                            for c in range(8):
                                pb = 1 + (c % 2)
                                for tt in range(4):
                                    OP("pe", "transpose", PS[pb][:, tt * 128:(tt + 1) * 128], xtm[tt][:, c * 128:(c + 1) * 128], ident,
                                       rd=[b_xtm[tt], b_const], wr=[PB[pb]])
                                OP("act" if c % 2 == 0 else "dve", "copy" if c % 2 == 0 else "tensor_copy", xblk[:, c, :], PS[pb],
                                   rd=[PB[pb]], wr=[b_x])
                            DMA("pool", XS[0][k][:, :, tok], xblk, rd=[b_x], wr=[b_xs[0][k][tb]])
                        else:
                            DMA("sp", xblk, XS[1][k][:, :, tok], rd=[b_xs[1][k][tb]], wr=[b_x])
                        if tb == 0 and k == 0:
                            check_stop("Ax%d" % l, [("xblk", xblk)])
                        OP("act", "activation", sq[:, 0:4, :], xblk[:, 0:4, :], AF.Square, rd=[b_x], wr=[b_sq])
                        OP("pool", "tensor_tensor", sq[:, 4:8, :], xblk[:, 4:8, :], xblk[:, 4:8, :], ALU.mult, rd=[b_x], wr=[b_sq])
                        for c in range(8):
                            MM(PS[0], ones_b, sq[:, c, :], c == 0, c == 7, rd=[b_sq, b_const], wr=[PB[0]])
                        rstd_from(PS[0], PB[0], rstd, b_rstd, 1.0 / D, tl, b_tl)
                        for c in range(8):
                            i = cnt["tmpf"] % 2
                            cnt["tmpf"] += 1
                            OP("dve", "scalar_tensor_tensor", tmpf[i], xblk[:, c, :], c_[:, 0, c:c + 1], rstd, ALU.mult, ALU.mult,
                               rd=[b_x, b_rstd, b_cst], wr=[b_tmpf[i]])
                            OP("act", "activation", u[:, c, :], tmpf[i], AF.Identity, bias=c_[:, 1, c:c + 1], scale=1.0,
                               rd=[b_tmpf[i], b_cst], wr=[b_u])

                        if tb == 0 and k == 0:
                            check_stop("An%d" % l, [("u", u), ("rstd", rstd)])

                        if tb < 4:
                            OP("act", "copy", rc_t, rope_c[:, tok], rd=[b_rope], wr=[b_rt])
                            OP("act", "copy", rs_t, rope_s[:, tok], rd=[b_rope], wr=[b_rt])

                        def proj_fm(j):
                            pb = 3 + (cnt["pj"] % 3)
                            cnt["pj"] += 1
                            for kc in range(8):
                                MM(PS[pb], win[:, kc, j * 128:(j + 1) * 128], u[:, kc, :], kc == 0, kc == 7,
                                   rd=[b_win, b_u], wr=[PB[pb]])
                            return pb

                        def proj_tm(tt, c0):
                            pb = 7
                            for kc in range(8):
                                MM(PS[pb], u[:, kc, tt * 128:(tt + 1) * 128], win[:, kc, c0:c0 + 512], kc == 0, kc == 7,
                                   rd=[b_win, b_u], wr=[PB[pb]])
                            return pb

                        def nstg():
                            i = cnt["stg"] % 3
                            cnt["stg"] += 1
                            return i

                        for j in range(18):
                            kind = ("q", "k", "v", "xr", "gr", "xf")[[0, 0, 0, 0, 1, 1, 1, 1, 2, 2, 2, 2, 3, 3, 4, 4, 5, 5][j]]
                            if kind == "v":
                                continue
                            pb = proj_fm(j)
                            dbg0 = (tb == 0 and k == 0 and l == 0 and j == 0)
                            if dbg0:
                                check_stop("As1", [("u", u)])
                            if kind in ("q", "k"):
                                h = j % 4
                                if tb < 4:
                                    i = cnt["qb"] % 2
                                    cnt["qb"] += 1
                                    OP("act", "copy", qb_[i], PS[pb], rd=[PB[pb]], wr=[b_qb[i]])
                                    if dbg0:
                                        check_stop("As2", [("qb", qb_[i])])
                                    MM(PS[6], rmat_b, qb_[i], True, True, rd=[b_qb[i], b_const], wr=[PB[6]])
                                    if dbg0:
                                        check_stop("As3", [("qb", qb_[i])])
                                    OP("dve", "tensor_tensor", t1_[i], PS[pb], rc_t, ALU.mult, rd=[PB[pb], b_rt], wr=[b_t1[i]])
                                    OP("dve", "tensor_tensor", t2_[i], PS[6], rs_t, ALU.mult, rd=[PB[6], b_rt], wr=[b_t2[i]])
                                    if dbg0:
                                        check_stop("As4", [("t1", t1_[i]), ("t2", t2_[i])])
                                    if kind == "q":
                                        OP("pool", "tensor_tensor", q_all[:, h, tok], t1_[i], t2_[i], ALU.add,
                                           rd=[b_t1[i], b_t2[i]], wr=[b_q[tb]])
                                    else:
                                        si = nstg()
                                        OP("pool", "tensor_tensor", stg[si], t1_[i], t2_[i], ALU.add,
                                           rd=[b_t1[i], b_t2[i]], wr=[b_stg[si]])
                                        DMA("pool", exbk[h * 128:(h + 1) * 128, tok], stg[si], rd=[b_stg[si]], wr=[b_exb_k])
                                else:
                                    if kind == "q":
                                        OP("act", "copy", q_all[:, h, tok], PS[pb], rd=[PB[pb]], wr=[b_q[tb]])
                                    else:
                                        OP("act", "copy", kTp[:, h, :], PS[pb], rd=[PB[pb]], wr=[b_prm])
                            elif kind == "gr":
                                OP("act", "activation", gr_all[:, j - 14, tok], PS[pb], AF.Gelu_apprx_tanh, rd=[PB[pb]], wr=[b_gr[tb]])
                            else:
                                c2 = j % 2
                                if tb < 4:
                                    si = nstg()
                                    OP("act", "copy", stg[si], PS[pb], rd=[PB[pb]], wr=[b_stg[si]])
                                    base = 512 if kind == "xr" else 768
                                    DMA("pool", exbk[base + c2 * 128: base + (c2 + 1) * 128, tok], stg[si], rd=[b_stg[si]], wr=[b_exb_k])
                                else:
                                    dst = xr_p if kind == "xr" else xf_p
                                    OP("act", "copy", dst[:, c2, :], PS[pb], rd=[PB[pb]], wr=[b_prm])
                            if tb == 0 and k == 0 and l == 0:
                                check_stop("Aj%d" % j, [("q", q_all[:, :, 0:512])])
                        for tt in range(4):
                            pb = proj_tm(tt, 1024)
                            if tb < 4:
                                si = nstg()
                                OP("dve", "tensor_copy", stg[si], PS[pb], rd=[PB[pb]], wr=[b_stg[si]])
                                vview = exbk[1024:1536, :].rearrange("r (t f) -> (r t) f", f=512)
                                vdst = vview[tb * 512 + tt * 128: tb * 512 + (tt + 1) * 128, :]
                                DMA("pool", vdst, stg[si], rd=[b_stg[si]], wr=[b_exb_k])
                            else:
                                seq, hh = tt // 2, tt % 2
                                fi = cnt["stgf"] % 2
                                cnt["stgf"] += 1
                                OP("dve", "tensor_copy", stgf[fi], PS[pb], rd=[PB[pb]], wr=[b_stgf[fi]])
                                OP("act", "copy", Vp[:, tt, :], PS[pb], rd=[PB[pb]], wr=[b_prm])
                                DMA("pool", nv_d[seq, l, hh * 128:(hh + 1) * 128, :], stgf[fi], rd=[b_stgf[fi]])
                                pb = proj_tm(tt, 512)
                                fi = cnt["stgf"] % 2
                                cnt["stgf"] += 1
                                OP("dve", "tensor_copy", stgf[fi], PS[pb], rd=[PB[pb]], wr=[b_stgf[fi]])
                                DMA("pool", nk_d[seq, l, hh * 128:(hh + 1) * 128, :], stgf[fi], rd=[b_stgf[fi]])
                        if tb == 0 and k == 0:
                            check_stop("Ap%d" % l, [("q", q_all), ("gr", gr_all)])
                    DMA("pool", QS[l][k], q_all[:, :, 0:TS], rd=b_q[0:4], wr=[b_qs[l][k]])
                    DMA("pool", GS[l][k], gr_all[:, :, 0:TS], rd=b_gr[0:4], wr=[b_qs[l][k]])
                    b_exg[l].w = None
                    S.barrier()
                    check_stop("A%d_%d" % (l, k), [("q", q_all), ("gr", gr_all), ("exg", EXG[l]), ("ktp", kTp), ("vp", Vp), ("xrp", xr_p), ("xs", XS[l][k])])

                def phaseB1(k, own=False):
                    AR.reset(P_SCR)
                    NT = 4608
                    xr_c = AR.alloc([NT], BF16)
                    xc = AR.alloc([NT], F32)
                    xcb = AR.alloc([NT], BF16)
                    HS = AR.alloc([NT], F32)
                    _sv = AR.cur
                    AR.reset(P_MFOUR)
                    HB = AR.alloc([2560], F32)
                    GR = AR.alloc([2560], F32)
                    assert AR.cur <= P_Q
                    AR.reset(_sv)
                    GI = AR.alloc([2560], F32)
                    GC = AR.alloc([2560], F32)
                    sp_ = AR.alloc([8], F32)
                    b_xr, b_xc, b_xcb, b_hs, b_hb = Buf("xr"), Buf("xc"), Buf("xcb"), Buf("hs"), Buf("hb")
                    b_g = Buf("gates")
                    b_sp = Buf("sp")
                    NTK = 4608 if k == 0 else 4096
                    seqs = [(0, 4096), (4096, 4352), (4352, 4608)] if k == 0 else [(0, 4096)]
                    _sv2 = AR.cur
                    AR.reset(P_MFOUR + 20480)
                    tmpA = AR.alloc([2048], BF16)
                    tmpB = AR.alloc([2048], BF16)
                    assert AR.cur <= P_Q
                    AR.reset(_sv2)
                    b_tA, b_tB = Buf("tA"), Buf("tB")
                    if own:
                        _sv3 = AR.cur
                        AR.reset(P_MFOUR + 20480)
                        tmpG = AR.alloc([2, TS], BF16)
                        assert AR.cur <= P_Q
                        AR.reset(_sv3)
                        b_tg = Buf("tmpG")
                        DMA("sp", gr_all[:, :, 0:TS], GS[l][0], rd=[b_qs[l][0]], wr=b_gr[0:4])
                        DMA("sp", tmpG, GS[l][1], rd=[b_qs[l][1]], wr=[b_tg])
                        blend(gr_all[:, :, 0:TS], tmpG, b_gr[0:4], [b_tg], b_gr[0:4])
                    else:
                        DMA("sp", gr_all[:, :, 0:TS], GS[l][k], rd=[b_qs[l][k]], wr=b_gr[0:4])
                    a0 = sv[:, SV_MISC + 0:SV_MISC + 1]
                    a1 = sv[:, SV_MISC + 1:SV_MISC + 2]
                    OP("act", "activation", sp_[:, 0:4], svc(l, "lam", 0, 4), AF.Exp, scale=-1.0, rd=[b_const], wr=[b_sp])
                    OP("act", "activation", sp_[:, 4:8], sp_[:, 0:4], AF.Ln, bias=small[:, 1:2], scale=1.0, rd=[b_sp, b_const], wr=[b_sp])
                    OP("dve", "tensor_scalar", sp_[:, 0:4], sp_[:, 4:8], -8.0, None, ALU.mult, rd=[b_sp], wr=[b_sp])
                    OP("dve", "tensor_scalar", sp_[:, 4:8], sp_[:, 4:8], -16.0, None, ALU.mult, rd=[b_sp], wr=[b_sp])
                    for c2 in range(2):
                        for r in range(2):
                            DMA("sp", xr_c[:, r * 2048:(r + 1) * 2048],
                                EXG[l][r * 1536 + 512 + c2 * 128: r * 1536 + 512 + (c2 + 1) * 128, :], wr=[b_xr])
                        if k == 0:
                            OP("pool", "tensor_copy", xr_c[:, 4096:4608], xr_p[:, c2, :], rd=[b_prm], wr=[b_xr])
                        cw = lambda t_: svc(l, "convw", t_ * 2 + c2)
                        OP("dve", "tensor_scalar", xc[:, 0:NTK], xr_c[:, 0:NTK], cw(2), svc(l, "convb", c2), ALU.mult, ALU.add,
                           rd=[b_xr, b_const], wr=[b_xc])
                        for (s0, s1) in seqs:
                            OP("dve", "scalar_tensor_tensor", xc[:, s0 + 2:s1], xr_c[:, s0:s1 - 2], cw(0), xc[:, s0 + 2:s1],
                               ALU.mult, ALU.add, rd=[b_xr, b_const, b_xc], wr=[b_xc])
                            OP("dve", "scalar_tensor_tensor", xc[:, s0 + 1:s1], xr_c[:, s0:s1 - 1], cw(1), xc[:, s0 + 1:s1],
                               ALU.mult, ALU.add, rd=[b_xr, b_const, b_xc], wr=[b_xc])
                            OP("dve", "scalar_tensor_tensor", xc[:, s0:s1 - 1], xr_c[:, s0 + 1:s1], cw(3), xc[:, s0:s1 - 1],
                               ALU.mult, ALU.add, rd=[b_xr, b_const, b_xc], wr=[b_xc])
                        OP("pool", "tensor_copy", xcb[:, 0:NTK], xc[:, 0:NTK], rd=[b_xc], wr=[b_xcb])
                        for d in range(2):
                            halves = [(0, 2048), (2048, NTK)]
                            if d == 1:
                                halves = halves[::-1]
                            col = d * 2 + c2
                            wa = lw_b[:, ((l * 2 + d) * 2 + 0) * 2 + c2, :]
                            wx = lw_b[:, ((l * 2 + d) * 2 + 1) * 2 + c2, :]
                            for (t0, t1) in halves:
                                n = t1 - t0
                                for s_ in range(n // 512):
                                    a0 = t0 + s_ * 512
                                    pa, pi = 1 + 2 * (s_ % 2), 2 + 2 * (s_ % 2)
                                    MM(PS[pa], wa, xcb[:, a0:a0 + 512], True, True, rd=[b_xcb, b_const], wr=[PB[pa]])
                                    MM(PS[pi], wx, xcb[:, a0:a0 + 512], True, True, rd=[b_xcb, b_const], wr=[PB[pi]])
                                    OP("act", "activation", GR[:, s_ * 512:(s_ + 1) * 512], PS[pa], AF.Sigmoid,
                                       bias=svc(l, "ba", col), scale=1.0, rd=[PB[pa], b_const], wr=[b_g])
                                    OP("act", "activation", GI[:, s_ * 512:(s_ + 1) * 512], PS[pi], AF.Sigmoid,
                                       bias=svc(l, "bx", col), scale=1.0, rd=[PB[pi], b_const], wr=[b_g])
                                OP("act", "activation", GC[:, 0:n], GR[:, 0:n], AF.Exp, scale=sp_[:, 4 + col:5 + col], rd=[b_g, b_sp], wr=[b_g])
                                OP("act", "activation", GR[:, 0:n], GR[:, 0:n], AF.Exp, scale=sp_[:, col:col + 1], rd=[b_g, b_sp], wr=[b_g])
                                OP("act", "activation", GC[:, 0:n], GC[:, 0:n], AF.Ln, bias=small[:, 1:2], scale=-1.0, rd=[b_g, b_const], wr=[b_g])
                                OP("act", "activation", GC[:, 0:n], GC[:, 0:n], AF.Exp, scale=0.5, rd=[b_g], wr=[b_g])
                                OP("dve", "tensor_tensor", GI[:, 0:n], GI[:, 0:n], xc[:, t0:t1], ALU.mult, rd=[b_g, b_xc], wr=[b_g])
                                OP("dve", "tensor_tensor", GI[:, 0:n], GI[:, 0:n], GC[:, 0:n], ALU.mult, rd=[b_g], wr=[b_g])
                                for (s0, s1) in seqs:
                                    lo, hi = max(s0, t0), min(s1, t1)
                                    if lo >= hi:
                                        continue
                                    if d == 0:
                                        if lo == s0:
                                            init = svc(l, "h0", col) if s0 == 0 else 0.0
                                        else:
                                            init = HS[:, lo - 1:lo]
                                        OP("dve", "tensor_tensor_scan", HS[:, lo:hi], GR[:, lo - t0:hi - t0], GI[:, lo - t0:hi - t0],
                                           init, ALU.mult, ALU.add, rd=[b_g, b_hs, b_const], wr=[b_hs])
                                    else:
                                        if hi == s1:
                                            init = svc(l, "h0", col) if s0 == 0 else 0.0
                                        else:
                                            init = small[:, 8:9]
                                        OP("dve", "tensor_tensor_scan", HB[:, lo - t0:hi - t0][:, ::-1], GR[:, lo - t0:hi - t0][:, ::-1],
                                           GI[:, lo - t0:hi - t0][:, ::-1], init, ALU.mult, ALU.add,
                                           rd=[b_g, b_hb, b_const], wr=[b_hb])
                                if d == 0 and k == 0 and t1 == 4608:
                                    for sq_ in range(2):
                                        e_ = 4096 + sq_ * 256 + 255
                                        OP("dve", "tensor_copy", nhst[:, (sq_ * 2 + 0) * 2 + c2:(sq_ * 2 + 0) * 2 + c2 + 1], HS[:, e_:e_ + 1],
                                           rd=[b_hs], wr=[b_nh])
                                if d == 1:
                                    if t1 == NTK:
                                        for sq_ in (range(2) if k == 0 else ()):
                                            e_ = 4096 + sq_ * 256 - t0
                                            OP("dve", "tensor_copy", nhst[:, (sq_ * 2 + 1) * 2 + c2:(sq_ * 2 + 1) * 2 + c2 + 1], HB[:, e_:e_ + 1],
                                               rd=[b_hb], wr=[b_nh])
                                        OP("dve", "tensor_copy", small[:, 8:9], HB[:, 0:1], rd=[b_hb], wr=[b_const])
                                    OP("dve", "tensor_tensor", HS[:, t0:t1], HS[:, t0:t1], HB[:, 0:n], ALU.add, rd=[b_hs, b_hb], wr=[b_hs])
                        if own:
                            OP("act", "activation", HB[:, 0:2048], HS[:, 0:2048], AF.Identity, scale=m0, rd=[b_hs, b_const, b_hb], wr=[b_hb])
                            OP("dve", "scalar_tensor_tensor", HB[:, 0:2048], HS[:, 2048:4096], m1, HB[:, 0:2048], ALU.mult, ALU.add,
                               rd=[b_hs, b_const, b_hb], wr=[b_hb])
                            OP("dve", "tensor_tensor", mix_lru[:, c2, 0:2048], HB[:, 0:2048], gr_all[:, c2, 0:2048], ALU.mult,
                               rd=[b_hb] + b_gr, wr=[b_mixl])
                        else:
                            OP("dve", "tensor_tensor", mix_lru[:, c2, 0:2048], HS[:, k * 2048:(k + 1) * 2048], gr_all[:, c2, 0:2048], ALU.mult,
                               rd=[b_hs] + b_gr, wr=[b_mixl])
                        if k == 0:
                            OP("dve", "tensor_tensor", mix_lru[:, c2, 2048:2560], HS[:, 4096:4608], gr_all[:, c2, 2048:2560], ALU.mult,
                               rd=[b_hs] + b_gr, wr=[b_mixl])
                    for sq_ in (range(2) if k == 0 else ()):
                        for d in range(2):
                            for c2 in range(2):
                                k_ = (sq_ * 2 + d) * 2 + c2
                                dst = bass.AP(nh_d.tensor, ((sq_ * 2 + l) * 2 + d) * 256 + c2 * 128, [[1, 128], [1, 1]])
                                DMA("pool", dst, nhst[:, k_:k_ + 1], rd=[b_nh])
                    S.barrier()
                    check_stop("B1_%d_%d" % (l, k), [("mixl", mix_lru), ("nhst", nhst), ("hs", HS), ("xc", xc), ("xrc", xr_c), ("hb", HB), ("svh0", sv[:, 359:363])])

                def phaseB2(k, own=False):
                    AR.reset(P_ATTN)
                    xf_c = AR.alloc([2, 4096], BF16)
                    assert AR.cur <= P_Q
                    AR.reset(P_SCR)
                    Ytm = AR.alloc([32, 512], BF16)
                    Ytp = AR.alloc([4, 512], BF16)
                    tabs = [[AR.alloc([4, 512], BF16) for _ in range(3)] for cs in range(2)]
                    b_xf, b_y, b_yp = Buf("xf"), Buf("ytm"), Buf("ytp")
                    b_tab = [[Buf("tab%d_%d" % (cs, i)) for i in range(3)] for cs in range(2)]
                    for c2 in range(2):
                        for r in range(2):
                            DMA("sp", xf_c[:, c2, r * 2048:(r + 1) * 2048],
                                EXG[l][r * 1536 + 768 + c2 * 128: r * 1536 + 768 + (c2 + 1) * 128, :], rd=[b_exg[l]], wr=[b_xf])
                    for tc in range(32):
                        pb = 1 + tc % 2
                        for c2 in range(2):
                            MM(PS[pb][:, c2 * 256:(c2 + 1) * 256], xf_c[:, c2, tc * 128:(tc + 1) * 128], cs64_b, True, True,
                               rd=[b_xf, b_const], wr=[PB[pb]])
                        OP("act", "activation", Ytm[:, tc, :], PS[pb], AF.Identity, scale=(sv[:, SV_MISC + 2:SV_MISC + 3] if own else sv[:, SV_MISC + 6 + k:SV_MISC + 7 + k]),
                           rd=[PB[pb], b_const], wr=[b_y])
                    for tc in (range(4) if k == 0 else ()):
                        pb = 1 + tc % 2
                        for c2 in range(2):
                            MM(PS[pb][:, c2 * 256:(c2 + 1) * 256], xf_p[:, c2, tc * 128:(tc + 1) * 128], cs64_b, True, True,
                               rd=[b_prm, b_const], wr=[PB[pb]])
                        OP("act", "copy", Ytp[:, tc, :], PS[pb], rd=[PB[pb]], wr=[b_yp])
                    kk = 0
                    for jb in range(4):
                        for tg in range(8):
                            i = kk % 3
                            kk += 1
                            for cs in range(2):
                                DMA("sp", tabs[cs][i],
                                    dft_d[cs, tg * 512:(tg + 1) * 512, jb * 512:(jb + 1) * 512].rearrange("(t p) j -> p t j", p=128),
                                    wr=[b_tab[cs][i]])
                            for c2 in range(2):
                                pb = 3 + c2
                                for t_ in range(4):
                                    for cs in range(2):
                                        tc = tg * 4 + t_
                                        MM(PS[pb], Ytm[:, tc, c2 * 256 + cs * 128: c2 * 256 + (cs + 1) * 128], tabs[cs][i][:, t_, :],
                                           tg == 0 and t_ == 0 and cs == 0, tg == 7 and t_ == 3 and cs == 1,
                                           rd=[b_y, b_tab[cs][i]], wr=[PB[pb]])
                        for c2 in range(2):
                            OP("dve", "tensor_tensor", mix_four.rearrange("p a b -> p (a b)")[:, c2 * T + jb * 512: c2 * T + (jb + 1) * 512], PS[3 + c2], csign, ALU.mult,
                               rd=[PB[3 + c2], b_const], wr=[b_mixf])
                    for sq_ in (range(2) if k == 0 else ()):
                        for c2 in range(2):
                            pb = 5 + c2
                            for t_ in range(2):
                                for cs in range(2):
                                    MM(PS[pb][:, 0:256], Ytp[:, sq_ * 2 + t_, c2 * 256 + cs * 128: c2 * 256 + (cs + 1) * 128],
                                       dftp_b[:, cs, t_, :], t_ == 0 and cs == 0, t_ == 1 and cs == 1,
                                       rd=[b_yp, b_const], wr=[PB[pb]])
                            OP("act", "copy", mix_four[:, c2, 2048 + sq_ * 256: 2048 + (sq_ + 1) * 256], PS[pb][:, 0:256],
                               rd=[PB[pb]], wr=[b_mixf])
                    S.barrier()
                    check_stop("B2_%d_%d" % (l, k), [("mixf", mix_four)])

                def phaseC(k, own=False):
                    AR.reset(P_SCR)
                    KT = [AR.alloc([4352], BF16) for _ in range(2)]
                    VH = [AR.alloc([34, 128], BF16) for _ in range(2)]
                    b_kt = [Buf("kt0"), Buf("kt1")]
                    b_vh = [Buf("vh0"), Buf("vh1")]
                    ckf = AR.alloc([2, 512], F32)
                    cvf = AR.alloc([2, 512], F32)
                    b_ckf, b_cvf = Buf("ckf"), Buf("cvf")
                    pTP = [AR.alloc([1024], BF16) for _ in range(2)]
                    pT = [pTP[0][:, 0:512], pTP[0][:, 512:1024], pTP[1][:, 0:512], pTP[1][:, 512:1024]]
                    b_pT = [Buf("pT%d" % i) for i in range(4)]
                    fR = [AR.alloc([512], F32) for _ in range(2)]
                    fT = [AR.alloc([512], F32) for _ in range(2)]
                    fo = AR.alloc([512], F32)
                    fsq = AR.alloc([512], BF16)
                    frs = AR.alloc([512], F32)
                    ftl = AR.alloc([512], F32)
                    b_fR = [Buf("fR0"), Buf("fR1")]
                    b_fT = [Buf("fT0"), Buf("fT1")]
                    b_fo, b_fsq, b_frs, b_ftl = Buf("fo"), Buf("fsq"), Buf("frs"), Buf("ftl")
                    lamv = AR.alloc([8], F32)
                    lprod = AR.alloc([128], F32)
                    b_lam = Buf("lam")
                    OP("dve", "tensor_tensor", lprod[:, 0:64], svc(l, "wl", 0, 64), svc(l, "wl", 64, 64), ALU.mult, rd=[b_const], wr=[b_lam])
                    OP("dve", "tensor_tensor", lprod[:, 64:128], svc(l, "wl", 128, 64), svc(l, "wl", 192, 64), ALU.mult, rd=[b_const], wr=[b_lam])
                    OP("dve", "reduce_sum", lamv[:, 0:1], lprod[:, 0:64], mybir.AxisListType.X, rd=[b_lam], wr=[b_lam])
                    OP("dve", "reduce_sum", lamv[:, 1:2], lprod[:, 64:128], mybir.AxisListType.X, rd=[b_lam], wr=[b_lam])
                    OP("act", "activation", lamv[:, 2:4], lamv[:, 0:2], AF.Exp, rd=[b_lam], wr=[b_lam])
                    OP("dve", "tensor_tensor", lamv[:, 4:5], lamv[:, 3:4], lamv[:, 2:3], ALU.subtract, rd=[b_lam], wr=[b_lam])
                    OP("dve", "tensor_scalar", lamv[:, 4:5], lamv[:, 4:5], -li, None, ALU.add, rd=[b_lam], wr=[b_lam])
                    OP("dve", "tensor_scalar", lamv[:, 5:6], svc(l, "gsub"), 1.0 - li, None, ALU.mult, rd=[b_const], wr=[b_lam])
                    if own:
                        tmpQ = AR.alloc([2, TS], BF16)
                        b_tq = Buf("tmpQ")
                        DMA("act", q_all[:, :, 0:TS], QS[l][0], rd=[b_qs[l][0]], wr=b_q[0:4])
                        for hp in range(2):
                            DMA("act", tmpQ, QS[l][1][:, hp * 2:(hp + 1) * 2, :], rd=[b_qs[l][1]], wr=[b_tq])
                            blend(q_all[:, hp * 2:(hp + 1) * 2, 0:TS], tmpQ, b_q[0:4], [b_tq], b_q[0:4])
                    else:
                        DMA("act", q_all[:, :, 0:TS], QS[l][k], rd=[b_qs[l][k]], wr=b_q[0:4])
                    DMA("act", ckf, ck_d[l].rearrange("(t p) f -> p t f", p=128), wr=[b_ckf])
                    DMA("act", cvf, cv_d[l].rearrange("(t p) f -> p t f", p=128), wr=[b_cvf])
                    if l == 0 and k == 0:
                        convert_weights(1, ["pool"], pwmax=1024)
                    scale = 64 ** -0.5
                    state = {"pt": 0}

                    def attn_block(q_ap, nq, kt_ap, v_fn, nkc, rd_q, rd_k, rd_v, out_ap, wr_out):
                        def qk(kc):
                            par = kc % 2
                            for m in range(2):
                                sb = par * 2 + m
                                MM(PS[sb][:, 0:nq], kt_ap[m * 64:(m + 1) * 64, kc * 128:(kc + 1) * 128], q_ap[m * 64:(m + 1) * 64, :],
                                   True, True, rd=rd_q + rd_k, wr=[PB[sb]])
                                if nq != 512:
                                    OP("act", "activation", pT[sb][:, 0:nq], PS[sb][:, 0:nq], AF.Exp, scale=scale, rd=[PB[sb]], wr=[b_pT[sb]])
                            if nq == 512:
                                OP("act", "activation", pTP[par], PSP[par], AF.Exp, scale=scale,
                                   rd=[PB[par * 2], PB[par * 2 + 1]], wr=[b_pT[par * 2], b_pT[par * 2 + 1]])

                        def pv(kc):
                            par = kc % 2
                            for m in range(2):
                                sb = par * 2 + m
                                MM(PS[4 + m][:, 0:nq], v_fn(kc), pT[sb][:, 0:nq], kc == 0, kc == nkc - 1, rd=rd_v + [b_pT[sb]], wr=[PB[4 + m]])
                                MM(PS[6 + m][:, 0:nq], ones_b, pT[sb][:, 0:nq], kc == 0, kc == nkc - 1, rd=[b_pT[sb], b_const], wr=[PB[6 + m]])

                        qk(0)
                        for kc in range(nkc):
                            if kc + 1 < nkc:
                                qk(kc + 1)
                            pv(kc)
                        for m in range(2):
                            OP("dve", "reciprocal", fR[m][:, 0:nq], PS[6 + m][:, 0:nq], rd=[PB[6 + m]], wr=[b_fR[m]])
                            OP("dve", "tensor_tensor", fT[m][:, 0:nq], PS[4 + m][:, 0:nq], fR[m][:, 0:nq], ALU.mult, rd=[PB[4 + m], b_fR[m]], wr=[b_fT[m]])
                        OP("dve", "scalar_tensor_tensor", fo[:, 0:nq], fT[1][:, 0:nq], lamv[:, 4:5], fT[0][:, 0:nq], ALU.mult, ALU.add,
                           rd=[b_fT[0], b_fT[1], b_lam], wr=[b_fo])
                        OP("act", "activation", fsq[:, 0:nq], fo[:, 0:nq], AF.Square, rd=[b_fo], wr=[b_fsq])
                        MM(PS[0][:, 0:nq], ones_b, fsq[:, 0:nq], True, True, rd=[b_fsq, b_const], wr=[PB[0]])
                        rstd_from(PS[0][:, 0:nq], PB[0], frs[:, 0:nq], b_frs, 1.0 / 128, ftl[:, 0:nq], b_ftl)
                        OP("dve", "scalar_tensor_tensor", out_ap, fo[:, 0:nq], lamv[:, 5:6], frs[:, 0:nq], ALU.mult, ALU.mult,
                           rd=[b_fo, b_frs, b_lam], wr=wr_out)

                    for h in range(4):
                        i = h % 2
                        for r in range(2):
                            DMA("act", KT[i][:, r * 2048:(r + 1) * 2048], EXG[l][r * 1536 + h * 128: r * 1536 + (h + 1) * 128, :],
                                rd=[b_exg[l]], wr=[b_kt[i]])
                            vview = EXG[l][r * 1536 + 1024: r * 1536 + 1536, :].rearrange("r (t f) -> (r t) f", f=512)
                            vsrc = vview[:, h * 128:(h + 1) * 128].rearrange("(c p) e -> p c e", p=128)
                            DMA("act", VH[i][:, r * 16:(r + 1) * 16, :], vsrc, rd=[b_exg[l]], wr=[b_vh[i]])
                        for t_ in range(2):
                            OP("pe", "transpose", PS[0][:, t_ * 128:(t_ + 1) * 128], ckf[:, t_, h * 128:(h + 1) * 128], ident,
                               rd=[b_ckf, b_const], wr=[PB[0]])
                        OP("act", "copy", KT[i][:, 4096:4352], PS[0][:, 0:256], rd=[PB[0]], wr=[b_kt[i]])
                        OP("dve", "tensor_copy", VH[i][:, 32:34, :], cvf[:, :, h * 128:(h + 1) * 128], rd=[b_cvf], wr=[b_vh[i]])
                        for qb in range(4):
                            attn_block(q_all[:, h, qb * 512:(qb + 1) * 512], 512, KT[i], lambda kc, i=i: VH[i][:, kc, :], 34,
                                       [b_q[qb]], [b_kt[i]], [b_vh[i]], mix_attn[:, h, qb * 512:(qb + 1) * 512], [b_mixa[qb]])
                        for sq_ in (range(2) if k == 0 else ()):
                            attn_block(q_all[:, h, 2048 + sq_ * 256: 2048 + (sq_ + 1) * 256], 256, kTp[:, h, sq_ * 256:(sq_ + 1) * 256],
                                       lambda kc, sq_=sq_, h=h: Vp[:, sq_ * 2 + kc, h * 128:(h + 1) * 128], 2,
                                       [b_q[4]], [b_prm], [b_prm], mix_attn[:, h, 2048 + sq_ * 256: 2048 + (sq_ + 1) * 256], [b_mixa[4]])
                    S.barrier()
                    check_stop("C%d_%d" % (l, k), [("mixf", mix_four), ("xfp", xf_p), ("dftp", dftp_b), ("cs64", cs64_b)])

                def phaseD(k, own=False):
                    AR.reset(P_Q)
                    wo = AR.alloc([8, 1024], BF16)
                    b_wo = Buf("wo")
                    xblk = AR.alloc([8, 512], F32)
                    b_x = Buf("xblk")
                    osb = AR.alloc([8, 512], F32)
                    b_o = Buf("osb")
                    sq = AR.alloc([8, 512], BF16)
                    b_sq = Buf("sq")
                    u = sq
                    b_u = b_sq
                    hid = AR.alloc([22, 512], BF16)
                    b_hid = Buf("hid")
                    gu = [AR.alloc([8, 512], BF16) for _ in range(2)]
                    b_gu = [Buf("gu0"), Buf("gu1")]
                    dn = [AR.alloc([22, 128], BF16) for _ in range(2)]
                    b_dn = [Buf("dn0"), Buf("dn1")]
                    rstd = AR.alloc([512], F32)
                    tl = AR.alloc([512], F32)
                    b_rstd, b_tl = Buf("rstd"), Buf("tl")
                    tmpf = [AR.alloc([512], F32) for _ in range(2)]
                    b_tmpf = [Buf("tmpf0"), Buf("tmpf1")]
                    sg = [AR.alloc([512], F32) for _ in range(2)]
                    b_sg = [Buf("sg0"), Buf("sg1")]
                    ytm = [osb[:, 0:2, :].rearrange("p a b -> p (a b)"), osb[:, 2:4, :].rearrange("p a b -> p (a b)")]
                    b_ytm = [b_o, b_o]
                    for h_ in range(2):
                        DMA("sp", wo[:, h_ * 4:(h_ + 1) * 4, :],
                            WGB[l][h_ * 512:(h_ + 1) * 512, OFF_WOUT:OFF_WOUT + 1024].rearrange("(k p) n -> p k n", p=128),
                            rd=[b_wgb[l]], wr=[b_wo])
                    cnt = {"o": 0, "tmpf": 0, "gu": 0, "dn": 0, "sg": 0, "y": 0}

                    def mixk(kc, tok):
                        if kc < 4:
                            return mix_attn[:, kc, tok]
                        if kc < 6:
                            return mix_lru[:, kc - 4, tok]
                        return mix_four[:, kc - 6, tok]

                    def post_norm_update(gidx, c_):
                        OP("act", "activation", sq[:, 0:4, :], osb[:, 0:4, :], AF.Square, rd=[b_o], wr=[b_sq])
                        OP("pool", "tensor_tensor", sq[:, 4:8, :], osb[:, 4:8, :], osb[:, 4:8, :], ALU.mult, rd=[b_o], wr=[b_sq])
                        for c in range(8):
                            MM(PS[0], ones_b, sq[:, c, :], c == 0, c == 7, rd=[b_sq, b_const], wr=[PB[0]])
                        rstd_from(PS[0], PB[0], rstd, b_rstd, 1.0 / D, tl, b_tl)
                        for c in range(8):
                            i = cnt["tmpf"] % 2
                            cnt["tmpf"] += 1
                            OP("dve", "tensor_tensor", tmpf[i], osb[:, c, :], rstd, ALU.mult, rd=[b_o, b_rstd], wr=[b_tmpf[i]])
                            OP("dve", "scalar_tensor_tensor", xblk[:, c, :], tmpf[i], c_[:, gidx, c:c + 1], xblk[:, c, :], ALU.mult, ALU.add,
                               rd=[b_tmpf[i], b_cst, b_x], wr=[b_x])

                    for tb in range(5 if k == 0 else 4):
                        w = 0 if tb < 4 else 1
                        c_ = cst[l][w]
                        tok = slice(tb * 512, (tb + 1) * 512)
                        DMA("sp", xblk, XS[l][k][:, :, tok], rd=[b_xs[l][k][tb]], wr=[b_x])
                        if own and tb < 4:
                            OP("act", "activation", xblk, xblk, AF.Identity, scale=m0, rd=[b_x, b_const], wr=[b_x])
                            for c in range(8):
                                i = cnt["tmpf"] % 2
                                cnt["tmpf"] += 1
                                DMA("sp", tmpf[i], XS[l][1][:, c, tok], rd=[b_xs[l][1][tb]], wr=[b_tmpf[i]])
                                OP("dve", "scalar_tensor_tensor", xblk[:, c, :], tmpf[i], m1, xblk[:, c, :], ALU.mult, ALU.add,
                                   rd=[b_tmpf[i], b_const, b_x], wr=[b_x])
                        for c in range(8):
                            pb = 1 + cnt["o"] % 2
                            cnt["o"] += 1
                            for kc in range(8):
                                MM(PS[pb], wo[:, kc, c * 128:(c + 1) * 128], mixk(kc, tok), kc == 0, kc == 7,
                                   rd=[b_wo, b_mixa[tb], b_mixl, b_mixf], wr=[PB[pb]])
                            OP("act", "copy", osb[:, c, :], PS[pb], rd=[PB[pb]], wr=[b_o])
                        post_norm_update(2, c_)
                        OP("act", "activation", sq[:, 0:4, :], xblk[:, 0:4, :], AF.Square, rd=[b_x], wr=[b_sq])
                        OP("pool", "tensor_tensor", sq[:, 4:8, :], xblk[:, 4:8, :], xblk[:, 4:8, :], ALU.mult, rd=[b_x], wr=[b_sq])
                        for c in range(8):
                            MM(PS[0], ones_b, sq[:, c, :], c == 0, c == 7, rd=[b_sq, b_const], wr=[PB[0]])
                        rstd_from(PS[0], PB[0], rstd, b_rstd, 1.0 / D, tl, b_tl)
                        for c in range(8):
                            i = cnt["tmpf"] % 2
                            cnt["tmpf"] += 1
                            OP("dve", "scalar_tensor_tensor", tmpf[i], xblk[:, c, :], c_[:, 3, c:c + 1], rstd, ALU.mult, ALU.mult,
                               rd=[b_x, b_rstd, b_cst], wr=[b_tmpf[i]])
                            OP("act", "activation", u[:, c, :], tmpf[i], AF.Identity, bias=c_[:, 4, c:c + 1], scale=1.0,
                               rd=[b_tmpf[i], b_cst], wr=[b_u])
                        for js in range(11):
                            i = cnt["gu"] % 2
                            cnt["gu"] += 1
                            DMA("sp", gu[i], WGU[l][js], rd=[b_wgb[l]], wr=[b_gu[i]])
                            for jj in range(2):
                                j = js * 2 + jj
                                pg, pu = 3 + (j % 2), 5 + (j % 2)
                                for kc in range(8):
                                    MM(PS[pg], gu[i][:, kc, jj * 128:(jj + 1) * 128], u[:, kc, :], kc == 0, kc == 7, rd=[b_gu[i], b_u], wr=[PB[pg]])
                                for kc in range(8):
                                    MM(PS[pu], gu[i][:, kc, 256 + jj * 128: 256 + (jj + 1) * 128], u[:, kc, :], kc == 0, kc == 7,
                                       rd=[b_gu[i], b_u], wr=[PB[pu]])
                                si = cnt["sg"] % 2
                                cnt["sg"] += 1
                                OP("act", "activation", sg[si], PS[pg], AF.Silu, rd=[PB[pg]], wr=[b_sg[si]])
                                OP("dve", "tensor_tensor", hid.rearrange("p a b -> p (a b)")[:, j * 512:(j + 1) * 512], PS[pu], sg[si], ALU.mult, rd=[PB[pu], b_sg[si]], wr=[b_hid])
                        for c in range(8):
                            i = cnt["dn"] % 2
                            cnt["dn"] += 1
                            DMA("sp", dn[i], WGB[l][c * 128:(c + 1) * 128, OFF_DN:OFF_DN + 2816].rearrange("p (k j) -> p k j", j=128),
                                rd=[b_wgb[l]], wr=[b_dn[i]])
                            pb = 1 + cnt["o"] % 2
                            cnt["o"] += 1
                            for kc in range(22):
                                MM(PS[pb], dn[i][:, kc, :], hid[:, kc, :], kc == 0, kc == 21, rd=[b_dn[i], b_hid], wr=[PB[pb]])
                            OP("act", "copy", osb[:, c, :], PS[pb], rd=[PB[pb]], wr=[b_o])
                        post_norm_update(5, c_)
                        if l == 0:
                            DMA("pool", XS[1][k][:, :, tok], xblk, rd=[b_x], wr=[b_xs[1][k][tb]])
                        else:
                            dst = ys_d if tb < 4 else yp_d
                            r0 = tb * 512 if tb < 4 else 0
                            for tt in range(4):
                                yi = cnt["y"] % 2
                                cnt["y"] += 1
                                for g in range(2):
                                    pb = 7 if g == 0 else 0
                                    for cc in range(4):
                                        c = g * 4 + cc
                                        OP("pe", "transpose", PS[pb][:, cc * 128:(cc + 1) * 128], xblk[:, c, tt * 128:(tt + 1) * 128], ident,
                                           rd=[b_x, b_const], wr=[PB[pb]])
                                    OP("act" if g == 0 else "dve", "copy" if g == 0 else "tensor_copy",
                                       ytm[yi][:, g * 512:(g + 1) * 512], PS[pb], rd=[PB[pb]], wr=[b_ytm[yi]])
                                DMA("pool", dst[r0 + tt * 128: r0 + (tt + 1) * 128, :], ytm[yi], rd=[b_ytm[yi]])
                    S.barrier()
                    check_stop("D%d_%d" % (l, k), [("xs1", XS[1][k])])
                phaseA(0)
                phaseA(1)
                if l == 0:
                    for k in (0, 1):
                        phaseB1(k)
                        phaseB2(k)
                        phaseC(k)
                        phaseD(k)
                else:
                    phaseB1(0, True)
                    phaseB2(0, True)
                    phaseC(0, True)
                    phaseD(0, True)
        except _Stop:
            pass

        S.final_wait("sp")
        with nc.Block() as block:
            S.emit(block)
    nc._taps = TAPS
    return nc


def _fm(v):
    v = np.asarray(v, np.float32)
    return np.ascontiguousarray(v.reshape(-1, 128).T)


_CONST_CACHE = {}


def _constants():
    if _CONST_CACHE:
        return _CONST_CACHE
    t = np.arange(4096, dtype=np.float64)[:, None]
    j = np.arange(2048, dtype=np.float64)[None, :]
    ang = 2.0 * np.pi * ((t * j) % 4096) / 4096.0
    s = 1.0 / math.sqrt(4096 * 64)
    dft = np.stack([np.cos(ang) * s, -np.sin(ang) * s]).astype(np.float32).astype(ml_dtypes.bfloat16)
    t = np.arange(256, dtype=np.float64)[:, None]
    j = np.arange(256, dtype=np.float64)[None, :]
    ang = 2.0 * np.pi * ((t * j) % 256) / 256.0
    s = 1.0 / math.sqrt(256 * 64)
    dftp = np.stack([np.cos(ang) * s, -np.sin(ang) * s]).astype(np.float32).astype(ml_dtypes.bfloat16)
    c = np.arange(64, dtype=np.float64)
    a64 = 2.0 * np.pi * np.outer(c, c) / 64.0
    cs64 = np.zeros((128, 256), np.float32)
    for g in range(2):
        cs64[g * 64:(g + 1) * 64, g * 64:(g + 1) * 64] = np.cos(a64)
        cs64[g * 64:(g + 1) * 64, 128 + g * 64:128 + (g + 1) * 64] = np.sin(a64)
    rmat = np.zeros((128, 128), np.float32)
    for dd in range(128):
        partner = dd + 16 if (dd % 32) < 16 else dd - 16
        rmat[partner, dd] = 1.0
    _CONST_CACHE.update(dft=dft, dftp=dftp, cs64=cs64, rmat=rmat)
    return _CONST_CACHE


def _rope_table(hf):
    n = 16
    inv = (10000.0 ** (-np.arange(n, dtype=np.float32) / n)).astype(np.float32)
    tpos = np.arange(hf * 2048, (hf + 1) * 2048)
    row = (tpos // 64).astype(np.float32)
    col = (tpos % 64).astype(np.float32)
    tab = np.zeros((128, 2, 2048), np.float32)
    for p in range(128):
        dd = p % 64
        pos = row if dd < 32 else col
        i = dd % 16
        ang = (pos * inv[i]).astype(np.float32)
        sign = -1.0 if (dd % 32) < 16 else 1.0
        tab[p, 0] = np.cos(ang)
        tab[p, 1] = sign * np.sin(ang)
    return tab


_NC = None
_DEBUG = {}


def kernel(x_prompt, x_sample, cache_k, cache_v, state_lru, c, c_ctx,
           w_mod, b_mod, g_pre_mix, g_post_mix, g_pre_ffn, g_post_ffn,
           w_in, w_out, w_lambda, g_subln, conv_w, conv_b,
           lru_wa, lru_ba, lru_wx, lru_bx, lru_lambda, w_gate_up, w_down):
    global _NC
    f = lambda a: np.asarray(a, np.float32)
    x_prompt, x_sample, cache_k, cache_v, state_lru, c, c_ctx = map(f, (x_prompt, x_sample, cache_k, cache_v, state_lru, c, c_ctx))
    w_mod, b_mod, w_in, w_out, w_gate_up, w_down = map(f, (w_mod, b_mod, w_in, w_out, w_gate_up, w_down))
    lru_wa, lru_wx = f(lru_wa), f(lru_wx)
    K = _constants()
    if _NC is None:
        _NC = build_program()
    nc = _NC
    lw = np.zeros((128, 16, 128), np.float32)
    for l in range(2):
        for d in range(2):
            for g, W in enumerate((lru_wa, lru_wx)):
                for c2 in range(2):
                    idx = ((l * 2 + d) * 2 + g) * 2 + c2
                    for bb in range(2):
                        lw[bb * 64:(bb + 1) * 64, idx, bb * 64:(bb + 1) * 64] = W[l, d, c2 * 2 + bb]
    lw = lw.reshape(128, 2048)
    wsl = np.empty((2, 8, 128, WA_W + WB_W), np.float32)
    for l in range(2):
        for r in range(8):
            rows = slice(r * 128, (r + 1) * 128)
            wsl[l, r, :, 0:2304] = w_in[l, rows, :]
            wsl[l, r, :, 2304:8448] = w_mod[l, rows, :]
            wsl[l, r, :, 8448:9472] = w_out[l, rows, :]
            wsl[l, r, :, 9472:15104] = w_gate_up[l, rows, :]
            wd = w_down[l][:, r * 128:(r + 1) * 128]
            wsl[l, r, :, 15104:17920] = wd.reshape(22, 128, 128).transpose(1, 0, 2).reshape(128, 2816)
    in_maps = []
    cores = _DEBUG.get("cores", list(range(8)))
    for core in cores:
        p, hf = core // 2, core % 2
        sv = np.zeros((128, NSV), np.float32)
        for l in range(2):
            def put(name, arr):
                o, w = SVL[name]
                sv[:, l * SV_PER_L + o: l * SV_PER_L + o + w] = arr
            put("gpm", _fm(g_pre_mix[l]))
            put("gqm", _fm(g_post_mix[l]))
            put("gpf", _fm(g_pre_ffn[l]))
            put("gqf", _fm(g_post_ffn[l]))
            put("bmod", _fm(b_mod[l]))
            put("gsub", f(g_subln[l]).reshape(128, 1))
            cw = f(conv_w[l])
            put("convw", np.stack([_fm(cw[t_]) for t_ in range(4)], axis=1).reshape(128, 8))
            put("convb", _fm(conv_b[l]))
            put("ba", np.concatenate([_fm(f(lru_ba)[l, d]) for d in range(2)], axis=1))
            put("bx", np.concatenate([_fm(f(lru_bx)[l, d]) for d in range(2)], axis=1))
            put("lam", np.concatenate([_fm(f(lru_lambda)[l, d]) for d in range(2)], axis=1))
            put("wl", np.broadcast_to(f(w_lambda[l]).reshape(1, 256), (128, 256)))
            put("h0", np.concatenate([_fm(state_lru[p, l, d]) for d in range(2)], axis=1))
        par = (np.arange(128) % 2 == 1)
        sv[:, SV_MISC + 0] = 1.0 - hf
        sv[:, SV_MISC + 1] = float(hf)
        for k in range(2):
            o = k
            sv[:, SV_MISC + 2 + 2 * k] = 1.0 - o
            sv[:, SV_MISC + 3 + 2 * k] = float(o)
            sv[:, SV_MISC + 6 + k] = np.where(par & (o == 1), -1.0, 1.0)
        sv[:, SV_MISC + 2] = np.where(par & (hf == 1), -1.0, 1.0)
        cond = np.stack([_fm(c[p]), _fm(c_ctx)], axis=2)
        sv[:, SV_MISC + 8: SV_MISC + 24] = cond.reshape(128, 16)
        csign = np.ones((128, 512), np.float32)
        if hf == 1:
            csign[:, 1::2] = -1.0
        xs2 = np.stack([x_sample[p, 0:2048, :], x_sample[p, 2048:4096, :]])
        in_maps.append({
            "xs": np.ascontiguousarray(xs2),
            "xp": np.ascontiguousarray(x_prompt[2 * core:2 * core + 2].reshape(512, 1024)),
            "ck": np.ascontiguousarray(cache_k[p].reshape(2, 256, 512)),
            "cv": np.ascontiguousarray(cache_v[p].reshape(2, 256, 512)),
            "sv": sv, "lw": lw, "wsl": wsl,
            "rope": np.stack([_rope_table(0), _rope_table(1)]), "csign": np.ones((128, 512), np.float32),
            "rmat": K["rmat"], "cs64": K["cs64"],
            "dft": K["dft"], "dftp": K["dftp"],
        })
    if _DEBUG.get("stop") is not None:
        nc = build_program(stop=_DEBUG["stop"])
        res = run_bass_kernel_spmd(nc, in_maps, core_ids=list(range(len(cores))), trace=bool(_DEBUG.get("trace")))
        _DEBUG["exec_ns"] = getattr(res, "exec_time_ns", None)
        _DEBUG["results"] = res.results
        _DEBUG["taps"] = nc._taps
    else:
        res = run_bass_kernel_spmd(nc, in_maps, core_ids=list(range(8)))
    R = res.results
    y_prompt = np.empty((16, 256, 1024), np.float32)
    y_sample = np.empty((4, 4096, 1024), np.float32)
    new_k = np.empty((16, 2, 256, 4, 2, 64), np.float32)
    new_v = np.empty((16, 2, 256, 4, 128), np.float32)
    new_h = np.empty((16, 2, 2, 256), np.float32)
    for core in range(8):
        p, hf = core // 2, core % 2
        r = R[core]
        y_sample[p, hf * 2048:(hf + 1) * 2048] = np.asarray(r["ys"])
        y_prompt[2 * core:2 * core + 2] = np.asarray(r["yp"]).reshape(2, 256, 1024)
        new_k[2 * core:2 * core + 2] = np.asarray(r["nk"]).reshape(2, 2, 256, 4, 2, 64)
        new_v[2 * core:2 * core + 2] = np.asarray(r["nv"]).reshape(2, 2, 256, 4, 128)
        new_h[2 * core:2 * core + 2] = np.asarray(r["nh"]).reshape(2, 2, 2, 256)
    return (y_prompt, y_sample, new_k, new_v, new_h)
```

```python
import math
from contextlib import ExitStack
import numpy as np
import ml_dtypes
import concourse.bass as bass
import concourse.mybir as mybir
from concourse.bass_utils import run_bass_kernel_spmd

F32 = mybir.dt.float32
BF16 = mybir.dt.bfloat16
ALU = mybir.AluOpType
AF = mybir.ActivationFunctionType

D = 1024
TS = 2048
TP = 512
T = 2560
EPS = 1e-6
WA_W = 8448
WB_W = 9472
OFF_WOUT, OFF_GU, OFF_DN = 0, 1024, 6656

SVL = {}
_o = 0
for _n, _w in (("gpm", 8), ("gqm", 8), ("gpf", 8), ("gqf", 8), ("bmod", 48), ("gsub", 1), ("convw", 8),
               ("convb", 2), ("ba", 4), ("bx", 4), ("lam", 4), ("wl", 256), ("h0", 4)):
    SVL[_n] = (_o, _w)
    _o += _w
SV_PER_L = _o
SV_MISC = 2 * SV_PER_L
NSV = SV_MISC + 8 + 16


def lambda_init(l):
    return 0.8 - 0.6 * math.exp(-0.3 * l)


class Buf:
    __slots__ = ("name", "w", "r")

    def __init__(self, name):
        self.name = name
        self.w = None
        self.r = []


class Sched:
    ENGS = ("pe", "act", "dve", "pool", "sp")
    NDMA = 16

    def __init__(self, nc, es):
        self.nc = nc
        self.q = {e: [] for e in self.ENGS}
        self.sem = {e: es.enter_context(nc.semaphore("c_" + e)) for e in self.ENGS}
        self.cnt = {e: 0 for e in self.ENGS}
        self.known = {e: {} for e in self.ENGS}
        self.dsem, self.dcnt, self.dnext, self.dlast = {}, {}, {}, {}
        for e in ("sp", "pool", "act"):
            self.dsem[e] = [es.enter_context(nc.semaphore("d_%s%d" % (e, i))) for i in range(self.NDMA)]
            self.dcnt[e] = [0] * self.NDMA
            self.dlast[e] = [None] * self.NDMA
            self.dnext[e] = 0

    def _deps(self, reads, writes):
        deps = []
        for b in reads:
            if b.w is not None:
                deps.append(b.w)
        for b in writes:
            if b.w is not None:
                deps.append(b.w)
            deps.extend(b.r)
        return deps

    def _emit_waits(self, eng, deps, skip_own=False):
        kn = self.known[eng]
        best = {}
        for (s, v, key) in deps:
            if skip_own and key == ("c", eng):
                continue
            if kn.get(key, 0) >= v:
                continue
            if best.get(key, (None, 0))[1] < v:
                best[key] = (s, v)
        for key, (s, v) in best.items():
            kn[key] = v
            self.q[eng].append(lambda e, s=s, v=v: e.wait_ge(s, v))

    def _commit(self, ev, reads, writes):
        for b in reads:
            b.r.append(ev)
            if len(b.r) > 64:
                last = {}
                for x in b.r:
                    if last.get(x[2], (None, 0, None))[1] < x[1]:
                        last[x[2]] = x
                b.r = list(last.values())
        for b in writes:
            b.w = ev
            b.r = []

    def op(self, eng, fn, reads=(), writes=()):
        writes = list(writes) + [b for b in reads if b.name.startswith("ps") and b not in writes]
        deps = self._deps(reads, writes)
        self._emit_waits(eng, deps, skip_own=(eng == "pe"))
        self.cnt[eng] += 1
        s = self.sem[eng]
        ev = (s, self.cnt[eng], ("c", eng))
        self.q[eng].append(lambda e, fn=fn, s=s: fn(e).then_inc(s, 1))
        self._commit(ev, reads, writes)
        return ev

    def dma(self, eng, fn, reads=(), writes=()):
        i = self.dnext[eng]
        self.dnext[eng] = (i + 1) % self.NDMA
        deps = self._deps(reads, writes)
        if self.dlast[eng][i] is not None:
            deps.append(self.dlast[eng][i])
        self._emit_waits(eng, deps)
        self.dcnt[eng][i] += 16
        s = self.dsem[eng][i]
        ev = (s, self.dcnt[eng][i], ("d", eng, i))
        self.dlast[eng][i] = ev
        self.q[eng].append(lambda e, fn=fn, s=s: fn(e).then_inc(s, 16))
        self._commit(ev, reads, writes)
        return ev

    def _all_events(self):
        evs = []
        for e in self.ENGS:
            if self.cnt[e] > 0:
                evs.append((self.sem[e], self.cnt[e], ("c", e)))
        for e in self.dsem:
            for i in range(self.NDMA):
                if self.dlast[e][i] is not None:
                    evs.append(self.dlast[e][i])
        ex = getattr(self, "extra_events", [])
        if ex:
            evs.append(ex[-1])
        return evs

    def barrier(self):
        evs = self._all_events()
        for e in self.ENGS:
            self._emit_waits(e, evs)

    def final_wait(self, eng="sp"):
        self._emit_waits(eng, self._all_events())

    def emit(self, block):
        q = self.q

        @block.tensor
        def _(e):
            for t in q["pe"]:
                t(e)

        @block.scalar
        def _(e):
            for t in q["act"]:
                t(e)

        @block.vector
        def _(e):
            for t in q["dve"]:
                t(e)

        @block.gpsimd
        def _(e):
            for t in q["pool"]:
                t(e)

        @block.sync
        def _(e):
            for t in q["sp"]:
                t(e)


def build_program(stop=None):
    nc = bass.Bass("TRN2", target_bir_lowering=False)
    TAPS = []

    def din(name, shape, dt=F32):
        return nc.dram_tensor(name, shape, dt, kind="ExternalInput").ap()

    def dout(name, shape, dt=F32):
        return nc.dram_tensor(name, shape, dt, kind="ExternalOutput").ap()

    def dint(name, shape, dt):
        return nc.dram_tensor(name, shape, dt).ap()

    xs_d = din("xs", [2, TS, D])
    xp_d = din("xp", [TP, D])
    ck_d = din("ck", [2, 256, 512])
    cv_d = din("cv", [2, 256, 512])
    sv_d = din("sv", [128, NSV])
    lw_d = din("lw", [128, 2048])
    wsl_d = din("wsl", [2, 8, 128, WA_W + WB_W])
    rope_d = din("rope", [2, 128, 2, TS])
    csign_d = din("csign", [128, 512])
    rmat_d = din("rmat", [128, 128])
    cs64_d = din("cs64", [128, 256])
    dft_d = din("dft", [2, 4096, 2048], BF16)
    dftp_d = din("dftp", [2, 256, 256], BF16)
    ys_d = dout("ys", [TS, D])
    yp_d = dout("yp", [TP, D])
    nk_d = dout("nk", [2, 2, 256, 512])
    nv_d = dout("nv", [2, 2, 256, 512])
    nh_d = dout("nh", [2, 2, 2, 256])

    WGA = [dint("WGA%d" % l, [1024, WA_W], BF16) for l in range(2)]
    WGB = [dint("WGB%d" % l, [1024, WB_W], BF16) for l in range(2)]
    WGU = [dint("WGU%d" % l, [11, 128, 8, 512], BF16) for l in range(2)]
    EXG = [dint("EXG%d" % l, [3072, 2048], BF16) for l in range(2)]
    QS = [[dint("QS%d_%d" % (l, k), [128, 4, TS], BF16) for k in range(2)] for l in range(2)]
    GS = [[dint("GS%d_%d" % (l, k), [128, 2, TS], BF16) for k in range(2)] for l in range(2)]
    XS = [[dint("XS%d_%d" % (l, k), [128, 8, T], F32) for k in range(2)] for l in range(2)]

    es = ExitStack()
    with es:
        S = Sched(nc, es)

        def OP(eng, method, *args, rd=(), wr=(), **kw):
            return S.op(eng, lambda e: getattr(e, method)(*args, **kw), rd, wr)

        def DMA(eng, out, in_, rd=(), wr=()):
            return S.dma(eng, lambda e: e.dma_start(out=out, in_=in_), rd, wr)

        def MM(out, lhsT, rhs, start, stop, rd=(), wr=()):
            return S.op("pe", lambda e: e.matmul(out, lhsT, rhs, start=start, stop=stop), rd, wr)

        class _Stop(Exception):
            pass

        def tap(name, src):
            o = nc.dram_tensor("dbg_" + name, list(src.shape), src.dtype, kind="ExternalOutput").ap()
            S.dma("sp", lambda e: e.dma_start(out=o, in_=src), (), ())
            TAPS.append("dbg_" + name)

        def check_stop(tag, taps=()):
            if stop == tag:
                S.barrier()
                for (n_, a_) in taps:
                    tap(n_, a_)
                raise _Stop()

        cc_sem = es.enter_context(nc.semaphore("cc_sem"))
        cc_state = {"n": 0}

        def CC(groups, src, dst, rd, wr):
            deps = S._deps(rd, wr)
            S._emit_waits("pool", deps)
            cc_state["n"] += 1
            v = cc_state["n"]
            ev = (cc_sem, v, ("cc",))
            S.q["pool"].append(lambda e: e.collective_compute(
                "AllGather", ALU.bypass, replica_groups=groups, ins=[src], outs=[dst]).then_inc(cc_sem, 1))
            S._commit(ev, rd, wr)
            S.extra_events = getattr(S, "extra_events", [])
            S.extra_events.append(ev)

        NBIG = 188 * 1024 // 4
        BIG = es.enter_context(nc.sbuf_tensor("big", [128, NBIG], F32))
        PSP = [es.enter_context(nc.psum_tensor("psp%d" % i, [128, 1024], F32))[:, :] for i in range(2)]
        PS = [PSP[0][:, 0:512], PSP[0][:, 512:1024], PSP[1][:, 0:512], PSP[1][:, 512:1024]] + \
             [es.enter_context(nc.psum_tensor("ps%d" % i, [128, 512], F32))[:, :] for i in range(4, 8)]
        PB = [Buf("ps%d" % i) for i in range(8)]

        class Arena:
            def __init__(self, start, end):
                self.start, self.cur, self.end = start, start, end

            def alloc(self, free_shape, dt):
                n = 1
                for s_ in free_shape:
                    n *= s_
                nbytes = n * (4 if dt == F32 else 2)
                nbytes = (nbytes + 63) // 64 * 64
                off = self.cur
                self.cur += nbytes
                assert self.cur <= self.end, ("arena overflow", self.cur, self.end)
                ap = BIG[:, off // 4:(off + nbytes) // 4]
                if dt != F32:
                    ap = ap.bitcast(dt)
                ap = ap[:, 0:n]
                if len(free_shape) == 2:
                    ap = ap.rearrange("p (a b) -> p a b", a=free_shape[0], b=free_shape[1])
                elif len(free_shape) == 3:
                    ap = ap.rearrange("p (a b c) -> p a b c", a=free_shape[0], b=free_shape[1], c=free_shape[2])
                return ap

            def reset(self, to=None):
                self.cur = self.start if to is None else to

        TOTAL = NBIG * 4
        AR = Arena(0, TOTAL)
        sv = AR.alloc([NSV], F32)
        ident = AR.alloc([128], F32)
        ones_b = AR.alloc([128], BF16)
        rmat_b = AR.alloc([128], BF16)
        cs64_b = AR.alloc([256], BF16)
        lw_b = AR.alloc([16, 128], BF16)
        rope_c = AR.alloc([TS], F32)
        rope_s = AR.alloc([TS], F32)
        csign = AR.alloc([512], F32)
        cst = [[AR.alloc([6, 8], F32) for w in range(2)] for l in range(2)]
        scond = AR.alloc([8, 2], BF16)
        small = AR.alloc([64], F32)
        nhst = AR.alloc([16], F32)
        dftp_b = AR.alloc([2, 2, 256], BF16)
        P_MLRU = AR.cur
        mix_lru = AR.alloc([2, T], BF16)
        P_MFOUR = AR.cur
        mix_four = AR.alloc([2, T], BF16)
        P_ATTN = AR.cur
        mix_attn = AR.alloc([4, T], BF16)
        P_Q = AR.cur
        kTp = AR.alloc([4, TP], BF16)
        Vp = AR.alloc([4, 512], BF16)
        xr_p = AR.alloc([2, TP], BF16)
        xf_p = AR.alloc([2, TP], BF16)
        q_all = AR.alloc([4, T], BF16)
        gr_all = AR.alloc([2, T], BF16)
        P_SCR = AR.cur
        print("persistent bytes", P_MLRU, P_Q, P_SCR, TOTAL)
        b_const = Buf("const")
        b_cst = Buf("cst")
        b_q = [Buf("q%d" % i) for i in range(5)]
        b_gr = [Buf("gr%d" % i) for i in range(5)]
        b_prm = Buf("prm")
        b_mixa = [Buf("mixa%d" % i) for i in range(5)]
        b_mixl = Buf("mixl")
        b_mixf = Buf("mixf")
        b_exg = [Buf("exg0"), Buf("exg1")]
        b_wga = [Buf("wga0"), Buf("wga1")]
        b_wgb = [Buf("wgb0"), Buf("wgb1")]
        b_xs = [[[Buf("xs%d_%d_%d" % (l, k, i)) for i in range(5)] for k in range(2)] for l in range(2)]
        b_qs = [[Buf("qs%d_%d" % (l, k)) for k in range(2)] for l in range(2)]
        b_rope = Buf("rope")
        b_nh = Buf("nh")

        def svc(l, name, i=0, n=1):
            o, w = SVL[name]
            return sv[:, l * SV_PER_L + o + i: l * SV_PER_L + o + i + n]

        try:
            DMA("sp", sv, sv_d, wr=[b_const])
            DMA("sp", csign, csign_d, wr=[b_const])
            DMA("sp", dftp_b, dftp_d.rearrange("c (t p) j -> p c t j", p=128), wr=[b_const])
            OP("pool", "memset", ident, 0.0, wr=[b_const])
            OP("pool", "affine_select", ident, ident, [[-1, 128]], ALU.not_equal, 1.0, base=0, channel_multiplier=1,
               rd=[b_const], wr=[b_const])
            OP("pool", "memset", ones_b, 1.0, wr=[b_const])
            OP("pool", "memset", nhst, 0.0, wr=[b_nh])
            AR.reset(P_SCR)
            st_a = AR.alloc([2048], F32)
            st_b = AR.alloc([128], F32)
            st_c = AR.alloc([256], F32)
            b_st = Buf("st")
            DMA("sp", st_a, lw_d, wr=[b_st])
            DMA("sp", st_b, rmat_d, wr=[b_st])
            DMA("sp", st_c, cs64_d, wr=[b_st])
            OP("dve", "tensor_copy", lw_b.rearrange("p a b -> p (a b)"), st_a, rd=[b_st], wr=[b_const])
            OP("dve", "tensor_copy", rmat_b, st_b, rd=[b_st], wr=[b_const])
            OP("dve", "tensor_copy", cs64_b, st_c, rd=[b_st], wr=[b_const])
            OP("act", "activation", scond.rearrange("p a b -> p (a b)"), sv[:, SV_MISC + 8: SV_MISC + 24], AF.Silu,
               rd=[b_const], wr=[b_const])

            def convert_weights(l, cast_engs, pwmax=2368, reset_to=None, kinds=None):
                if reset_to is not None:
                    AR.reset(reset_to)
                stf = [AR.alloc([pwmax], F32) for _ in range(2)]
                stb = [AR.alloc([pwmax], BF16) for _ in range(2)]
                bf = [Buf("stf0"), Buf("stf1")]
                bb = [Buf("stb0"), Buf("stb1")]
                jobs = []
                def split(c0, wtot, kind, base, align=1):
                    step = (pwmax // align) * align
                    o = 0
                    while o < wtot:
                        w_ = min(step, wtot - o)
                        jobs.append((c0 + o, w_, kind, base + o))
                        o += w_
                split(0, WA_W, "A", 0)
                split(WA_W + OFF_WOUT, 1024, "B", OFF_WOUT)
                split(WA_W + OFF_GU, 2816, "G0", 0, 256)
                split(WA_W + OFF_GU + 2816, 2816, "G1", 0, 256)
                split(WA_W + OFF_DN, 2816, "B", OFF_DN)
                if kinds is not None:
                    jobs = [j_ for j_ in jobs if j_[2] in kinds]
                k_ = 0
                for r in range(8):
                    for (c0, w_, kind, base) in jobs:
                        i = k_ % 2
                        eng = cast_engs[k_ % len(cast_engs)]
                        k_ += 1
                        DMA("sp", stf[i][:, 0:w_], wsl_d[l, r, :, c0:c0 + w_], wr=[bf[i]])
                        OP(eng, "tensor_copy", stb[i][:, 0:w_], stf[i][:, 0:w_], rd=[bf[i]], wr=[bb[i]])
                        if kind == "A":
                            DMA("pool", WGA[l][r * 128:(r + 1) * 128, base:base + w_], stb[i][:, 0:w_], rd=[bb[i]], wr=[b_wga[l]])
                        elif kind == "B":
                            DMA("pool", WGB[l][r * 128:(r + 1) * 128, base:base + w_], stb[i][:, 0:w_], rd=[bb[i]], wr=[b_wgb[l]])
                        else:
                            part = int(kind[1])
                            ns_ = w_ // 256
                            js0 = base // 256
                            DMA("pool", WGU[l][js0:js0 + ns_, :, r, part * 256:(part + 1) * 256].rearrange("s p n -> p s n"),
                                stb[i][:, 0:w_].rearrange("p (s n) -> p s n", n=256), rd=[bb[i]], wr=[b_wgb[l]])

            convert_weights(0, ["dve", "pool"], reset_to=P_SCR + 12 * 1024, kinds=("A",))
            S.barrier()
            check_stop("setup", [("wga", WGA[0][:, 0:2304])])

            def rstd_from(ps_ap, pbuf, out_ap, obuf, scale, tmp_ap, tbuf):
                OP("act", "activation", tmp_ap, ps_ap, AF.Ln, bias=small[:, 0:1], scale=scale, rd=[pbuf, b_const], wr=[tbuf])
                OP("act", "activation", out_ap, tmp_ap, AF.Exp, scale=-0.5, rd=[tbuf], wr=[obuf])

            OP("pool", "memset", small[:, 0:1], EPS, wr=[b_const])
            OP("pool", "memset", small[:, 1:2], 1.0, wr=[b_const])

            m0 = sv[:, SV_MISC + 0:SV_MISC + 1]
            m1 = sv[:, SV_MISC + 1:SV_MISC + 2]

            def blend(dst, other, rd_dst, rd_other, wr_dst):
                OP("act", "activation", dst, dst, AF.Identity, scale=m0, rd=list(rd_dst) + [b_const], wr=wr_dst)
                OP("dve", "scalar_tensor_tensor", dst, other, m1, dst, ALU.mult, ALU.add,
                   rd=list(rd_other) + [b_const] + list(wr_dst), wr=wr_dst)

            for l in range(2):
                li = lambda_init(l)
                AR.reset(P_SCR)
                wm = [AR.alloc([8, 1536], BF16) for _ in range(2)]
                bwm = [Buf("wm0"), Buf("wm1")]
                modsb = AR.alloc([48, 2], F32)
                bmod_ = Buf("modsb")
                for sl in range(4):
                    i = sl % 2
                    DMA("sp", wm[i], WGA[l][:, 2304 + sl * 1536: 2304 + (sl + 1) * 1536].rearrange("(k p) n -> p k n", p=128),
                        rd=[b_wga[l]], wr=[bwm[i]])
                    for jj in range(12):
                        j = sl * 12 + jj
                        for kc in range(8):
                            MM(PS[0][:, 2 * j:2 * j + 2], wm[i][:, kc, jj * 128:(jj + 1) * 128], scond[:, kc, :],
                               kc == 0, kc == 7, rd=[bwm[i], b_const], wr=[PB[0]])
                check_stop("Ma%d" % l, [("wm0", wm[0]), ("wm1", wm[1])])
                psm = PS[0][:, 0:96].rearrange("p (a b) -> p a b", b=2)
                for w in range(2):
                    OP("dve", "tensor_tensor", modsb[:, :, w], psm[:, :, w], svc(l, "bmod", 0, 48), ALU.add,
                       rd=[PB[0], b_const], wr=[bmod_])
                for w in range(2):
                    c_ = cst[l][w]
                    OP("dve", "scalar_tensor_tensor", c_[:, 0, :], modsb[:, 8:16, w], 1.0, svc(l, "gpm", 0, 8), ALU.add, ALU.mult,
                       rd=[bmod_, b_const], wr=[b_cst])
                    OP("dve", "tensor_copy", c_[:, 1, :], modsb[:, 0:8, w], rd=[bmod_], wr=[b_cst])
                    OP("dve", "tensor_tensor", c_[:, 2, :], modsb[:, 16:24, w], svc(l, "gqm", 0, 8), ALU.mult,
                       rd=[bmod_, b_const], wr=[b_cst])
                    OP("dve", "scalar_tensor_tensor", c_[:, 3, :], modsb[:, 32:40, w], 1.0, svc(l, "gpf", 0, 8), ALU.add, ALU.mult,
                       rd=[bmod_, b_const], wr=[b_cst])
                    OP("dve", "tensor_copy", c_[:, 4, :], modsb[:, 24:32, w], rd=[bmod_], wr=[b_cst])
                    OP("dve", "tensor_tensor", c_[:, 5, :], modsb[:, 40:48, w], svc(l, "gqf", 0, 8), ALU.mult,
                       rd=[bmod_, b_const], wr=[b_cst])
                S.barrier()
                check_stop("M%d" % l, [("cst0", cst[l][0]), ("cst1", cst[l][1]), ("modsb", modsb)])

                def phaseA(k):
                    AR.reset(P_MLRU)
                    win = AR.alloc([8, 2304], BF16)
                    rc_t = AR.alloc([512], F32)
                    rs_t = AR.alloc([512], F32)
                    assert AR.cur <= P_Q
                    AR.reset(P_SCR)
                    b_win = Buf("win")
                    xblk = AR.alloc([8, 512], F32)
                    b_x = Buf("xblk")
                    xtm = [AR.alloc([1024], F32) for _ in range(4)]
                    b_xtm = [Buf("xtm%d" % i) for i in range(4)]
                    sq = AR.alloc([8, 512], BF16)
                    b_sq = Buf("sq")
                    u = AR.alloc([8, 512], BF16)
                    b_u = Buf("u")
                    rstd = AR.alloc([512], F32)
                    b_rstd = Buf("rstd")
                    tl = AR.alloc([512], F32)
                    b_tl = Buf("tl")
                    tmpf = [AR.alloc([512], F32) for _ in range(2)]
                    b_tmpf = [Buf("tmpf0"), Buf("tmpf1")]
                    qb_ = [AR.alloc([512], BF16) for _ in range(2)]
                    b_qb = [Buf("qb0"), Buf("qb1")]
                    t1_ = [AR.alloc([512], F32) for _ in range(2)]
                    t2_ = [AR.alloc([512], F32) for _ in range(2)]
                    b_t1 = [Buf("t1_0"), Buf("t1_1")]
                    b_t2 = [Buf("t2_0"), Buf("t2_1")]
                    stg = [AR.alloc([512], BF16) for _ in range(3)]
                    b_stg = [Buf("stg%d" % i) for i in range(3)]
                    stgf = [AR.alloc([512], F32) for _ in range(2)]
                    b_stgf = [Buf("stgf0"), Buf("stgf1")]
                    b_rt = Buf("rt")
                    for h_ in range(2):
                        DMA("sp", win[:, h_ * 4:(h_ + 1) * 4, :],
                            WGA[l][h_ * 512:(h_ + 1) * 512, 0:2304].rearrange("(k p) n -> p k n", p=128),
                            rd=[b_wga[l]], wr=[b_win])
                    cnt = {"pj": 0, "rot": 0, "tm": 0, "stg": 0, "stgf": 0, "tmpf": 0, "qb": 0}
                    DMA("sp", rope_c, rope_d[k, :, 0, :], wr=[b_rope])
                    DMA("sp", rope_s, rope_d[k, :, 1, :], wr=[b_rope])
                    b_exb_k = Buf("exbk")
                    exbk = EXG[l][k * 1536:(k + 1) * 1536, :]
                    for tb in range(5 if k == 0 else 4):
                        w = 0 if tb < 4 else 1
                        c_ = cst[l][w]
                        tok = slice(tb * 512, (tb + 1) * 512)
                        if l == 0:
                            src = xs_d[k] if tb < 4 else xp_d
                            r0 = tb * 512 if tb < 4 else 0
                            for tt in range(4):
                                DMA("sp", xtm[tt], src[r0 + tt * 128: r0 + (tt + 1) * 128, :], wr=[b_xtm[tt]])
                            for c in range(8):
                                pb = 1 + (c % 2)
                                for tt in range(4):
                                    OP("pe", "transpose", PS[pb][:, tt * 128:(tt + 1) * 128], xtm[tt][:, c * 128:(c + 1) * 128], ident,
                                       rd=[b_xtm[tt], b_const], wr=[PB[pb]])
                                OP("act" if c % 2 == 0 else "dve", "copy" if c % 2 == 0 else "tensor_copy", xblk[:, c, :], PS[pb],
                                   rd=[PB[pb]], wr=[b_x])
                            DMA("pool", XS[0][k][:, :, tok], xblk, rd=[b_x], wr=[b_xs[0][k][tb]])
                        else:
                            DMA("sp", xblk, XS[1][k][:, :, tok], rd=[b_xs[1][k][tb]], wr=[b_x])
                        if tb == 0 and k == 0:
                            check_stop("Ax%d" % l, [("xblk", xblk)])
                        OP("act", "activation", sq[:, 0:4, :], xblk[:, 0:4, :], AF.Square, rd=[b_x], wr=[b_sq])
                        OP("pool", "tensor_tensor", sq[:, 4:8, :], xblk[:, 4:8, :], xblk[:, 4:8, :], ALU.mult, rd=[b_x], wr=[b_sq])
                        for c in range(8):
                            MM(PS[0], ones_b, sq[:, c, :], c == 0, c == 7, rd=[b_sq, b_const], wr=[PB[0]])
                        rstd_from(PS[0], PB[0], rstd, b_rstd, 1.0 / D, tl, b_tl)
                        for c in range(8):
                            i = cnt["tmpf"] % 2
                            cnt["tmpf"] += 1
                            OP("dve", "scalar_tensor_tensor", tmpf[i], xblk[:, c, :], c_[:, 0, c:c + 1], rstd, ALU.mult, ALU.mult,
                               rd=[b_x, b_rstd, b_cst], wr=[b_tmpf[i]])
                            OP("act", "activation", u[:, c, :], tmpf[i], AF.Identity, bias=c_[:, 1, c:c + 1], scale=1.0,
                               rd=[b_tmpf[i], b_cst], wr=[b_u])

                        if tb == 0 and k == 0:
                            check_stop("An%d" % l, [("u", u), ("rstd", rstd)])

                        if tb < 4:
                            OP("act", "copy", rc_t, rope_c[:, tok], rd=[b_rope], wr=[b_rt])
                            OP("act", "copy", rs_t, rope_s[:, tok], rd=[b_rope], wr=[b_rt])

                        def proj_fm(j):
                            pb = 3 + (cnt["pj"] % 3)
                            cnt["pj"] += 1
                            for kc in range(8):
                                MM(PS[pb], win[:, kc, j * 128:(j + 1) * 128], u[:, kc, :], kc == 0, kc == 7,
                                   rd=[b_win, b_u], wr=[PB[pb]])
                            return pb

                        def proj_tm(tt, c0):
                            pb = 7
                            for kc in range(8):
                                MM(PS[pb], u[:, kc, tt * 128:(tt + 1) * 128], win[:, kc, c0:c0 + 512], kc == 0, kc == 7,
                                   rd=[b_win, b_u], wr=[PB[pb]])
                            return pb

                        def nstg():
                            i = cnt["stg"] % 3
                            cnt["stg"] += 1
                            return i

                        for j in range(18):
                            kind = ("q", "k", "v", "xr", "gr", "xf")[[0, 0, 0, 0, 1, 1, 1, 1, 2, 2, 2, 2, 3, 3, 4, 4, 5, 5][j]]
                            if kind == "v":
                                continue
                            pb = proj_fm(j)
                            dbg0 = (tb == 0 and k == 0 and l == 0 and j == 0)
                            if dbg0:
                                check_stop("As1", [("u", u)])
                            if kind in ("q", "k"):
                                h = j % 4
                                if tb < 4:
                                    i = cnt["qb"] % 2
                                    cnt["qb"] += 1
                                    OP("act", "copy", qb_[i], PS[pb], rd=[PB[pb]], wr=[b_qb[i]])
                                    if dbg0:
                                        check_stop("As2", [("qb", qb_[i])])
                                    MM(PS[6], rmat_b, qb_[i], True, True, rd=[b_qb[i], b_const], wr=[PB[6]])
                                    if dbg0:
                                        check_stop("As3", [("qb", qb_[i])])
                                    OP("dve", "tensor_tensor", t1_[i], PS[pb], rc_t, ALU.mult, rd=[PB[pb], b_rt], wr=[b_t1[i]])
                                    OP("dve", "tensor_tensor", t2_[i], PS[6], rs_t, ALU.mult, rd=[PB[6], b_rt], wr=[b_t2[i]])
                                    if dbg0:
                                        check_stop("As4", [("t1", t1_[i]), ("t2", t2_[i])])
                                    if kind == "q":
                                        OP("pool", "tensor_tensor", q_all[:, h, tok], t1_[i], t2_[i], ALU.add,
                                           rd=[b_t1[i], b_t2[i]], wr=[b_q[tb]])
                                    else:
                                        si = nstg()
                                        OP("pool", "tensor_tensor", stg[si], t1_[i], t2_[i], ALU.add,
                                           rd=[b_t1[i], b_t2[i]], wr=[b_stg[si]])
                                        DMA("pool", exbk[h * 128:(h + 1) * 128, tok], stg[si], rd=[b_stg[si]], wr=[b_exb_k])
                                else:
                                    if kind == "q":
                                        OP("act", "copy", q_all[:, h, tok], PS[pb], rd=[PB[pb]], wr=[b_q[tb]])
                                    else:
                                        OP("act", "copy", kTp[:, h, :], PS[pb], rd=[PB[pb]], wr=[b_prm])
                            elif kind == "gr":
                                OP("act", "activation", gr_all[:, j - 14, tok], PS[pb], AF.Gelu_apprx_tanh, rd=[PB[pb]], wr=[b_gr[tb]])
                            else:
                                c2 = j % 2
                                if tb < 4:
                                    si = nstg()
                                    OP("act", "copy", stg[si], PS[pb], rd=[PB[pb]], wr=[b_stg[si]])
                                    base = 512 if kind == "xr" else 768
                                    DMA("pool", exbk[base + c2 * 128: base + (c2 + 1) * 128, tok], stg[si], rd=[b_stg[si]], wr=[b_exb_k])
                                else:
                                    dst = xr_p if kind == "xr" else xf_p
                                    OP("act", "copy", dst[:, c2, :], PS[pb], rd=[PB[pb]], wr=[b_prm])
                            if tb == 0 and k == 0 and l == 0:
                                check_stop("Aj%d" % j, [("q", q_all[:, :, 0:512])])
                        for tt in range(4):
                            pb = proj_tm(tt, 1024)
                            if tb < 4:
                                si = nstg()
                                OP("dve", "tensor_copy", stg[si], PS[pb], rd=[PB[pb]], wr=[b_stg[si]])
                                vview = exbk[1024:1536, :].rearrange("r (t f) -> (r t) f", f=512)
                                vdst = vview[tb * 512 + tt * 128: tb * 512 + (tt + 1) * 128, :]
                                DMA("pool", vdst, stg[si], rd=[b_stg[si]], wr=[b_exb_k])
                            else:
                                seq, hh = tt // 2, tt % 2
                                fi = cnt["stgf"] % 2
                                cnt["stgf"] += 1
                                OP("dve", "tensor_copy", stgf[fi], PS[pb], rd=[PB[pb]], wr=[b_stgf[fi]])
                                OP("act", "copy", Vp[:, tt, :], PS[pb], rd=[PB[pb]], wr=[b_prm])
                                DMA("pool", nv_d[seq, l, hh * 128:(hh + 1) * 128, :], stgf[fi], rd=[b_stgf[fi]])
                                pb = proj_tm(tt, 512)
                                fi = cnt["stgf"] % 2
                                cnt["stgf"] += 1
                                OP("dve", "tensor_copy", stgf[fi], PS[pb], rd=[PB[pb]], wr=[b_stgf[fi]])
                                DMA("pool", nk_d[seq, l, hh * 128:(hh + 1) * 128, :], stgf[fi], rd=[b_stgf[fi]])
                        if tb == 0 and k == 0:
                            check_stop("Ap%d" % l, [("q", q_all), ("gr", gr_all)])
                    DMA("pool", QS[l][k], q_all[:, :, 0:TS], rd=b_q[0:4], wr=[b_qs[l][k]])
                    DMA("pool", GS[l][k], gr_all[:, :, 0:TS], rd=b_gr[0:4], wr=[b_qs[l][k]])
                    b_exg[l].w = None
                    S.barrier()
                    check_stop("A%d_%d" % (l, k), [("q", q_all), ("gr", gr_all), ("exg", EXG[l]), ("ktp", kTp), ("vp", Vp), ("xrp", xr_p), ("xs", XS[l][k])])

                def phaseB1(k, own=False):
                    AR.reset(P_SCR)
                    NT = 4608
                    xr_c = AR.alloc([NT], BF16)
                    xc = AR.alloc([NT], F32)
                    xcb = AR.alloc([NT], BF16)
                    HS = AR.alloc([NT], F32)
                    _sv = AR.cur
                    AR.reset(P_MFOUR)
                    HB = AR.alloc([2560], F32)
                    GR = AR.alloc([2560], F32)
                    assert AR.cur <= P_Q
                    AR.reset(_sv)
                    GI = AR.alloc([2560], F32)
                    GC = AR.alloc([2560], F32)
                    sp_ = AR.alloc([8], F32)
                    b_xr, b_xc, b_xcb, b_hs, b_hb = Buf("xr"), Buf("xc"), Buf("xcb"), Buf("hs"), Buf("hb")
                    b_g = Buf("gates")
                    b_sp = Buf("sp")
                    NTK = 4608 if k == 0 else 4096
                    seqs = [(0, 4096), (4096, 4352), (4352, 4608)] if k == 0 else [(0, 4096)]
                    _sv2 = AR.cur
                    AR.reset(P_MFOUR + 20480)
                    tmpA = AR.alloc([2048], BF16)
                    tmpB = AR.alloc([2048], BF16)
                    assert AR.cur <= P_Q
                    AR.reset(_sv2)
                    b_tA, b_tB = Buf("tA"), Buf("tB")
                    if own:
                        _sv3 = AR.cur
                        AR.reset(P_MFOUR + 20480)
                        tmpG = AR.alloc([2, TS], BF16)
                        assert AR.cur <= P_Q
                        AR.reset(_sv3)
                        b_tg = Buf("tmpG")
                        DMA("sp", gr_all[:, :, 0:TS], GS[l][0], rd=[b_qs[l][0]], wr=b_gr[0:4])
                        DMA("sp", tmpG, GS[l][1], rd=[b_qs[l][1]], wr=[b_tg])
                        blend(gr_all[:, :, 0:TS], tmpG, b_gr[0:4], [b_tg], b_gr[0:4])
                    else:
                        DMA("sp", gr_all[:, :, 0:TS], GS[l][k], rd=[b_qs[l][k]], wr=b_gr[0:4])
                    a0 = sv[:, SV_MISC + 0:SV_MISC + 1]
                    a1 = sv[:, SV_MISC + 1:SV_MISC + 2]
                    OP("act", "activation", sp_[:, 0:4], svc(l, "lam", 0, 4), AF.Exp, scale=-1.0, rd=[b_const], wr=[b_sp])
                    OP("act", "activation", sp_[:, 4:8], sp_[:, 0:4], AF.Ln, bias=small[:, 1:2], scale=1.0, rd=[b_sp, b_const], wr=[b_sp])
                    OP("dve", "tensor_scalar", sp_[:, 0:4], sp_[:, 4:8], -8.0, None, ALU.mult, rd=[b_sp], wr=[b_sp])
                    OP("dve", "tensor_scalar", sp_[:, 4:8], sp_[:, 4:8], -16.0, None, ALU.mult, rd=[b_sp], wr=[b_sp])
                    for c2 in range(2):
                        for r in range(2):
                            DMA("sp", xr_c[:, r * 2048:(r + 1) * 2048],
                                EXG[l][r * 1536 + 512 + c2 * 128: r * 1536 + 512 + (c2 + 1) * 128, :], wr=[b_xr])
                        if k == 0:
                            OP("pool", "tensor_copy", xr_c[:, 4096:4608], xr_p[:, c2, :], rd=[b_prm], wr=[b_xr])
                        cw = lambda t_: svc(l, "convw", t_ * 2 + c2)
                        OP("dve", "tensor_scalar", xc[:, 0:NTK], xr_c[:, 0:NTK], cw(2), svc(l, "convb", c2), ALU.mult, ALU.add,
                           rd=[b_xr, b_const], wr=[b_xc])
                        for (s0, s1) in seqs:
                            OP("dve", "scalar_tensor_tensor", xc[:, s0 + 2:s1], xr_c[:, s0:s1 - 2], cw(0), xc[:, s0 + 2:s1],
                               ALU.mult, ALU.add, rd=[b_xr, b_const, b_xc], wr=[b_xc])
                            OP("dve", "scalar_tensor_tensor", xc[:, s0 + 1:s1], xr_c[:, s0:s1 - 1], cw(1), xc[:, s0 + 1:s1],
                               ALU.mult, ALU.add, rd=[b_xr, b_const, b_xc], wr=[b_xc])
                            OP("dve", "scalar_tensor_tensor", xc[:, s0:s1 - 1], xr_c[:, s0 + 1:s1], cw(3), xc[:, s0:s1 - 1],
                               ALU.mult, ALU.add, rd=[b_xr, b_const, b_xc], wr=[b_xc])
                        OP("pool", "tensor_copy", xcb[:, 0:NTK], xc[:, 0:NTK], rd=[b_xc], wr=[b_xcb])
                        for d in range(2):
                            halves = [(0, 2048), (2048, NTK)]
                            if d == 1:
                                halves = halves[::-1]
                            col = d * 2 + c2
                            wa = lw_b[:, ((l * 2 + d) * 2 + 0) * 2 + c2, :]
                            wx = lw_b[:, ((l * 2 + d) * 2 + 1) * 2 + c2, :]
                            for (t0, t1) in halves:
                                n = t1 - t0
                                for s_ in range(n // 512):
                                    a0 = t0 + s_ * 512
                                    pa, pi = 1 + 2 * (s_ % 2), 2 + 2 * (s_ % 2)
                                    MM(PS[pa], wa, xcb[:, a0:a0 + 512], True, True, rd=[b_xcb, b_const], wr=[PB[pa]])
                                    MM(PS[pi], wx, xcb[:, a0:a0 + 512], True, True, rd=[b_xcb, b_const], wr=[PB[pi]])
                                    OP("act", "activation", GR[:, s_ * 512:(s_ + 1) * 512], PS[pa], AF.Sigmoid,
                                       bias=svc(l, "ba", col), scale=1.0, rd=[PB[pa], b_const], wr=[b_g])
                                    OP("act", "activation", GI[:, s_ * 512:(s_ + 1) * 512], PS[pi], AF.Sigmoid,
                                       bias=svc(l, "bx", col), scale=1.0, rd=[PB[pi], b_const], wr=[b_g])
                                OP("act", "activation", GC[:, 0:n], GR[:, 0:n], AF.Exp, scale=sp_[:, 4 + col:5 + col], rd=[b_g, b_sp], wr=[b_g])
                                OP("act", "activation", GR[:, 0:n], GR[:, 0:n], AF.Exp, scale=sp_[:, col:col + 1], rd=[b_g, b_sp], wr=[b_g])
                                OP("act", "activation", GC[:, 0:n], GC[:, 0:n], AF.Ln, bias=small[:, 1:2], scale=-1.0, rd=[b_g, b_const], wr=[b_g])
                                OP("act", "activation", GC[:, 0:n], GC[:, 0:n], AF.Exp, scale=0.5, rd=[b_g], wr=[b_g])
                                OP("dve", "tensor_tensor", GI[:, 0:n], GI[:, 0:n], xc[:, t0:t1], ALU.mult, rd=[b_g, b_xc], wr=[b_g])
                                OP("dve", "tensor_tensor", GI[:, 0:n], GI[:, 0:n], GC[:, 0:n], ALU.mult, rd=[b_g], wr=[b_g])
                                for (s0, s1) in seqs:
                                    lo, hi = max(s0, t0), min(s1, t1)
                                    if lo >= hi:
                                        continue
                                    if d == 0:
                                        if lo == s0:
                                            init = svc(l, "h0", col) if s0 == 0 else 0.0
                                        else:
                                            init = HS[:, lo - 1:lo]
                                        OP("dve", "tensor_tensor_scan", HS[:, lo:hi], GR[:, lo - t0:hi - t0], GI[:, lo - t0:hi - t0],
                                           init, ALU.mult, ALU.add, rd=[b_g, b_hs, b_const], wr=[b_hs])
                                    else:
                                        if hi == s1:
                                            init = svc(l, "h0", col) if s0 == 0 else 0.0
                                        else:
                                            init = small[:, 8:9]
                                        OP("dve", "tensor_tensor_scan", HB[:, lo - t0:hi - t0][:, ::-1], GR[:, lo - t0:hi - t0][:, ::-1],
                                           GI[:, lo - t0:hi - t0][:, ::-1], init, ALU.mult, ALU.add,
                                           rd=[b_g, b_hb, b_const], wr=[b_hb])
                                if d == 0 and k == 0 and t1 == 4608:
                                    for sq_ in range(2):
                                        e_ = 4096 + sq_ * 256 + 255
                                        OP("dve", "tensor_copy", nhst[:, (sq_ * 2 + 0) * 2 + c2:(sq_ * 2 + 0) * 2 + c2 + 1], HS[:, e_:e_ + 1],
                                           rd=[b_hs], wr=[b_nh])
                                if d == 1:
                                    if t1 == NTK:
                                        for sq_ in (range(2) if k == 0 else ()):
                                            e_ = 4096 + sq_ * 256 - t0
                                            OP("dve", "tensor_copy", nhst[:, (sq_ * 2 + 1) * 2 + c2:(sq_ * 2 + 1) * 2 + c2 + 1], HB[:, e_:e_ + 1],
                                               rd=[b_hb], wr=[b_nh])
                                        OP("dve", "tensor_copy", small[:, 8:9], HB[:, 0:1], rd=[b_hb], wr=[b_const])
                                    OP("dve", "tensor_tensor", HS[:, t0:t1], HS[:, t0:t1], HB[:, 0:n], ALU.add, rd=[b_hs, b_hb], wr=[b_hs])
                        if own:
                            OP("act", "activation", HB[:, 0:2048], HS[:, 0:2048], AF.Identity, scale=m0, rd=[b_hs, b_const, b_hb], wr=[b_hb])
                            OP("dve", "scalar_tensor_tensor", HB[:, 0:2048], HS[:, 2048:4096], m1, HB[:, 0:2048], ALU.mult, ALU.add,
                               rd=[b_hs, b_const, b_hb], wr=[b_hb])
                            OP("dve", "tensor_tensor", mix_lru[:, c2, 0:2048], HB[:, 0:2048], gr_all[:, c2, 0:2048], ALU.mult,
                               rd=[b_hb] + b_gr, wr=[b_mixl])
                        else:
                            OP("dve", "tensor_tensor", mix_lru[:, c2, 0:2048], HS[:, k * 2048:(k + 1) * 2048], gr_all[:, c2, 0:2048], ALU.mult,
                               rd=[b_hs] + b_gr, wr=[b_mixl])
                        if k == 0:
                            OP("dve", "tensor_tensor", mix_lru[:, c2, 2048:2560], HS[:, 4096:4608], gr_all[:, c2, 2048:2560], ALU.mult,
                               rd=[b_hs] + b_gr, wr=[b_mixl])
                    for sq_ in (range(2) if k == 0 else ()):
                        for d in range(2):
                            for c2 in range(2):
                                k_ = (sq_ * 2 + d) * 2 + c2
                                dst = bass.AP(nh_d.tensor, ((sq_ * 2 + l) * 2 + d) * 256 + c2 * 128, [[1, 128], [1, 1]])
                                DMA("pool", dst, nhst[:, k_:k_ + 1], rd=[b_nh])
                    S.barrier()
                    check_stop("B1_%d_%d" % (l, k), [("mixl", mix_lru), ("nhst", nhst), ("hs", HS), ("xc", xc), ("xrc", xr_c), ("hb", HB), ("svh0", sv[:, 359:363])])

                def phaseB2(k, own=False):
                    AR.reset(P_ATTN)
                    xf_c = AR.alloc([2, 4096], BF16)
                    assert AR.cur <= P_Q
                    AR.reset(P_SCR)
                    Ytm = AR.alloc([32, 512], BF16)
                    Ytp = AR.alloc([4, 512], BF16)
                    tabs = [[AR.alloc([4, 512], BF16) for _ in range(3)] for cs in range(2)]
                    b_xf, b_y, b_yp = Buf("xf"), Buf("ytm"), Buf("ytp")
                    b_tab = [[Buf("tab%d_%d" % (cs, i)) for i in range(3)] for cs in range(2)]
                    for c2 in range(2):
                        for r in range(2):
                            DMA("sp", xf_c[:, c2, r * 2048:(r + 1) * 2048],
                                EXG[l][r * 1536 + 768 + c2 * 128: r * 1536 + 768 + (c2 + 1) * 128, :], rd=[b_exg[l]], wr=[b_xf])
                    for tc in range(32):
                        pb = 1 + tc % 2
                        for c2 in range(2):
                            MM(PS[pb][:, c2 * 256:(c2 + 1) * 256], xf_c[:, c2, tc * 128:(tc + 1) * 128], cs64_b, True, True,
                               rd=[b_xf, b_const], wr=[PB[pb]])
                        OP("act", "activation", Ytm[:, tc, :], PS[pb], AF.Identity, scale=(sv[:, SV_MISC + 2:SV_MISC + 3] if own else sv[:, SV_MISC + 6 + k:SV_MISC + 7 + k]),
                           rd=[PB[pb], b_const], wr=[b_y])
                    for tc in (range(4) if k == 0 else ()):
                        pb = 1 + tc % 2
                        for c2 in range(2):
                            MM(PS[pb][:, c2 * 256:(c2 + 1) * 256], xf_p[:, c2, tc * 128:(tc + 1) * 128], cs64_b, True, True,
                               rd=[b_prm, b_const], wr=[PB[pb]])
                        OP("act", "copy", Ytp[:, tc, :], PS[pb], rd=[PB[pb]], wr=[b_yp])
                    kk = 0
                    for jb in range(4):
                        for tg in range(8):
                            i = kk % 3
                            kk += 1
                            for cs in range(2):
                                DMA("sp", tabs[cs][i],
                                    dft_d[cs, tg * 512:(tg + 1) * 512, jb * 512:(jb + 1) * 512].rearrange("(t p) j -> p t j", p=128),
                                    wr=[b_tab[cs][i]])
                            for c2 in range(2):
                                pb = 3 + c2
                                for t_ in range(4):
                                    for cs in range(2):
                                        tc = tg * 4 + t_
                                        MM(PS[pb], Ytm[:, tc, c2 * 256 + cs * 128: c2 * 256 + (cs + 1) * 128], tabs[cs][i][:, t_, :],
                                           tg == 0 and t_ == 0 and cs == 0, tg == 7 and t_ == 3 and cs == 1,
                                           rd=[b_y, b_tab[cs][i]], wr=[PB[pb]])
                        for c2 in range(2):
                            OP("dve", "tensor_tensor", mix_four.rearrange("p a b -> p (a b)")[:, c2 * T + jb * 512: c2 * T + (jb + 1) * 512], PS[3 + c2], csign, ALU.mult,
                               rd=[PB[3 + c2], b_const], wr=[b_mixf])
                    for sq_ in (range(2) if k == 0 else ()):
                        for c2 in range(2):
                            pb = 5 + c2
                            for t_ in range(2):
                                for cs in range(2):
                                    MM(PS[pb][:, 0:256], Ytp[:, sq_ * 2 + t_, c2 * 256 + cs * 128: c2 * 256 + (cs + 1) * 128],
                                       dftp_b[:, cs, t_, :], t_ == 0 and cs == 0, t_ == 1 and cs == 1,
                                       rd=[b_yp, b_const], wr=[PB[pb]])
                            OP("act", "copy", mix_four[:, c2, 2048 + sq_ * 256: 2048 + (sq_ + 1) * 256], PS[pb][:, 0:256],
                               rd=[PB[pb]], wr=[b_mixf])
                    S.barrier()
                    check_stop("B2_%d_%d" % (l, k), [("mixf", mix_four)])

                def phaseC(k, own=False):
                    AR.reset(P_SCR)
                    KT = [AR.alloc([4352], BF16) for _ in range(2)]
                    VH = [AR.alloc([34, 128], BF16) for _ in range(2)]
                    b_kt = [Buf("kt0"), Buf("kt1")]
                    b_vh = [Buf("vh0"), Buf("vh1")]
                    ckf = AR.alloc([2, 512], F32)
                    cvf = AR.alloc([2, 512], F32)
                    b_ckf, b_cvf = Buf("ckf"), Buf("cvf")
                    pTP = [AR.alloc([1024], BF16) for _ in range(2)]
                    pT = [pTP[0][:, 0:512], pTP[0][:, 512:1024], pTP[1][:, 0:512], pTP[1][:, 512:1024]]
                    b_pT = [Buf("pT%d" % i) for i in range(4)]
                    fR = [AR.alloc([512], F32) for _ in range(2)]
                    fT = [AR.alloc([512], F32) for _ in range(2)]
                    fo = AR.alloc([512], F32)
                    fsq = AR.alloc([512], BF16)
                    frs = AR.alloc([512], F32)
                    ftl = AR.alloc([512], F32)
                    b_fR = [Buf("fR0"), Buf("fR1")]
                    b_fT = [Buf("fT0"), Buf("fT1")]
                    b_fo, b_fsq, b_frs, b_ftl = Buf("fo"), Buf("fsq"), Buf("frs"), Buf("ftl")
                    lamv = AR.alloc([8], F32)
                    lprod = AR.alloc([128], F32)
                    b_lam = Buf("lam")
                    OP("dve", "tensor_tensor", lprod[:, 0:64], svc(l, "wl", 0, 64), svc(l, "wl", 64, 64), ALU.mult, rd=[b_const], wr=[b_lam])
                    OP("dve", "tensor_tensor", lprod[:, 64:128], svc(l, "wl", 128, 64), svc(l, "wl", 192, 64), ALU.mult, rd=[b_const], wr=[b_lam])
                    OP("dve", "reduce_sum", lamv[:, 0:1], lprod[:, 0:64], mybir.AxisListType.X, rd=[b_lam], wr=[b_lam])
                    OP("dve", "reduce_sum", lamv[:, 1:2], lprod[:, 64:128], mybir.AxisListType.X, rd=[b_lam], wr=[b_lam])
                    OP("act", "activation", lamv[:, 2:4], lamv[:, 0:2], AF.Exp, rd=[b_lam], wr=[b_lam])
                    OP("dve", "tensor_tensor", lamv[:, 4:5], lamv[:, 3:4], lamv[:, 2:3], ALU.subtract, rd=[b_lam], wr=[b_lam])
                    OP("dve", "tensor_scalar", lamv[:, 4:5], lamv[:, 4:5], -li, None, ALU.add, rd=[b_lam], wr=[b_lam])
                    OP("dve", "tensor_scalar", lamv[:, 5:6], svc(l, "gsub"), 1.0 - li, None, ALU.mult, rd=[b_const], wr=[b_lam])
                    if own:
                        tmpQ = AR.alloc([2, TS], BF16)
                        b_tq = Buf("tmpQ")
                        DMA("act", q_all[:, :, 0:TS], QS[l][0], rd=[b_qs[l][0]], wr=b_q[0:4])
                        for hp in range(2):
                            DMA("act", tmpQ, QS[l][1][:, hp * 2:(hp + 1) * 2, :], rd=[b_qs[l][1]], wr=[b_tq])
                            blend(q_all[:, hp * 2:(hp + 1) * 2, 0:TS], tmpQ, b_q[0:4], [b_tq], b_q[0:4])
                    else:
                        DMA("act", q_all[:, :, 0:TS], QS[l][k], rd=[b_qs[l][k]], wr=b_q[0:4])
                    DMA("act", ckf, ck_d[l].rearrange("(t p) f -> p t f", p=128), wr=[b_ckf])
                    DMA("act", cvf, cv_d[l].rearrange("(t p) f -> p t f", p=128), wr=[b_cvf])
                    if l == 0 and k == 0:
                        convert_weights(0, ["pool"], pwmax=1024, kinds=("B", "G0", "G1"))
                    if l == 0 and k == 1:
                        convert_weights(1, ["pool"], pwmax=1024)
                    scale = 64 ** -0.5
                    state = {"pt": 0}

                    def attn_block(q_ap, nq, kt_ap, v_fn, nkc, rd_q, rd_k, rd_v, out_ap, wr_out):
                        def qk(kc):
                            par = kc % 2
                            for m in range(2):
                                sb = par * 2 + m
                                MM(PS[sb][:, 0:nq], kt_ap[m * 64:(m + 1) * 64, kc * 128:(kc + 1) * 128], q_ap[m * 64:(m + 1) * 64, :],
                                   True, True, rd=rd_q + rd_k, wr=[PB[sb]])
                                if nq != 512:
                                    OP("act", "activation", pT[sb][:, 0:nq], PS[sb][:, 0:nq], AF.Exp, scale=scale, rd=[PB[sb]], wr=[b_pT[sb]])
                            if nq == 512:
                                OP("act", "activation", pTP[par], PSP[par], AF.Exp, scale=scale,
                                   rd=[PB[par * 2], PB[par * 2 + 1]], wr=[b_pT[par * 2], b_pT[par * 2 + 1]])

                        def pv(kc):
                            par = kc % 2
                            for m in range(2):
                                sb = par * 2 + m
                                MM(PS[4 + m][:, 0:nq], v_fn(kc), pT[sb][:, 0:nq], kc == 0, kc == nkc - 1, rd=rd_v + [b_pT[sb]], wr=[PB[4 + m]])
                                MM(PS[6 + m][:, 0:nq], ones_b, pT[sb][:, 0:nq], kc == 0, kc == nkc - 1, rd=[b_pT[sb], b_const], wr=[PB[6 + m]])

                        qk(0)
                        for kc in range(nkc):
                            if kc + 1 < nkc:
                                qk(kc + 1)
                            pv(kc)
                        for m in range(2):
                            OP("dve", "reciprocal", fR[m][:, 0:nq], PS[6 + m][:, 0:nq], rd=[PB[6 + m]], wr=[b_fR[m]])
                            OP("dve", "tensor_tensor", fT[m][:, 0:nq], PS[4 + m][:, 0:nq], fR[m][:, 0:nq], ALU.mult, rd=[PB[4 + m], b_fR[m]], wr=[b_fT[m]])
                        OP("dve", "scalar_tensor_tensor", fo[:, 0:nq], fT[1][:, 0:nq], lamv[:, 4:5], fT[0][:, 0:nq], ALU.mult, ALU.add,
                           rd=[b_fT[0], b_fT[1], b_lam], wr=[b_fo])
                        OP("act", "activation", fsq[:, 0:nq], fo[:, 0:nq], AF.Square, rd=[b_fo], wr=[b_fsq])
                        MM(PS[0][:, 0:nq], ones_b, fsq[:, 0:nq], True, True, rd=[b_fsq, b_const], wr=[PB[0]])
                        rstd_from(PS[0][:, 0:nq], PB[0], frs[:, 0:nq], b_frs, 1.0 / 128, ftl[:, 0:nq], b_ftl)
                        OP("dve", "scalar_tensor_tensor", out_ap, fo[:, 0:nq], lamv[:, 5:6], frs[:, 0:nq], ALU.mult, ALU.mult,
                           rd=[b_fo, b_frs, b_lam], wr=wr_out)

                    for h in range(4):
                        i = h % 2
                        for r in range(2):
                            DMA("act", KT[i][:, r * 2048:(r + 1) * 2048], EXG[l][r * 1536 + h * 128: r * 1536 + (h + 1) * 128, :],
                                rd=[b_exg[l]], wr=[b_kt[i]])
                            vview = EXG[l][r * 1536 + 1024: r * 1536 + 1536, :].rearrange("r (t f) -> (r t) f", f=512)
                            vsrc = vview[:, h * 128:(h + 1) * 128].rearrange("(c p) e -> p c e", p=128)
                            DMA("act", VH[i][:, r * 16:(r + 1) * 16, :], vsrc, rd=[b_exg[l]], wr=[b_vh[i]])
                        for t_ in range(2):
                            OP("pe", "transpose", PS[0][:, t_ * 128:(t_ + 1) * 128], ckf[:, t_, h * 128:(h + 1) * 128], ident,
                               rd=[b_ckf, b_const], wr=[PB[0]])
                        OP("act", "copy", KT[i][:, 4096:4352], PS[0][:, 0:256], rd=[PB[0]], wr=[b_kt[i]])
                        OP("dve", "tensor_copy", VH[i][:, 32:34, :], cvf[:, :, h * 128:(h + 1) * 128], rd=[b_cvf], wr=[b_vh[i]])
                        for qb in range(4):
                            attn_block(q_all[:, h, qb * 512:(qb + 1) * 512], 512, KT[i], lambda kc, i=i: VH[i][:, kc, :], 34,
                                       [b_q[qb]], [b_kt[i]], [b_vh[i]], mix_attn[:, h, qb * 512:(qb + 1) * 512], [b_mixa[qb]])
                        for sq_ in (range(2) if k == 0 else ()):
                            attn_block(q_all[:, h, 2048 + sq_ * 256: 2048 + (sq_ + 1) * 256], 256, kTp[:, h, sq_ * 256:(sq_ + 1) * 256],
                                       lambda kc, sq_=sq_, h=h: Vp[:, sq_ * 2 + kc, h * 128:(h + 1) * 128], 2,
                                       [b_q[4]], [b_prm], [b_prm], mix_attn[:, h, 2048 + sq_ * 256: 2048 + (sq_ + 1) * 256], [b_mixa[4]])
                    S.barrier()
                    check_stop("C%d_%d" % (l, k), [("mixf", mix_four), ("xfp", xf_p), ("dftp", dftp_b), ("cs64", cs64_b)])

                def phaseD(k, own=False):
                    AR.reset(P_Q)
                    wo = AR.alloc([8, 1024], BF16)
                    b_wo = Buf("wo")
                    xblk = AR.alloc([8, 512], F32)
                    b_x = Buf("xblk")
                    osb = AR.alloc([8, 512], F32)
                    b_o = Buf("osb")
                    sq = AR.alloc([8, 512], BF16)
                    b_sq = Buf("sq")
                    u = sq
                    b_u = b_sq
                    hid = AR.alloc([22, 512], BF16)
                    b_hid = Buf("hid")
                    gu = [AR.alloc([8, 512], BF16) for _ in range(2)]
                    b_gu = [Buf("gu0"), Buf("gu1")]
                    dn = [AR.alloc([22, 128], BF16) for _ in range(2)]
                    b_dn = [Buf("dn0"), Buf("dn1")]
                    rstd = AR.alloc([512], F32)
                    tl = AR.alloc([512], F32)
                    b_rstd, b_tl = Buf("rstd"), Buf("tl")
                    tmpf = [AR.alloc([512], F32) for _ in range(2)]
                    b_tmpf = [Buf("tmpf0"), Buf("tmpf1")]
                    sg = [AR.alloc([512], F32) for _ in range(2)]
                    b_sg = [Buf("sg0"), Buf("sg1")]
                    ytm = [osb[:, 0:2, :].rearrange("p a b -> p (a b)"), osb[:, 2:4, :].rearrange("p a b -> p (a b)")]
                    b_ytm = [b_o, b_o]
                    for h_ in range(2):
                        DMA("sp", wo[:, h_ * 4:(h_ + 1) * 4, :],
                            WGB[l][h_ * 512:(h_ + 1) * 512, OFF_WOUT:OFF_WOUT + 1024].rearrange("(k p) n -> p k n", p=128),
                            rd=[b_wgb[l]], wr=[b_wo])
                    cnt = {"o": 0, "tmpf": 0, "gu": 0, "dn": 0, "sg": 0, "y": 0}

                    def mixk(kc, tok):
                        if kc < 4:
                            return mix_attn[:, kc, tok]
                        if kc < 6:
                            return mix_lru[:, kc - 4, tok]
                        return mix_four[:, kc - 6, tok]

                    def post_norm_update(gidx, c_):
                        OP("act", "activation", sq[:, 0:4, :], osb[:, 0:4, :], AF.Square, rd=[b_o], wr=[b_sq])
                        OP("pool", "tensor_tensor", sq[:, 4:8, :], osb[:, 4:8, :], osb[:, 4:8, :], ALU.mult, rd=[b_o], wr=[b_sq])
                        for c in range(8):
                            MM(PS[0], ones_b, sq[:, c, :], c == 0, c == 7, rd=[b_sq, b_const], wr=[PB[0]])
                        rstd_from(PS[0], PB[0], rstd, b_rstd, 1.0 / D, tl, b_tl)
                        for c in range(8):
                            i = cnt["tmpf"] % 2
                            cnt["tmpf"] += 1
                            OP("dve", "tensor_tensor", tmpf[i], osb[:, c, :], rstd, ALU.mult, rd=[b_o, b_rstd], wr=[b_tmpf[i]])
                            OP("dve", "scalar_tensor_tensor", xblk[:, c, :], tmpf[i], c_[:, gidx, c:c + 1], xblk[:, c, :], ALU.mult, ALU.add,
                               rd=[b_tmpf[i], b_cst, b_x], wr=[b_x])

                    for tb in range(5 if k == 0 else 4):
                        w = 0 if tb < 4 else 1
                        c_ = cst[l][w]
                        tok = slice(tb * 512, (tb + 1) * 512)
                        DMA("sp", xblk, XS[l][k][:, :, tok], rd=[b_xs[l][k][tb]], wr=[b_x])
                        if own and tb < 4:
                            OP("act", "activation", xblk, xblk, AF.Identity, scale=m0, rd=[b_x, b_const], wr=[b_x])
                            for c in range(8):
                                i = cnt["tmpf"] % 2
                                cnt["tmpf"] += 1
                                DMA("sp", tmpf[i], XS[l][1][:, c, tok], rd=[b_xs[l][1][tb]], wr=[b_tmpf[i]])
                                OP("dve", "scalar_tensor_tensor", xblk[:, c, :], tmpf[i], m1, xblk[:, c, :], ALU.mult, ALU.add,
                                   rd=[b_tmpf[i], b_const, b_x], wr=[b_x])
                        for c in range(8):
                            pb = 1 + cnt["o"] % 2
                            cnt["o"] += 1
                            for kc in range(8):
                                MM(PS[pb], wo[:, kc, c * 128:(c + 1) * 128], mixk(kc, tok), kc == 0, kc == 7,
                                   rd=[b_wo, b_mixa[tb], b_mixl, b_mixf], wr=[PB[pb]])
                            OP("act", "copy", osb[:, c, :], PS[pb], rd=[PB[pb]], wr=[b_o])
                        post_norm_update(2, c_)
                        OP("act", "activation", sq[:, 0:4, :], xblk[:, 0:4, :], AF.Square, rd=[b_x], wr=[b_sq])
                        OP("pool", "tensor_tensor", sq[:, 4:8, :], xblk[:, 4:8, :], xblk[:, 4:8, :], ALU.mult, rd=[b_x], wr=[b_sq])
                        for c in range(8):
                            MM(PS[0], ones_b, sq[:, c, :], c == 0, c == 7, rd=[b_sq, b_const], wr=[PB[0]])
                        rstd_from(PS[0], PB[0], rstd, b_rstd, 1.0 / D, tl, b_tl)
                        for c in range(8):
                            i = cnt["tmpf"] % 2
                            cnt["tmpf"] += 1
                            OP("dve", "scalar_tensor_tensor", tmpf[i], xblk[:, c, :], c_[:, 3, c:c + 1], rstd, ALU.mult, ALU.mult,
                               rd=[b_x, b_rstd, b_cst], wr=[b_tmpf[i]])
                            OP("act", "activation", u[:, c, :], tmpf[i], AF.Identity, bias=c_[:, 4, c:c + 1], scale=1.0,
                               rd=[b_tmpf[i], b_cst], wr=[b_u])
                        for js in range(11):
                            i = cnt["gu"] % 2
                            cnt["gu"] += 1
                            DMA("sp", gu[i], WGU[l][js], rd=[b_wgb[l]], wr=[b_gu[i]])
                            for jj in range(2):
                                j = js * 2 + jj
                                pg, pu = 3 + (j % 2), 5 + (j % 2)
                                for kc in range(8):
                                    MM(PS[pg], gu[i][:, kc, jj * 128:(jj + 1) * 128], u[:, kc, :], kc == 0, kc == 7, rd=[b_gu[i], b_u], wr=[PB[pg]])
                                for kc in range(8):
                                    MM(PS[pu], gu[i][:, kc, 256 + jj * 128: 256 + (jj + 1) * 128], u[:, kc, :], kc == 0, kc == 7,
                                       rd=[b_gu[i], b_u], wr=[PB[pu]])
                                si = cnt["sg"] % 2
                                cnt["sg"] += 1
                                OP("act", "activation", sg[si], PS[pg], AF.Silu, rd=[PB[pg]], wr=[b_sg[si]])
                                OP("dve", "tensor_tensor", hid.rearrange("p a b -> p (a b)")[:, j * 512:(j + 1) * 512], PS[pu], sg[si], ALU.mult, rd=[PB[pu], b_sg[si]], wr=[b_hid])
                        for c in range(8):
                            i = cnt["dn"] % 2
                            cnt["dn"] += 1
                            DMA("sp", dn[i], WGB[l][c * 128:(c + 1) * 128, OFF_DN:OFF_DN + 2816].rearrange("p (k j) -> p k j", j=128),
                                rd=[b_wgb[l]], wr=[b_dn[i]])
                            pb = 1 + cnt["o"] % 2
                            cnt["o"] += 1
                            for kc in range(22):
                                MM(PS[pb], dn[i][:, kc, :], hid[:, kc, :], kc == 0, kc == 21, rd=[b_dn[i], b_hid], wr=[PB[pb]])
                            OP("act", "copy", osb[:, c, :], PS[pb], rd=[PB[pb]], wr=[b_o])
                        post_norm_update(5, c_)
                        if l == 0:
                            DMA("pool", XS[1][k][:, :, tok], xblk, rd=[b_x], wr=[b_xs[1][k][tb]])
                        else:
                            dst = ys_d if tb < 4 else yp_d
                            r0 = tb * 512 if tb < 4 else 0
                            for tt in range(4):
                                yi = cnt["y"] % 2
                                cnt["y"] += 1
                                for g in range(2):
                                    pb = 7 if g == 0 else 0
                                    for cc in range(4):
                                        c = g * 4 + cc
                                        OP("pe", "transpose", PS[pb][:, cc * 128:(cc + 1) * 128], xblk[:, c, tt * 128:(tt + 1) * 128], ident,
                                           rd=[b_x, b_const], wr=[PB[pb]])
                                    OP("act" if g == 0 else "dve", "copy" if g == 0 else "tensor_copy",
                                       ytm[yi][:, g * 512:(g + 1) * 512], PS[pb], rd=[PB[pb]], wr=[b_ytm[yi]])
                                DMA("pool", dst[r0 + tt * 128: r0 + (tt + 1) * 128, :], ytm[yi], rd=[b_ytm[yi]])
                    S.barrier()
                    check_stop("D%d_%d" % (l, k), [("xs1", XS[1][k])])
                phaseA(0)
                phaseA(1)
                if l == 0:
                    for k in (0, 1):
                        phaseB1(k)
                        phaseB2(k)
                        phaseC(k)
                        phaseD(k)
                else:
                    phaseB1(0, True)
                    phaseB2(0, True)
                    phaseC(0, True)
                    phaseD(0, True)
        except _Stop:
            pass

        S.final_wait("sp")
        with nc.Block() as block:
            S.emit(block)
    nc._taps = TAPS
    return nc


def _fm(v):
    v = np.asarray(v, np.float32)
    return np.ascontiguousarray(v.reshape(-1, 128).T)


_CONST_CACHE = {}


def _constants():
    if _CONST_CACHE:
        return _CONST_CACHE
    t = np.arange(4096, dtype=np.float64)[:, None]
    j = np.arange(2048, dtype=np.float64)[None, :]
    ang = 2.0 * np.pi * ((t * j) % 4096) / 4096.0
    s = 1.0 / math.sqrt(4096 * 64)
    dft = np.stack([np.cos(ang) * s, -np.sin(ang) * s]).astype(np.float32).astype(ml_dtypes.bfloat16)
    t = np.arange(256, dtype=np.float64)[:, None]
    j = np.arange(256, dtype=np.float64)[None, :]
    ang = 2.0 * np.pi * ((t * j) % 256) / 256.0
    s = 1.0 / math.sqrt(256 * 64)
    dftp = np.stack([np.cos(ang) * s, -np.sin(ang) * s]).astype(np.float32).astype(ml_dtypes.bfloat16)
    c = np.arange(64, dtype=np.float64)
    a64 = 2.0 * np.pi * np.outer(c, c) / 64.0
    cs64 = np.zeros((128, 256), np.float32)
    for g in range(2):
        cs64[g * 64:(g + 1) * 64, g * 64:(g + 1) * 64] = np.cos(a64)
        cs64[g * 64:(g + 1) * 64, 128 + g * 64:128 + (g + 1) * 64] = np.sin(a64)
    rmat = np.zeros((128, 128), np.float32)
    for dd in range(128):
        partner = dd + 16 if (dd % 32) < 16 else dd - 16
        rmat[partner, dd] = 1.0
    _CONST_CACHE.update(dft=dft, dftp=dftp, cs64=cs64, rmat=rmat)
    return _CONST_CACHE


def _rope_table(hf):
    n = 16
    inv = (10000.0 ** (-np.arange(n, dtype=np.float32) / n)).astype(np.float32)
    tpos = np.arange(hf * 2048, (hf + 1) * 2048)
    row = (tpos // 64).astype(np.float32)
    col = (tpos % 64).astype(np.float32)
    tab = np.zeros((128, 2, 2048), np.float32)
    for p in range(128):
        dd = p % 64
        pos = row if dd < 32 else col
        i = dd % 16
        ang = (pos * inv[i]).astype(np.float32)
        sign = -1.0 if (dd % 32) < 16 else 1.0
        tab[p, 0] = np.cos(ang)
        tab[p, 1] = sign * np.sin(ang)
    return tab


_NC = None
_DEBUG = {}


def kernel(x_prompt, x_sample, cache_k, cache_v, state_lru, c, c_ctx,
           w_mod, b_mod, g_pre_mix, g_post_mix, g_pre_ffn, g_post_ffn,
           w_in, w_out, w_lambda, g_subln, conv_w, conv_b,
           lru_wa, lru_ba, lru_wx, lru_bx, lru_lambda, w_gate_up, w_down):
    global _NC
    f = lambda a: np.asarray(a, np.float32)
    x_prompt, x_sample, cache_k, cache_v, state_lru, c, c_ctx = map(f, (x_prompt, x_sample, cache_k, cache_v, state_lru, c, c_ctx))
    w_mod, b_mod, w_in, w_out, w_gate_up, w_down = map(f, (w_mod, b_mod, w_in, w_out, w_gate_up, w_down))
    lru_wa, lru_wx = f(lru_wa), f(lru_wx)
    K = _constants()
    if _NC is None:
        _NC = build_program()
    nc = _NC
    lw = np.zeros((128, 16, 128), np.float32)
    for l in range(2):
        for d in range(2):
            for g, W in enumerate((lru_wa, lru_wx)):
                for c2 in range(2):
                    idx = ((l * 2 + d) * 2 + g) * 2 + c2
                    for bb in range(2):
                        lw[bb * 64:(bb + 1) * 64, idx, bb * 64:(bb + 1) * 64] = W[l, d, c2 * 2 + bb]
    lw = lw.reshape(128, 2048)
    wsl = np.empty((2, 8, 128, WA_W + WB_W), np.float32)
    for l in range(2):
        for r in range(8):
            rows = slice(r * 128, (r + 1) * 128)
            wsl[l, r, :, 0:2304] = w_in[l, rows, :]
            wsl[l, r, :, 2304:8448] = w_mod[l, rows, :]
            wsl[l, r, :, 8448:9472] = w_out[l, rows, :]
            wsl[l, r, :, 9472:15104] = w_gate_up[l, rows, :]
            wd = w_down[l][:, r * 128:(r + 1) * 128]
            wsl[l, r, :, 15104:17920] = wd.reshape(22, 128, 128).transpose(1, 0, 2).reshape(128, 2816)
    in_maps = []
    cores = _DEBUG.get("cores", list(range(8)))
    for core in cores:
        p, hf = core // 2, core % 2
        sv = np.zeros((128, NSV), np.float32)
        for l in range(2):
            def put(name, arr):
                o, w = SVL[name]
                sv[:, l * SV_PER_L + o: l * SV_PER_L + o + w] = arr
            put("gpm", _fm(g_pre_mix[l]))
            put("gqm", _fm(g_post_mix[l]))
            put("gpf", _fm(g_pre_ffn[l]))
            put("gqf", _fm(g_post_ffn[l]))
            put("bmod", _fm(b_mod[l]))
            put("gsub", f(g_subln[l]).reshape(128, 1))
            cw = f(conv_w[l])
            put("convw", np.stack([_fm(cw[t_]) for t_ in range(4)], axis=1).reshape(128, 8))
            put("convb", _fm(conv_b[l]))
            put("ba", np.concatenate([_fm(f(lru_ba)[l, d]) for d in range(2)], axis=1))
            put("bx", np.concatenate([_fm(f(lru_bx)[l, d]) for d in range(2)], axis=1))
            put("lam", np.concatenate([_fm(f(lru_lambda)[l, d]) for d in range(2)], axis=1))
            put("wl", np.broadcast_to(f(w_lambda[l]).reshape(1, 256), (128, 256)))
            put("h0", np.concatenate([_fm(state_lru[p, l, d]) for d in range(2)], axis=1))
        par = (np.arange(128) % 2 == 1)
        sv[:, SV_MISC + 0] = 1.0 - hf
        sv[:, SV_MISC + 1] = float(hf)
        for k in range(2):
            o = k
            sv[:, SV_MISC + 2 + 2 * k] = 1.0 - o
            sv[:, SV_MISC + 3 + 2 * k] = float(o)
            sv[:, SV_MISC + 6 + k] = np.where(par & (o == 1), -1.0, 1.0)
        sv[:, SV_MISC + 2] = np.where(par & (hf == 1), -1.0, 1.0)
        cond = np.stack([_fm(c[p]), _fm(c_ctx)], axis=2)
        sv[:, SV_MISC + 8: SV_MISC + 24] = cond.reshape(128, 16)
        csign = np.ones((128, 512), np.float32)
        if hf == 1:
            csign[:, 1::2] = -1.0
        xs2 = np.stack([x_sample[p, 0:2048, :], x_sample[p, 2048:4096, :]])
        in_maps.append({
            "xs": np.ascontiguousarray(xs2),
            "xp": np.ascontiguousarray(x_prompt[2 * core:2 * core + 2].reshape(512, 1024)),
            "ck": np.ascontiguousarray(cache_k[p].reshape(2, 256, 512)),
            "cv": np.ascontiguousarray(cache_v[p].reshape(2, 256, 512)),
            "sv": sv, "lw": lw, "wsl": wsl,
            "rope": np.stack([_rope_table(0), _rope_table(1)]), "csign": np.ones((128, 512), np.float32),
            "rmat": K["rmat"], "cs64": K["cs64"],
            "dft": K["dft"], "dftp": K["dftp"],
        })
    if _DEBUG.get("stop") is not None:
        nc = build_program(stop=_DEBUG["stop"])
        res = run_bass_kernel_spmd(nc, in_maps, core_ids=list(range(len(cores))), trace=bool(_DEBUG.get("trace")))
        _DEBUG["exec_ns"] = getattr(res, "exec_time_ns", None)
        _DEBUG["results"] = res.results
        _DEBUG["taps"] = nc._taps
    else:
        res = run_bass_kernel_spmd(nc, in_maps, core_ids=list(range(8)))
    R = res.results
    y_prompt = np.empty((16, 256, 1024), np.float32)
    y_sample = np.empty((4, 4096, 1024), np.float32)
    new_k = np.empty((16, 2, 256, 4, 2, 64), np.float32)
    new_v = np.empty((16, 2, 256, 4, 128), np.float32)
    new_h = np.empty((16, 2, 2, 256), np.float32)
    for core in range(8):
        p, hf = core // 2, core % 2
        r = R[core]
        y_sample[p, hf * 2048:(hf + 1) * 2048] = np.asarray(r["ys"])
        y_prompt[2 * core:2 * core + 2] = np.asarray(r["yp"]).reshape(2, 256, 1024)
        new_k[2 * core:2 * core + 2] = np.asarray(r["nk"]).reshape(2, 2, 256, 4, 2, 64)
        new_v[2 * core:2 * core + 2] = np.asarray(r["nv"]).reshape(2, 2, 256, 4, 128)
        new_h[2 * core:2 * core + 2] = np.asarray(r["nh"]).reshape(2, 2, 2, 256)
    return (y_prompt, y_sample, new_k, new_v, new_h)
```

```python
import math
from contextlib import ExitStack
import numpy as np
import ml_dtypes
import concourse.bass as bass
import concourse.mybir as mybir
from concourse.bass_utils import run_bass_kernel_spmd

F32 = mybir.dt.float32
BF16 = mybir.dt.bfloat16
ALU = mybir.AluOpType
AF = mybir.ActivationFunctionType

D = 1024
TS = 2048
TP = 512
T = 2560
EPS = 1e-6
WA_W = 8448
WB_W = 9472
OFF_WOUT, OFF_GU, OFF_DN = 0, 1024, 6656

SVL = {}
_o = 0
for _n, _w in (("gpm", 8), ("gqm", 8), ("gpf", 8), ("gqf", 8), ("bmod", 48), ("gsub", 1), ("convw", 8),
               ("convb", 2), ("ba", 4), ("bx", 4), ("lam", 4), ("wl", 256), ("h0", 4)):
    SVL[_n] = (_o, _w)
    _o += _w
SV_PER_L = _o
SV_MISC = 2 * SV_PER_L
NSV = SV_MISC + 8 + 16


def lambda_init(l):
    return 0.8 - 0.6 * math.exp(-0.3 * l)


class Buf:
    __slots__ = ("name", "w", "r")

    def __init__(self, name):
        self.name = name
        self.w = None
        self.r = []


class Sched:
    ENGS = ("pe", "act", "dve", "pool", "sp")
    NDMA = 16

    def __init__(self, nc, es):
        self.nc = nc
        self.q = {e: [] for e in self.ENGS}
        self.sem = {e: es.enter_context(nc.semaphore("c_" + e)) for e in self.ENGS}
        self.cnt = {e: 0 for e in self.ENGS}
        self.known = {e: {} for e in self.ENGS}
        self.dsem, self.dcnt, self.dnext, self.dlast = {}, {}, {}, {}
        for e in ("sp", "pool", "act"):
            self.dsem[e] = [es.enter_context(nc.semaphore("d_%s%d" % (e, i))) for i in range(self.NDMA)]
            self.dcnt[e] = [0] * self.NDMA
            self.dlast[e] = [None] * self.NDMA
            self.dnext[e] = 0

    def _deps(self, reads, writes):
        deps = []
        for b in reads:
            if b.w is not None:
                deps.append(b.w)
        for b in writes:
            if b.w is not None:
                deps.append(b.w)
            deps.extend(b.r)
        return deps

    def _emit_waits(self, eng, deps, skip_own=False):
        kn = self.known[eng]
        best = {}
        for (s, v, key) in deps:
            if skip_own and key == ("c", eng):
                continue
            if kn.get(key, 0) >= v:
                continue
            if best.get(key, (None, 0))[1] < v:
                best[key] = (s, v)
        for key, (s, v) in best.items():
            kn[key] = v
            self.q[eng].append(lambda e, s=s, v=v: e.wait_ge(s, v))

    def _commit(self, ev, reads, writes):
        for b in reads:
            b.r.append(ev)
            if len(b.r) > 64:
                last = {}
                for x in b.r:
                    if last.get(x[2], (None, 0, None))[1] < x[1]:
                        last[x[2]] = x
                b.r = list(last.values())
        for b in writes:
            b.w = ev
            b.r = []

    def op(self, eng, fn, reads=(), writes=()):
        writes = list(writes) + [b for b in reads if b.name.startswith("ps") and b not in writes]
        deps = self._deps(reads, writes)
        self._emit_waits(eng, deps, skip_own=(eng == "pe"))
        self.cnt[eng] += 1
        s = self.sem[eng]
        ev = (s, self.cnt[eng], ("c", eng))
        self.q[eng].append(lambda e, fn=fn, s=s: fn(e).then_inc(s, 1))
        self._commit(ev, reads, writes)
        return ev

    def dma(self, eng, fn, reads=(), writes=()):
        i = self.dnext[eng]
        self.dnext[eng] = (i + 1) % self.NDMA
        deps = self._deps(reads, writes)
        if self.dlast[eng][i] is not None:
            deps.append(self.dlast[eng][i])
        self._emit_waits(eng, deps)
        self.dcnt[eng][i] += 16
        s = self.dsem[eng][i]
        ev = (s, self.dcnt[eng][i], ("d", eng, i))
        self.dlast[eng][i] = ev
        self.q[eng].append(lambda e, fn=fn, s=s: fn(e).then_inc(s, 16))
        self._commit(ev, reads, writes)
        return ev

    def _all_events(self):
        evs = []
        for e in self.ENGS:
            if self.cnt[e] > 0:
                evs.append((self.sem[e], self.cnt[e], ("c", e)))
        for e in self.dsem:
            for i in range(self.NDMA):
                if self.dlast[e][i] is not None:
                    evs.append(self.dlast[e][i])
        ex = getattr(self, "extra_events", [])
        if ex:
            evs.append(ex[-1])
        return evs

    def barrier(self):
        evs = self._all_events()
        for e in self.ENGS:
            self._emit_waits(e, evs)

    def final_wait(self, eng="sp"):
        self._emit_waits(eng, self._all_events())

    def emit(self, block):
        q = self.q

        @block.tensor
        def _(e):
            for t in q["pe"]:
                t(e)

        @block.scalar
        def _(e):
            for t in q["act"]:
                t(e)

        @block.vector
        def _(e):
            for t in q["dve"]:
                t(e)

        @block.gpsimd
        def _(e):
            for t in q["pool"]:
                t(e)

        @block.sync
        def _(e):
            for t in q["sp"]:
                t(e)


def build_program(stop=None):
    nc = bass.Bass("TRN2", target_bir_lowering=False)
    TAPS = []

    def din(name, shape, dt=F32):
        return nc.dram_tensor(name, shape, dt, kind="ExternalInput").ap()

    def dout(name, shape, dt=F32):
        return nc.dram_tensor(name, shape, dt, kind="ExternalOutput").ap()

    def dint(name, shape, dt):
        return nc.dram_tensor(name, shape, dt).ap()

    xs_d = din("xs", [2, TS, D])
    xp_d = din("xp", [TP, D])
    ck_d = din("ck", [2, 256, 512])
    cv_d = din("cv", [2, 256, 512])
    sv_d = din("sv", [128, NSV])
    lw_d = din("lw", [128, 2048])
    wsl_d = din("wsl", [2, 8, 128, WA_W + WB_W])
    rope_d = din("rope", [2, 128, 2, TS])
    csign_d = din("csign", [128, 512])
    rmat_d = din("rmat", [128, 128])
    cs64_d = din("cs64", [128, 256])
    dft_d = din("dft", [2, 4096, 2048], BF16)
    dftp_d = din("dftp", [2, 256, 256], BF16)
    ys_d = dout("ys", [TS, D])
    yp_d = dout("yp", [TP, D])
    nk_d = dout("nk", [2, 2, 256, 512])
    nv_d = dout("nv", [2, 2, 256, 512])
    nh_d = dout("nh", [2, 2, 2, 256])

    WGA = [dint("WGA%d" % l, [1024, WA_W], BF16) for l in range(2)]
    WGB = [dint("WGB%d" % l, [1024, WB_W], BF16) for l in range(2)]
    WGU = [dint("WGU%d" % l, [11, 128, 8, 512], BF16) for l in range(2)]
    LS = dint("LS0", [128, 2, TS], BF16)
    EXG = [dint("EXG%d" % l, [3072, 2048], BF16) for l in range(2)]
    QS = [[dint("QS%d_%d" % (l, k), [128, 4, TS], BF16) for k in range(2)] for l in range(2)]
    GS = [[dint("GS%d_%d" % (l, k), [128, 2, TS], BF16) for k in range(2)] for l in range(2)]
    XS = [[dint("XS%d_%d" % (l, k), [128, 8, T], F32) for k in range(2)] for l in range(2)]

    es = ExitStack()
    with es:
        S = Sched(nc, es)

        def OP(eng, method, *args, rd=(), wr=(), **kw):
            return S.op(eng, lambda e: getattr(e, method)(*args, **kw), rd, wr)

        def DMA(eng, out, in_, rd=(), wr=()):
            return S.dma(eng, lambda e: e.dma_start(out=out, in_=in_), rd, wr)

        def MM(out, lhsT, rhs, start, stop, rd=(), wr=()):
            return S.op("pe", lambda e: e.matmul(out, lhsT, rhs, start=start, stop=stop), rd, wr)

        class _Stop(Exception):
            pass

        def tap(name, src):
            o = nc.dram_tensor("dbg_" + name, list(src.shape), src.dtype, kind="ExternalOutput").ap()
            S.dma("sp", lambda e: e.dma_start(out=o, in_=src), (), ())
            TAPS.append("dbg_" + name)

        def check_stop(tag, taps=()):
            if stop == tag:
                S.barrier()
                for (n_, a_) in taps:
                    tap(n_, a_)
                raise _Stop()

        cc_sem = es.enter_context(nc.semaphore("cc_sem"))
        cc_state = {"n": 0}

        def CC(groups, src, dst, rd, wr):
            deps = S._deps(rd, wr)
            S._emit_waits("pool", deps)
            cc_state["n"] += 1
            v = cc_state["n"]
            ev = (cc_sem, v, ("cc",))
            S.q["pool"].append(lambda e: e.collective_compute(
                "AllGather", ALU.bypass, replica_groups=groups, ins=[src], outs=[dst]).then_inc(cc_sem, 1))
            S._commit(ev, rd, wr)
            S.extra_events = getattr(S, "extra_events", [])
            S.extra_events.append(ev)

        NBIG = 188 * 1024 // 4
        BIG = es.enter_context(nc.sbuf_tensor("big", [128, NBIG], F32))
        PSP = [es.enter_context(nc.psum_tensor("psp%d" % i, [128, 1024], F32))[:, :] for i in range(2)]
        PS = [PSP[0][:, 0:512], PSP[0][:, 512:1024], PSP[1][:, 0:512], PSP[1][:, 512:1024]] + \
             [es.enter_context(nc.psum_tensor("ps%d" % i, [128, 512], F32))[:, :] for i in range(4, 8)]
        PB = [Buf("ps%d" % i) for i in range(8)]

        class Arena:
            def __init__(self, start, end):
                self.start, self.cur, self.end = start, start, end

            def alloc(self, free_shape, dt):
                n = 1
                for s_ in free_shape:
                    n *= s_
                nbytes = n * (4 if dt == F32 else 2)
                nbytes = (nbytes + 63) // 64 * 64
                off = self.cur
                self.cur += nbytes
                assert self.cur <= self.end, ("arena overflow", self.cur, self.end)
                ap = BIG[:, off // 4:(off + nbytes) // 4]
                if dt != F32:
                    ap = ap.bitcast(dt)
                ap = ap[:, 0:n]
                if len(free_shape) == 2:
                    ap = ap.rearrange("p (a b) -> p a b", a=free_shape[0], b=free_shape[1])
                elif len(free_shape) == 3:
                    ap = ap.rearrange("p (a b c) -> p a b c", a=free_shape[0], b=free_shape[1], c=free_shape[2])
                return ap

            def reset(self, to=None):
                self.cur = self.start if to is None else to

        TOTAL = NBIG * 4
        AR = Arena(0, TOTAL)
        sv = AR.alloc([NSV], F32)
        ident = AR.alloc([128], F32)
        ones_b = AR.alloc([128], BF16)
        rmat_b = AR.alloc([128], BF16)
        cs64_b = AR.alloc([256], BF16)
        lw_b = AR.alloc([16, 128], BF16)
        rope_c = AR.alloc([TS], F32)
        rope_s = AR.alloc([TS], F32)
        csign = AR.alloc([512], F32)
        cst = [[AR.alloc([6, 8], F32) for w in range(2)] for l in range(2)]
        scond = AR.alloc([8, 2], BF16)
        small = AR.alloc([64], F32)
        nhst = AR.alloc([16], F32)
        dftp_b = AR.alloc([2, 2, 256], BF16)
        P_MLRU = AR.cur
        mix_lru = AR.alloc([2, T], BF16)
        P_MFOUR = AR.cur
        mix_four = AR.alloc([2, T], BF16)
        P_ATTN = AR.cur
        mix_attn = AR.alloc([4, T], BF16)
        P_Q = AR.cur
        kTp = AR.alloc([4, TP], BF16)
        Vp = AR.alloc([4, 512], BF16)
        xr_p = AR.alloc([2, TP], BF16)
        xf_p = AR.alloc([2, TP], BF16)
        q_all = AR.alloc([4, T], BF16)
        gr_all = AR.alloc([2, T], BF16)
        P_SCR = AR.cur
        print("persistent bytes", P_MLRU, P_Q, P_SCR, TOTAL)
        b_const = Buf("const")
        b_cst = Buf("cst")
        b_q = [Buf("q%d" % i) for i in range(5)]
        b_gr = [Buf("gr%d" % i) for i in range(5)]
        b_prm = Buf("prm")
        b_mixa = [Buf("mixa%d" % i) for i in range(5)]
        b_mixl = Buf("mixl")
        b_mixf = Buf("mixf")
        b_exg = [Buf("exg0"), Buf("exg1")]
        b_wga = [Buf("wga0"), Buf("wga1")]
        b_wgb = [Buf("wgb0"), Buf("wgb1")]
        b_xs = [[[Buf("xs%d_%d_%d" % (l, k, i)) for i in range(5)] for k in range(2)] for l in range(2)]
        b_qs = [[Buf("qs%d_%d" % (l, k)) for k in range(2)] for l in range(2)]
        b_rope = Buf("rope")
        b_ls = Buf("ls")
        b_nh = Buf("nh")

        def svc(l, name, i=0, n=1):
            o, w = SVL[name]
            return sv[:, l * SV_PER_L + o + i: l * SV_PER_L + o + i + n]

        try:
            DMA("sp", sv, sv_d, wr=[b_const])
            DMA("sp", csign, csign_d, wr=[b_const])
            DMA("sp", dftp_b, dftp_d.rearrange("c (t p) j -> p c t j", p=128), wr=[b_const])
            OP("pool", "memset", ident, 0.0, wr=[b_const])
            OP("pool", "affine_select", ident, ident, [[-1, 128]], ALU.not_equal, 1.0, base=0, channel_multiplier=1,
               rd=[b_const], wr=[b_const])
            OP("pool", "memset", ones_b, 1.0, wr=[b_const])
            OP("pool", "memset", nhst, 0.0, wr=[b_nh])
            AR.reset(P_SCR)
            st_a = AR.alloc([2048], F32)
            st_b = AR.alloc([128], F32)
            st_c = AR.alloc([256], F32)
            b_st = Buf("st")
            DMA("sp", st_a, lw_d, wr=[b_st])
            DMA("sp", st_b, rmat_d, wr=[b_st])
            DMA("sp", st_c, cs64_d, wr=[b_st])
            OP("dve", "tensor_copy", lw_b.rearrange("p a b -> p (a b)"), st_a, rd=[b_st], wr=[b_const])
            OP("dve", "tensor_copy", rmat_b, st_b, rd=[b_st], wr=[b_const])
            OP("dve", "tensor_copy", cs64_b, st_c, rd=[b_st], wr=[b_const])
            OP("act", "activation", scond.rearrange("p a b -> p (a b)"), sv[:, SV_MISC + 8: SV_MISC + 24], AF.Silu,
               rd=[b_const], wr=[b_const])

            def convert_weights(l, cast_engs, pwmax=2368, reset_to=None, kinds=None):
                if reset_to is not None:
                    AR.reset(reset_to)
                stf = [AR.alloc([pwmax], F32) for _ in range(2)]
                stb = [AR.alloc([pwmax], BF16) for _ in range(2)]
                bf = [Buf("stf0"), Buf("stf1")]
                bb = [Buf("stb0"), Buf("stb1")]
                jobs = []
                def split(c0, wtot, kind, base, align=1):
                    step = (pwmax // align) * align
                    o = 0
                    while o < wtot:
                        w_ = min(step, wtot - o)
                        jobs.append((c0 + o, w_, kind, base + o))
                        o += w_
                split(0, WA_W, "A", 0)
                split(WA_W + OFF_WOUT, 1024, "B", OFF_WOUT)
                split(WA_W + OFF_GU, 2816, "G0", 0, 256)
                split(WA_W + OFF_GU + 2816, 2816, "G1", 0, 256)
                split(WA_W + OFF_DN, 2816, "B", OFF_DN)
                if kinds is not None:
                    jobs = [j_ for j_ in jobs if j_[2] in kinds]
                k_ = 0
                for r in range(8):
                    for (c0, w_, kind, base) in jobs:
                        i = k_ % 2
                        eng = cast_engs[k_ % len(cast_engs)]
                        k_ += 1
                        DMA("sp", stf[i][:, 0:w_], wsl_d[l, r, :, c0:c0 + w_], wr=[bf[i]])
                        OP(eng, "tensor_copy", stb[i][:, 0:w_], stf[i][:, 0:w_], rd=[bf[i]], wr=[bb[i]])
                        if kind == "A":
                            DMA("pool", WGA[l][r * 128:(r + 1) * 128, base:base + w_], stb[i][:, 0:w_], rd=[bb[i]], wr=[b_wga[l]])
                        elif kind == "B":
                            DMA("pool", WGB[l][r * 128:(r + 1) * 128, base:base + w_], stb[i][:, 0:w_], rd=[bb[i]], wr=[b_wgb[l]])
                        else:
                            part = int(kind[1])
                            ns_ = w_ // 256
                            js0 = base // 256
                            DMA("pool", WGU[l][js0:js0 + ns_, :, r, part * 256:(part + 1) * 256].rearrange("s p n -> p s n"),
                                stb[i][:, 0:w_].rearrange("p (s n) -> p s n", n=256), rd=[bb[i]], wr=[b_wgb[l]])

            convert_weights(0, ["dve", "pool"], reset_to=P_SCR + 12 * 1024, kinds=("A",))
            S.barrier()
            check_stop("setup", [("wga", WGA[0][:, 0:2304])])

            def rstd_from(ps_ap, pbuf, out_ap, obuf, scale, tmp_ap, tbuf):
                OP("act", "activation", tmp_ap, ps_ap, AF.Ln, bias=small[:, 0:1], scale=scale, rd=[pbuf, b_const], wr=[tbuf])
                OP("act", "activation", out_ap, tmp_ap, AF.Exp, scale=-0.5, rd=[tbuf], wr=[obuf])

            OP("pool", "memset", small[:, 0:1], EPS, wr=[b_const])
            OP("pool", "memset", small[:, 1:2], 1.0, wr=[b_const])

            m0 = sv[:, SV_MISC + 0:SV_MISC + 1]
            m1 = sv[:, SV_MISC + 1:SV_MISC + 2]

            def blend(dst, other, rd_dst, rd_other, wr_dst):
                OP("act", "activation", dst, dst, AF.Identity, scale=m0, rd=list(rd_dst) + [b_const], wr=wr_dst)
                OP("dve", "scalar_tensor_tensor", dst, other, m1, dst, ALU.mult, ALU.add,
                   rd=list(rd_other) + [b_const] + list(wr_dst), wr=wr_dst)

            for l in range(2):
                li = lambda_init(l)
                AR.reset(P_SCR)
                wm = [AR.alloc([8, 1536], BF16) for _ in range(2)]
                bwm = [Buf("wm0"), Buf("wm1")]
                modsb = AR.alloc([48, 2], F32)
                bmod_ = Buf("modsb")
                for sl in range(4):
                    i = sl % 2
                    DMA("sp", wm[i], WGA[l][:, 2304 + sl * 1536: 2304 + (sl + 1) * 1536].rearrange("(k p) n -> p k n", p=128),
                        rd=[b_wga[l]], wr=[bwm[i]])
                    for jj in range(12):
                        j = sl * 12 + jj
                        for kc in range(8):
                            MM(PS[0][:, 2 * j:2 * j + 2], wm[i][:, kc, jj * 128:(jj + 1) * 128], scond[:, kc, :],
                               kc == 0, kc == 7, rd=[bwm[i], b_const], wr=[PB[0]])
                check_stop("Ma%d" % l, [("wm0", wm[0]), ("wm1", wm[1])])
                psm = PS[0][:, 0:96].rearrange("p (a b) -> p a b", b=2)
                for w in range(2):
                    OP("dve", "tensor_tensor", modsb[:, :, w], psm[:, :, w], svc(l, "bmod", 0, 48), ALU.add,
                       rd=[PB[0], b_const], wr=[bmod_])
                for w in range(2):
                    c_ = cst[l][w]
                    OP("dve", "scalar_tensor_tensor", c_[:, 0, :], modsb[:, 8:16, w], 1.0, svc(l, "gpm", 0, 8), ALU.add, ALU.mult,
                       rd=[bmod_, b_const], wr=[b_cst])
                    OP("dve", "tensor_copy", c_[:, 1, :], modsb[:, 0:8, w], rd=[bmod_], wr=[b_cst])
                    OP("dve", "tensor_tensor", c_[:, 2, :], modsb[:, 16:24, w], svc(l, "gqm", 0, 8), ALU.mult,
                       rd=[bmod_, b_const], wr=[b_cst])
                    OP("dve", "scalar_tensor_tensor", c_[:, 3, :], modsb[:, 32:40, w], 1.0, svc(l, "gpf", 0, 8), ALU.add, ALU.mult,
                       rd=[bmod_, b_const], wr=[b_cst])
                    OP("dve", "tensor_copy", c_[:, 4, :], modsb[:, 24:32, w], rd=[bmod_], wr=[b_cst])
                    OP("dve", "tensor_tensor", c_[:, 5, :], modsb[:, 40:48, w], svc(l, "gqf", 0, 8), ALU.mult,
                       rd=[bmod_, b_const], wr=[b_cst])
                S.barrier()
                check_stop("M%d" % l, [("cst0", cst[l][0]), ("cst1", cst[l][1]), ("modsb", modsb)])

                def phaseA(k):
                    AR.reset(P_MLRU)
                    win = AR.alloc([8, 2304], BF16)
                    rc_t = AR.alloc([512], F32)
                    rs_t = AR.alloc([512], F32)
                    assert AR.cur <= P_Q
                    AR.reset(P_SCR)
                    b_win = Buf("win")
                    xblk = AR.alloc([8, 512], F32)
                    b_x = Buf("xblk")
                    xtm = [AR.alloc([1024], F32) for _ in range(4)]
                    b_xtm = [Buf("xtm%d" % i) for i in range(4)]
                    sq = AR.alloc([8, 512], BF16)
                    b_sq = Buf("sq")
                    u = AR.alloc([8, 512], BF16)
                    b_u = Buf("u")
                    rstd = AR.alloc([512], F32)
                    b_rstd = Buf("rstd")
                    tl = AR.alloc([512], F32)
                    b_tl = Buf("tl")
                    tmpf = [AR.alloc([512], F32) for _ in range(2)]
                    b_tmpf = [Buf("tmpf0"), Buf("tmpf1")]
                    qb_ = [AR.alloc([512], BF16) for _ in range(2)]
                    b_qb = [Buf("qb0"), Buf("qb1")]
                    t1_ = [AR.alloc([512], F32) for _ in range(2)]
                    t2_ = [AR.alloc([512], F32) for _ in range(2)]
                    b_t1 = [Buf("t1_0"), Buf("t1_1")]
                    b_t2 = [Buf("t2_0"), Buf("t2_1")]
                    stg = [AR.alloc([512], BF16) for _ in range(3)]
                    b_stg = [Buf("stg%d" % i) for i in range(3)]
                    stgf = [AR.alloc([512], F32) for _ in range(2)]
                    b_stgf = [Buf("stgf0"), Buf("stgf1")]
                    b_rt = Buf("rt")
                    for h_ in range(2):
                        DMA("sp", win[:, h_ * 4:(h_ + 1) * 4, :],
                            WGA[l][h_ * 512:(h_ + 1) * 512, 0:2304].rearrange("(k p) n -> p k n", p=128),
                            rd=[b_wga[l]], wr=[b_win])
                    cnt = {"pj": 0, "rot": 0, "tm": 0, "stg": 0, "stgf": 0, "tmpf": 0, "qb": 0}
                    DMA("sp", rope_c, rope_d[k, :, 0, :], wr=[b_rope])
                    DMA("sp", rope_s, rope_d[k, :, 1, :], wr=[b_rope])
                    b_exb_k = Buf("exbk")
                    exbk = EXG[l][k * 1536:(k + 1) * 1536, :]
                    for tb in range(5 if k == 0 else 4):
                        w = 0 if tb < 4 else 1
                        c_ = cst[l][w]
                        tok = slice(tb * 512, (tb + 1) * 512)
                        if l == 0:
                            src = xs_d[k] if tb < 4 else xp_d
                            r0 = tb * 512 if tb < 4 else 0
                            for tt in range(4):
                                DMA("sp", xtm[tt], src[r0 + tt * 128: r0 + (tt + 1) * 128, :], wr=[b_xtm[tt]])
                            for c in range(8):
                                pb = 1 + (c % 2)
                                for tt in range(4):
                                    OP("pe", "transpose", PS[pb][:, tt * 128:(tt + 1) * 128], xtm[tt][:, c * 128:(c + 1) * 128], ident,
                                       rd=[b_xtm[tt], b_const], wr=[PB[pb]])
                                OP("act" if c % 2 == 0 else "dve", "copy" if c % 2 == 0 else "tensor_copy", xblk[:, c, :], PS[pb],
                                   rd=[PB[pb]], wr=[b_x])
                            DMA("pool", XS[0][k][:, :, tok], xblk, rd=[b_x], wr=[b_xs[0][k][tb]])
                        else:
                            DMA("sp", xblk, XS[1][k][:, :, tok], rd=[b_xs[1][k][tb]], wr=[b_x])
                        if tb == 0 and k == 0:
                            check_stop("Ax%d" % l, [("xblk", xblk)])
                        OP("act", "activation", sq[:, 0:4, :], xblk[:, 0:4, :], AF.Square, rd=[b_x], wr=[b_sq])
                        OP("pool", "tensor_tensor", sq[:, 4:8, :], xblk[:, 4:8, :], xblk[:, 4:8, :], ALU.mult, rd=[b_x], wr=[b_sq])
                        for c in range(8):
                            MM(PS[0], ones_b, sq[:, c, :], c == 0, c == 7, rd=[b_sq, b_const], wr=[PB[0]])
                        rstd_from(PS[0], PB[0], rstd, b_rstd, 1.0 / D, tl, b_tl)
                        for c in range(8):
                            i = cnt["tmpf"] % 2
                            cnt["tmpf"] += 1
                            OP("dve", "scalar_tensor_tensor", tmpf[i], xblk[:, c, :], c_[:, 0, c:c + 1], rstd, ALU.mult, ALU.mult,
                               rd=[b_x, b_rstd, b_cst], wr=[b_tmpf[i]])
                            OP("act", "activation", u[:, c, :], tmpf[i], AF.Identity, bias=c_[:, 1, c:c + 1], scale=1.0,
                               rd=[b_tmpf[i], b_cst], wr=[b_u])

                        if tb == 0 and k == 0:
                            check_stop("An%d" % l, [("u", u), ("rstd", rstd)])

                        if tb < 4:
                            OP("act", "copy", rc_t, rope_c[:, tok], rd=[b_rope], wr=[b_rt])
                            OP("act", "copy", rs_t, rope_s[:, tok], rd=[b_rope], wr=[b_rt])

                        def proj_fm(j):
                            pb = 3 + (cnt["pj"] % 3)
                            cnt["pj"] += 1
                            for kc in range(8):
                                MM(PS[pb], win[:, kc, j * 128:(j + 1) * 128], u[:, kc, :], kc == 0, kc == 7,
                                   rd=[b_win, b_u], wr=[PB[pb]])
                            return pb

                        def proj_tm(tt, c0):
                            pb = 7
                            for kc in range(8):
                                MM(PS[pb], u[:, kc, tt * 128:(tt + 1) * 128], win[:, kc, c0:c0 + 512], kc == 0, kc == 7,
                                   rd=[b_win, b_u], wr=[PB[pb]])
                            return pb

                        def nstg():
                            i = cnt["stg"] % 3
                            cnt["stg"] += 1
                            return i

                        for j in range(18):
                            kind = ("q", "k", "v", "xr", "gr", "xf")[[0, 0, 0, 0, 1, 1, 1, 1, 2, 2, 2, 2, 3, 3, 4, 4, 5, 5][j]]
                            if kind == "v":
                                continue
                            pb = proj_fm(j)
                            dbg0 = (tb == 0 and k == 0 and l == 0 and j == 0)
                            if dbg0:
                                check_stop("As1", [("u", u)])
                            if kind in ("q", "k"):
                                h = j % 4
                                if tb < 4:
                                    i = cnt["qb"] % 2
                                    cnt["qb"] += 1
                                    OP("act", "copy", qb_[i], PS[pb], rd=[PB[pb]], wr=[b_qb[i]])
                                    if dbg0:
                                        check_stop("As2", [("qb", qb_[i])])
                                    MM(PS[6], rmat_b, qb_[i], True, True, rd=[b_qb[i], b_const], wr=[PB[6]])
                                    if dbg0:
                                        check_stop("As3", [("qb", qb_[i])])
                                    OP("dve", "tensor_tensor", t1_[i], PS[pb], rc_t, ALU.mult, rd=[PB[pb], b_rt], wr=[b_t1[i]])
                                    OP("dve", "tensor_tensor", t2_[i], PS[6], rs_t, ALU.mult, rd=[PB[6], b_rt], wr=[b_t2[i]])
                                    if dbg0:
                                        check_stop("As4", [("t1", t1_[i]), ("t2", t2_[i])])
                                    if kind == "q":
                                        OP("pool", "tensor_tensor", q_all[:, h, tok], t1_[i], t2_[i], ALU.add,
                                           rd=[b_t1[i], b_t2[i]], wr=[b_q[tb]])
                                    else:
                                        si = nstg()
                                        OP("pool", "tensor_tensor", stg[si], t1_[i], t2_[i], ALU.add,
                                           rd=[b_t1[i], b_t2[i]], wr=[b_stg[si]])
                                        DMA("pool", exbk[h * 128:(h + 1) * 128, tok], stg[si], rd=[b_stg[si]], wr=[b_exb_k])
                                else:
                                    if kind == "q":
                                        OP("act", "copy", q_all[:, h, tok], PS[pb], rd=[PB[pb]], wr=[b_q[tb]])
                                    else:
                                        OP("act", "copy", kTp[:, h, :], PS[pb], rd=[PB[pb]], wr=[b_prm])
                            elif kind == "gr":
                                OP("act", "activation", gr_all[:, j - 14, tok], PS[pb], AF.Gelu_apprx_tanh, rd=[PB[pb]], wr=[b_gr[tb]])
                            else:
                                c2 = j % 2
                                if tb < 4:
                                    si = nstg()
                                    OP("act", "copy", stg[si], PS[pb], rd=[PB[pb]], wr=[b_stg[si]])
                                    base = 512 if kind == "xr" else 768
                                    DMA("pool", exbk[base + c2 * 128: base + (c2 + 1) * 128, tok], stg[si], rd=[b_stg[si]], wr=[b_exb_k])
                                else:
                                    dst = xr_p if kind == "xr" else xf_p
                                    OP("act", "copy", dst[:, c2, :], PS[pb], rd=[PB[pb]], wr=[b_prm])
                            if tb == 0 and k == 0 and l == 0:
                                check_stop("Aj%d" % j, [("q", q_all[:, :, 0:512])])
                        for tt in range(4):
                            pb = proj_tm(tt, 1024)
                            if tb < 4:
                                si = nstg()
                                OP("dve", "tensor_copy", stg[si], PS[pb], rd=[PB[pb]], wr=[b_stg[si]])
                                vview = exbk[1024:1536, :].rearrange("r (t f) -> (r t) f", f=512)
                                vdst = vview[tb * 512 + tt * 128: tb * 512 + (tt + 1) * 128, :]
                                DMA("pool", vdst, stg[si], rd=[b_stg[si]], wr=[b_exb_k])
                            else:
                                seq, hh = tt // 2, tt % 2
                                fi = cnt["stgf"] % 2
                                cnt["stgf"] += 1
                                OP("dve", "tensor_copy", stgf[fi], PS[pb], rd=[PB[pb]], wr=[b_stgf[fi]])
                                OP("act", "copy", Vp[:, tt, :], PS[pb], rd=[PB[pb]], wr=[b_prm])
                                DMA("pool", nv_d[seq, l, hh * 128:(hh + 1) * 128, :], stgf[fi], rd=[b_stgf[fi]])
                                pb = proj_tm(tt, 512)
                                fi = cnt["stgf"] % 2
                                cnt["stgf"] += 1
                                OP("dve", "tensor_copy", stgf[fi], PS[pb], rd=[PB[pb]], wr=[b_stgf[fi]])
                                DMA("pool", nk_d[seq, l, hh * 128:(hh + 1) * 128, :], stgf[fi], rd=[b_stgf[fi]])
                        if tb == 0 and k == 0:
                            check_stop("Ap%d" % l, [("q", q_all), ("gr", gr_all)])
                    DMA("pool", QS[l][k], q_all[:, :, 0:TS], rd=b_q[0:4], wr=[b_qs[l][k]])
                    DMA("pool", GS[l][k], gr_all[:, :, 0:TS], rd=b_gr[0:4], wr=[b_qs[l][k]])
                    b_exg[l].w = None
                    S.barrier()
                    check_stop("A%d_%d" % (l, k), [("q", q_all), ("gr", gr_all), ("exg", EXG[l]), ("ktp", kTp), ("vp", Vp), ("xrp", xr_p), ("xs", XS[l][k])])

                def phaseB1(k, own=False):
                    if l == 0 and k == 1:
                        DMA("sp", mix_lru[:, :, 0:TS], LS, rd=[b_ls], wr=[b_mixl])
                        S.barrier()
                        return
                    AR.reset(P_SCR)
                    NT = 4608
                    xr_c = AR.alloc([NT], BF16)
                    xc = AR.alloc([NT], F32)
                    xcb = AR.alloc([NT], BF16)
                    HS = AR.alloc([NT], F32)
                    _sv = AR.cur
                    AR.reset(P_MFOUR)
                    HB = AR.alloc([2560], F32)
                    GR = AR.alloc([2560], F32)
                    assert AR.cur <= P_Q
                    AR.reset(_sv)
                    GI = AR.alloc([2560], F32)
                    GC = AR.alloc([2560], F32)
                    sp_ = AR.alloc([8], F32)
                    b_xr, b_xc, b_xcb, b_hs, b_hb = Buf("xr"), Buf("xc"), Buf("xcb"), Buf("hs"), Buf("hb")
                    b_g = Buf("gates")
                    b_sp = Buf("sp")
                    NTK = 4608 if k == 0 else 4096
                    seqs = [(0, 4096), (4096, 4352), (4352, 4608)] if k == 0 else [(0, 4096)]
                    _sv2 = AR.cur
                    AR.reset(P_MFOUR + 20480)
                    tmpA = AR.alloc([2048], BF16)
                    tmpB = AR.alloc([2048], BF16)
                    assert AR.cur <= P_Q
                    AR.reset(_sv2)
                    b_tA, b_tB = Buf("tA"), Buf("tB")
                    if own:
                        _sv3 = AR.cur
                        AR.reset(P_MFOUR + 20480)
                        tmpG = AR.alloc([2, TS], BF16)
                        assert AR.cur <= P_Q
                        AR.reset(_sv3)
                        b_tg = Buf("tmpG")
                        DMA("sp", gr_all[:, :, 0:TS], GS[l][0], rd=[b_qs[l][0]], wr=b_gr[0:4])
                        DMA("sp", tmpG, GS[l][1], rd=[b_qs[l][1]], wr=[b_tg])
                        blend(gr_all[:, :, 0:TS], tmpG, b_gr[0:4], [b_tg], b_gr[0:4])
                    else:
                        DMA("sp", gr_all[:, :, 0:TS], GS[l][k], rd=[b_qs[l][k]], wr=b_gr[0:4])
                        if l == 0:
                            _sv4 = AR.cur
                            AR.reset(P_MFOUR + 20480)
                            grO = AR.alloc([2, TS], BF16)
                            assert AR.cur <= P_Q
                            AR.reset(_sv4)
                            b_gro = Buf("grO")
                            DMA("sp", grO, GS[l][1], rd=[b_qs[l][1]], wr=[b_gro])
                    a0 = sv[:, SV_MISC + 0:SV_MISC + 1]
                    a1 = sv[:, SV_MISC + 1:SV_MISC + 2]
                    OP("act", "activation", sp_[:, 0:4], svc(l, "lam", 0, 4), AF.Exp, scale=-1.0, rd=[b_const], wr=[b_sp])
                    OP("act", "activation", sp_[:, 4:8], sp_[:, 0:4], AF.Ln, bias=small[:, 1:2], scale=1.0, rd=[b_sp, b_const], wr=[b_sp])
                    OP("dve", "tensor_scalar", sp_[:, 0:4], sp_[:, 4:8], -8.0, None, ALU.mult, rd=[b_sp], wr=[b_sp])
                    OP("dve", "tensor_scalar", sp_[:, 4:8], sp_[:, 4:8], -16.0, None, ALU.mult, rd=[b_sp], wr=[b_sp])
                    for c2 in range(2):
                        for r in range(2):
                            DMA("sp", xr_c[:, r * 2048:(r + 1) * 2048],
                                EXG[l][r * 1536 + 512 + c2 * 128: r * 1536 + 512 + (c2 + 1) * 128, :], wr=[b_xr])
                        if k == 0:
                            OP("pool", "tensor_copy", xr_c[:, 4096:4608], xr_p[:, c2, :], rd=[b_prm], wr=[b_xr])
                        cw = lambda t_: svc(l, "convw", t_ * 2 + c2)
                        OP("dve", "tensor_scalar", xc[:, 0:NTK], xr_c[:, 0:NTK], cw(2), svc(l, "convb", c2), ALU.mult, ALU.add,
                           rd=[b_xr, b_const], wr=[b_xc])
                        for (s0, s1) in seqs:
                            OP("dve", "scalar_tensor_tensor", xc[:, s0 + 2:s1], xr_c[:, s0:s1 - 2], cw(0), xc[:, s0 + 2:s1],
                               ALU.mult, ALU.add, rd=[b_xr, b_const, b_xc], wr=[b_xc])
                            OP("dve", "scalar_tensor_tensor", xc[:, s0 + 1:s1], xr_c[:, s0:s1 - 1], cw(1), xc[:, s0 + 1:s1],
                               ALU.mult, ALU.add, rd=[b_xr, b_const, b_xc], wr=[b_xc])
                            OP("dve", "scalar_tensor_tensor", xc[:, s0:s1 - 1], xr_c[:, s0 + 1:s1], cw(3), xc[:, s0:s1 - 1],
                               ALU.mult, ALU.add, rd=[b_xr, b_const, b_xc], wr=[b_xc])
                        OP("pool", "tensor_copy", xcb[:, 0:NTK], xc[:, 0:NTK], rd=[b_xc], wr=[b_xcb])
                        for d in range(2):
                            halves = [(0, 2048), (2048, NTK)]
                            if d == 1:
                                halves = halves[::-1]
                            col = d * 2 + c2
                            wa = lw_b[:, ((l * 2 + d) * 2 + 0) * 2 + c2, :]
                            wx = lw_b[:, ((l * 2 + d) * 2 + 1) * 2 + c2, :]
                            for (t0, t1) in halves:
                                n = t1 - t0
                                for s_ in range(n // 512):
                                    a0 = t0 + s_ * 512
                                    pa, pi = 1 + 2 * (s_ % 2), 2 + 2 * (s_ % 2)
                                    MM(PS[pa], wa, xcb[:, a0:a0 + 512], True, True, rd=[b_xcb, b_const], wr=[PB[pa]])
                                    MM(PS[pi], wx, xcb[:, a0:a0 + 512], True, True, rd=[b_xcb, b_const], wr=[PB[pi]])
                                    OP("act", "activation", GR[:, s_ * 512:(s_ + 1) * 512], PS[pa], AF.Sigmoid,
                                       bias=svc(l, "ba", col), scale=1.0, rd=[PB[pa], b_const], wr=[b_g])
                                    OP("act", "activation", GI[:, s_ * 512:(s_ + 1) * 512], PS[pi], AF.Sigmoid,
                                       bias=svc(l, "bx", col), scale=1.0, rd=[PB[pi], b_const], wr=[b_g])
                                OP("act", "activation", GC[:, 0:n], GR[:, 0:n], AF.Exp, scale=sp_[:, 4 + col:5 + col], rd=[b_g, b_sp], wr=[b_g])
                                OP("act", "activation", GR[:, 0:n], GR[:, 0:n], AF.Exp, scale=sp_[:, col:col + 1], rd=[b_g, b_sp], wr=[b_g])
                                OP("act", "activation", GC[:, 0:n], GC[:, 0:n], AF.Ln, bias=small[:, 1:2], scale=-1.0, rd=[b_g, b_const], wr=[b_g])
                                OP("act", "activation", GC[:, 0:n], GC[:, 0:n], AF.Exp, scale=0.5, rd=[b_g], wr=[b_g])
                                OP("dve", "tensor_tensor", GI[:, 0:n], GI[:, 0:n], xc[:, t0:t1], ALU.mult, rd=[b_g, b_xc], wr=[b_g])
                                OP("dve", "tensor_tensor", GI[:, 0:n], GI[:, 0:n], GC[:, 0:n], ALU.mult, rd=[b_g], wr=[b_g])
                                for (s0, s1) in seqs:
                                    lo, hi = max(s0, t0), min(s1, t1)
                                    if lo >= hi:
                                        continue
                                    if d == 0:
                                        if lo == s0:
                                            init = svc(l, "h0", col) if s0 == 0 else 0.0
                                        else:
                                            init = HS[:, lo - 1:lo]
                                        OP("dve", "tensor_tensor_scan", HS[:, lo:hi], GR[:, lo - t0:hi - t0], GI[:, lo - t0:hi - t0],
                                           init, ALU.mult, ALU.add, rd=[b_g, b_hs, b_const], wr=[b_hs])
                                    else:
                                        if hi == s1:
                                            init = svc(l, "h0", col) if s0 == 0 else 0.0
                                        else:
                                            init = small[:, 8:9]
                                        OP("dve", "tensor_tensor_scan", HB[:, lo - t0:hi - t0][:, ::-1], GR[:, lo - t0:hi - t0][:, ::-1],
                                           GI[:, lo - t0:hi - t0][:, ::-1], init, ALU.mult, ALU.add,
                                           rd=[b_g, b_hb, b_const], wr=[b_hb])
                                if d == 0 and k == 0 and t1 == 4608:
                                    for sq_ in range(2):
                                        e_ = 4096 + sq_ * 256 + 255
                                        OP("dve", "tensor_copy", nhst[:, (sq_ * 2 + 0) * 2 + c2:(sq_ * 2 + 0) * 2 + c2 + 1], HS[:, e_:e_ + 1],
                                           rd=[b_hs], wr=[b_nh])
                                if d == 1:
                                    if t1 == NTK:
                                        for sq_ in (range(2) if k == 0 else ()):
                                            e_ = 4096 + sq_ * 256 - t0
                                            OP("dve", "tensor_copy", nhst[:, (sq_ * 2 + 1) * 2 + c2:(sq_ * 2 + 1) * 2 + c2 + 1], HB[:, e_:e_ + 1],
                                               rd=[b_hb], wr=[b_nh])
                                        OP("dve", "tensor_copy", small[:, 8:9], HB[:, 0:1], rd=[b_hb], wr=[b_const])
                                    OP("dve", "tensor_tensor", HS[:, t0:t1], HS[:, t0:t1], HB[:, 0:n], ALU.add, rd=[b_hs, b_hb], wr=[b_hs])
                        if own:
                            OP("act", "activation", HB[:, 0:2048], HS[:, 0:2048], AF.Identity, scale=m0, rd=[b_hs, b_const, b_hb], wr=[b_hb])
                            OP("dve", "scalar_tensor_tensor", HB[:, 0:2048], HS[:, 2048:4096], m1, HB[:, 0:2048], ALU.mult, ALU.add,
                               rd=[b_hs, b_const, b_hb], wr=[b_hb])
                            OP("dve", "tensor_tensor", mix_lru[:, c2, 0:2048], HB[:, 0:2048], gr_all[:, c2, 0:2048], ALU.mult,
                               rd=[b_hb] + b_gr, wr=[b_mixl])
                        else:
                            OP("dve", "tensor_tensor", mix_lru[:, c2, 0:2048], HS[:, k * 2048:(k + 1) * 2048], gr_all[:, c2, 0:2048], ALU.mult,
                               rd=[b_hs] + b_gr, wr=[b_mixl])
                            if l == 0:
                                OP("dve", "tensor_tensor", xcb[:, 0:2048], HS[:, 2048:4096], grO[:, c2, :], ALU.mult,
                                   rd=[b_hs, b_gro, b_xcb], wr=[b_xcb])
                                DMA("pool", LS[:, c2, :], xcb[:, 0:2048], rd=[b_xcb], wr=[b_ls])
                        if k == 0:
                            OP("dve", "tensor_tensor", mix_lru[:, c2, 2048:2560], HS[:, 4096:4608], gr_all[:, c2, 2048:2560], ALU.mult,
                               rd=[b_hs] + b_gr, wr=[b_mixl])
                    for sq_ in (range(2) if k == 0 else ()):
                        for d in range(2):
                            for c2 in range(2):
                                k_ = (sq_ * 2 + d) * 2 + c2
                                dst = bass.AP(nh_d.tensor, ((sq_ * 2 + l) * 2 + d) * 256 + c2 * 128, [[1, 128], [1, 1]])
                                DMA("pool", dst, nhst[:, k_:k_ + 1], rd=[b_nh])
                    S.barrier()
                    check_stop("B1_%d_%d" % (l, k), [("mixl", mix_lru), ("nhst", nhst), ("hs", HS), ("xc", xc), ("xrc", xr_c), ("hb", HB), ("svh0", sv[:, 359:363])])

                def phaseB2(k, own=False):
                    AR.reset(P_ATTN)
                    xf_c = AR.alloc([2, 4096], BF16)
                    assert AR.cur <= P_Q
                    AR.reset(P_SCR)
                    Ytm = AR.alloc([32, 512], BF16)
                    Ytp = AR.alloc([4, 512], BF16)
                    tabs = [[AR.alloc([4, 512], BF16) for _ in range(3)] for cs in range(2)]
                    b_xf, b_y, b_yp = Buf("xf"), Buf("ytm"), Buf("ytp")
                    b_tab = [[Buf("tab%d_%d" % (cs, i)) for i in range(3)] for cs in range(2)]
                    for c2 in range(2):
                        for r in range(2):
                            DMA("sp", xf_c[:, c2, r * 2048:(r + 1) * 2048],
                                EXG[l][r * 1536 + 768 + c2 * 128: r * 1536 + 768 + (c2 + 1) * 128, :], rd=[b_exg[l]], wr=[b_xf])
                    for tc in range(32):
                        pb = 1 + tc % 2
                        for c2 in range(2):
                            MM(PS[pb][:, c2 * 256:(c2 + 1) * 256], xf_c[:, c2, tc * 128:(tc + 1) * 128], cs64_b, True, True,
                               rd=[b_xf, b_const], wr=[PB[pb]])
                        OP("act", "activation", Ytm[:, tc, :], PS[pb], AF.Identity, scale=(sv[:, SV_MISC + 2:SV_MISC + 3] if own else sv[:, SV_MISC + 6 + k:SV_MISC + 7 + k]),
                           rd=[PB[pb], b_const], wr=[b_y])
                    for tc in (range(4) if k == 0 else ()):
                        pb = 1 + tc % 2
                        for c2 in range(2):
                            MM(PS[pb][:, c2 * 256:(c2 + 1) * 256], xf_p[:, c2, tc * 128:(tc + 1) * 128], cs64_b, True, True,
                               rd=[b_prm, b_const], wr=[PB[pb]])
                        OP("act", "copy", Ytp[:, tc, :], PS[pb], rd=[PB[pb]], wr=[b_yp])
                    kk = 0
                    for jb in range(4):
                        for tg in range(8):
                            i = kk % 3
                            kk += 1
                            for cs in range(2):
                                DMA("sp", tabs[cs][i],
                                    dft_d[cs, tg * 512:(tg + 1) * 512, jb * 512:(jb + 1) * 512].rearrange("(t p) j -> p t j", p=128),
                                    wr=[b_tab[cs][i]])
                            for c2 in range(2):
                                pb = 3 + c2
                                for t_ in range(4):
                                    for cs in range(2):
                                        tc = tg * 4 + t_
                                        MM(PS[pb], Ytm[:, tc, c2 * 256 + cs * 128: c2 * 256 + (cs + 1) * 128], tabs[cs][i][:, t_, :],
                                           tg == 0 and t_ == 0 and cs == 0, tg == 7 and t_ == 3 and cs == 1,
                                           rd=[b_y, b_tab[cs][i]], wr=[PB[pb]])
                        for c2 in range(2):
                            OP("dve", "tensor_tensor", mix_four.rearrange("p a b -> p (a b)")[:, c2 * T + jb * 512: c2 * T + (jb + 1) * 512], PS[3 + c2], csign, ALU.mult,
                               rd=[PB[3 + c2], b_const], wr=[b_mixf])
                    for sq_ in (range(2) if k == 0 else ()):
                        for c2 in range(2):
                            pb = 5 + c2
                            for t_ in range(2):
                                for cs in range(2):
                                    MM(PS[pb][:, 0:256], Ytp[:, sq_ * 2 + t_, c2 * 256 + cs * 128: c2 * 256 + (cs + 1) * 128],
                                       dftp_b[:, cs, t_, :], t_ == 0 and cs == 0, t_ == 1 and cs == 1,
                                       rd=[b_yp, b_const], wr=[PB[pb]])
                            OP("act", "copy", mix_four[:, c2, 2048 + sq_ * 256: 2048 + (sq_ + 1) * 256], PS[pb][:, 0:256],
                               rd=[PB[pb]], wr=[b_mixf])
                    S.barrier()
                    check_stop("B2_%d_%d" % (l, k), [("mixf", mix_four)])

                def phaseC(k, own=False):
                    AR.reset(P_SCR)
                    KT = [AR.alloc([4352], BF16) for _ in range(2)]
                    VH = [AR.alloc([34, 128], BF16) for _ in range(2)]
                    b_kt = [Buf("kt0"), Buf("kt1")]
                    b_vh = [Buf("vh0"), Buf("vh1")]
                    ckf = AR.alloc([2, 512], F32)
                    cvf = AR.alloc([2, 512], F32)
                    b_ckf, b_cvf = Buf("ckf"), Buf("cvf")
                    pTP = [AR.alloc([1024], BF16) for _ in range(2)]
                    pT = [pTP[0][:, 0:512], pTP[0][:, 512:1024], pTP[1][:, 0:512], pTP[1][:, 512:1024]]
                    b_pT = [Buf("pT%d" % i) for i in range(4)]
                    fR = [AR.alloc([512], F32) for _ in range(2)]
                    fT = [AR.alloc([512], F32) for _ in range(2)]
                    fo = AR.alloc([512], F32)
                    fsq = AR.alloc([512], BF16)
                    frs = AR.alloc([512], F32)
                    ftl = AR.alloc([512], F32)
                    b_fR = [Buf("fR0"), Buf("fR1")]
                    b_fT = [Buf("fT0"), Buf("fT1")]
                    b_fo, b_fsq, b_frs, b_ftl = Buf("fo"), Buf("fsq"), Buf("frs"), Buf("ftl")
                    lamv = AR.alloc([8], F32)
                    lprod = AR.alloc([128], F32)
                    b_lam = Buf("lam")
                    OP("dve", "tensor_tensor", lprod[:, 0:64], svc(l, "wl", 0, 64), svc(l, "wl", 64, 64), ALU.mult, rd=[b_const], wr=[b_lam])
                    OP("dve", "tensor_tensor", lprod[:, 64:128], svc(l, "wl", 128, 64), svc(l, "wl", 192, 64), ALU.mult, rd=[b_const], wr=[b_lam])
                    OP("dve", "reduce_sum", lamv[:, 0:1], lprod[:, 0:64], mybir.AxisListType.X, rd=[b_lam], wr=[b_lam])
                    OP("dve", "reduce_sum", lamv[:, 1:2], lprod[:, 64:128], mybir.AxisListType.X, rd=[b_lam], wr=[b_lam])
                    OP("act", "activation", lamv[:, 2:4], lamv[:, 0:2], AF.Exp, rd=[b_lam], wr=[b_lam])
                    OP("dve", "tensor_tensor", lamv[:, 4:5], lamv[:, 3:4], lamv[:, 2:3], ALU.subtract, rd=[b_lam], wr=[b_lam])
                    OP("dve", "tensor_scalar", lamv[:, 4:5], lamv[:, 4:5], -li, None, ALU.add, rd=[b_lam], wr=[b_lam])
                    OP("dve", "tensor_scalar", lamv[:, 5:6], svc(l, "gsub"), 1.0 - li, None, ALU.mult, rd=[b_const], wr=[b_lam])
                    if own:
                        tmpQ = AR.alloc([2, TS], BF16)
                        b_tq = Buf("tmpQ")
                        DMA("act", q_all[:, :, 0:TS], QS[l][0], rd=[b_qs[l][0]], wr=b_q[0:4])
                        for hp in range(2):
                            DMA("act", tmpQ, QS[l][1][:, hp * 2:(hp + 1) * 2, :], rd=[b_qs[l][1]], wr=[b_tq])
                            blend(q_all[:, hp * 2:(hp + 1) * 2, 0:TS], tmpQ, b_q[0:4], [b_tq], b_q[0:4])
                    else:
                        DMA("act", q_all[:, :, 0:TS], QS[l][k], rd=[b_qs[l][k]], wr=b_q[0:4])
                    DMA("act", ckf, ck_d[l].rearrange("(t p) f -> p t f", p=128), wr=[b_ckf])
                    DMA("act", cvf, cv_d[l].rearrange("(t p) f -> p t f", p=128), wr=[b_cvf])
                    if l == 0 and k == 0:
                        convert_weights(0, ["pool"], pwmax=1024, kinds=("B", "G0", "G1"))
                    if l == 0 and k == 1:
                        convert_weights(1, ["pool"], pwmax=1024)
                    scale = 64 ** -0.5
                    state = {"pt": 0}

                    def attn_block(q_ap, nq, kt_ap, v_fn, nkc, rd_q, rd_k, rd_v, out_ap, wr_out):
                        def qk(kc):
                            par = kc % 2
                            for m in range(2):
                                sb = par * 2 + m
                                MM(PS[sb][:, 0:nq], kt_ap[m * 64:(m + 1) * 64, kc * 128:(kc + 1) * 128], q_ap[m * 64:(m + 1) * 64, :],
                                   True, True, rd=rd_q + rd_k, wr=[PB[sb]])
                                if nq != 512:
                                    OP("act", "activation", pT[sb][:, 0:nq], PS[sb][:, 0:nq], AF.Exp, scale=scale, rd=[PB[sb]], wr=[b_pT[sb]])
                            if nq == 512:
                                OP("act", "activation", pTP[par], PSP[par], AF.Exp, scale=scale,
                                   rd=[PB[par * 2], PB[par * 2 + 1]], wr=[b_pT[par * 2], b_pT[par * 2 + 1]])

                        def pv(kc):
                            par = kc % 2
                            for m in range(2):
                                sb = par * 2 + m
                                MM(PS[4 + m][:, 0:nq], v_fn(kc), pT[sb][:, 0:nq], kc == 0, kc == nkc - 1, rd=rd_v + [b_pT[sb]], wr=[PB[4 + m]])
                                MM(PS[6 + m][:, 0:nq], ones_b, pT[sb][:, 0:nq], kc == 0, kc == nkc - 1, rd=[b_pT[sb], b_const], wr=[PB[6 + m]])

                        qk(0)
                        for kc in range(nkc):
                            if kc + 1 < nkc:
                                qk(kc + 1)
                            pv(kc)
                        for m in range(2):
                            OP("dve", "reciprocal", fR[m][:, 0:nq], PS[6 + m][:, 0:nq], rd=[PB[6 + m]], wr=[b_fR[m]])
                            OP("dve", "tensor_tensor", fT[m][:, 0:nq], PS[4 + m][:, 0:nq], fR[m][:, 0:nq], ALU.mult, rd=[PB[4 + m], b_fR[m]], wr=[b_fT[m]])
                        OP("dve", "scalar_tensor_tensor", fo[:, 0:nq], fT[1][:, 0:nq], lamv[:, 4:5], fT[0][:, 0:nq], ALU.mult, ALU.add,
                           rd=[b_fT[0], b_fT[1], b_lam], wr=[b_fo])
                        OP("act", "activation", fsq[:, 0:nq], fo[:, 0:nq], AF.Square, rd=[b_fo], wr=[b_fsq])
                        MM(PS[0][:, 0:nq], ones_b, fsq[:, 0:nq], True, True, rd=[b_fsq, b_const], wr=[PB[0]])
                        rstd_from(PS[0][:, 0:nq], PB[0], frs[:, 0:nq], b_frs, 1.0 / 128, ftl[:, 0:nq], b_ftl)
                        OP("dve", "scalar_tensor_tensor", out_ap, fo[:, 0:nq], lamv[:, 5:6], frs[:, 0:nq], ALU.mult, ALU.mult,
                           rd=[b_fo, b_frs, b_lam], wr=wr_out)

                    for h in range(4):
                        i = h % 2
                        for r in range(2):
                            DMA("act", KT[i][:, r * 2048:(r + 1) * 2048], EXG[l][r * 1536 + h * 128: r * 1536 + (h + 1) * 128, :],
                                rd=[b_exg[l]], wr=[b_kt[i]])
                            vview = EXG[l][r * 1536 + 1024: r * 1536 + 1536, :].rearrange("r (t f) -> (r t) f", f=512)
                            vsrc = vview[:, h * 128:(h + 1) * 128].rearrange("(c p) e -> p c e", p=128)
                            DMA("act", VH[i][:, r * 16:(r + 1) * 16, :], vsrc, rd=[b_exg[l]], wr=[b_vh[i]])
                        for t_ in range(2):
                            OP("pe", "transpose", PS[0][:, t_ * 128:(t_ + 1) * 128], ckf[:, t_, h * 128:(h + 1) * 128], ident,
                               rd=[b_ckf, b_const], wr=[PB[0]])
                        OP("act", "copy", KT[i][:, 4096:4352], PS[0][:, 0:256], rd=[PB[0]], wr=[b_kt[i]])
                        OP("dve", "tensor_copy", VH[i][:, 32:34, :], cvf[:, :, h * 128:(h + 1) * 128], rd=[b_cvf], wr=[b_vh[i]])
                        for qb in range(4):
                            attn_block(q_all[:, h, qb * 512:(qb + 1) * 512], 512, KT[i], lambda kc, i=i: VH[i][:, kc, :], 34,
                                       [b_q[qb]], [b_kt[i]], [b_vh[i]], mix_attn[:, h, qb * 512:(qb + 1) * 512], [b_mixa[qb]])
                        for sq_ in (range(2) if k == 0 else ()):
                            attn_block(q_all[:, h, 2048 + sq_ * 256: 2048 + (sq_ + 1) * 256], 256, kTp[:, h, sq_ * 256:(sq_ + 1) * 256],
                                       lambda kc, sq_=sq_, h=h: Vp[:, sq_ * 2 + kc, h * 128:(h + 1) * 128], 2,
                                       [b_q[4]], [b_prm], [b_prm], mix_attn[:, h, 2048 + sq_ * 256: 2048 + (sq_ + 1) * 256], [b_mixa[4]])
                    S.barrier()
                    check_stop("C%d_%d" % (l, k), [("mixf", mix_four), ("xfp", xf_p), ("dftp", dftp_b), ("cs64", cs64_b)])

                def phaseD(k, own=False):
                    AR.reset(P_Q)
                    wo = AR.alloc([8, 1024], BF16)
                    b_wo = Buf("wo")
                    xblk = AR.alloc([8, 512], F32)
                    b_x = Buf("xblk")
                    osb = AR.alloc([8, 512], F32)
                    b_o = Buf("osb")
                    sq = AR.alloc([8, 512], BF16)
                    b_sq = Buf("sq")
                    u = sq
                    b_u = b_sq
                    hid = AR.alloc([22, 512], BF16)
                    b_hid = Buf("hid")
                    gu = [AR.alloc([8, 512], BF16) for _ in range(2)]
                    b_gu = [Buf("gu0"), Buf("gu1")]
                    dn = [AR.alloc([22, 128], BF16) for _ in range(2)]
                    b_dn = [Buf("dn0"), Buf("dn1")]
                    rstd = AR.alloc([512], F32)
                    tl = AR.alloc([512], F32)
                    b_rstd, b_tl = Buf("rstd"), Buf("tl")
                    tmpf = [AR.alloc([512], F32) for _ in range(2)]
                    b_tmpf = [Buf("tmpf0"), Buf("tmpf1")]
                    sg = [AR.alloc([512], F32) for _ in range(2)]
                    b_sg = [Buf("sg0"), Buf("sg1")]
                    ytm = [osb[:, 0:2, :].rearrange("p a b -> p (a b)"), osb[:, 2:4, :].rearrange("p a b -> p (a b)")]
                    b_ytm = [b_o, b_o]
                    for h_ in range(2):
                        DMA("sp", wo[:, h_ * 4:(h_ + 1) * 4, :],
                            WGB[l][h_ * 512:(h_ + 1) * 512, OFF_WOUT:OFF_WOUT + 1024].rearrange("(k p) n -> p k n", p=128),
                            rd=[b_wgb[l]], wr=[b_wo])
                    cnt = {"o": 0, "tmpf": 0, "gu": 0, "dn": 0, "sg": 0, "y": 0}

                    def mixk(kc, tok):
                        if kc < 4:
                            return mix_attn[:, kc, tok]
                        if kc < 6:
                            return mix_lru[:, kc - 4, tok]
                        return mix_four[:, kc - 6, tok]

                    def post_norm_update(gidx, c_):
                        OP("act", "activation", sq[:, 0:4, :], osb[:, 0:4, :], AF.Square, rd=[b_o], wr=[b_sq])
                        OP("pool", "tensor_tensor", sq[:, 4:8, :], osb[:, 4:8, :], osb[:, 4:8, :], ALU.mult, rd=[b_o], wr=[b_sq])
                        for c in range(8):
                            MM(PS[0], ones_b, sq[:, c, :], c == 0, c == 7, rd=[b_sq, b_const], wr=[PB[0]])
                        rstd_from(PS[0], PB[0], rstd, b_rstd, 1.0 / D, tl, b_tl)
                        for c in range(8):
                            i = cnt["tmpf"] % 2
                            cnt["tmpf"] += 1
                            OP("dve", "tensor_tensor", tmpf[i], osb[:, c, :], rstd, ALU.mult, rd=[b_o, b_rstd], wr=[b_tmpf[i]])
                            OP("dve", "scalar_tensor_tensor", xblk[:, c, :], tmpf[i], c_[:, gidx, c:c + 1], xblk[:, c, :], ALU.mult, ALU.add,
                               rd=[b_tmpf[i], b_cst, b_x], wr=[b_x])

                    for tb in range(5 if k == 0 else 4):
                        w = 0 if tb < 4 else 1
                        c_ = cst[l][w]
                        tok = slice(tb * 512, (tb + 1) * 512)
                        DMA("sp", xblk, XS[l][k][:, :, tok], rd=[b_xs[l][k][tb]], wr=[b_x])
                        if own and tb < 4:
                            OP("act", "activation", xblk, xblk, AF.Identity, scale=m0, rd=[b_x, b_const], wr=[b_x])
                            for c in range(8):
                                i = cnt["tmpf"] % 2
                                cnt["tmpf"] += 1
                                DMA("sp", tmpf[i], XS[l][1][:, c, tok], rd=[b_xs[l][1][tb]], wr=[b_tmpf[i]])
                                OP("dve", "scalar_tensor_tensor", xblk[:, c, :], tmpf[i], m1, xblk[:, c, :], ALU.mult, ALU.add,
                                   rd=[b_tmpf[i], b_const, b_x], wr=[b_x])
                        for c in range(8):
                            pb = 1 + cnt["o"] % 2
                            cnt["o"] += 1
                            for kc in range(8):
                                MM(PS[pb], wo[:, kc, c * 128:(c + 1) * 128], mixk(kc, tok), kc == 0, kc == 7,
                                   rd=[b_wo, b_mixa[tb], b_mixl, b_mixf], wr=[PB[pb]])
                            OP("act", "copy", osb[:, c, :], PS[pb], rd=[PB[pb]], wr=[b_o])
                        post_norm_update(2, c_)
                        OP("act", "activation", sq[:, 0:4, :], xblk[:, 0:4, :], AF.Square, rd=[b_x], wr=[b_sq])
                        OP("pool", "tensor_tensor", sq[:, 4:8, :], xblk[:, 4:8, :], xblk[:, 4:8, :], ALU.mult, rd=[b_x], wr=[b_sq])
                        for c in range(8):
                            MM(PS[0], ones_b, sq[:, c, :], c == 0, c == 7, rd=[b_sq, b_const], wr=[PB[0]])
                        rstd_from(PS[0], PB[0], rstd, b_rstd, 1.0 / D, tl, b_tl)
                        for c in range(8):
                            i = cnt["tmpf"] % 2
                            cnt["tmpf"] += 1
                            OP("dve", "scalar_tensor_tensor", tmpf[i], xblk[:, c, :], c_[:, 3, c:c + 1], rstd, ALU.mult, ALU.mult,
                               rd=[b_x, b_rstd, b_cst], wr=[b_tmpf[i]])
                            OP("act", "activation", u[:, c, :], tmpf[i], AF.Identity, bias=c_[:, 4, c:c + 1], scale=1.0,
                               rd=[b_tmpf[i], b_cst], wr=[b_u])
                        for js in range(11):
                            i = cnt["gu"] % 2
                            cnt["gu"] += 1
                            DMA("sp", gu[i], WGU[l][js], rd=[b_wgb[l]], wr=[b_gu[i]])
                            for jj in range(2):
                                j = js * 2 + jj
                                pg, pu = 3 + (j % 2), 5 + (j % 2)
                                for kc in range(8):
                                    MM(PS[pg], gu[i][:, kc, jj * 128:(jj + 1) * 128], u[:, kc, :], kc == 0, kc == 7, rd=[b_gu[i], b_u], wr=[PB[pg]])
                                for kc in range(8):
                                    MM(PS[pu], gu[i][:, kc, 256 + jj * 128: 256 + (jj + 1) * 128], u[:, kc, :], kc == 0, kc == 7,
                                       rd=[b_gu[i], b_u], wr=[PB[pu]])
                                si = cnt["sg"] % 2
                                cnt["sg"] += 1
                                OP("act", "activation", sg[si], PS[pg], AF.Silu, rd=[PB[pg]], wr=[b_sg[si]])
                                OP("dve", "tensor_tensor", hid.rearrange("p a b -> p (a b)")[:, j * 512:(j + 1) * 512], PS[pu], sg[si], ALU.mult, rd=[PB[pu], b_sg[si]], wr=[b_hid])
                        for c in range(8):
                            i = cnt["dn"] % 2
                            cnt["dn"] += 1
                            DMA("sp", dn[i], WGB[l][c * 128:(c + 1) * 128, OFF_DN:OFF_DN + 2816].rearrange("p (k j) -> p k j", j=128),
                                rd=[b_wgb[l]], wr=[b_dn[i]])
                            pb = 1 + cnt["o"] % 2
                            cnt["o"] += 1
                            for kc in range(22):
                                MM(PS[pb], dn[i][:, kc, :], hid[:, kc, :], kc == 0, kc == 21, rd=[b_dn[i], b_hid], wr=[PB[pb]])
                            OP("act", "copy", osb[:, c, :], PS[pb], rd=[PB[pb]], wr=[b_o])
                        post_norm_update(5, c_)
                        if l == 0:
                            DMA("pool", XS[1][k][:, :, tok], xblk, rd=[b_x], wr=[b_xs[1][k][tb]])
                        else:
                            dst = ys_d if tb < 4 else yp_d
                            r0 = tb * 512 if tb < 4 else 0
                            for tt in range(4):
                                yi = cnt["y"] % 2
                                cnt["y"] += 1
                                for g in range(2):
                                    pb = 7 if g == 0 else 0
                                    for cc in range(4):
                                        c = g * 4 + cc
                                        OP("pe", "transpose", PS[pb][:, cc * 128:(cc + 1) * 128], xblk[:, c, tt * 128:(tt + 1) * 128], ident,
                                           rd=[b_x, b_const], wr=[PB[pb]])
                                    OP("act" if g == 0 else "dve", "copy" if g == 0 else "tensor_copy",
                                       ytm[yi][:, g * 512:(g + 1) * 512], PS[pb], rd=[PB[pb]], wr=[b_ytm[yi]])
                                DMA("pool", dst[r0 + tt * 128: r0 + (tt + 1) * 128, :], ytm[yi], rd=[b_ytm[yi]])
                    S.barrier()
                    check_stop("D%d_%d" % (l, k), [("xs1", XS[1][k])])
                phaseA(0)
                phaseA(1)
                if l == 0:
                    for k in (0, 1):
                        phaseB1(k)
                        phaseB2(k)
                        phaseC(k)
                        phaseD(k)
                else:
                    phaseB1(0, True)
                    phaseB2(0, True)
                    phaseC(0, True)
                    phaseD(0, True)
        except _Stop:
            pass

        S.final_wait("sp")
        with nc.Block() as block:
            S.emit(block)
    nc._taps = TAPS
    return nc


def _fm(v):
    v = np.asarray(v, np.float32)
    return np.ascontiguousarray(v.reshape(-1, 128).T)


_CONST_CACHE = {}


def _constants():
    if _CONST_CACHE:
        return _CONST_CACHE
    t = np.arange(4096, dtype=np.float64)[:, None]
    j = np.arange(2048, dtype=np.float64)[None, :]
    ang = 2.0 * np.pi * ((t * j) % 4096) / 4096.0
    s = 1.0 / math.sqrt(4096 * 64)
    dft = np.stack([np.cos(ang) * s, -np.sin(ang) * s]).astype(np.float32).astype(ml_dtypes.bfloat16)
    t = np.arange(256, dtype=np.float64)[:, None]
    j = np.arange(256, dtype=np.float64)[None, :]
    ang = 2.0 * np.pi * ((t * j) % 256) / 256.0
    s = 1.0 / math.sqrt(256 * 64)
    dftp = np.stack([np.cos(ang) * s, -np.sin(ang) * s]).astype(np.float32).astype(ml_dtypes.bfloat16)
    c = np.arange(64, dtype=np.float64)
    a64 = 2.0 * np.pi * np.outer(c, c) / 64.0
    cs64 = np.zeros((128, 256), np.float32)
    for g in range(2):
        cs64[g * 64:(g + 1) * 64, g * 64:(g + 1) * 64] = np.cos(a64)
        cs64[g * 64:(g + 1) * 64, 128 + g * 64:128 + (g + 1) * 64] = np.sin(a64)
    rmat = np.zeros((128, 128), np.float32)
    for dd in range(128):
        partner = dd + 16 if (dd % 32) < 16 else dd - 16
        rmat[partner, dd] = 1.0
    _CONST_CACHE.update(dft=dft, dftp=dftp, cs64=cs64, rmat=rmat)
    return _CONST_CACHE


def _rope_table(hf):
    n = 16
    inv = (10000.0 ** (-np.arange(n, dtype=np.float32) / n)).astype(np.float32)
    tpos = np.arange(hf * 2048, (hf + 1) * 2048)
    row = (tpos // 64).astype(np.float32)
    col = (tpos % 64).astype(np.float32)
    tab = np.zeros((128, 2, 2048), np.float32)
    for p in range(128):
        dd = p % 64
        pos = row if dd < 32 else col
        i = dd % 16
        ang = (pos * inv[i]).astype(np.float32)
        sign = -1.0 if (dd % 32) < 16 else 1.0
        tab[p, 0] = np.cos(ang)
        tab[p, 1] = sign * np.sin(ang)
    return tab


_NC = None
_DEBUG = {}


def kernel(x_prompt, x_sample, cache_k, cache_v, state_lru, c, c_ctx,
           w_mod, b_mod, g_pre_mix, g_post_mix, g_pre_ffn, g_post_ffn,
           w_in, w_out, w_lambda, g_subln, conv_w, conv_b,
           lru_wa, lru_ba, lru_wx, lru_bx, lru_lambda, w_gate_up, w_down):
    global _NC
    f = lambda a: np.asarray(a, np.float32)
    x_prompt, x_sample, cache_k, cache_v, state_lru, c, c_ctx = map(f, (x_prompt, x_sample, cache_k, cache_v, state_lru, c, c_ctx))
    w_mod, b_mod, w_in, w_out, w_gate_up, w_down = map(f, (w_mod, b_mod, w_in, w_out, w_gate_up, w_down))
    lru_wa, lru_wx = f(lru_wa), f(lru_wx)
    K = _constants()
    if _NC is None:
        _NC = build_program()
    nc = _NC
    lw = np.zeros((128, 16, 128), np.float32)
    for l in range(2):
        for d in range(2):
            for g, W in enumerate((lru_wa, lru_wx)):
                for c2 in range(2):
                    idx = ((l * 2 + d) * 2 + g) * 2 + c2
                    for bb in range(2):
                        lw[bb * 64:(bb + 1) * 64, idx, bb * 64:(bb + 1) * 64] = W[l, d, c2 * 2 + bb]
    lw = lw.reshape(128, 2048)
    wsl = np.empty((2, 8, 128, WA_W + WB_W), np.float32)
    for l in range(2):
        for r in range(8):
            rows = slice(r * 128, (r + 1) * 128)
            wsl[l, r, :, 0:2304] = w_in[l, rows, :]
            wsl[l, r, :, 2304:8448] = w_mod[l, rows, :]
            wsl[l, r, :, 8448:9472] = w_out[l, rows, :]
            wsl[l, r, :, 9472:15104] = w_gate_up[l, rows, :]
            wd = w_down[l][:, r * 128:(r + 1) * 128]
            wsl[l, r, :, 15104:17920] = wd.reshape(22, 128, 128).transpose(1, 0, 2).reshape(128, 2816)
    in_maps = []
    cores = _DEBUG.get("cores", list(range(8)))
    for core in cores:
        p, hf = core // 2, core % 2
        sv = np.zeros((128, NSV), np.float32)
        for l in range(2):
            def put(name, arr):
                o, w = SVL[name]
                sv[:, l * SV_PER_L + o: l * SV_PER_L + o + w] = arr
            put("gpm", _fm(g_pre_mix[l]))
            put("gqm", _fm(g_post_mix[l]))
            put("gpf", _fm(g_pre_ffn[l]))
            put("gqf", _fm(g_post_ffn[l]))
            put("bmod", _fm(b_mod[l]))
            put("gsub", f(g_subln[l]).reshape(128, 1))
            cw = f(conv_w[l])
            put("convw", np.stack([_fm(cw[t_]) for t_ in range(4)], axis=1).reshape(128, 8))
            put("convb", _fm(conv_b[l]))
            put("ba", np.concatenate([_fm(f(lru_ba)[l, d]) for d in range(2)], axis=1))
            put("bx", np.concatenate([_fm(f(lru_bx)[l, d]) for d in range(2)], axis=1))
            put("lam", np.concatenate([_fm(f(lru_lambda)[l, d]) for d in range(2)], axis=1))
            put("wl", np.broadcast_to(f(w_lambda[l]).reshape(1, 256), (128, 256)))
            put("h0", np.concatenate([_fm(state_lru[p, l, d]) for d in range(2)], axis=1))
        par = (np.arange(128) % 2 == 1)
        sv[:, SV_MISC + 0] = 1.0 - hf
        sv[:, SV_MISC + 1] = float(hf)
        for k in range(2):
            o = k
            sv[:, SV_MISC + 2 + 2 * k] = 1.0 - o
            sv[:, SV_MISC + 3 + 2 * k] = float(o)
            sv[:, SV_MISC + 6 + k] = np.where(par & (o == 1), -1.0, 1.0)
        sv[:, SV_MISC + 2] = np.where(par & (hf == 1), -1.0, 1.0)
        cond = np.stack([_fm(c[p]), _fm(c_ctx)], axis=2)
        sv[:, SV_MISC + 8: SV_MISC + 24] = cond.reshape(128, 16)
        csign = np.ones((128, 512), np.float32)
        if hf == 1:
            csign[:, 1::2] = -1.0
        xs2 = np.stack([x_sample[p, 0:2048, :], x_sample[p, 2048:4096, :]])
        in_maps.append({
            "xs": np.ascontiguousarray(xs2),
            "xp": np.ascontiguousarray(x_prompt[2 * core:2 * core + 2].reshape(512, 1024)),
            "ck": np.ascontiguousarray(cache_k[p].reshape(2, 256, 512)),
            "cv": np.ascontiguousarray(cache_v[p].reshape(2, 256, 512)),
            "sv": sv, "lw": lw, "wsl": wsl,
            "rope": np.stack([_rope_table(0), _rope_table(1)]), "csign": np.ones((128, 512), np.float32),
            "rmat": K["rmat"], "cs64": K["cs64"],
            "dft": K["dft"], "dftp": K["dftp"],
        })
    if _DEBUG.get("stop") is not None:
        nc = build_program(stop=_DEBUG["stop"])
        res = run_bass_kernel_spmd(nc, in_maps, core_ids=list(range(len(cores))), trace=bool(_DEBUG.get("trace")))
        _DEBUG["exec_ns"] = getattr(res, "exec_time_ns", None)
        _DEBUG["results"] = res.results
        _DEBUG["taps"] = nc._taps
    else:
        res = run_bass_kernel_spmd(nc, in_maps, core_ids=list(range(8)))
    R = res.results
    y_prompt = np.empty((16, 256, 1024), np.float32)
    y_sample = np.empty((4, 4096, 1024), np.float32)
    new_k = np.empty((16, 2, 256, 4, 2, 64), np.float32)
    new_v = np.empty((16, 2, 256, 4, 128), np.float32)
    new_h = np.empty((16, 2, 2, 256), np.float32)
    for core in range(8):
        p, hf = core // 2, core % 2
        r = R[core]
        y_sample[p, hf * 2048:(hf + 1) * 2048] = np.asarray(r["ys"])
        y_prompt[2 * core:2 * core + 2] = np.asarray(r["yp"]).reshape(2, 256, 1024)
        new_k[2 * core:2 * core + 2] = np.asarray(r["nk"]).reshape(2, 2, 256, 4, 2, 64)
        new_v[2 * core:2 * core + 2] = np.asarray(r["nv"]).reshape(2, 2, 256, 4, 128)
        new_h[2 * core:2 * core + 2] = np.asarray(r["nh"]).reshape(2, 2, 2, 256)
    return (y_prompt, y_sample, new_k, new_v, new_h)
```

```python
import math
from contextlib import ExitStack
import numpy as np
import ml_dtypes
import concourse.bass as bass
import concourse.mybir as mybir
from concourse.bass_utils import run_bass_kernel_spmd

F32 = mybir.dt.float32
BF16 = mybir.dt.bfloat16
ALU = mybir.AluOpType
AF = mybir.ActivationFunctionType

D = 1024
TS = 2048
TP = 512
T = 2560
EPS = 1e-6
WA_W = 8448
WB_W = 9472
OFF_WOUT, OFF_GU, OFF_DN = 0, 1024, 6656

SVL = {}
_o = 0
for _n, _w in (("gpm", 8), ("gqm", 8), ("gpf", 8), ("gqf", 8), ("bmod", 48), ("gsub", 1), ("convw", 8),
               ("convb", 2), ("ba", 4), ("bx", 4), ("lam", 4), ("wl", 256), ("h0", 4)):
    SVL[_n] = (_o, _w)
    _o += _w
SV_PER_L = _o
SV_MISC = 2 * SV_PER_L
NSV = SV_MISC + 8 + 16


def lambda_init(l):
    return 0.8 - 0.6 * math.exp(-0.3 * l)


class Buf:
    __slots__ = ("name", "w", "r")

    def __init__(self, name):
        self.name = name
        self.w = None
        self.r = []


class Sched:
    ENGS = ("pe", "act", "dve", "pool", "sp")
    NDMA = 16

    def __init__(self, nc, es):
        self.nc = nc
        self.q = {e: [] for e in self.ENGS}
        self.sem = {e: es.enter_context(nc.semaphore("c_" + e)) for e in self.ENGS}
        self.cnt = {e: 0 for e in self.ENGS}
        self.known = {e: {} for e in self.ENGS}
        self.dsem, self.dcnt, self.dnext, self.dlast = {}, {}, {}, {}
        for e in ("sp", "pool", "act"):
            self.dsem[e] = [es.enter_context(nc.semaphore("d_%s%d" % (e, i))) for i in range(self.NDMA)]
            self.dcnt[e] = [0] * self.NDMA
            self.dlast[e] = [None] * self.NDMA
            self.dnext[e] = 0

    def _deps(self, reads, writes):
        deps = []
        for b in reads:
            if b.w is not None:
                deps.append(b.w)
        for b in writes:
            if b.w is not None:
                deps.append(b.w)
            deps.extend(b.r)
        return deps

    def _emit_waits(self, eng, deps, skip_own=False):
        kn = self.known[eng]
        best = {}
        for (s, v, key) in deps:
            if skip_own and key == ("c", eng):
                continue
            if kn.get(key, 0) >= v:
                continue
            if best.get(key, (None, 0))[1] < v:
                best[key] = (s, v)
        for key, (s, v) in best.items():
            kn[key] = v
            self.q[eng].append(lambda e, s=s, v=v: e.wait_ge(s, v))

    def _commit(self, ev, reads, writes):
        for b in reads:
            b.r.append(ev)
            if len(b.r) > 64:
                last = {}
                for x in b.r:
                    if last.get(x[2], (None, 0, None))[1] < x[1]:
                        last[x[2]] = x
                b.r = list(last.values())
        for b in writes:
            b.w = ev
            b.r = []

    def op(self, eng, fn, reads=(), writes=()):
        writes = list(writes) + [b for b in reads if b.name.startswith("ps") and b not in writes]
        deps = self._deps(reads, writes)
        self._emit_waits(eng, deps, skip_own=(eng == "pe"))
        self.cnt[eng] += 1
        s = self.sem[eng]
        ev = (s, self.cnt[eng], ("c", eng))
        self.q[eng].append(lambda e, fn=fn, s=s: fn(e).then_inc(s, 1))
        self._commit(ev, reads, writes)
        return ev

    def dma(self, eng, fn, reads=(), writes=()):
        i = self.dnext[eng]
        self.dnext[eng] = (i + 1) % self.NDMA
        deps = self._deps(reads, writes)
        if self.dlast[eng][i] is not None:
            deps.append(self.dlast[eng][i])
        self._emit_waits(eng, deps)
        self.dcnt[eng][i] += 16
        s = self.dsem[eng][i]
        ev = (s, self.dcnt[eng][i], ("d", eng, i))
        self.dlast[eng][i] = ev
        self.q[eng].append(lambda e, fn=fn, s=s: fn(e).then_inc(s, 16))
        self._commit(ev, reads, writes)
        return ev

    def _all_events(self):
        evs = []
        for e in self.ENGS:
            if self.cnt[e] > 0:
                evs.append((self.sem[e], self.cnt[e], ("c", e)))
        for e in self.dsem:
            for i in range(self.NDMA):
                if self.dlast[e][i] is not None:
                    evs.append(self.dlast[e][i])
        ex = getattr(self, "extra_events", [])
        if ex:
            evs.append(ex[-1])
        return evs

    def barrier(self):
        evs = self._all_events()
        for e in self.ENGS:
            self._emit_waits(e, evs)

    def final_wait(self, eng="sp"):
        self._emit_waits(eng, self._all_events())

    def emit(self, block):
        q = self.q

        @block.tensor
        def _(e):
            for t in q["pe"]:
                t(e)

        @block.scalar
        def _(e):
            for t in q["act"]:
                t(e)

        @block.vector
        def _(e):
            for t in q["dve"]:
                t(e)

        @block.gpsimd
        def _(e):
            for t in q["pool"]:
                t(e)

        @block.sync
        def _(e):
            for t in q["sp"]:
                t(e)


def build_program(stop=None):
    nc = bass.Bass("TRN2", target_bir_lowering=False)
    TAPS = []

    def din(name, shape, dt=F32):
        return nc.dram_tensor(name, shape, dt, kind="ExternalInput").ap()

    def dout(name, shape, dt=F32):
        return nc.dram_tensor(name, shape, dt, kind="ExternalOutput").ap()

    def dint(name, shape, dt):
        return nc.dram_tensor(name, shape, dt).ap()

    xs_d = din("xs", [2, TS, D])
    xp_d = din("xp", [TP, D])
    ck_d = din("ck", [2, 256, 512])
    cv_d = din("cv", [2, 256, 512])
    sv_d = din("sv", [128, NSV])
    lw_d = din("lw", [128, 2048])
    wsl_d = din("wsl", [2, 8, 128, WA_W + WB_W])
    rope_d = din("rope", [2, 128, 2, TS])
    csign_d = din("csign", [128, 512])
    rmat_d = din("rmat", [128, 128])
    cs64_d = din("cs64", [128, 256])
    dft_d = din("dft", [2, 4096, 2048], BF16)
    dftp_d = din("dftp", [2, 256, 256], BF16)
    ys_d = dout("ys", [TS, D])
    yp_d = dout("yp", [TP, D])
    nk_d = dout("nk", [2, 2, 256, 512])
    nv_d = dout("nv", [2, 2, 256, 512])
    nh_d = dout("nh", [2, 2, 2, 256])

    WGA = [dint("WGA%d" % l, [1024, WA_W], BF16) for l in range(2)]
    WGB = [dint("WGB%d" % l, [1024, WB_W], BF16) for l in range(2)]
    WGU = [dint("WGU%d" % l, [11, 128, 8, 512], BF16) for l in range(2)]
    LS = dint("LS0", [128, 2, TS], BF16)
    EXG = [dint("EXG%d" % l, [3072, 2048], BF16) for l in range(2)]
    QS = [[dint("QS%d_%d" % (l, k), [128, 4, TS], BF16) for k in range(2)] for l in range(2)]
    GS = [[dint("GS%d_%d" % (l, k), [128, 2, TS], BF16) for k in range(2)] for l in range(2)]
    XS = [[dint("XS%d_%d" % (l, k), [128, 8, T], F32) for k in range(2)] for l in range(2)]

    es = ExitStack()
    with es:
        S = Sched(nc, es)

        def OP(eng, method, *args, rd=(), wr=(), **kw):
            return S.op(eng, lambda e: getattr(e, method)(*args, **kw), rd, wr)

        def DMA(eng, out, in_, rd=(), wr=()):
            return S.dma(eng, lambda e: e.dma_start(out=out, in_=in_), rd, wr)

        def MM(out, lhsT, rhs, start, stop, rd=(), wr=()):
            return S.op("pe", lambda e: e.matmul(out, lhsT, rhs, start=start, stop=stop), rd, wr)

        class _Stop(Exception):
            pass

        def tap(name, src):
            o = nc.dram_tensor("dbg_" + name, list(src.shape), src.dtype, kind="ExternalOutput").ap()
            S.dma("sp", lambda e: e.dma_start(out=o, in_=src), (), ())
            TAPS.append("dbg_" + name)

        def check_stop(tag, taps=()):
            if stop == tag:
                S.barrier()
                for (n_, a_) in taps:
                    tap(n_, a_)
                raise _Stop()

        cc_sem = es.enter_context(nc.semaphore("cc_sem"))
        cc_state = {"n": 0}

        def CC(groups, src, dst, rd, wr):
            deps = S._deps(rd, wr)
            S._emit_waits("pool", deps)
            cc_state["n"] += 1
            v = cc_state["n"]
            ev = (cc_sem, v, ("cc",))
            S.q["pool"].append(lambda e: e.collective_compute(
                "AllGather", ALU.bypass, replica_groups=groups, ins=[src], outs=[dst]).then_inc(cc_sem, 1))
            S._commit(ev, rd, wr)
            S.extra_events = getattr(S, "extra_events", [])
            S.extra_events.append(ev)

        NBIG = 188 * 1024 // 4
        BIG = es.enter_context(nc.sbuf_tensor("big", [128, NBIG], F32))
        PSP = [es.enter_context(nc.psum_tensor("psp%d" % i, [128, 1024], F32))[:, :] for i in range(2)]
        PS = [PSP[0][:, 0:512], PSP[0][:, 512:1024], PSP[1][:, 0:512], PSP[1][:, 512:1024]] + \
             [es.enter_context(nc.psum_tensor("ps%d" % i, [128, 512], F32))[:, :] for i in range(4, 8)]
        PB = [Buf("ps%d" % i) for i in range(8)]

        class Arena:
            def __init__(self, start, end):
                self.start, self.cur, self.end = start, start, end

            def alloc(self, free_shape, dt):
                n = 1
                for s_ in free_shape:
                    n *= s_
                nbytes = n * (4 if dt == F32 else 2)
                nbytes = (nbytes + 63) // 64 * 64
                off = self.cur
                self.cur += nbytes
                assert self.cur <= self.end, ("arena overflow", self.cur, self.end)
                ap = BIG[:, off // 4:(off + nbytes) // 4]
                if dt != F32:
                    ap = ap.bitcast(dt)
                ap = ap[:, 0:n]
                if len(free_shape) == 2:
                    ap = ap.rearrange("p (a b) -> p a b", a=free_shape[0], b=free_shape[1])
                elif len(free_shape) == 3:
                    ap = ap.rearrange("p (a b c) -> p a b c", a=free_shape[0], b=free_shape[1], c=free_shape[2])
                return ap

            def reset(self, to=None):
                self.cur = self.start if to is None else to

        TOTAL = NBIG * 4
        AR = Arena(0, TOTAL)
        sv = AR.alloc([NSV], F32)
        ident = AR.alloc([128], F32)
        ones_b = AR.alloc([128], BF16)
        rmat_b = AR.alloc([128], BF16)
        cs64_b = AR.alloc([256], BF16)
        lw_b = AR.alloc([16, 128], BF16)
        rope_c = AR.alloc([TS], F32)
        rope_s = AR.alloc([TS], F32)
        csign = AR.alloc([512], F32)
        cst = [[AR.alloc([6, 8], F32) for w in range(2)] for l in range(2)]
        scond = AR.alloc([8, 2], BF16)
        small = AR.alloc([64], F32)
        nhst = AR.alloc([16], F32)
        dftp_b = AR.alloc([2, 2, 256], BF16)
        P_MLRU = AR.cur
        mix_lru = AR.alloc([2, T], BF16)
        P_MFOUR = AR.cur
        mix_four = AR.alloc([2, T], BF16)
        P_ATTN = AR.cur
        mix_attn = AR.alloc([4, T], BF16)
        P_Q = AR.cur
        kTp = AR.alloc([4, TP], BF16)
        Vp = AR.alloc([4, 512], BF16)
        xr_p = AR.alloc([2, TP], BF16)
        xf_p = AR.alloc([2, TP], BF16)
        q_all = AR.alloc([4, T], BF16)
        gr_all = AR.alloc([2, T], BF16)
        P_SCR = AR.cur
        print("persistent bytes", P_MLRU, P_Q, P_SCR, TOTAL)
        b_const = Buf("const")
        b_cst = Buf("cst")
        b_q = [Buf("q%d" % i) for i in range(5)]
        b_gr = [Buf("gr%d" % i) for i in range(5)]
        b_prm = Buf("prm")
        b_mixa = [Buf("mixa%d" % i) for i in range(5)]
        b_mixl = Buf("mixl")
        b_mixf = Buf("mixf")
        b_exg = [Buf("exg0"), Buf("exg1")]
        b_wga = [Buf("wga0"), Buf("wga1")]
        b_wgb = [Buf("wgb0"), Buf("wgb1")]
        b_xs = [[[Buf("xs%d_%d_%d" % (l, k, i)) for i in range(5)] for k in range(2)] for l in range(2)]
        b_qs = [[Buf("qs%d_%d" % (l, k)) for k in range(2)] for l in range(2)]
        b_rope = Buf("rope")
        b_ls = Buf("ls")
        b_nh = Buf("nh")

        def svc(l, name, i=0, n=1):
            o, w = SVL[name]
            return sv[:, l * SV_PER_L + o + i: l * SV_PER_L + o + i + n]

        try:
            DMA("sp", sv, sv_d, wr=[b_const])
            DMA("sp", csign, csign_d, wr=[b_const])
            DMA("sp", dftp_b, dftp_d.rearrange("c (t p) j -> p c t j", p=128), wr=[b_const])
            OP("pool", "memset", ident, 0.0, wr=[b_const])
            OP("pool", "affine_select", ident, ident, [[-1, 128]], ALU.not_equal, 1.0, base=0, channel_multiplier=1,
               rd=[b_const], wr=[b_const])
            OP("pool", "memset", ones_b, 1.0, wr=[b_const])
            OP("pool", "memset", nhst, 0.0, wr=[b_nh])
            AR.reset(P_SCR)
            st_a = AR.alloc([2048], F32)
            st_b = AR.alloc([128], F32)
            st_c = AR.alloc([256], F32)
            b_st = Buf("st")
            DMA("sp", st_a, lw_d, wr=[b_st])
            DMA("sp", st_b, rmat_d, wr=[b_st])
            DMA("sp", st_c, cs64_d, wr=[b_st])
            OP("dve", "tensor_copy", lw_b.rearrange("p a b -> p (a b)"), st_a, rd=[b_st], wr=[b_const])
            OP("dve", "tensor_copy", rmat_b, st_b, rd=[b_st], wr=[b_const])
            OP("dve", "tensor_copy", cs64_b, st_c, rd=[b_st], wr=[b_const])
            OP("act", "activation", scond.rearrange("p a b -> p (a b)"), sv[:, SV_MISC + 8: SV_MISC + 24], AF.Silu,
               rd=[b_const], wr=[b_const])

            def convert_weights(l, cast_engs, pwmax=2368, reset_to=None, kinds=None):
                if reset_to is not None:
                    AR.reset(reset_to)
                stf = [AR.alloc([pwmax], F32) for _ in range(2)]
                stb = [AR.alloc([pwmax], BF16) for _ in range(2)]
                bf = [Buf("stf0"), Buf("stf1")]
                bb = [Buf("stb0"), Buf("stb1")]
                jobs = []
                def split(c0, wtot, kind, base, align=1):
                    step = (pwmax // align) * align
                    o = 0
                    while o < wtot:
                        w_ = min(step, wtot - o)
                        jobs.append((c0 + o, w_, kind, base + o))
                        o += w_
                split(0, WA_W, "A", 0)
                split(WA_W + OFF_WOUT, 1024, "B", OFF_WOUT)
                split(WA_W + OFF_GU, 2816, "G0", 0, 256)
                split(WA_W + OFF_GU + 2816, 2816, "G1", 0, 256)
                split(WA_W + OFF_DN, 2816, "B", OFF_DN)
                if kinds is not None:
                    jobs = [j_ for j_ in jobs if j_[2] in kinds]
                k_ = 0
                for r in range(8):
                    for (c0, w_, kind, base) in jobs:
                        i = k_ % 2
                        eng = cast_engs[k_ % len(cast_engs)]
                        k_ += 1
                        DMA("sp", stf[i][:, 0:w_], wsl_d[l, r, :, c0:c0 + w_], wr=[bf[i]])
                        OP(eng, "tensor_copy", stb[i][:, 0:w_], stf[i][:, 0:w_], rd=[bf[i]], wr=[bb[i]])
                        if kind == "A":
                            DMA("pool", WGA[l][r * 128:(r + 1) * 128, base:base + w_], stb[i][:, 0:w_], rd=[bb[i]], wr=[b_wga[l]])
                        elif kind == "B":
                            DMA("pool", WGB[l][r * 128:(r + 1) * 128, base:base + w_], stb[i][:, 0:w_], rd=[bb[i]], wr=[b_wgb[l]])
                        else:
                            part = int(kind[1])
                            ns_ = w_ // 256
                            js0 = base // 256
                            DMA("pool", WGU[l][js0:js0 + ns_, :, r, part * 256:(part + 1) * 256].rearrange("s p n -> p s n"),
                                stb[i][:, 0:w_].rearrange("p (s n) -> p s n", n=256), rd=[bb[i]], wr=[b_wgb[l]])

            convert_weights(0, ["dve", "pool"], reset_to=P_SCR + 12 * 1024, kinds=("A",))
            S.barrier()
            check_stop("setup", [("wga", WGA[0][:, 0:2304])])

            def rstd_from(ps_ap, pbuf, out_ap, obuf, scale, tmp_ap, tbuf):
                OP("act", "activation", tmp_ap, ps_ap, AF.Ln, bias=small[:, 0:1], scale=scale, rd=[pbuf, b_const], wr=[tbuf])
                OP("act", "activation", out_ap, tmp_ap, AF.Exp, scale=-0.5, rd=[tbuf], wr=[obuf])

            OP("pool", "memset", small[:, 0:1], EPS, wr=[b_const])
            OP("pool", "memset", small[:, 1:2], 1.0, wr=[b_const])

            m0 = sv[:, SV_MISC + 0:SV_MISC + 1]
            m1 = sv[:, SV_MISC + 1:SV_MISC + 2]

            def blend(dst, other, rd_dst, rd_other, wr_dst):
                OP("act", "activation", dst, dst, AF.Identity, scale=m0, rd=list(rd_dst) + [b_const], wr=wr_dst)
                OP("dve", "scalar_tensor_tensor", dst, other, m1, dst, ALU.mult, ALU.add,
                   rd=list(rd_other) + [b_const] + list(wr_dst), wr=wr_dst)

            for l in range(2):
                li = lambda_init(l)
                AR.reset(P_SCR)
                wm = [AR.alloc([8, 1536], BF16) for _ in range(2)]
                bwm = [Buf("wm0"), Buf("wm1")]
                modsb = AR.alloc([48, 2], F32)
                bmod_ = Buf("modsb")
                for sl in range(4):
                    i = sl % 2
                    DMA("sp", wm[i], WGA[l][:, 2304 + sl * 1536: 2304 + (sl + 1) * 1536].rearrange("(k p) n -> p k n", p=128),
                        rd=[b_wga[l]], wr=[bwm[i]])
                    for jj in range(12):
                        j = sl * 12 + jj
                        for kc in range(8):
                            MM(PS[0][:, 2 * j:2 * j + 2], wm[i][:, kc, jj * 128:(jj + 1) * 128], scond[:, kc, :],
                               kc == 0, kc == 7, rd=[bwm[i], b_const], wr=[PB[0]])
                check_stop("Ma%d" % l, [("wm0", wm[0]), ("wm1", wm[1])])
                psm = PS[0][:, 0:96].rearrange("p (a b) -> p a b", b=2)
                for w in range(2):
                    OP("dve", "tensor_tensor", modsb[:, :, w], psm[:, :, w], svc(l, "bmod", 0, 48), ALU.add,
                       rd=[PB[0], b_const], wr=[bmod_])
                for w in range(2):
                    c_ = cst[l][w]
                    OP("dve", "scalar_tensor_tensor", c_[:, 0, :], modsb[:, 8:16, w], 1.0, svc(l, "gpm", 0, 8), ALU.add, ALU.mult,
                       rd=[bmod_, b_const], wr=[b_cst])
                    OP("dve", "tensor_copy", c_[:, 1, :], modsb[:, 0:8, w], rd=[bmod_], wr=[b_cst])
                    OP("dve", "tensor_tensor", c_[:, 2, :], modsb[:, 16:24, w], svc(l, "gqm", 0, 8), ALU.mult,
                       rd=[bmod_, b_const], wr=[b_cst])
                    OP("dve", "scalar_tensor_tensor", c_[:, 3, :], modsb[:, 32:40, w], 1.0, svc(l, "gpf", 0, 8), ALU.add, ALU.mult,
                       rd=[bmod_, b_const], wr=[b_cst])
                    OP("dve", "tensor_copy", c_[:, 4, :], modsb[:, 24:32, w], rd=[bmod_], wr=[b_cst])
                    OP("dve", "tensor_tensor", c_[:, 5, :], modsb[:, 40:48, w], svc(l, "gqf", 0, 8), ALU.mult,
                       rd=[bmod_, b_const], wr=[b_cst])
                S.barrier()
                check_stop("M%d" % l, [("cst0", cst[l][0]), ("cst1", cst[l][1]), ("modsb", modsb)])

                def phaseA(k):
                    AR.reset(P_MLRU)
                    win = AR.alloc([8, 2304], BF16)
                    rc_t = AR.alloc([512], F32)
                    rs_t = AR.alloc([512], F32)
                    assert AR.cur <= P_Q
                    AR.reset(P_SCR)
                    b_win = Buf("win")
                    xblk = AR.alloc([8, 512], F32)
                    b_x = Buf("xblk")
                    xtm = [AR.alloc([1024], F32) for _ in range(4)]
                    b_xtm = [Buf("xtm%d" % i) for i in range(4)]
                    sq = AR.alloc([8, 512], BF16)
                    b_sq = Buf("sq")
                    u = AR.alloc([8, 512], BF16)
                    b_u = Buf("u")
                    rstd = AR.alloc([512], F32)
                    b_rstd = Buf("rstd")
                    tl = AR.alloc([512], F32)
                    b_tl = Buf("tl")
                    tmpf = [AR.alloc([512], F32) for _ in range(2)]
                    b_tmpf = [Buf("tmpf0"), Buf("tmpf1")]
                    qb_ = [AR.alloc([512], BF16) for _ in range(2)]
                    b_qb = [Buf("qb0"), Buf("qb1")]
                    t1_ = [AR.alloc([512], F32) for _ in range(2)]
                    t2_ = [AR.alloc([512], F32) for _ in range(2)]
                    b_t1 = [Buf("t1_0"), Buf("t1_1")]
                    b_t2 = [Buf("t2_0"), Buf("t2_1")]
                    stg = [AR.alloc([512], BF16) for _ in range(3)]
                    b_stg = [Buf("stg%d" % i) for i in range(3)]
                    stgf = [AR.alloc([512], F32) for _ in range(2)]
                    b_stgf = [Buf("stgf0"), Buf("stgf1")]
                    b_rt = Buf("rt")
                    for h_ in range(2):
                        DMA("sp", win[:, h_ * 4:(h_ + 1) * 4, :],
                            WGA[l][h_ * 512:(h_ + 1) * 512, 0:2304].rearrange("(k p) n -> p k n", p=128),
                            rd=[b_wga[l]], wr=[b_win])
                    cnt = {"pj": 0, "rot": 0, "tm": 0, "stg": 0, "stgf": 0, "tmpf": 0, "qb": 0}
                    DMA("sp", rope_c, rope_d[k, :, 0, :], wr=[b_rope])
                    DMA("sp", rope_s, rope_d[k, :, 1, :], wr=[b_rope])
                    b_exb_k = Buf("exbk")
                    exbk = EXG[l][k * 1536:(k + 1) * 1536, :]
                    for tb in range(5 if k == 0 else 4):
                        w = 0 if tb < 4 else 1
                        c_ = cst[l][w]
                        tok = slice(tb * 512, (tb + 1) * 512)
                        if l == 0:
                            src = xs_d[k] if tb < 4 else xp_d
                            r0 = tb * 512 if tb < 4 else 0
                            for tt in range(4):
                                DMA("sp", xtm[tt], src[r0 + tt * 128: r0 + (tt + 1) * 128, :], wr=[b_xtm[tt]])
                            for c in range(8):
                                pb = 1 + (c % 2)
                                for tt in range(4):
                                    OP("pe", "transpose", PS[pb][:, tt * 128:(tt + 1) * 128], xtm[tt][:, c * 128:(c + 1) * 128], ident,
                                       rd=[b_xtm[tt], b_const], wr=[PB[pb]])
                                OP("act" if c % 2 == 0 else "dve", "copy" if c % 2 == 0 else "tensor_copy", xblk[:, c, :], PS[pb],
                                   rd=[PB[pb]], wr=[b_x])
                            DMA("pool", XS[0][k][:, :, tok], xblk, rd=[b_x], wr=[b_xs[0][k][tb]])
                        else:
                            DMA("sp", xblk, XS[1][k][:, :, tok], rd=[b_xs[1][k][tb]], wr=[b_x])
                        if tb == 0 and k == 0:
                            check_stop("Ax%d" % l, [("xblk", xblk)])
                        OP("act", "activation", sq[:, 0:4, :], xblk[:, 0:4, :], AF.Square, rd=[b_x], wr=[b_sq])
                        OP("pool", "tensor_tensor", sq[:, 4:8, :], xblk[:, 4:8, :], xblk[:, 4:8, :], ALU.mult, rd=[b_x], wr=[b_sq])
                        for c in range(8):
                            MM(PS[0], ones_b, sq[:, c, :], c == 0, c == 7, rd=[b_sq, b_const], wr=[PB[0]])
                        rstd_from(PS[0], PB[0], rstd, b_rstd, 1.0 / D, tl, b_tl)
                        for c in range(8):
                            i = cnt["tmpf"] % 2
                            cnt["tmpf"] += 1
                            OP("dve", "scalar_tensor_tensor", tmpf[i], xblk[:, c, :], c_[:, 0, c:c + 1], rstd, ALU.mult, ALU.mult,
                               rd=[b_x, b_rstd, b_cst], wr=[b_tmpf[i]])
                            OP("act", "activation", u[:, c, :], tmpf[i], AF.Identity, bias=c_[:, 1, c:c + 1], scale=1.0,
                               rd=[b_tmpf[i], b_cst], wr=[b_u])

                        if tb == 0 and k == 0:
                            check_stop("An%d" % l, [("u", u), ("rstd", rstd)])

                        if tb < 4:
                            OP("act", "copy", rc_t, rope_c[:, tok], rd=[b_rope], wr=[b_rt])
                            OP("act", "copy", rs_t, rope_s[:, tok], rd=[b_rope], wr=[b_rt])

                        def proj_fm(j):
                            pb = 3 + (cnt["pj"] % 3)
                            cnt["pj"] += 1
                            for kc in range(8):
                                MM(PS[pb], win[:, kc, j * 128:(j + 1) * 128], u[:, kc, :], kc == 0, kc == 7,
                                   rd=[b_win, b_u], wr=[PB[pb]])
                            return pb

                        def proj_tm(tt, c0):
                            pb = 7
                            for kc in range(8):
                                MM(PS[pb], u[:, kc, tt * 128:(tt + 1) * 128], win[:, kc, c0:c0 + 512], kc == 0, kc == 7,
                                   rd=[b_win, b_u], wr=[PB[pb]])
                            return pb

                        def nstg():
                            i = cnt["stg"] % 3
                            cnt["stg"] += 1
                            return i

                        for j in range(18):
                            kind = ("q", "k", "v", "xr", "gr", "xf")[[0, 0, 0, 0, 1, 1, 1, 1, 2, 2, 2, 2, 3, 3, 4, 4, 5, 5][j]]
                            if kind == "v":
                                continue
                            pb = proj_fm(j)
                            dbg0 = (tb == 0 and k == 0 and l == 0 and j == 0)
                            if dbg0:
                                check_stop("As1", [("u", u)])
                            if kind in ("q", "k"):
                                h = j % 4
                                if tb < 4:
                                    i = cnt["qb"] % 2
                                    cnt["qb"] += 1
                                    OP("act", "copy", qb_[i], PS[pb], rd=[PB[pb]], wr=[b_qb[i]])
                                    if dbg0:
                                        check_stop("As2", [("qb", qb_[i])])
                                    MM(PS[6], rmat_b, qb_[i], True, True, rd=[b_qb[i], b_const], wr=[PB[6]])
                                    if dbg0:
                                        check_stop("As3", [("qb", qb_[i])])
                                    OP("dve", "tensor_tensor", t1_[i], PS[pb], rc_t, ALU.mult, rd=[PB[pb], b_rt], wr=[b_t1[i]])
                                    OP("dve", "tensor_tensor", t2_[i], PS[6], rs_t, ALU.mult, rd=[PB[6], b_rt], wr=[b_t2[i]])
                                    if dbg0:
                                        check_stop("As4", [("t1", t1_[i]), ("t2", t2_[i])])
                                    if kind == "q":
                                        OP("pool", "tensor_tensor", q_all[:, h, tok], t1_[i], t2_[i], ALU.add,
                                           rd=[b_t1[i], b_t2[i]], wr=[b_q[tb]])
                                    else:
                                        si = nstg()
                                        OP("pool", "tensor_tensor", stg[si], t1_[i], t2_[i], ALU.add,
                                           rd=[b_t1[i], b_t2[i]], wr=[b_stg[si]])
                                        DMA("pool", exbk[h * 128:(h + 1) * 128, tok], stg[si], rd=[b_stg[si]], wr=[b_exb_k])
                                else:
                                    if kind == "q":
                                        OP("act", "copy", q_all[:, h, tok], PS[pb], rd=[PB[pb]], wr=[b_q[tb]])
                                    else:
                                        OP("act", "copy", kTp[:, h, :], PS[pb], rd=[PB[pb]], wr=[b_prm])
                            elif kind == "gr":
                                OP("act", "activation", gr_all[:, j - 14, tok], PS[pb], AF.Gelu_apprx_tanh, rd=[PB[pb]], wr=[b_gr[tb]])
                            else:
                                c2 = j % 2
                                if tb < 4:
                                    si = nstg()
                                    OP("act", "copy", stg[si], PS[pb], rd=[PB[pb]], wr=[b_stg[si]])
                                    base = 512 if kind == "xr" else 768
                                    DMA("pool", exbk[base + c2 * 128: base + (c2 + 1) * 128, tok], stg[si], rd=[b_stg[si]], wr=[b_exb_k])
                                else:
                                    dst = xr_p if kind == "xr" else xf_p
                                    OP("act", "copy", dst[:, c2, :], PS[pb], rd=[PB[pb]], wr=[b_prm])
                            if tb == 0 and k == 0 and l == 0:
                                check_stop("Aj%d" % j, [("q", q_all[:, :, 0:512])])
                        for tt in range(4):
                            pb = proj_tm(tt, 1024)
                            if tb < 4:
                                si = nstg()
                                OP("dve", "tensor_copy", stg[si], PS[pb], rd=[PB[pb]], wr=[b_stg[si]])
                                vview = exbk[1024:1536, :].rearrange("r (t f) -> (r t) f", f=512)
                                vdst = vview[tb * 512 + tt * 128: tb * 512 + (tt + 1) * 128, :]
                                DMA("pool", vdst, stg[si], rd=[b_stg[si]], wr=[b_exb_k])
                            else:
                                seq, hh = tt // 2, tt % 2
                                fi = cnt["stgf"] % 2
                                cnt["stgf"] += 1
                                OP("dve", "tensor_copy", stgf[fi], PS[pb], rd=[PB[pb]], wr=[b_stgf[fi]])
                                OP("act", "copy", Vp[:, tt, :], PS[pb], rd=[PB[pb]], wr=[b_prm])
                                DMA("pool", nv_d[seq, l, hh * 128:(hh + 1) * 128, :], stgf[fi], rd=[b_stgf[fi]])
                                pb = proj_tm(tt, 512)
                                fi = cnt["stgf"] % 2
                                cnt["stgf"] += 1
                                OP("dve", "tensor_copy", stgf[fi], PS[pb], rd=[PB[pb]], wr=[b_stgf[fi]])
                                DMA("pool", nk_d[seq, l, hh * 128:(hh + 1) * 128, :], stgf[fi], rd=[b_stgf[fi]])
                        if tb == 0 and k == 0:
                            check_stop("Ap%d" % l, [("q", q_all), ("gr", gr_all)])
                    DMA("pool", QS[l][k], q_all[:, :, 0:TS], rd=b_q[0:4], wr=[b_qs[l][k]])
                    DMA("pool", GS[l][k], gr_all[:, :, 0:TS], rd=b_gr[0:4], wr=[b_qs[l][k]])
                    b_exg[l].w = None
                    S.barrier()
                    check_stop("A%d_%d" % (l, k), [("q", q_all), ("gr", gr_all), ("exg", EXG[l]), ("ktp", kTp), ("vp", Vp), ("xrp", xr_p), ("xs", XS[l][k])])

                def phaseB1(k, own=False):
                    if l == 0 and k == 1:
                        DMA("sp", mix_lru[:, :, 0:TS], LS, rd=[b_ls], wr=[b_mixl])
                        S.barrier()
                        return
                    AR.reset(P_SCR)
                    NT = 4608
                    xr_c = AR.alloc([NT], BF16)
                    xc = AR.alloc([NT], F32)
                    xcb = AR.alloc([NT], BF16)
                    HS = AR.alloc([NT], F32)
                    _sv = AR.cur
                    AR.reset(P_MFOUR)
                    HB = AR.alloc([2560], F32)
                    GR = AR.alloc([2560], F32)
                    assert AR.cur <= P_Q
                    AR.reset(_sv)
                    GI = AR.alloc([2560], F32)
                    GC = AR.alloc([2560], F32)
                    sp_ = AR.alloc([8], F32)
                    b_xr, b_xc, b_xcb, b_hs, b_hb = Buf("xr"), Buf("xc"), Buf("xcb"), Buf("hs"), Buf("hb")
                    b_g = Buf("gates")
                    b_sp = Buf("sp")
                    NTK = 4608 if k == 0 else 4096
                    seqs = [(0, 4096), (4096, 4352), (4352, 4608)] if k == 0 else [(0, 4096)]
                    _sv2 = AR.cur
                    AR.reset(P_MFOUR + 20480)
                    tmpA = AR.alloc([2048], BF16)
                    tmpB = AR.alloc([2048], BF16)
                    assert AR.cur <= P_Q
                    AR.reset(_sv2)
                    b_tA, b_tB = Buf("tA"), Buf("tB")
                    if own:
                        _sv3 = AR.cur
                        AR.reset(P_MFOUR + 20480)
                        tmpG = AR.alloc([2, TS], BF16)
                        assert AR.cur <= P_Q
                        AR.reset(_sv3)
                        b_tg = Buf("tmpG")
                        DMA("sp", gr_all[:, :, 0:TS], GS[l][0], rd=[b_qs[l][0]], wr=b_gr[0:4])
                        DMA("sp", tmpG, GS[l][1], rd=[b_qs[l][1]], wr=[b_tg])
                        blend(gr_all[:, :, 0:TS], tmpG, b_gr[0:4], [b_tg], b_gr[0:4])
                    else:
                        DMA("sp", gr_all[:, :, 0:TS], GS[l][k], rd=[b_qs[l][k]], wr=b_gr[0:4])
                        if l == 0:
                            _sv4 = AR.cur
                            AR.reset(P_MFOUR + 20480)
                            grO = AR.alloc([2, TS], BF16)
                            assert AR.cur <= P_Q
                            AR.reset(_sv4)
                            b_gro = Buf("grO")
                            DMA("sp", grO, GS[l][1], rd=[b_qs[l][1]], wr=[b_gro])
                    a0 = sv[:, SV_MISC + 0:SV_MISC + 1]
                    a1 = sv[:, SV_MISC + 1:SV_MISC + 2]
                    OP("act", "activation", sp_[:, 0:4], svc(l, "lam", 0, 4), AF.Exp, scale=-1.0, rd=[b_const], wr=[b_sp])
                    OP("act", "activation", sp_[:, 4:8], sp_[:, 0:4], AF.Ln, bias=small[:, 1:2], scale=1.0, rd=[b_sp, b_const], wr=[b_sp])
                    OP("dve", "tensor_scalar", sp_[:, 0:4], sp_[:, 4:8], -8.0, None, ALU.mult, rd=[b_sp], wr=[b_sp])
                    OP("dve", "tensor_scalar", sp_[:, 4:8], sp_[:, 4:8], -16.0, None, ALU.mult, rd=[b_sp], wr=[b_sp])
                    for c2 in range(2):
                        for r in range(2):
                            DMA("sp", xr_c[:, r * 2048:(r + 1) * 2048],
                                EXG[l][r * 1536 + 512 + c2 * 128: r * 1536 + 512 + (c2 + 1) * 128, :], wr=[b_xr])
                        if k == 0:
                            OP("pool", "tensor_copy", xr_c[:, 4096:4608], xr_p[:, c2, :], rd=[b_prm], wr=[b_xr])
                        cw = lambda t_: svc(l, "convw", t_ * 2 + c2)
                        OP("dve", "tensor_scalar", xc[:, 0:NTK], xr_c[:, 0:NTK], cw(2), svc(l, "convb", c2), ALU.mult, ALU.add,
                           rd=[b_xr, b_const], wr=[b_xc])
                        for (s0, s1) in seqs:
                            OP("dve", "scalar_tensor_tensor", xc[:, s0 + 2:s1], xr_c[:, s0:s1 - 2], cw(0), xc[:, s0 + 2:s1],
                               ALU.mult, ALU.add, rd=[b_xr, b_const, b_xc], wr=[b_xc])
                            OP("dve", "scalar_tensor_tensor", xc[:, s0 + 1:s1], xr_c[:, s0:s1 - 1], cw(1), xc[:, s0 + 1:s1],
                               ALU.mult, ALU.add, rd=[b_xr, b_const, b_xc], wr=[b_xc])
                            OP("dve", "scalar_tensor_tensor", xc[:, s0:s1 - 1], xr_c[:, s0 + 1:s1], cw(3), xc[:, s0:s1 - 1],
                               ALU.mult, ALU.add, rd=[b_xr, b_const, b_xc], wr=[b_xc])
                        OP("pool", "tensor_copy", xcb[:, 0:NTK], xc[:, 0:NTK], rd=[b_xc], wr=[b_xcb])
                        for d in range(2):
                            halves = [(0, 2048), (2048, NTK)]
                            if d == 1:
                                halves = halves[::-1]
                            col = d * 2 + c2
                            wa = lw_b[:, ((l * 2 + d) * 2 + 0) * 2 + c2, :]
                            wx = lw_b[:, ((l * 2 + d) * 2 + 1) * 2 + c2, :]
                            for (t0, t1) in halves:
                                n = t1 - t0
                                for s_ in range(n // 512):
                                    a0 = t0 + s_ * 512
                                    pa, pi = 1 + 2 * (s_ % 2), 2 + 2 * (s_ % 2)
                                    MM(PS[pa], wa, xcb[:, a0:a0 + 512], True, True, rd=[b_xcb, b_const], wr=[PB[pa]])
                                    MM(PS[pi], wx, xcb[:, a0:a0 + 512], True, True, rd=[b_xcb, b_const], wr=[PB[pi]])
                                    OP("act", "activation", GR[:, s_ * 512:(s_ + 1) * 512], PS[pa], AF.Sigmoid,
                                       bias=svc(l, "ba", col), scale=1.0, rd=[PB[pa], b_const], wr=[b_g])
                                    OP("act", "activation", GI[:, s_ * 512:(s_ + 1) * 512], PS[pi], AF.Sigmoid,
                                       bias=svc(l, "bx", col), scale=1.0, rd=[PB[pi], b_const], wr=[b_g])
                                OP("act", "activation", GC[:, 0:n], GR[:, 0:n], AF.Exp, scale=sp_[:, 4 + col:5 + col], rd=[b_g, b_sp], wr=[b_g])
                                OP("act", "activation", GR[:, 0:n], GR[:, 0:n], AF.Exp, scale=sp_[:, col:col + 1], rd=[b_g, b_sp], wr=[b_g])
                                OP("act", "activation", GC[:, 0:n], GC[:, 0:n], AF.Ln, bias=small[:, 1:2], scale=-1.0, rd=[b_g, b_const], wr=[b_g])
                                OP("act", "activation", GC[:, 0:n], GC[:, 0:n], AF.Exp, scale=0.5, rd=[b_g], wr=[b_g])
                                OP("dve", "tensor_tensor", GI[:, 0:n], GI[:, 0:n], xc[:, t0:t1], ALU.mult, rd=[b_g, b_xc], wr=[b_g])
                                OP("dve", "tensor_tensor", GI[:, 0:n], GI[:, 0:n], GC[:, 0:n], ALU.mult, rd=[b_g], wr=[b_g])
                                for (s0, s1) in seqs:
                                    lo, hi = max(s0, t0), min(s1, t1)
                                    if lo >= hi:
                                        continue
                                    if d == 0:
                                        if lo == s0:
                                            init = svc(l, "h0", col) if s0 == 0 else 0.0
                                        else:
                                            init = HS[:, lo - 1:lo]
                                        OP("dve", "tensor_tensor_scan", HS[:, lo:hi], GR[:, lo - t0:hi - t0], GI[:, lo - t0:hi - t0],
                                           init, ALU.mult, ALU.add, rd=[b_g, b_hs, b_const], wr=[b_hs])
                                    else:
                                        if hi == s1:
                                            init = svc(l, "h0", col) if s0 == 0 else 0.0
                                        else:
                                            init = small[:, 8:9]
                                        OP("dve", "tensor_tensor_scan", HB[:, lo - t0:hi - t0][:, ::-1], GR[:, lo - t0:hi - t0][:, ::-1],
                                           GI[:, lo - t0:hi - t0][:, ::-1], init, ALU.mult, ALU.add,
                                           rd=[b_g, b_hb, b_const], wr=[b_hb])
                                if d == 0 and k == 0 and t1 == 4608:
                                    for sq_ in range(2):
                                        e_ = 4096 + sq_ * 256 + 255
                                        OP("dve", "tensor_copy", nhst[:, (sq_ * 2 + 0) * 2 + c2:(sq_ * 2 + 0) * 2 + c2 + 1], HS[:, e_:e_ + 1],
                                           rd=[b_hs], wr=[b_nh])
                                if d == 1:
                                    if t1 == NTK:
                                        for sq_ in (range(2) if k == 0 else ()):
                                            e_ = 4096 + sq_ * 256 - t0
                                            OP("dve", "tensor_copy", nhst[:, (sq_ * 2 + 1) * 2 + c2:(sq_ * 2 + 1) * 2 + c2 + 1], HB[:, e_:e_ + 1],
                                               rd=[b_hb], wr=[b_nh])
                                        OP("dve", "tensor_copy", small[:, 8:9], HB[:, 0:1], rd=[b_hb], wr=[b_const])
                                    OP("dve", "tensor_tensor", HS[:, t0:t1], HS[:, t0:t1], HB[:, 0:n], ALU.add, rd=[b_hs, b_hb], wr=[b_hs])
                        if own:
                            OP("act", "activation", HB[:, 0:2048], HS[:, 0:2048], AF.Identity, scale=m0, rd=[b_hs, b_const, b_hb], wr=[b_hb])
                            OP("dve", "scalar_tensor_tensor", HB[:, 0:2048], HS[:, 2048:4096], m1, HB[:, 0:2048], ALU.mult, ALU.add,
                               rd=[b_hs, b_const, b_hb], wr=[b_hb])
                            OP("dve", "tensor_tensor", mix_lru[:, c2, 0:2048], HB[:, 0:2048], gr_all[:, c2, 0:2048], ALU.mult,
                               rd=[b_hb] + b_gr, wr=[b_mixl])
                        else:
                            OP("dve", "tensor_tensor", mix_lru[:, c2, 0:2048], HS[:, k * 2048:(k + 1) * 2048], gr_all[:, c2, 0:2048], ALU.mult,
                               rd=[b_hs] + b_gr, wr=[b_mixl])
                            if l == 0:
                                OP("dve", "tensor_tensor", xcb[:, 0:2048], HS[:, 2048:4096], grO[:, c2, :], ALU.mult,
                                   rd=[b_hs, b_gro, b_xcb], wr=[b_xcb])
                                DMA("pool", LS[:, c2, :], xcb[:, 0:2048], rd=[b_xcb], wr=[b_ls])
                        if k == 0:
                            OP("dve", "tensor_tensor", mix_lru[:, c2, 2048:2560], HS[:, 4096:4608], gr_all[:, c2, 2048:2560], ALU.mult,
                               rd=[b_hs] + b_gr, wr=[b_mixl])
                    for sq_ in (range(2) if k == 0 else ()):
                        for d in range(2):
                            for c2 in range(2):
                                k_ = (sq_ * 2 + d) * 2 + c2
                                dst = bass.AP(nh_d.tensor, ((sq_ * 2 + l) * 2 + d) * 256 + c2 * 128, [[1, 128], [1, 1]])
                                DMA("pool", dst, nhst[:, k_:k_ + 1], rd=[b_nh])
                    S.barrier()
                    check_stop("B1_%d_%d" % (l, k), [("mixl", mix_lru), ("nhst", nhst), ("hs", HS), ("xc", xc), ("xrc", xr_c), ("hb", HB), ("svh0", sv[:, 359:363])])

                def phaseB2(k, own=False):
                    AR.reset(P_ATTN)
                    xf_c = AR.alloc([2, 4096], BF16)
                    assert AR.cur <= P_Q
                    AR.reset(P_SCR)
                    Ytm = AR.alloc([32, 512], BF16)
                    Ytp = AR.alloc([4, 512], BF16)
                    tabs = [[AR.alloc([4, 512], BF16) for _ in range(3)] for cs in range(2)]
                    b_xf, b_y, b_yp = Buf("xf"), Buf("ytm"), Buf("ytp")
                    b_tab = [[Buf("tab%d_%d" % (cs, i)) for i in range(3)] for cs in range(2)]
                    for c2 in range(2):
                        for r in range(2):
                            DMA("sp", xf_c[:, c2, r * 2048:(r + 1) * 2048],
                                EXG[l][r * 1536 + 768 + c2 * 128: r * 1536 + 768 + (c2 + 1) * 128, :], rd=[b_exg[l]], wr=[b_xf])
                    for tc in range(32):
                        pb = 1 + tc % 2
                        for c2 in range(2):
                            MM(PS[pb][:, c2 * 256:(c2 + 1) * 256], xf_c[:, c2, tc * 128:(tc + 1) * 128], cs64_b, True, True,
                               rd=[b_xf, b_const], wr=[PB[pb]])
                        OP("act", "activation", Ytm[:, tc, :], PS[pb], AF.Identity, scale=(sv[:, SV_MISC + 2:SV_MISC + 3] if own else sv[:, SV_MISC + 6 + k:SV_MISC + 7 + k]),
                           rd=[PB[pb], b_const], wr=[b_y])
                    for tc in (range(4) if k == 0 else ()):
                        pb = 1 + tc % 2
                        for c2 in range(2):
                            MM(PS[pb][:, c2 * 256:(c2 + 1) * 256], xf_p[:, c2, tc * 128:(tc + 1) * 128], cs64_b, True, True,
                               rd=[b_prm, b_const], wr=[PB[pb]])
                        OP("act", "copy", Ytp[:, tc, :], PS[pb], rd=[PB[pb]], wr=[b_yp])
                    kk = 0
                    for jb in range(4):
                        for tg in range(8):
                            i = kk % 3
                            kk += 1
                            for cs in range(2):
                                DMA("sp", tabs[cs][i],
                                    dft_d[cs, tg * 512:(tg + 1) * 512, jb * 512:(jb + 1) * 512].rearrange("(t p) j -> p t j", p=128),
                                    wr=[b_tab[cs][i]])
                            for c2 in range(2):
                                pb = 3 + c2
                                for t_ in range(4):
                                    for cs in range(2):
                                        tc = tg * 4 + t_
                                        MM(PS[pb], Ytm[:, tc, c2 * 256 + cs * 128: c2 * 256 + (cs + 1) * 128], tabs[cs][i][:, t_, :],
                                           tg == 0 and t_ == 0 and cs == 0, tg == 7 and t_ == 3 and cs == 1,
                                           rd=[b_y, b_tab[cs][i]], wr=[PB[pb]])
                        for c2 in range(2):
                            OP("dve", "tensor_tensor", mix_four.rearrange("p a b -> p (a b)")[:, c2 * T + jb * 512: c2 * T + (jb + 1) * 512], PS[3 + c2], csign, ALU.mult,
                               rd=[PB[3 + c2], b_const], wr=[b_mixf])
                    for sq_ in (range(2) if k == 0 else ()):
                        for c2 in range(2):
                            pb = 5 + c2
                            for t_ in range(2):
                                for cs in range(2):
                                    MM(PS[pb][:, 0:256], Ytp[:, sq_ * 2 + t_, c2 * 256 + cs * 128: c2 * 256 + (cs + 1) * 128],
                                       dftp_b[:, cs, t_, :], t_ == 0 and cs == 0, t_ == 1 and cs == 1,
                                       rd=[b_yp, b_const], wr=[PB[pb]])
                            OP("act", "copy", mix_four[:, c2, 2048 + sq_ * 256: 2048 + (sq_ + 1) * 256], PS[pb][:, 0:256],
                               rd=[PB[pb]], wr=[b_mixf])
                    S.barrier()
                    check_stop("B2_%d_%d" % (l, k), [("mixf", mix_four)])

                def phaseC(k, own=False):
                    AR.reset(P_SCR)
                    KT = [AR.alloc([4352], BF16) for _ in range(2)]
                    VH = [AR.alloc([34, 128], BF16) for _ in range(2)]
                    b_kt = [Buf("kt0"), Buf("kt1")]
                    b_vh = [Buf("vh0"), Buf("vh1")]
                    ckf = AR.alloc([2, 512], F32)
                    cvf = AR.alloc([2, 512], F32)
                    b_ckf, b_cvf = Buf("ckf"), Buf("cvf")
                    pTP = [AR.alloc([1024], BF16) for _ in range(2)]
                    pT = [pTP[0][:, 0:512], pTP[0][:, 512:1024], pTP[1][:, 0:512], pTP[1][:, 512:1024]]
                    b_pT = [Buf("pT%d" % i) for i in range(4)]
                    fR = [AR.alloc([512], F32) for _ in range(2)]
                    fT = [AR.alloc([512], F32) for _ in range(2)]
                    fo = AR.alloc([512], F32)
                    fsq = AR.alloc([512], BF16)
                    frs = AR.alloc([512], F32)
                    ftl = AR.alloc([512], F32)
                    b_fR = [Buf("fR0"), Buf("fR1")]
                    b_fT = [Buf("fT0"), Buf("fT1")]
                    b_fo, b_fsq, b_frs, b_ftl = Buf("fo"), Buf("fsq"), Buf("frs"), Buf("ftl")
                    lamv = AR.alloc([8], F32)
                    lprod = AR.alloc([128], F32)
                    b_lam = Buf("lam")
                    OP("dve", "tensor_tensor", lprod[:, 0:64], svc(l, "wl", 0, 64), svc(l, "wl", 64, 64), ALU.mult, rd=[b_const], wr=[b_lam])
                    OP("dve", "tensor_tensor", lprod[:, 64:128], svc(l, "wl", 128, 64), svc(l, "wl", 192, 64), ALU.mult, rd=[b_const], wr=[b_lam])
                    OP("dve", "reduce_sum", lamv[:, 0:1], lprod[:, 0:64], mybir.AxisListType.X, rd=[b_lam], wr=[b_lam])
                    OP("dve", "reduce_sum", lamv[:, 1:2], lprod[:, 64:128], mybir.AxisListType.X, rd=[b_lam], wr=[b_lam])
                    OP("act", "activation", lamv[:, 2:4], lamv[:, 0:2], AF.Exp, rd=[b_lam], wr=[b_lam])
                    OP("dve", "tensor_tensor", lamv[:, 4:5], lamv[:, 3:4], lamv[:, 2:3], ALU.subtract, rd=[b_lam], wr=[b_lam])
                    OP("dve", "tensor_scalar", lamv[:, 4:5], lamv[:, 4:5], -li, None, ALU.add, rd=[b_lam], wr=[b_lam])
                    OP("dve", "tensor_scalar", lamv[:, 5:6], svc(l, "gsub"), 1.0 - li, None, ALU.mult, rd=[b_const], wr=[b_lam])
                    if own:
                        tmpQ = AR.alloc([2, TS], BF16)
                        b_tq = Buf("tmpQ")
                        DMA("act", q_all[:, :, 0:TS], QS[l][0], rd=[b_qs[l][0]], wr=b_q[0:4])
                        for hp in range(2):
                            DMA("act", tmpQ, QS[l][1][:, hp * 2:(hp + 1) * 2, :], rd=[b_qs[l][1]], wr=[b_tq])
                            blend(q_all[:, hp * 2:(hp + 1) * 2, 0:TS], tmpQ, b_q[0:4], [b_tq], b_q[0:4])
                    else:
                        DMA("act", q_all[:, :, 0:TS], QS[l][k], rd=[b_qs[l][k]], wr=b_q[0:4])
                    DMA("act", ckf, ck_d[l].rearrange("(t p) f -> p t f", p=128), wr=[b_ckf])
                    DMA("act", cvf, cv_d[l].rearrange("(t p) f -> p t f", p=128), wr=[b_cvf])
                    if l == 0 and k == 0:
                        convert_weights(0, ["pool"], pwmax=1024, kinds=("B", "G0", "G1"))
                    if l == 0 and k == 1:
                        convert_weights(1, ["pool"], pwmax=1024)
                    scale = 64 ** -0.5
                    state = {"pt": 0}

                    def attn_block(q_ap, nq, kt_ap, v_fn, nkc, rd_q, rd_k, rd_v, out_ap, wr_out):
                        def qk(kc):
                            par = kc % 2
                            for m in range(2):
                                sb = par * 2 + m
                                MM(PS[sb][:, 0:nq], kt_ap[m * 64:(m + 1) * 64, kc * 128:(kc + 1) * 128], q_ap[m * 64:(m + 1) * 64, :],
                                   True, True, rd=rd_q + rd_k, wr=[PB[sb]])
                                if nq != 512:
                                    OP("act", "activation", pT[sb][:, 0:nq], PS[sb][:, 0:nq], AF.Exp, scale=scale, rd=[PB[sb]], wr=[b_pT[sb]])
                            if nq == 512:
                                OP("act", "activation", pTP[par], PSP[par], AF.Exp, scale=scale,
                                   rd=[PB[par * 2], PB[par * 2 + 1]], wr=[b_pT[par * 2], b_pT[par * 2 + 1]])

                        def pv(kc):
                            par = kc % 2
                            for m in range(2):
                                sb = par * 2 + m
                                MM(PS[4 + m][:, 0:nq], v_fn(kc), pT[sb][:, 0:nq], kc == 0, kc == nkc - 1, rd=rd_v + [b_pT[sb]], wr=[PB[4 + m]])
                                MM(PS[6 + m][:, 0:nq], ones_b, pT[sb][:, 0:nq], kc == 0, kc == nkc - 1, rd=[b_pT[sb], b_const], wr=[PB[6 + m]])

                        qk(0)
                        if state.get("fin") is not None:
                            state["fin"]()
                            state["fin"] = None
                        for kc in range(nkc):
                            if kc + 1 < nkc:
                                qk(kc + 1)
                            pv(kc)
                        state["fin"] = lambda: finalize(nq, out_ap, wr_out)

                    def finalize(nq, out_ap, wr_out):
                        for m in range(2):
                            OP("dve", "reciprocal", fR[m][:, 0:nq], PS[6 + m][:, 0:nq], rd=[PB[6 + m]], wr=[b_fR[m]])
                            OP("dve", "tensor_tensor", fT[m][:, 0:nq], PS[4 + m][:, 0:nq], fR[m][:, 0:nq], ALU.mult, rd=[PB[4 + m], b_fR[m]], wr=[b_fT[m]])
                        OP("dve", "scalar_tensor_tensor", fo[:, 0:nq], fT[1][:, 0:nq], lamv[:, 4:5], fT[0][:, 0:nq], ALU.mult, ALU.add,
                           rd=[b_fT[0], b_fT[1], b_lam], wr=[b_fo])
                        OP("act", "activation", fsq[:, 0:nq], fo[:, 0:nq], AF.Square, rd=[b_fo], wr=[b_fsq])
                        MM(PS[6][:, 0:nq], ones_b, fsq[:, 0:nq], True, True, rd=[b_fsq, b_const], wr=[PB[6]])
                        rstd_from(PS[6][:, 0:nq], PB[6], frs[:, 0:nq], b_frs, 1.0 / 128, ftl[:, 0:nq], b_ftl)
                        OP("dve", "scalar_tensor_tensor", out_ap, fo[:, 0:nq], lamv[:, 5:6], frs[:, 0:nq], ALU.mult, ALU.mult,
                           rd=[b_fo, b_frs, b_lam], wr=wr_out)

                    for h in range(4):
                        i = h % 2
                        for r in range(2):
                            DMA("act", KT[i][:, r * 2048:(r + 1) * 2048], EXG[l][r * 1536 + h * 128: r * 1536 + (h + 1) * 128, :],
                                rd=[b_exg[l]], wr=[b_kt[i]])
                            vview = EXG[l][r * 1536 + 1024: r * 1536 + 1536, :].rearrange("r (t f) -> (r t) f", f=512)
                            vsrc = vview[:, h * 128:(h + 1) * 128].rearrange("(c p) e -> p c e", p=128)
                            DMA("act", VH[i][:, r * 16:(r + 1) * 16, :], vsrc, rd=[b_exg[l]], wr=[b_vh[i]])
                        for t_ in range(2):
                            OP("pe", "transpose", PS[0][:, t_ * 128:(t_ + 1) * 128], ckf[:, t_, h * 128:(h + 1) * 128], ident,
                               rd=[b_ckf, b_const], wr=[PB[0]])
                        OP("act", "copy", KT[i][:, 4096:4352], PS[0][:, 0:256], rd=[PB[0]], wr=[b_kt[i]])
                        OP("dve", "tensor_copy", VH[i][:, 32:34, :], cvf[:, :, h * 128:(h + 1) * 128], rd=[b_cvf], wr=[b_vh[i]])
                        for qb in range(4):
                            attn_block(q_all[:, h, qb * 512:(qb + 1) * 512], 512, KT[i], lambda kc, i=i: VH[i][:, kc, :], 34,
                                       [b_q[qb]], [b_kt[i]], [b_vh[i]], mix_attn[:, h, qb * 512:(qb + 1) * 512], [b_mixa[qb]])
                        for sq_ in (range(2) if k == 0 else ()):
                            attn_block(q_all[:, h, 2048 + sq_ * 256: 2048 + (sq_ + 1) * 256], 256, kTp[:, h, sq_ * 256:(sq_ + 1) * 256],
                                       lambda kc, sq_=sq_, h=h: Vp[:, sq_ * 2 + kc, h * 128:(h + 1) * 128], 2,
                                       [b_q[4]], [b_prm], [b_prm], mix_attn[:, h, 2048 + sq_ * 256: 2048 + (sq_ + 1) * 256], [b_mixa[4]])
                    if state.get("fin") is not None:
                        state["fin"]()
                        state["fin"] = None
                    S.barrier()
                    check_stop("C%d_%d" % (l, k), [("mixf", mix_four), ("xfp", xf_p), ("dftp", dftp_b), ("cs64", cs64_b)])

                def phaseD(k, own=False):
                    AR.reset(P_Q)
                    wo = AR.alloc([8, 1024], BF16)
                    b_wo = Buf("wo")
                    xblk = AR.alloc([8, 512], F32)
                    b_x = Buf("xblk")
                    osb = AR.alloc([8, 512], F32)
                    b_o = Buf("osb")
                    sq = AR.alloc([8, 512], BF16)
                    b_sq = Buf("sq")
                    u = sq
                    b_u = b_sq
                    hid = AR.alloc([22, 512], BF16)
                    b_hid = Buf("hid")
                    gu = [AR.alloc([8, 512], BF16) for _ in range(2)]
                    b_gu = [Buf("gu0"), Buf("gu1")]
                    dn = [AR.alloc([22, 128], BF16) for _ in range(2)]
                    b_dn = [Buf("dn0"), Buf("dn1")]
                    rstd = AR.alloc([512], F32)
                    tl = AR.alloc([512], F32)
                    b_rstd, b_tl = Buf("rstd"), Buf("tl")
                    tmpf = [AR.alloc([512], F32) for _ in range(2)]
                    b_tmpf = [Buf("tmpf0"), Buf("tmpf1")]
                    sg = [AR.alloc([512], F32) for _ in range(2)]
                    b_sg = [Buf("sg0"), Buf("sg1")]
                    ytm = [osb[:, 0:2, :].rearrange("p a b -> p (a b)"), osb[:, 2:4, :].rearrange("p a b -> p (a b)")]
                    b_ytm = [b_o, b_o]
                    for h_ in range(2):
                        DMA("sp", wo[:, h_ * 4:(h_ + 1) * 4, :],
                            WGB[l][h_ * 512:(h_ + 1) * 512, OFF_WOUT:OFF_WOUT + 1024].rearrange("(k p) n -> p k n", p=128),
                            rd=[b_wgb[l]], wr=[b_wo])
                    cnt = {"o": 0, "tmpf": 0, "gu": 0, "dn": 0, "sg": 0, "y": 0}

                    def mixk(kc, tok):
                        if kc < 4:
                            return mix_attn[:, kc, tok]
                        if kc < 6:
                            return mix_lru[:, kc - 4, tok]
                        return mix_four[:, kc - 6, tok]

                    def post_norm_update(gidx, c_):
                        OP("act", "activation", sq[:, 0:4, :], osb[:, 0:4, :], AF.Square, rd=[b_o], wr=[b_sq])
                        OP("pool", "tensor_tensor", sq[:, 4:8, :], osb[:, 4:8, :], osb[:, 4:8, :], ALU.mult, rd=[b_o], wr=[b_sq])
                        for c in range(8):
                            MM(PS[0], ones_b, sq[:, c, :], c == 0, c == 7, rd=[b_sq, b_const], wr=[PB[0]])
                        rstd_from(PS[0], PB[0], rstd, b_rstd, 1.0 / D, tl, b_tl)
                        for c in range(8):
                            i = cnt["tmpf"] % 2
                            cnt["tmpf"] += 1
                            OP("dve", "tensor_tensor", tmpf[i], osb[:, c, :], rstd, ALU.mult, rd=[b_o, b_rstd], wr=[b_tmpf[i]])
                            OP("dve", "scalar_tensor_tensor", xblk[:, c, :], tmpf[i], c_[:, gidx, c:c + 1], xblk[:, c, :], ALU.mult, ALU.add,
                               rd=[b_tmpf[i], b_cst, b_x], wr=[b_x])

                    for tb in range(5 if k == 0 else 4):
                        w = 0 if tb < 4 else 1
                        c_ = cst[l][w]
                        tok = slice(tb * 512, (tb + 1) * 512)
                        DMA("sp", xblk, XS[l][k][:, :, tok], rd=[b_xs[l][k][tb]], wr=[b_x])
                        if own and tb < 4:
                            OP("act", "activation", xblk, xblk, AF.Identity, scale=m0, rd=[b_x, b_const], wr=[b_x])
                            for c in range(8):
                                i = cnt["tmpf"] % 2
                                cnt["tmpf"] += 1
                                DMA("sp", tmpf[i], XS[l][1][:, c, tok], rd=[b_xs[l][1][tb]], wr=[b_tmpf[i]])
                                OP("dve", "scalar_tensor_tensor", xblk[:, c, :], tmpf[i], m1, xblk[:, c, :], ALU.mult, ALU.add,
                                   rd=[b_tmpf[i], b_const, b_x], wr=[b_x])
                        for c in range(8):
                            pb = 1 + cnt["o"] % 2
                            cnt["o"] += 1
                            for kc in range(8):
                                MM(PS[pb], wo[:, kc, c * 128:(c + 1) * 128], mixk(kc, tok), kc == 0, kc == 7,
                                   rd=[b_wo, b_mixa[tb], b_mixl, b_mixf], wr=[PB[pb]])
                            OP("act", "copy", osb[:, c, :], PS[pb], rd=[PB[pb]], wr=[b_o])
                        post_norm_update(2, c_)
                        OP("act", "activation", sq[:, 0:4, :], xblk[:, 0:4, :], AF.Square, rd=[b_x], wr=[b_sq])
                        OP("pool", "tensor_tensor", sq[:, 4:8, :], xblk[:, 4:8, :], xblk[:, 4:8, :], ALU.mult, rd=[b_x], wr=[b_sq])
                        for c in range(8):
                            MM(PS[0], ones_b, sq[:, c, :], c == 0, c == 7, rd=[b_sq, b_const], wr=[PB[0]])
                        rstd_from(PS[0], PB[0], rstd, b_rstd, 1.0 / D, tl, b_tl)
                        for c in range(8):
                            i = cnt["tmpf"] % 2
                            cnt["tmpf"] += 1
                            OP("dve", "scalar_tensor_tensor", tmpf[i], xblk[:, c, :], c_[:, 3, c:c + 1], rstd, ALU.mult, ALU.mult,
                               rd=[b_x, b_rstd, b_cst], wr=[b_tmpf[i]])
                            OP("act", "activation", u[:, c, :], tmpf[i], AF.Identity, bias=c_[:, 4, c:c + 1], scale=1.0,
                               rd=[b_tmpf[i], b_cst], wr=[b_u])
                        for js in range(11):
                            i = cnt["gu"] % 2
                            cnt["gu"] += 1
                            DMA("sp", gu[i], WGU[l][js], rd=[b_wgb[l]], wr=[b_gu[i]])
                            for jj in range(2):
                                j = js * 2 + jj
                                pg, pu = 3 + (j % 2), 5 + (j % 2)
                                for kc in range(8):
                                    MM(PS[pg], gu[i][:, kc, jj * 128:(jj + 1) * 128], u[:, kc, :], kc == 0, kc == 7, rd=[b_gu[i], b_u], wr=[PB[pg]])
                                for kc in range(8):
                                    MM(PS[pu], gu[i][:, kc, 256 + jj * 128: 256 + (jj + 1) * 128], u[:, kc, :], kc == 0, kc == 7,
                                       rd=[b_gu[i], b_u], wr=[PB[pu]])
                                si = cnt["sg"] % 2
                                cnt["sg"] += 1
                                OP("act", "activation", sg[si], PS[pg], AF.Silu, rd=[PB[pg]], wr=[b_sg[si]])
                                OP("dve", "tensor_tensor", hid.rearrange("p a b -> p (a b)")[:, j * 512:(j + 1) * 512], PS[pu], sg[si], ALU.mult, rd=[PB[pu], b_sg[si]], wr=[b_hid])
                        for c in range(8):
                            i = cnt["dn"] % 2
                            cnt["dn"] += 1
                            DMA("sp", dn[i], WGB[l][c * 128:(c + 1) * 128, OFF_DN:OFF_DN + 2816].rearrange("p (k j) -> p k j", j=128),
                                rd=[b_wgb[l]], wr=[b_dn[i]])
                            pb = 1 + cnt["o"] % 2
                            cnt["o"] += 1
                            for kc in range(22):
                                MM(PS[pb], dn[i][:, kc, :], hid[:, kc, :], kc == 0, kc == 21, rd=[b_dn[i], b_hid], wr=[PB[pb]])
                            OP("act", "copy", osb[:, c, :], PS[pb], rd=[PB[pb]], wr=[b_o])
                        post_norm_update(5, c_)
                        if l == 0:
                            DMA("pool", XS[1][k][:, :, tok], xblk, rd=[b_x], wr=[b_xs[1][k][tb]])
                        else:
                            dst = ys_d if tb < 4 else yp_d
                            r0 = tb * 512 if tb < 4 else 0
                            for tt in range(4):
                                yi = cnt["y"] % 2
                                cnt["y"] += 1
                                for g in range(2):
                                    pb = 7 if g == 0 else 0
                                    for cc in range(4):
                                        c = g * 4 + cc
                                        OP("pe", "transpose", PS[pb][:, cc * 128:(cc + 1) * 128], xblk[:, c, tt * 128:(tt + 1) * 128], ident,
                                           rd=[b_x, b_const], wr=[PB[pb]])
                                    OP("act" if g == 0 else "dve", "copy" if g == 0 else "tensor_copy",
                                       ytm[yi][:, g * 512:(g + 1) * 512], PS[pb], rd=[PB[pb]], wr=[b_ytm[yi]])
                                DMA("pool", dst[r0 + tt * 128: r0 + (tt + 1) * 128, :], ytm[yi], rd=[b_ytm[yi]])
                    S.barrier()
                    check_stop("D%d_%d" % (l, k), [("xs1", XS[1][k])])
                phaseA(0)
                phaseA(1)
                if l == 0:
                    for k in (0, 1):
                        phaseB1(k)
                        phaseB2(k)
                        phaseC(k)
                        phaseD(k)
                else:
                    phaseB1(0, True)
                    phaseB2(0, True)
                    phaseC(0, True)
                    phaseD(0, True)
        except _Stop:
            pass

        S.final_wait("sp")
        with nc.Block() as block:
            S.emit(block)
    nc._taps = TAPS
    return nc


def _fm(v):
    v = np.asarray(v, np.float32)
    return np.ascontiguousarray(v.reshape(-1, 128).T)


_CONST_CACHE = {}


def _constants():
    if _CONST_CACHE:
        return _CONST_CACHE
    t = np.arange(4096, dtype=np.float64)[:, None]
    j = np.arange(2048, dtype=np.float64)[None, :]
    ang = 2.0 * np.pi * ((t * j) % 4096) / 4096.0
    s = 1.0 / math.sqrt(4096 * 64)
    dft = np.stack([np.cos(ang) * s, -np.sin(ang) * s]).astype(np.float32).astype(ml_dtypes.bfloat16)
    t = np.arange(256, dtype=np.float64)[:, None]
    j = np.arange(256, dtype=np.float64)[None, :]
    ang = 2.0 * np.pi * ((t * j) % 256) / 256.0
    s = 1.0 / math.sqrt(256 * 64)
    dftp = np.stack([np.cos(ang) * s, -np.sin(ang) * s]).astype(np.float32).astype(ml_dtypes.bfloat16)
    c = np.arange(64, dtype=np.float64)
    a64 = 2.0 * np.pi * np.outer(c, c) / 64.0
    cs64 = np.zeros((128, 256), np.float32)
    for g in range(2):
        cs64[g * 64:(g + 1) * 64, g * 64:(g + 1) * 64] = np.cos(a64)
        cs64[g * 64:(g + 1) * 64, 128 + g * 64:128 + (g + 1) * 64] = np.sin(a64)
    rmat = np.zeros((128, 128), np.float32)
    for dd in range(128):
        partner = dd + 16 if (dd % 32) < 16 else dd - 16
        rmat[partner, dd] = 1.0
    _CONST_CACHE.update(dft=dft, dftp=dftp, cs64=cs64, rmat=rmat)
    return _CONST_CACHE


def _rope_table(hf):
    n = 16
    inv = (10000.0 ** (-np.arange(n, dtype=np.float32) / n)).astype(np.float32)
    tpos = np.arange(hf * 2048, (hf + 1) * 2048)
    row = (tpos // 64).astype(np.float32)
    col = (tpos % 64).astype(np.float32)
    tab = np.zeros((128, 2, 2048), np.float32)
    for p in range(128):
        dd = p % 64
        pos = row if dd < 32 else col
        i = dd % 16
        ang = (pos * inv[i]).astype(np.float32)
        sign = -1.0 if (dd % 32) < 16 else 1.0
        tab[p, 0] = np.cos(ang)
        tab[p, 1] = sign * np.sin(ang)
    return tab


_NC = None
_DEBUG = {}


def kernel(x_prompt, x_sample, cache_k, cache_v, state_lru, c, c_ctx,
           w_mod, b_mod, g_pre_mix, g_post_mix, g_pre_ffn, g_post_ffn,
           w_in, w_out, w_lambda, g_subln, conv_w, conv_b,
           lru_wa, lru_ba, lru_wx, lru_bx, lru_lambda, w_gate_up, w_down):
    global _NC
    f = lambda a: np.asarray(a, np.float32)
    x_prompt, x_sample, cache_k, cache_v, state_lru, c, c_ctx = map(f, (x_prompt, x_sample, cache_k, cache_v, state_lru, c, c_ctx))
    w_mod, b_mod, w_in, w_out, w_gate_up, w_down = map(f, (w_mod, b_mod, w_in, w_out, w_gate_up, w_down))
    lru_wa, lru_wx = f(lru_wa), f(lru_wx)
    K = _constants()
    if _NC is None:
        _NC = build_program()
    nc = _NC
    lw = np.zeros((128, 16, 128), np.float32)
    for l in range(2):
        for d in range(2):
            for g, W in enumerate((lru_wa, lru_wx)):
                for c2 in range(2):
                    idx = ((l * 2 + d) * 2 + g) * 2 + c2
                    for bb in range(2):
                        lw[bb * 64:(bb + 1) * 64, idx, bb * 64:(bb + 1) * 64] = W[l, d, c2 * 2 + bb]
    lw = lw.reshape(128, 2048)
    wsl = np.empty((2, 8, 128, WA_W + WB_W), np.float32)
    for l in range(2):
        for r in range(8):
            rows = slice(r * 128, (r + 1) * 128)
            wsl[l, r, :, 0:2304] = w_in[l, rows, :]
            wsl[l, r, :, 2304:8448] = w_mod[l, rows, :]
            wsl[l, r, :, 8448:9472] = w_out[l, rows, :]
            wsl[l, r, :, 9472:15104] = w_gate_up[l, rows, :]
            wd = w_down[l][:, r * 128:(r + 1) * 128]
            wsl[l, r, :, 15104:17920] = wd.reshape(22, 128, 128).transpose(1, 0, 2).reshape(128, 2816)
    in_maps = []
    cores = _DEBUG.get("cores", list(range(8)))
    for core in cores:
        p, hf = core // 2, core % 2
        sv = np.zeros((128, NSV), np.float32)
        for l in range(2):
            def put(name, arr):
                o, w = SVL[name]
                sv[:, l * SV_PER_L + o: l * SV_PER_L + o + w] = arr
            put("gpm", _fm(g_pre_mix[l]))
            put("gqm", _fm(g_post_mix[l]))
            put("gpf", _fm(g_pre_ffn[l]))
            put("gqf", _fm(g_post_ffn[l]))
            put("bmod", _fm(b_mod[l]))
            put("gsub", f(g_subln[l]).reshape(128, 1))
            cw = f(conv_w[l])
            put("convw", np.stack([_fm(cw[t_]) for t_ in range(4)], axis=1).reshape(128, 8))
            put("convb", _fm(conv_b[l]))
            put("ba", np.concatenate([_fm(f(lru_ba)[l, d]) for d in range(2)], axis=1))
            put("bx", np.concatenate([_fm(f(lru_bx)[l, d]) for d in range(2)], axis=1))
            put("lam", np.concatenate([_fm(f(lru_lambda)[l, d]) for d in range(2)], axis=1))
            put("wl", np.broadcast_to(f(w_lambda[l]).reshape(1, 256), (128, 256)))
            put("h0", np.concatenate([_fm(state_lru[p, l, d]) for d in range(2)], axis=1))
        par = (np.arange(128) % 2 == 1)
        sv[:, SV_MISC + 0] = 1.0 - hf
        sv[:, SV_MISC + 1] = float(hf)
        for k in range(2):
            o = k
            sv[:, SV_MISC + 2 + 2 * k] = 1.0 - o
            sv[:, SV_MISC + 3 + 2 * k] = float(o)
            sv[:, SV_MISC + 6 + k] = np.where(par & (o == 1), -1.0, 1.0)
        sv[:, SV_MISC + 2] = np.where(par & (hf == 1), -1.0, 1.0)
        cond = np.stack([_fm(c[p]), _fm(c_ctx)], axis=2)
        sv[:, SV_MISC + 8: SV_MISC + 24] = cond.reshape(128, 16)
        csign = np.ones((128, 512), np.float32)
        if hf == 1:
            csign[:, 1::2] = -1.0
        xs2 = np.stack([x_sample[p, 0:2048, :], x_sample[p, 2048:4096, :]])
        in_maps.append({
            "xs": np.ascontiguousarray(xs2),
            "xp": np.ascontiguousarray(x_prompt[2 * core:2 * core + 2].reshape(512, 1024)),
            "ck": np.ascontiguousarray(cache_k[p].reshape(2, 256, 512)),
            "cv": np.ascontiguousarray(cache_v[p].reshape(2, 256, 512)),
            "sv": sv, "lw": lw, "wsl": wsl,
            "rope": np.stack([_rope_table(0), _rope_table(1)]), "csign": np.ones((128, 512), np.float32),
            "rmat": K["rmat"], "cs64": K["cs64"],
            "dft": K["dft"], "dftp": K["dftp"],
        })
    if _DEBUG.get("stop") is not None:
        nc = build_program(stop=_DEBUG["stop"])
        res = run_bass_kernel_spmd(nc, in_maps, core_ids=list(range(len(cores))), trace=bool(_DEBUG.get("trace")))
        _DEBUG["exec_ns"] = getattr(res, "exec_time_ns", None)
        _DEBUG["results"] = res.results
        _DEBUG["taps"] = nc._taps
    else:
        res = run_bass_kernel_spmd(nc, in_maps, core_ids=list(range(8)))
    R = res.results
    y_prompt = np.empty((16, 256, 1024), np.float32)
    y_sample = np.empty((4, 4096, 1024), np.float32)
    new_k = np.empty((16, 2, 256, 4, 2, 64), np.float32)
    new_v = np.empty((16, 2, 256, 4, 128), np.float32)
    new_h = np.empty((16, 2, 2, 256), np.float32)
    for core in range(8):
        p, hf = core // 2, core % 2
        r = R[core]
        y_sample[p, hf * 2048:(hf + 1) * 2048] = np.asarray(r["ys"])
        y_prompt[2 * core:2 * core + 2] = np.asarray(r["yp"]).reshape(2, 256, 1024)
        new_k[2 * core:2 * core + 2] = np.asarray(r["nk"]).reshape(2, 2, 256, 4, 2, 64)
        new_v[2 * core:2 * core + 2] = np.asarray(r["nv"]).reshape(2, 2, 256, 4, 128)
        new_h[2 * core:2 * core + 2] = np.asarray(r["nh"]).reshape(2, 2, 2, 256)
    return (y_prompt, y_sample, new_k, new_v, new_h)
```

```python
import math
from contextlib import ExitStack
import numpy as np
import ml_dtypes
import concourse.bass as bass
import concourse.mybir as mybir
from concourse.bass_utils import run_bass_kernel_spmd

F32 = mybir.dt.float32
BF16 = mybir.dt.bfloat16
ALU = mybir.AluOpType
AF = mybir.ActivationFunctionType

D = 1024
TS = 2048
TP = 512
T = 2560
EPS = 1e-6
WA_W = 8448
WB_W = 9472
OFF_WOUT, OFF_GU, OFF_DN = 0, 1024, 6656

SVL = {}
_o = 0
for _n, _w in (("gpm", 8), ("gqm", 8), ("gpf", 8), ("gqf", 8), ("bmod", 48), ("gsub", 1), ("convw", 8),
               ("convb", 2), ("ba", 4), ("bx", 4), ("lam", 4), ("wl", 256), ("h0", 4)):
    SVL[_n] = (_o, _w)
    _o += _w
SV_PER_L = _o
SV_MISC = 2 * SV_PER_L
NSV = SV_MISC + 8 + 16


def lambda_init(l):
    return 0.8 - 0.6 * math.exp(-0.3 * l)


class Buf:
    __slots__ = ("name", "w", "r")

    def __init__(self, name):
        self.name = name
        self.w = None
        self.r = []


class Sched:
    ENGS = ("pe", "act", "dve", "pool", "sp")
    NDMA = 16

    def __init__(self, nc, es):
        self.nc = nc
        self.q = {e: [] for e in self.ENGS}
        self.sem = {e: es.enter_context(nc.semaphore("c_" + e)) for e in self.ENGS}
        self.cnt = {e: 0 for e in self.ENGS}
        self.known = {e: {} for e in self.ENGS}
        self.dsem, self.dcnt, self.dnext, self.dlast = {}, {}, {}, {}
        for e in ("sp", "pool", "act"):
            self.dsem[e] = [es.enter_context(nc.semaphore("d_%s%d" % (e, i))) for i in range(self.NDMA)]
            self.dcnt[e] = [0] * self.NDMA
            self.dlast[e] = [None] * self.NDMA
            self.dnext[e] = 0

    def _deps(self, reads, writes):
        deps = []
        for b in reads:
            if b.w is not None:
                deps.append(b.w)
        for b in writes:
            if b.w is not None:
                deps.append(b.w)
            deps.extend(b.r)
        return deps

    def _emit_waits(self, eng, deps, skip_own=False):
        kn = self.known[eng]
        best = {}
        for (s, v, key) in deps:
            if skip_own and key == ("c", eng):
                continue
            if kn.get(key, 0) >= v:
                continue
            if best.get(key, (None, 0))[1] < v:
                best[key] = (s, v)
        for key, (s, v) in best.items():
            kn[key] = v
            self.q[eng].append(lambda e, s=s, v=v: e.wait_ge(s, v))

    def _commit(self, ev, reads, writes):
        for b in reads:
            b.r.append(ev)
            if len(b.r) > 64:
                last = {}
                for x in b.r:
                    if last.get(x[2], (None, 0, None))[1] < x[1]:
                        last[x[2]] = x
                b.r = list(last.values())
        for b in writes:
            b.w = ev
            b.r = []

    def op(self, eng, fn, reads=(), writes=()):
        writes = list(writes) + [b for b in reads if b.name.startswith("ps") and b not in writes]
        deps = self._deps(reads, writes)
        self._emit_waits(eng, deps, skip_own=(eng == "pe"))
        self.cnt[eng] += 1
        s = self.sem[eng]
        ev = (s, self.cnt[eng], ("c", eng))
        self.q[eng].append(lambda e, fn=fn, s=s: fn(e).then_inc(s, 1))
        self._commit(ev, reads, writes)
        return ev

    def dma(self, eng, fn, reads=(), writes=()):
        i = self.dnext[eng]
        self.dnext[eng] = (i + 1) % self.NDMA
        deps = self._deps(reads, writes)
        if self.dlast[eng][i] is not None:
            deps.append(self.dlast[eng][i])
        self._emit_waits(eng, deps)
        self.dcnt[eng][i] += 16
        s = self.dsem[eng][i]
        ev = (s, self.dcnt[eng][i], ("d", eng, i))
        self.dlast[eng][i] = ev
        self.q[eng].append(lambda e, fn=fn, s=s: fn(e).then_inc(s, 16))
        self._commit(ev, reads, writes)
        return ev

    def _all_events(self):
        evs = []
        for e in self.ENGS:
            if self.cnt[e] > 0:
                evs.append((self.sem[e], self.cnt[e], ("c", e)))
        for e in self.dsem:
            for i in range(self.NDMA):
                if self.dlast[e][i] is not None:
                    evs.append(self.dlast[e][i])
        ex = getattr(self, "extra_events", [])
        if ex:
            evs.append(ex[-1])
        return evs

    def barrier(self):
        evs = self._all_events()
        for e in self.ENGS:
            self._emit_waits(e, evs)

    def final_wait(self, eng="sp"):
        self._emit_waits(eng, self._all_events())

    def emit(self, block):
        q = self.q

        @block.tensor
        def _(e):
            for t in q["pe"]:
                t(e)

        @block.scalar
        def _(e):
            for t in q["act"]:
                t(e)

        @block.vector
        def _(e):
            for t in q["dve"]:
                t(e)

        @block.gpsimd
        def _(e):
            for t in q["pool"]:
                t(e)

        @block.sync
        def _(e):
            for t in q["sp"]:
                t(e)


def build_program(stop=None):
    nc = bass.Bass("TRN2", target_bir_lowering=False)
    TAPS = []

    def din(name, shape, dt=F32):
        return nc.dram_tensor(name, shape, dt, kind="ExternalInput").ap()

    def dout(name, shape, dt=F32):
        return nc.dram_tensor(name, shape, dt, kind="ExternalOutput").ap()

    def dint(name, shape, dt):
        return nc.dram_tensor(name, shape, dt).ap()

    xs_d = din("xs", [2, TS, D])
    xp_d = din("xp", [TP, D])
    ck_d = din("ck", [2, 256, 512])
    cv_d = din("cv", [2, 256, 512])
    sv_d = din("sv", [128, NSV])
    lw_d = din("lw", [128, 2048])
    wsl_d = din("wsl", [2, 8, 128, WA_W + WB_W])
    rope_d = din("rope", [2, 128, 2, TS])
    csign_d = din("csign", [128, 512])
    rmat_d = din("rmat", [128, 128])
    cs64_d = din("cs64", [128, 256])
    dft_d = din("dft", [2, 4096, 2048], BF16)
    dftp_d = din("dftp", [2, 256, 256], BF16)
    ys_d = dout("ys", [TS, D])
    yp_d = dout("yp", [TP, D])
    nk_d = dout("nk", [2, 2, 256, 512])
    nv_d = dout("nv", [2, 2, 256, 512])
    nh_d = dout("nh", [2, 2, 2, 256])

    WGA = [dint("WGA%d" % l, [1024, WA_W], BF16) for l in range(2)]
    WGB = [dint("WGB%d" % l, [1024, WB_W], BF16) for l in range(2)]
    WGU = [dint("WGU%d" % l, [11, 128, 8, 512], BF16) for l in range(2)]
    LS = dint("LS0", [128, 2, TS], BF16)
    EXG = [dint("EXG%d" % l, [3072, 2048], BF16) for l in range(2)]
    QS = [[dint("QS%d_%d" % (l, k), [128, 4, TS], BF16) for k in range(2)] for l in range(2)]
    GS = [[dint("GS%d_%d" % (l, k), [128, 2, TS], BF16) for k in range(2)] for l in range(2)]
    XS = [[dint("XS%d_%d" % (l, k), [128, 8, T], F32) for k in range(2)] for l in range(2)]

    es = ExitStack()
    with es:
        S = Sched(nc, es)

        def OP(eng, method, *args, rd=(), wr=(), **kw):
            return S.op(eng, lambda e: getattr(e, method)(*args, **kw), rd, wr)

        def DMA(eng, out, in_, rd=(), wr=()):
            return S.dma(eng, lambda e: e.dma_start(out=out, in_=in_), rd, wr)

        def MM(out, lhsT, rhs, start, stop, rd=(), wr=()):
            return S.op("pe", lambda e: e.matmul(out, lhsT, rhs, start=start, stop=stop), rd, wr)

        class _Stop(Exception):
            pass

        def tap(name, src):
            o = nc.dram_tensor("dbg_" + name, list(src.shape), src.dtype, kind="ExternalOutput").ap()
            S.dma("sp", lambda e: e.dma_start(out=o, in_=src), (), ())
            TAPS.append("dbg_" + name)

        def check_stop(tag, taps=()):
            if stop == tag:
                S.barrier()
                for (n_, a_) in taps:
                    tap(n_, a_)
                raise _Stop()

        cc_sem = es.enter_context(nc.semaphore("cc_sem"))
        cc_state = {"n": 0}

        def CC(groups, src, dst, rd, wr):
            deps = S._deps(rd, wr)
            S._emit_waits("pool", deps)
            cc_state["n"] += 1
            v = cc_state["n"]
            ev = (cc_sem, v, ("cc",))
            S.q["pool"].append(lambda e: e.collective_compute(
                "AllGather", ALU.bypass, replica_groups=groups, ins=[src], outs=[dst]).then_inc(cc_sem, 1))
            S._commit(ev, rd, wr)
            S.extra_events = getattr(S, "extra_events", [])
            S.extra_events.append(ev)

        NBIG = 188 * 1024 // 4
        BIG = es.enter_context(nc.sbuf_tensor("big", [128, NBIG], F32))
        PSP = [es.enter_context(nc.psum_tensor("psp%d" % i, [128, 1024], F32))[:, :] for i in range(2)]
        PS = [PSP[0][:, 0:512], PSP[0][:, 512:1024], PSP[1][:, 0:512], PSP[1][:, 512:1024]] + \
             [es.enter_context(nc.psum_tensor("ps%d" % i, [128, 512], F32))[:, :] for i in range(4, 8)]
        PB = [Buf("ps%d" % i) for i in range(8)]

        class Arena:
            def __init__(self, start, end):
                self.start, self.cur, self.end = start, start, end

            def alloc(self, free_shape, dt):
                n = 1
                for s_ in free_shape:
                    n *= s_
                nbytes = n * (4 if dt == F32 else 2)
                nbytes = (nbytes + 63) // 64 * 64
                off = self.cur
                self.cur += nbytes
                assert self.cur <= self.end, ("arena overflow", self.cur, self.end)
                ap = BIG[:, off // 4:(off + nbytes) // 4]
                if dt != F32:
                    ap = ap.bitcast(dt)
                ap = ap[:, 0:n]
                if len(free_shape) == 2:
                    ap = ap.rearrange("p (a b) -> p a b", a=free_shape[0], b=free_shape[1])
                elif len(free_shape) == 3:
                    ap = ap.rearrange("p (a b c) -> p a b c", a=free_shape[0], b=free_shape[1], c=free_shape[2])
                return ap

            def reset(self, to=None):
                self.cur = self.start if to is None else to

        TOTAL = NBIG * 4
        AR = Arena(0, TOTAL)
        sv = AR.alloc([NSV], F32)
        ident = AR.alloc([128], F32)
        ones_b = AR.alloc([128], BF16)
        rmat_b = AR.alloc([128], BF16)
        cs64_b = AR.alloc([256], BF16)
        lw_b = AR.alloc([16, 128], BF16)
        P_ROPE = AR.cur
        rope_c = AR.alloc([TS], F32)
        rope_s = AR.alloc([TS], F32)
        csign = AR.alloc([512], F32)
        cst = [[AR.alloc([6, 8], F32) for w in range(2)] for l in range(2)]
        scond = AR.alloc([8, 2], BF16)
        small = AR.alloc([64], F32)
        nhst = AR.alloc([16], F32)
        dftp_b = AR.alloc([2, 2, 256], BF16)
        P_MLRU = AR.cur
        mix_lru = AR.alloc([2, T], BF16)
        P_MFOUR = AR.cur
        mix_four = AR.alloc([2, T], BF16)
        P_ATTN = AR.cur
        mix_attn = AR.alloc([4, T], BF16)
        P_Q = AR.cur
        kTp = AR.alloc([4, TP], BF16)
        Vp = AR.alloc([4, 512], BF16)
        xr_p = AR.alloc([2, TP], BF16)
        xf_p = AR.alloc([2, TP], BF16)
        q_all = AR.alloc([4, T], BF16)
        gr_all = AR.alloc([2, T], BF16)
        P_SCR = AR.cur
        print("persistent bytes", P_MLRU, P_Q, P_SCR, TOTAL)
        b_const = Buf("const")
        b_cst = Buf("cst")
        b_q = [Buf("q%d" % i) for i in range(5)]
        b_gr = [Buf("gr%d" % i) for i in range(5)]
        b_prm = Buf("prm")
        b_mixa = [Buf("mixa%d" % i) for i in range(5)]
        b_mixl = Buf("mixl")
        b_mixf = Buf("mixf")
        b_exg = [Buf("exg0"), Buf("exg1")]
        b_wga = [Buf("wga0"), Buf("wga1")]
        b_wgb = [Buf("wgb0"), Buf("wgb1")]
        b_xs = [[[Buf("xs%d_%d_%d" % (l, k, i)) for i in range(5)] for k in range(2)] for l in range(2)]
        b_qs = [[Buf("qs%d_%d" % (l, k)) for k in range(2)] for l in range(2)]
        b_rope = Buf("rope")
        b_ls = Buf("ls")
        b_nh = Buf("nh")

        def svc(l, name, i=0, n=1):
            o, w = SVL[name]
            return sv[:, l * SV_PER_L + o + i: l * SV_PER_L + o + i + n]

        try:
            DMA("sp", sv, sv_d, wr=[b_const])
            DMA("sp", csign, csign_d, wr=[b_const])
            DMA("sp", dftp_b, dftp_d.rearrange("c (t p) j -> p c t j", p=128), wr=[b_const])
            OP("pool", "memset", ident, 0.0, wr=[b_const])
            OP("pool", "affine_select", ident, ident, [[-1, 128]], ALU.not_equal, 1.0, base=0, channel_multiplier=1,
               rd=[b_const], wr=[b_const])
            OP("pool", "memset", ones_b, 1.0, wr=[b_const])
            OP("pool", "memset", nhst, 0.0, wr=[b_nh])
            AR.reset(P_SCR)
            st_a = AR.alloc([2048], F32)
            st_b = AR.alloc([128], F32)
            st_c = AR.alloc([256], F32)
            b_st = Buf("st")
            DMA("sp", st_a, lw_d, wr=[b_st])
            DMA("sp", st_b, rmat_d, wr=[b_st])
            DMA("sp", st_c, cs64_d, wr=[b_st])
            OP("dve", "tensor_copy", lw_b.rearrange("p a b -> p (a b)"), st_a, rd=[b_st], wr=[b_const])
            OP("dve", "tensor_copy", rmat_b, st_b, rd=[b_st], wr=[b_const])
            OP("dve", "tensor_copy", cs64_b, st_c, rd=[b_st], wr=[b_const])
            OP("act", "activation", scond.rearrange("p a b -> p (a b)"), sv[:, SV_MISC + 8: SV_MISC + 24], AF.Silu,
               rd=[b_const], wr=[b_const])

            def convert_weights(l, cast_engs, pwmax=2368, reset_to=None, kinds=None):
                if reset_to is not None:
                    AR.reset(reset_to)
                stf = [AR.alloc([pwmax], F32) for _ in range(2)]
                stb = [AR.alloc([pwmax], BF16) for _ in range(2)]
                bf = [Buf("stf0"), Buf("stf1")]
                bb = [Buf("stb0"), Buf("stb1")]
                jobs = []
                def split(c0, wtot, kind, base, align=1):
                    step = (pwmax // align) * align
                    o = 0
                    while o < wtot:
                        w_ = min(step, wtot - o)
                        jobs.append((c0 + o, w_, kind, base + o))
                        o += w_
                split(0, WA_W, "A", 0)
                split(WA_W + OFF_WOUT, 1024, "B", OFF_WOUT)
                split(WA_W + OFF_GU, 2816, "G0", 0, 256)
                split(WA_W + OFF_GU + 2816, 2816, "G1", 0, 256)
                split(WA_W + OFF_DN, 2816, "B", OFF_DN)
                if kinds is not None:
                    jobs = [j_ for j_ in jobs if j_[2] in kinds]
                k_ = 0
                for r in range(8):
                    for (c0, w_, kind, base) in jobs:
                        i = k_ % 2
                        eng = cast_engs[k_ % len(cast_engs)]
                        k_ += 1
                        DMA("sp", stf[i][:, 0:w_], wsl_d[l, r, :, c0:c0 + w_], wr=[bf[i]])
                        OP(eng, "tensor_copy", stb[i][:, 0:w_], stf[i][:, 0:w_], rd=[bf[i]], wr=[bb[i]])
                        if kind == "A":
                            DMA("pool", WGA[l][r * 128:(r + 1) * 128, base:base + w_], stb[i][:, 0:w_], rd=[bb[i]], wr=[b_wga[l]])
                        elif kind == "B":
                            DMA("pool", WGB[l][r * 128:(r + 1) * 128, base:base + w_], stb[i][:, 0:w_], rd=[bb[i]], wr=[b_wgb[l]])
                        else:
                            part = int(kind[1])
                            ns_ = w_ // 256
                            js0 = base // 256
                            DMA("pool", WGU[l][js0:js0 + ns_, :, r, part * 256:(part + 1) * 256].rearrange("s p n -> p s n"),
                                stb[i][:, 0:w_].rearrange("p (s n) -> p s n", n=256), rd=[bb[i]], wr=[b_wgb[l]])

            convert_weights(0, ["dve", "pool"], reset_to=P_SCR + 12 * 1024, kinds=("A",))
            S.barrier()
            check_stop("setup", [("wga", WGA[0][:, 0:2304])])

            def rstd_from(ps_ap, pbuf, out_ap, obuf, scale, tmp_ap, tbuf):
                OP("act", "activation", tmp_ap, ps_ap, AF.Ln, bias=small[:, 0:1], scale=scale, rd=[pbuf, b_const], wr=[tbuf])
                OP("act", "activation", out_ap, tmp_ap, AF.Exp, scale=-0.5, rd=[tbuf], wr=[obuf])

            OP("pool", "memset", small[:, 0:1], EPS, wr=[b_const])
            OP("pool", "memset", small[:, 1:2], 1.0, wr=[b_const])

            m0 = sv[:, SV_MISC + 0:SV_MISC + 1]
            m1 = sv[:, SV_MISC + 1:SV_MISC + 2]

            def blend(dst, other, rd_dst, rd_other, wr_dst):
                OP("act", "activation", dst, dst, AF.Identity, scale=m0, rd=list(rd_dst) + [b_const], wr=wr_dst)
                OP("dve", "scalar_tensor_tensor", dst, other, m1, dst, ALU.mult, ALU.add,
                   rd=list(rd_other) + [b_const] + list(wr_dst), wr=wr_dst)

            for l in range(2):
                li = lambda_init(l)
                AR.reset(P_SCR)
                wm = [AR.alloc([8, 1536], BF16) for _ in range(2)]
                bwm = [Buf("wm0"), Buf("wm1")]
                modsb = AR.alloc([48, 2], F32)
                bmod_ = Buf("modsb")
                for sl in range(4):
                    i = sl % 2
                    DMA("sp", wm[i], WGA[l][:, 2304 + sl * 1536: 2304 + (sl + 1) * 1536].rearrange("(k p) n -> p k n", p=128),
                        rd=[b_wga[l]], wr=[bwm[i]])
                    for jj in range(12):
                        j = sl * 12 + jj
                        for kc in range(8):
                            MM(PS[0][:, 2 * j:2 * j + 2], wm[i][:, kc, jj * 128:(jj + 1) * 128], scond[:, kc, :],
                               kc == 0, kc == 7, rd=[bwm[i], b_const], wr=[PB[0]])
                check_stop("Ma%d" % l, [("wm0", wm[0]), ("wm1", wm[1])])
                psm = PS[0][:, 0:96].rearrange("p (a b) -> p a b", b=2)
                for w in range(2):
                    OP("dve", "tensor_tensor", modsb[:, :, w], psm[:, :, w], svc(l, "bmod", 0, 48), ALU.add,
                       rd=[PB[0], b_const], wr=[bmod_])
                for w in range(2):
                    c_ = cst[l][w]
                    OP("dve", "scalar_tensor_tensor", c_[:, 0, :], modsb[:, 8:16, w], 1.0, svc(l, "gpm", 0, 8), ALU.add, ALU.mult,
                       rd=[bmod_, b_const], wr=[b_cst])
                    OP("dve", "tensor_copy", c_[:, 1, :], modsb[:, 0:8, w], rd=[bmod_], wr=[b_cst])
                    OP("dve", "tensor_tensor", c_[:, 2, :], modsb[:, 16:24, w], svc(l, "gqm", 0, 8), ALU.mult,
                       rd=[bmod_, b_const], wr=[b_cst])
                    OP("dve", "scalar_tensor_tensor", c_[:, 3, :], modsb[:, 32:40, w], 1.0, svc(l, "gpf", 0, 8), ALU.add, ALU.mult,
                       rd=[bmod_, b_const], wr=[b_cst])
                    OP("dve", "tensor_copy", c_[:, 4, :], modsb[:, 24:32, w], rd=[bmod_], wr=[b_cst])
                    OP("dve", "tensor_tensor", c_[:, 5, :], modsb[:, 40:48, w], svc(l, "gqf", 0, 8), ALU.mult,
                       rd=[bmod_, b_const], wr=[b_cst])
                S.barrier()
                check_stop("M%d" % l, [("cst0", cst[l][0]), ("cst1", cst[l][1]), ("modsb", modsb)])

                def phaseA(k):
                    AR.reset(P_MLRU)
                    win = AR.alloc([8, 2304], BF16)
                    rc_t = AR.alloc([512], F32)
                    rs_t = AR.alloc([512], F32)
                    assert AR.cur <= P_Q
                    AR.reset(P_SCR)
                    b_win = Buf("win")
                    xblk = AR.alloc([8, 512], F32)
                    b_x = Buf("xblk")
                    xtm = [AR.alloc([1024], F32) for _ in range(4)]
                    b_xtm = [Buf("xtm%d" % i) for i in range(4)]
                    sq = AR.alloc([8, 512], BF16)
                    b_sq = Buf("sq")
                    u = AR.alloc([8, 512], BF16)
                    b_u = Buf("u")
                    rstd = AR.alloc([512], F32)
                    b_rstd = Buf("rstd")
                    tl = AR.alloc([512], F32)
                    b_tl = Buf("tl")
                    tmpf = [AR.alloc([512], F32) for _ in range(2)]
                    b_tmpf = [Buf("tmpf0"), Buf("tmpf1")]
                    qb_ = [AR.alloc([512], BF16) for _ in range(2)]
                    b_qb = [Buf("qb0"), Buf("qb1")]
                    t1_ = [AR.alloc([512], F32) for _ in range(2)]
                    t2_ = [AR.alloc([512], F32) for _ in range(2)]
                    b_t1 = [Buf("t1_0"), Buf("t1_1")]
                    b_t2 = [Buf("t2_0"), Buf("t2_1")]
                    stg = [AR.alloc([512], BF16) for _ in range(3)]
                    b_stg = [Buf("stg%d" % i) for i in range(3)]
                    stgf = [AR.alloc([512], F32) for _ in range(2)]
                    b_stgf = [Buf("stgf0"), Buf("stgf1")]
                    b_rt = Buf("rt")
                    for h_ in range(2):
                        DMA("sp", win[:, h_ * 4:(h_ + 1) * 4, :],
                            WGA[l][h_ * 512:(h_ + 1) * 512, 0:2304].rearrange("(k p) n -> p k n", p=128),
                            rd=[b_wga[l]], wr=[b_win])
                    cnt = {"pj": 0, "rot": 0, "tm": 0, "stg": 0, "stgf": 0, "tmpf": 0, "qb": 0}
                    DMA("sp", rope_c, rope_d[k, :, 0, :], wr=[b_rope])
                    DMA("sp", rope_s, rope_d[k, :, 1, :], wr=[b_rope])
                    b_exb_k = Buf("exbk")
                    exbk = EXG[l][k * 1536:(k + 1) * 1536, :]
                    for tb in range(5 if k == 0 else 4):
                        w = 0 if tb < 4 else 1
                        c_ = cst[l][w]
                        tok = slice(tb * 512, (tb + 1) * 512)
                        if l == 0:
                            src = xs_d[k] if tb < 4 else xp_d
                            r0 = tb * 512 if tb < 4 else 0
                            for tt in range(4):
                                DMA("sp", xtm[tt], src[r0 + tt * 128: r0 + (tt + 1) * 128, :], wr=[b_xtm[tt]])
                            for c in range(8):
                                pb = 1 + (c % 2)
                                for tt in range(4):
                                    OP("pe", "transpose", PS[pb][:, tt * 128:(tt + 1) * 128], xtm[tt][:, c * 128:(c + 1) * 128], ident,
                                       rd=[b_xtm[tt], b_const], wr=[PB[pb]])
                                OP("act" if c % 2 == 0 else "dve", "copy" if c % 2 == 0 else "tensor_copy", xblk[:, c, :], PS[pb],
                                   rd=[PB[pb]], wr=[b_x])
                            DMA("pool", XS[0][k][:, :, tok], xblk, rd=[b_x], wr=[b_xs[0][k][tb]])
                        else:
                            DMA("sp", xblk, XS[1][k][:, :, tok], rd=[b_xs[1][k][tb]], wr=[b_x])
                        if tb == 0 and k == 0:
                            check_stop("Ax%d" % l, [("xblk", xblk)])
                        OP("act", "activation", sq[:, 0:4, :], xblk[:, 0:4, :], AF.Square, rd=[b_x], wr=[b_sq])
                        OP("pool", "tensor_tensor", sq[:, 4:8, :], xblk[:, 4:8, :], xblk[:, 4:8, :], ALU.mult, rd=[b_x], wr=[b_sq])
                        for c in range(8):
                            MM(PS[0], ones_b, sq[:, c, :], c == 0, c == 7, rd=[b_sq, b_const], wr=[PB[0]])
                        rstd_from(PS[0], PB[0], rstd, b_rstd, 1.0 / D, tl, b_tl)
                        for c in range(8):
                            i = cnt["tmpf"] % 2
                            cnt["tmpf"] += 1
                            OP("dve", "scalar_tensor_tensor", tmpf[i], xblk[:, c, :], c_[:, 0, c:c + 1], rstd, ALU.mult, ALU.mult,
                               rd=[b_x, b_rstd, b_cst], wr=[b_tmpf[i]])
                            OP("act", "activation", u[:, c, :], tmpf[i], AF.Identity, bias=c_[:, 1, c:c + 1], scale=1.0,
                               rd=[b_tmpf[i], b_cst], wr=[b_u])

                        if tb == 0 and k == 0:
                            check_stop("An%d" % l, [("u", u), ("rstd", rstd)])

                        if tb < 4:
                            OP("act", "copy", rc_t, rope_c[:, tok], rd=[b_rope], wr=[b_rt])
                            OP("act", "copy", rs_t, rope_s[:, tok], rd=[b_rope], wr=[b_rt])

                        def proj_fm(j):
                            pb = 3 + (cnt["pj"] % 3)
                            cnt["pj"] += 1
                            for kc in range(8):
                                MM(PS[pb], win[:, kc, j * 128:(j + 1) * 128], u[:, kc, :], kc == 0, kc == 7,
                                   rd=[b_win, b_u], wr=[PB[pb]])
                            return pb

                        def proj_tm(tt, c0):
                            pb = 7
                            for kc in range(8):
                                MM(PS[pb], u[:, kc, tt * 128:(tt + 1) * 128], win[:, kc, c0:c0 + 512], kc == 0, kc == 7,
                                   rd=[b_win, b_u], wr=[PB[pb]])
                            return pb

                        def nstg():
                            i = cnt["stg"] % 3
                            cnt["stg"] += 1
                            return i

                        for j in range(18):
                            kind = ("q", "k", "v", "xr", "gr", "xf")[[0, 0, 0, 0, 1, 1, 1, 1, 2, 2, 2, 2, 3, 3, 4, 4, 5, 5][j]]
                            if kind == "v":
                                continue
                            pb = proj_fm(j)
                            dbg0 = (tb == 0 and k == 0 and l == 0 and j == 0)
                            if dbg0:
                                check_stop("As1", [("u", u)])
                            if kind in ("q", "k"):
                                h = j % 4
                                if tb < 4:
                                    i = cnt["qb"] % 2
                                    cnt["qb"] += 1
                                    OP("act", "copy", qb_[i], PS[pb], rd=[PB[pb]], wr=[b_qb[i]])
                                    if dbg0:
                                        check_stop("As2", [("qb", qb_[i])])
                                    MM(PS[6], rmat_b, qb_[i], True, True, rd=[b_qb[i], b_const], wr=[PB[6]])
                                    if dbg0:
                                        check_stop("As3", [("qb", qb_[i])])
                                    OP("dve", "tensor_tensor", t1_[i], PS[pb], rc_t, ALU.mult, rd=[PB[pb], b_rt], wr=[b_t1[i]])
                                    OP("dve", "tensor_tensor", t2_[i], PS[6], rs_t, ALU.mult, rd=[PB[6], b_rt], wr=[b_t2[i]])
                                    if dbg0:
                                        check_stop("As4", [("t1", t1_[i]), ("t2", t2_[i])])
                                    if kind == "q":
                                        OP("pool", "tensor_tensor", q_all[:, h, tok], t1_[i], t2_[i], ALU.add,
                                           rd=[b_t1[i], b_t2[i]], wr=[b_q[tb]])
                                    else:
                                        si = nstg()
                                        OP("pool", "tensor_tensor", stg[si], t1_[i], t2_[i], ALU.add,
                                           rd=[b_t1[i], b_t2[i]], wr=[b_stg[si]])
                                        DMA("pool", exbk[h * 128:(h + 1) * 128, tok], stg[si], rd=[b_stg[si]], wr=[b_exb_k])
                                else:
                                    if kind == "q":
                                        OP("act", "copy", q_all[:, h, tok], PS[pb], rd=[PB[pb]], wr=[b_q[tb]])
                                    else:
                                        OP("act", "copy", kTp[:, h, :], PS[pb], rd=[PB[pb]], wr=[b_prm])
                            elif kind == "gr":
                                OP("act", "activation", gr_all[:, j - 14, tok], PS[pb], AF.Gelu_apprx_tanh, rd=[PB[pb]], wr=[b_gr[tb]])
                            else:
                                c2 = j % 2
                                if tb < 4:
                                    si = nstg()
                                    OP("act", "copy", stg[si], PS[pb], rd=[PB[pb]], wr=[b_stg[si]])
                                    base = 512 if kind == "xr" else 768
                                    DMA("pool", exbk[base + c2 * 128: base + (c2 + 1) * 128, tok], stg[si], rd=[b_stg[si]], wr=[b_exb_k])
                                else:
                                    dst = xr_p if kind == "xr" else xf_p
                                    OP("act", "copy", dst[:, c2, :], PS[pb], rd=[PB[pb]], wr=[b_prm])
                            if tb == 0 and k == 0 and l == 0:
                                check_stop("Aj%d" % j, [("q", q_all[:, :, 0:512])])
                        for tt in range(4):
                            pb = proj_tm(tt, 1024)
                            if tb < 4:
                                si = nstg()
                                OP("dve", "tensor_copy", stg[si], PS[pb], rd=[PB[pb]], wr=[b_stg[si]])
                                vview = exbk[1024:1536, :].rearrange("r (t f) -> (r t) f", f=512)
                                vdst = vview[tb * 512 + tt * 128: tb * 512 + (tt + 1) * 128, :]
                                DMA("pool", vdst, stg[si], rd=[b_stg[si]], wr=[b_exb_k])
                            else:
                                seq, hh = tt // 2, tt % 2
                                fi = cnt["stgf"] % 2
                                cnt["stgf"] += 1
                                OP("dve", "tensor_copy", stgf[fi], PS[pb], rd=[PB[pb]], wr=[b_stgf[fi]])
                                OP("act", "copy", Vp[:, tt, :], PS[pb], rd=[PB[pb]], wr=[b_prm])
                                DMA("pool", nv_d[seq, l, hh * 128:(hh + 1) * 128, :], stgf[fi], rd=[b_stgf[fi]])
                                pb = proj_tm(tt, 512)
                                fi = cnt["stgf"] % 2
                                cnt["stgf"] += 1
                                OP("dve", "tensor_copy", stgf[fi], PS[pb], rd=[PB[pb]], wr=[b_stgf[fi]])
                                DMA("pool", nk_d[seq, l, hh * 128:(hh + 1) * 128, :], stgf[fi], rd=[b_stgf[fi]])
                        if tb == 0 and k == 0:
                            check_stop("Ap%d" % l, [("q", q_all), ("gr", gr_all)])
                    DMA("pool", QS[l][k], q_all[:, :, 0:TS], rd=b_q[0:4], wr=[b_qs[l][k]])
                    DMA("pool", GS[l][k], gr_all[:, :, 0:TS], rd=b_gr[0:4], wr=[b_qs[l][k]])
                    b_exg[l].w = None
                    S.barrier()
                    check_stop("A%d_%d" % (l, k), [("q", q_all), ("gr", gr_all), ("exg", EXG[l]), ("ktp", kTp), ("vp", Vp), ("xrp", xr_p), ("xs", XS[l][k])])

                def phaseB1(k, own=False):
                    if l == 0 and k == 1:
                        DMA("sp", mix_lru[:, :, 0:TS], LS, rd=[b_ls], wr=[b_mixl])
                        S.barrier()
                        return
                    AR.reset(P_SCR)
                    NT = 4608
                    xr_c = AR.alloc([NT], BF16)
                    xc = AR.alloc([NT], F32)
                    xcb = AR.alloc([NT], BF16)
                    HS = AR.alloc([NT], F32)
                    _sv = AR.cur
                    AR.reset(P_MFOUR)
                    HB = AR.alloc([2560], F32)
                    GR = AR.alloc([2560], F32)
                    assert AR.cur <= P_Q
                    AR.reset(_sv)
                    GI = AR.alloc([2560], F32)
                    GC = AR.alloc([2560], F32)
                    sp_ = AR.alloc([8], F32)
                    b_xr, b_xc, b_xcb, b_hs, b_hb = Buf("xr"), Buf("xc"), Buf("xcb"), Buf("hs"), Buf("hb")
                    b_g = Buf("gates")
                    b_sp = Buf("sp")
                    NTK = 4608 if k == 0 else 4096
                    seqs = [(0, 4096), (4096, 4352), (4352, 4608)] if k == 0 else [(0, 4096)]
                    _sv2 = AR.cur
                    AR.reset(P_MFOUR + 20480)
                    tmpA = AR.alloc([2048], BF16)
                    tmpB = AR.alloc([2048], BF16)
                    assert AR.cur <= P_Q
                    AR.reset(_sv2)
                    b_tA, b_tB = Buf("tA"), Buf("tB")
                    if own:
                        _sv3 = AR.cur
                        AR.reset(P_MFOUR + 20480)
                        tmpG = AR.alloc([2, TS], BF16)
                        assert AR.cur <= P_Q
                        AR.reset(_sv3)
                        b_tg = Buf("tmpG")
                        DMA("sp", gr_all[:, :, 0:TS], GS[l][0], rd=[b_qs[l][0]], wr=b_gr[0:4])
                        DMA("sp", tmpG, GS[l][1], rd=[b_qs[l][1]], wr=[b_tg])
                        blend(gr_all[:, :, 0:TS], tmpG, b_gr[0:4], [b_tg], b_gr[0:4])
                    else:
                        DMA("sp", gr_all[:, :, 0:TS], GS[l][k], rd=[b_qs[l][k]], wr=b_gr[0:4])
                        if l == 0:
                            _sv4 = AR.cur
                            AR.reset(P_MFOUR + 20480)
                            grO = AR.alloc([2, TS], BF16)
                            assert AR.cur <= P_Q
                            AR.reset(_sv4)
                            b_gro = Buf("grO")
                            DMA("sp", grO, GS[l][1], rd=[b_qs[l][1]], wr=[b_gro])
                    a0 = sv[:, SV_MISC + 0:SV_MISC + 1]
                    a1 = sv[:, SV_MISC + 1:SV_MISC + 2]
                    OP("act", "activation", sp_[:, 0:4], svc(l, "lam", 0, 4), AF.Exp, scale=-1.0, rd=[b_const], wr=[b_sp])
                    OP("act", "activation", sp_[:, 4:8], sp_[:, 0:4], AF.Ln, bias=small[:, 1:2], scale=1.0, rd=[b_sp, b_const], wr=[b_sp])
                    OP("dve", "tensor_scalar", sp_[:, 0:4], sp_[:, 4:8], -8.0, None, ALU.mult, rd=[b_sp], wr=[b_sp])
                    OP("dve", "tensor_scalar", sp_[:, 4:8], sp_[:, 4:8], -16.0, None, ALU.mult, rd=[b_sp], wr=[b_sp])
                    for c2 in range(2):
                        for r in range(2):
                            DMA("sp", xr_c[:, r * 2048:(r + 1) * 2048],
                                EXG[l][r * 1536 + 512 + c2 * 128: r * 1536 + 512 + (c2 + 1) * 128, :], wr=[b_xr])
                        if k == 0:
                            OP("pool", "tensor_copy", xr_c[:, 4096:4608], xr_p[:, c2, :], rd=[b_prm], wr=[b_xr])
                        cw = lambda t_: svc(l, "convw", t_ * 2 + c2)
                        OP("dve", "tensor_scalar", xc[:, 0:NTK], xr_c[:, 0:NTK], cw(2), svc(l, "convb", c2), ALU.mult, ALU.add,
                           rd=[b_xr, b_const], wr=[b_xc])
                        for (s0, s1) in seqs:
                            OP("dve", "scalar_tensor_tensor", xc[:, s0 + 2:s1], xr_c[:, s0:s1 - 2], cw(0), xc[:, s0 + 2:s1],
                               ALU.mult, ALU.add, rd=[b_xr, b_const, b_xc], wr=[b_xc])
                            OP("dve", "scalar_tensor_tensor", xc[:, s0 + 1:s1], xr_c[:, s0:s1 - 1], cw(1), xc[:, s0 + 1:s1],
                               ALU.mult, ALU.add, rd=[b_xr, b_const, b_xc], wr=[b_xc])
                            OP("dve", "scalar_tensor_tensor", xc[:, s0:s1 - 1], xr_c[:, s0 + 1:s1], cw(3), xc[:, s0:s1 - 1],
                               ALU.mult, ALU.add, rd=[b_xr, b_const, b_xc], wr=[b_xc])
                        OP("pool", "tensor_copy", xcb[:, 0:NTK], xc[:, 0:NTK], rd=[b_xc], wr=[b_xcb])
                        for d in range(2):
                            halves = [(0, 2048), (2048, NTK)]
                            if d == 1:
                                halves = halves[::-1]
                            col = d * 2 + c2
                            wa = lw_b[:, ((l * 2 + d) * 2 + 0) * 2 + c2, :]
                            wx = lw_b[:, ((l * 2 + d) * 2 + 1) * 2 + c2, :]
                            for (t0, t1) in halves:
                                n = t1 - t0
                                for s_ in range(n // 512):
                                    a0 = t0 + s_ * 512
                                    pa, pi = 1 + 2 * (s_ % 2), 2 + 2 * (s_ % 2)
                                    MM(PS[pa], wa, xcb[:, a0:a0 + 512], True, True, rd=[b_xcb, b_const], wr=[PB[pa]])
                                    MM(PS[pi], wx, xcb[:, a0:a0 + 512], True, True, rd=[b_xcb, b_const], wr=[PB[pi]])
                                    OP("act", "activation", GR[:, s_ * 512:(s_ + 1) * 512], PS[pa], AF.Sigmoid,
                                       bias=svc(l, "ba", col), scale=1.0, rd=[PB[pa], b_const], wr=[b_g])
                                    OP("act", "activation", GI[:, s_ * 512:(s_ + 1) * 512], PS[pi], AF.Sigmoid,
                                       bias=svc(l, "bx", col), scale=1.0, rd=[PB[pi], b_const], wr=[b_g])
                                OP("act", "activation", GC[:, 0:n], GR[:, 0:n], AF.Exp, scale=sp_[:, 4 + col:5 + col], rd=[b_g, b_sp], wr=[b_g])
                                OP("act", "activation", GR[:, 0:n], GR[:, 0:n], AF.Exp, scale=sp_[:, col:col + 1], rd=[b_g, b_sp], wr=[b_g])
                                OP("act", "activation", GC[:, 0:n], GC[:, 0:n], AF.Ln, bias=small[:, 1:2], scale=-1.0, rd=[b_g, b_const], wr=[b_g])
                                OP("act", "activation", GC[:, 0:n], GC[:, 0:n], AF.Exp, scale=0.5, rd=[b_g], wr=[b_g])
                                OP("dve", "tensor_tensor", GI[:, 0:n], GI[:, 0:n], xc[:, t0:t1], ALU.mult, rd=[b_g, b_xc], wr=[b_g])
                                OP("dve", "tensor_tensor", GI[:, 0:n], GI[:, 0:n], GC[:, 0:n], ALU.mult, rd=[b_g], wr=[b_g])
                                for (s0, s1) in seqs:
                                    lo, hi = max(s0, t0), min(s1, t1)
                                    if lo >= hi:
                                        continue
                                    if d == 0:
                                        if lo == s0:
                                            init = svc(l, "h0", col) if s0 == 0 else 0.0
                                        else:
                                            init = HS[:, lo - 1:lo]
                                        OP("dve", "tensor_tensor_scan", HS[:, lo:hi], GR[:, lo - t0:hi - t0], GI[:, lo - t0:hi - t0],
                                           init, ALU.mult, ALU.add, rd=[b_g, b_hs, b_const], wr=[b_hs])
                                    else:
                                        if hi == s1:
                                            init = svc(l, "h0", col) if s0 == 0 else 0.0
                                        else:
                                            init = small[:, 8:9]
                                        OP("dve", "tensor_tensor_scan", HB[:, lo - t0:hi - t0][:, ::-1], GR[:, lo - t0:hi - t0][:, ::-1],
                                           GI[:, lo - t0:hi - t0][:, ::-1], init, ALU.mult, ALU.add,
                                           rd=[b_g, b_hb, b_const], wr=[b_hb])
                                if d == 0 and k == 0 and t1 == 4608:
                                    for sq_ in range(2):
                                        e_ = 4096 + sq_ * 256 + 255
                                        OP("dve", "tensor_copy", nhst[:, (sq_ * 2 + 0) * 2 + c2:(sq_ * 2 + 0) * 2 + c2 + 1], HS[:, e_:e_ + 1],
                                           rd=[b_hs], wr=[b_nh])
                                if d == 1:
                                    if t1 == NTK:
                                        for sq_ in (range(2) if k == 0 else ()):
                                            e_ = 4096 + sq_ * 256 - t0
                                            OP("dve", "tensor_copy", nhst[:, (sq_ * 2 + 1) * 2 + c2:(sq_ * 2 + 1) * 2 + c2 + 1], HB[:, e_:e_ + 1],
                                               rd=[b_hb], wr=[b_nh])
                                        OP("dve", "tensor_copy", small[:, 8:9], HB[:, 0:1], rd=[b_hb], wr=[b_const])
                                    OP("dve", "tensor_tensor", HS[:, t0:t1], HS[:, t0:t1], HB[:, 0:n], ALU.add, rd=[b_hs, b_hb], wr=[b_hs])
                        if own:
                            OP("act", "activation", HB[:, 0:2048], HS[:, 0:2048], AF.Identity, scale=m0, rd=[b_hs, b_const, b_hb], wr=[b_hb])
                            OP("dve", "scalar_tensor_tensor", HB[:, 0:2048], HS[:, 2048:4096], m1, HB[:, 0:2048], ALU.mult, ALU.add,
                               rd=[b_hs, b_const, b_hb], wr=[b_hb])
                            OP("dve", "tensor_tensor", mix_lru[:, c2, 0:2048], HB[:, 0:2048], gr_all[:, c2, 0:2048], ALU.mult,
                               rd=[b_hb] + b_gr, wr=[b_mixl])
                        else:
                            OP("dve", "tensor_tensor", mix_lru[:, c2, 0:2048], HS[:, k * 2048:(k + 1) * 2048], gr_all[:, c2, 0:2048], ALU.mult,
                               rd=[b_hs] + b_gr, wr=[b_mixl])
                            if l == 0:
                                OP("dve", "tensor_tensor", xcb[:, 0:2048], HS[:, 2048:4096], grO[:, c2, :], ALU.mult,
                                   rd=[b_hs, b_gro, b_xcb], wr=[b_xcb])
                                DMA("pool", LS[:, c2, :], xcb[:, 0:2048], rd=[b_xcb], wr=[b_ls])
                        if k == 0:
                            OP("dve", "tensor_tensor", mix_lru[:, c2, 2048:2560], HS[:, 4096:4608], gr_all[:, c2, 2048:2560], ALU.mult,
                               rd=[b_hs] + b_gr, wr=[b_mixl])
                    for sq_ in (range(2) if k == 0 else ()):
                        for d in range(2):
                            for c2 in range(2):
                                k_ = (sq_ * 2 + d) * 2 + c2
                                dst = bass.AP(nh_d.tensor, ((sq_ * 2 + l) * 2 + d) * 256 + c2 * 128, [[1, 128], [1, 1]])
                                DMA("pool", dst, nhst[:, k_:k_ + 1], rd=[b_nh])
                    S.barrier()
                    check_stop("B1_%d_%d" % (l, k), [("mixl", mix_lru), ("nhst", nhst), ("hs", HS), ("xc", xc), ("xrc", xr_c), ("hb", HB), ("svh0", sv[:, 359:363])])

                def phaseB2(k, own=False):
                    AR.reset(P_ATTN)
                    xf_c = AR.alloc([2, 4096], BF16)
                    assert AR.cur <= P_Q
                    AR.reset(P_SCR)
                    Ytm = AR.alloc([32, 512], BF16)
                    Ytp = AR.alloc([4, 512], BF16)
                    tabs = [[AR.alloc([4, 512], BF16) for _ in range(3)] for cs in range(2)]
                    b_xf, b_y, b_yp = Buf("xf"), Buf("ytm"), Buf("ytp")
                    b_tab = [[Buf("tab%d_%d" % (cs, i)) for i in range(3)] for cs in range(2)]
                    for c2 in range(2):
                        for r in range(2):
                            DMA("sp", xf_c[:, c2, r * 2048:(r + 1) * 2048],
                                EXG[l][r * 1536 + 768 + c2 * 128: r * 1536 + 768 + (c2 + 1) * 128, :], rd=[b_exg[l]], wr=[b_xf])
                    for tc in range(32):
                        pb = 1 + tc % 2
                        for c2 in range(2):
                            MM(PS[pb][:, c2 * 256:(c2 + 1) * 256], xf_c[:, c2, tc * 128:(tc + 1) * 128], cs64_b, True, True,
                               rd=[b_xf, b_const], wr=[PB[pb]])
                        OP("act", "activation", Ytm[:, tc, :], PS[pb], AF.Identity, scale=(sv[:, SV_MISC + 2:SV_MISC + 3] if own else sv[:, SV_MISC + 6 + k:SV_MISC + 7 + k]),
                           rd=[PB[pb], b_const], wr=[b_y])
                    for tc in (range(4) if k == 0 else ()):
                        pb = 1 + tc % 2
                        for c2 in range(2):
                            MM(PS[pb][:, c2 * 256:(c2 + 1) * 256], xf_p[:, c2, tc * 128:(tc + 1) * 128], cs64_b, True, True,
                               rd=[b_prm, b_const], wr=[PB[pb]])
                        OP("act", "copy", Ytp[:, tc, :], PS[pb], rd=[PB[pb]], wr=[b_yp])
                    kk = 0
                    for jb in range(4):
                        for tg in range(8):
                            i = kk % 3
                            kk += 1
                            for cs in range(2):
                                DMA("sp", tabs[cs][i],
                                    dft_d[cs, tg * 512:(tg + 1) * 512, jb * 512:(jb + 1) * 512].rearrange("(t p) j -> p t j", p=128),
                                    wr=[b_tab[cs][i]])
                            for c2 in range(2):
                                pb = 3 + c2
                                for t_ in range(4):
                                    for cs in range(2):
                                        tc = tg * 4 + t_
                                        MM(PS[pb], Ytm[:, tc, c2 * 256 + cs * 128: c2 * 256 + (cs + 1) * 128], tabs[cs][i][:, t_, :],
                                           tg == 0 and t_ == 0 and cs == 0, tg == 7 and t_ == 3 and cs == 1,
                                           rd=[b_y, b_tab[cs][i]], wr=[PB[pb]])
                        for c2 in range(2):
                            OP("dve", "tensor_tensor", mix_four.rearrange("p a b -> p (a b)")[:, c2 * T + jb * 512: c2 * T + (jb + 1) * 512], PS[3 + c2], csign, ALU.mult,
                               rd=[PB[3 + c2], b_const], wr=[b_mixf])
                    for sq_ in (range(2) if k == 0 else ()):
                        for c2 in range(2):
                            pb = 5 + c2
                            for t_ in range(2):
                                for cs in range(2):
                                    MM(PS[pb][:, 0:256], Ytp[:, sq_ * 2 + t_, c2 * 256 + cs * 128: c2 * 256 + (cs + 1) * 128],
                                       dftp_b[:, cs, t_, :], t_ == 0 and cs == 0, t_ == 1 and cs == 1,
                                       rd=[b_yp, b_const], wr=[PB[pb]])
                            OP("act", "copy", mix_four[:, c2, 2048 + sq_ * 256: 2048 + (sq_ + 1) * 256], PS[pb][:, 0:256],
                               rd=[PB[pb]], wr=[b_mixf])
                    S.barrier()
                    check_stop("B2_%d_%d" % (l, k), [("mixf", mix_four)])

                def phaseC(k, own=False):
                    AR.reset(P_SCR)
                    KT = [AR.alloc([4352], BF16) for _ in range(2)]
                    VH = [AR.alloc([34, 128], BF16) for _ in range(2)]
                    b_kt = [Buf("kt0"), Buf("kt1")]
                    b_vh = [Buf("vh0"), Buf("vh1")]
                    ckf = AR.alloc([2, 512], F32)
                    cvf = AR.alloc([2, 512], F32)
                    b_ckf, b_cvf = Buf("ckf"), Buf("cvf")
                    pTP = [AR.alloc([1024], BF16) for _ in range(2)]
                    pT = [pTP[0][:, 0:512], pTP[0][:, 512:1024], pTP[1][:, 0:512], pTP[1][:, 512:1024]]
                    b_pT = [Buf("pT%d" % i) for i in range(4)]
                    fR = [AR.alloc([512], F32) for _ in range(2)]
                    fT = [AR.alloc([512], F32) for _ in range(2)]
                    fo = AR.alloc([512], F32)
                    fsq = AR.alloc([512], BF16)
                    frs = AR.alloc([512], F32)
                    ftl = AR.alloc([512], F32)
                    b_fR = [Buf("fR0"), Buf("fR1")]
                    b_fT = [Buf("fT0"), Buf("fT1")]
                    b_fo, b_fsq, b_frs, b_ftl = Buf("fo"), Buf("fsq"), Buf("frs"), Buf("ftl")
                    lamv = AR.alloc([8], F32)
                    lprod = AR.alloc([128], F32)
                    b_lam = Buf("lam")
                    OP("dve", "tensor_tensor", lprod[:, 0:64], svc(l, "wl", 0, 64), svc(l, "wl", 64, 64), ALU.mult, rd=[b_const], wr=[b_lam])
                    OP("dve", "tensor_tensor", lprod[:, 64:128], svc(l, "wl", 128, 64), svc(l, "wl", 192, 64), ALU.mult, rd=[b_const], wr=[b_lam])
                    OP("dve", "reduce_sum", lamv[:, 0:1], lprod[:, 0:64], mybir.AxisListType.X, rd=[b_lam], wr=[b_lam])
                    OP("dve", "reduce_sum", lamv[:, 1:2], lprod[:, 64:128], mybir.AxisListType.X, rd=[b_lam], wr=[b_lam])
                    OP("act", "activation", lamv[:, 2:4], lamv[:, 0:2], AF.Exp, rd=[b_lam], wr=[b_lam])
                    OP("dve", "tensor_tensor", lamv[:, 4:5], lamv[:, 3:4], lamv[:, 2:3], ALU.subtract, rd=[b_lam], wr=[b_lam])
                    OP("dve", "tensor_scalar", lamv[:, 4:5], lamv[:, 4:5], -li, None, ALU.add, rd=[b_lam], wr=[b_lam])
                    OP("dve", "tensor_scalar", lamv[:, 5:6], svc(l, "gsub"), 1.0 - li, None, ALU.mult, rd=[b_const], wr=[b_lam])
                    if own:
                        tmpQ = AR.alloc([2, TS], BF16)
                        b_tq = Buf("tmpQ")
                        DMA("act", q_all[:, :, 0:TS], QS[l][0], rd=[b_qs[l][0]], wr=b_q[0:4])
                        for hp in range(2):
                            DMA("act", tmpQ, QS[l][1][:, hp * 2:(hp + 1) * 2, :], rd=[b_qs[l][1]], wr=[b_tq])
                            blend(q_all[:, hp * 2:(hp + 1) * 2, 0:TS], tmpQ, b_q[0:4], [b_tq], b_q[0:4])
                    else:
                        DMA("act", q_all[:, :, 0:TS], QS[l][k], rd=[b_qs[l][k]], wr=b_q[0:4])
                    DMA("act", ckf, ck_d[l].rearrange("(t p) f -> p t f", p=128), wr=[b_ckf])
                    DMA("act", cvf, cv_d[l].rearrange("(t p) f -> p t f", p=128), wr=[b_cvf])
                    if l == 0 and k == 0:
                        convert_weights(0, ["pool"], pwmax=1024, kinds=("B", "G0", "G1"))
                    if l == 0 and k == 1:
                        convert_weights(1, ["pool"], pwmax=1024)
                    scale = 64 ** -0.5
                    state = {"pt": 0}

                    def attn_block(q_ap, nq, kt_ap, v_fn, nkc, rd_q, rd_k, rd_v, out_ap, wr_out):
                        def qk(kc):
                            par = kc % 2
                            for m in range(2):
                                sb = par * 2 + m
                                MM(PS[sb][:, 0:nq], kt_ap[m * 64:(m + 1) * 64, kc * 128:(kc + 1) * 128], q_ap[m * 64:(m + 1) * 64, :],
                                   True, True, rd=rd_q + rd_k, wr=[PB[sb]])
                                if nq != 512:
                                    OP("act", "activation", pT[sb][:, 0:nq], PS[sb][:, 0:nq], AF.Exp, scale=scale, rd=[PB[sb]], wr=[b_pT[sb]])
                            if nq == 512:
                                OP("act", "activation", pTP[par], PSP[par], AF.Exp, scale=scale,
                                   rd=[PB[par * 2], PB[par * 2 + 1]], wr=[b_pT[par * 2], b_pT[par * 2 + 1]])

                        def pv(kc):
                            par = kc % 2
                            for m in range(2):
                                sb = par * 2 + m
                                MM(PS[4 + m][:, 0:nq], v_fn(kc), pT[sb][:, 0:nq], kc == 0, kc == nkc - 1, rd=rd_v + [b_pT[sb]], wr=[PB[4 + m]])
                                MM(PS[6 + m][:, 0:nq], ones_b, pT[sb][:, 0:nq], kc == 0, kc == nkc - 1, rd=[b_pT[sb], b_const], wr=[PB[6 + m]])

                        qk(0)
                        if state.get("fin") is not None:
                            state["fin"]()
                            state["fin"] = None
                        for kc in range(nkc):
                            if kc + 1 < nkc:
                                qk(kc + 1)
                            pv(kc)
                        state["fin"] = lambda: finalize(nq, out_ap, wr_out)

                    def finalize(nq, out_ap, wr_out):
                        for m in range(2):
                            OP("dve", "reciprocal", fR[m][:, 0:nq], PS[6 + m][:, 0:nq], rd=[PB[6 + m]], wr=[b_fR[m]])
                            OP("dve", "tensor_tensor", fT[m][:, 0:nq], PS[4 + m][:, 0:nq], fR[m][:, 0:nq], ALU.mult, rd=[PB[4 + m], b_fR[m]], wr=[b_fT[m]])
                        OP("dve", "scalar_tensor_tensor", fo[:, 0:nq], fT[1][:, 0:nq], lamv[:, 4:5], fT[0][:, 0:nq], ALU.mult, ALU.add,
                           rd=[b_fT[0], b_fT[1], b_lam], wr=[b_fo])
                        OP("act", "activation", fsq[:, 0:nq], fo[:, 0:nq], AF.Square, rd=[b_fo], wr=[b_fsq])
                        MM(PS[6][:, 0:nq], ones_b, fsq[:, 0:nq], True, True, rd=[b_fsq, b_const], wr=[PB[6]])
                        rstd_from(PS[6][:, 0:nq], PB[6], frs[:, 0:nq], b_frs, 1.0 / 128, ftl[:, 0:nq], b_ftl)
                        OP("dve", "scalar_tensor_tensor", out_ap, fo[:, 0:nq], lamv[:, 5:6], frs[:, 0:nq], ALU.mult, ALU.mult,
                           rd=[b_fo, b_frs, b_lam], wr=wr_out)

                    for h in range(4):
                        i = h % 2
                        for r in range(2):
                            DMA("act", KT[i][:, r * 2048:(r + 1) * 2048], EXG[l][r * 1536 + h * 128: r * 1536 + (h + 1) * 128, :],
                                rd=[b_exg[l]], wr=[b_kt[i]])
                            vview = EXG[l][r * 1536 + 1024: r * 1536 + 1536, :].rearrange("r (t f) -> (r t) f", f=512)
                            vsrc = vview[:, h * 128:(h + 1) * 128].rearrange("(c p) e -> p c e", p=128)
                            DMA("act", VH[i][:, r * 16:(r + 1) * 16, :], vsrc, rd=[b_exg[l]], wr=[b_vh[i]])
                        for t_ in range(2):
                            OP("pe", "transpose", PS[0][:, t_ * 128:(t_ + 1) * 128], ckf[:, t_, h * 128:(h + 1) * 128], ident,
                               rd=[b_ckf, b_const], wr=[PB[0]])
                        OP("act", "copy", KT[i][:, 4096:4352], PS[0][:, 0:256], rd=[PB[0]], wr=[b_kt[i]])
                        OP("dve", "tensor_copy", VH[i][:, 32:34, :], cvf[:, :, h * 128:(h + 1) * 128], rd=[b_cvf], wr=[b_vh[i]])
                        for qb in range(4):
                            attn_block(q_all[:, h, qb * 512:(qb + 1) * 512], 512, KT[i], lambda kc, i=i: VH[i][:, kc, :], 34,
                                       [b_q[qb]], [b_kt[i]], [b_vh[i]], mix_attn[:, h, qb * 512:(qb + 1) * 512], [b_mixa[qb]])
                        for sq_ in (range(2) if k == 0 else ()):
                            attn_block(q_all[:, h, 2048 + sq_ * 256: 2048 + (sq_ + 1) * 256], 256, kTp[:, h, sq_ * 256:(sq_ + 1) * 256],
                                       lambda kc, sq_=sq_, h=h: Vp[:, sq_ * 2 + kc, h * 128:(h + 1) * 128], 2,
                                       [b_q[4]], [b_prm], [b_prm], mix_attn[:, h, 2048 + sq_ * 256: 2048 + (sq_ + 1) * 256], [b_mixa[4]])
                    if state.get("fin") is not None:
                        state["fin"]()
                        state["fin"] = None
                    S.barrier()
                    check_stop("C%d_%d" % (l, k), [("mixf", mix_four), ("xfp", xf_p), ("dftp", dftp_b), ("cs64", cs64_b)])

                def phaseD(k, own=False):
                    AR.reset(P_Q)
                    wo = AR.alloc([8, 1024], BF16)
                    b_wo = Buf("wo")
                    xblk = AR.alloc([8, 512], F32)
                    b_x = Buf("xblk")
                    osb = AR.alloc([8, 512], F32)
                    b_o = Buf("osb")
                    _svd = AR.cur
                    AR.reset(P_ROPE)
                    osbW = AR.alloc([8, 512], F32)
                    AR.reset(_svd)
                    b_oW = Buf("osbW")
                    sq = AR.alloc([8, 512], BF16)
                    b_sq = Buf("sq")
                    u = sq
                    b_u = b_sq
                    hid = AR.alloc([22, 512], BF16)
                    b_hid = Buf("hid")
                    gu = [AR.alloc([8, 512], BF16) for _ in range(2)]
                    b_gu = [Buf("gu0"), Buf("gu1")]
                    dn = [AR.alloc([22, 128], BF16) for _ in range(2)]
                    b_dn = [Buf("dn0"), Buf("dn1")]
                    rstd = AR.alloc([512], F32)
                    tl = AR.alloc([512], F32)
                    b_rstd, b_tl = Buf("rstd"), Buf("tl")
                    tmpf = [AR.alloc([512], F32) for _ in range(2)]
                    b_tmpf = [Buf("tmpf0"), Buf("tmpf1")]
                    sg = [AR.alloc([512], F32) for _ in range(2)]
                    b_sg = [Buf("sg0"), Buf("sg1")]
                    ytm = [osb[:, 0:2, :].rearrange("p a b -> p (a b)"), osb[:, 2:4, :].rearrange("p a b -> p (a b)")]
                    b_ytm = [b_o, b_o]
                    for h_ in range(2):
                        DMA("sp", wo[:, h_ * 4:(h_ + 1) * 4, :],
                            WGB[l][h_ * 512:(h_ + 1) * 512, OFF_WOUT:OFF_WOUT + 1024].rearrange("(k p) n -> p k n", p=128),
                            rd=[b_wgb[l]], wr=[b_wo])
                    cnt = {"o": 0, "tmpf": 0, "gu": 0, "dn": 0, "sg": 0, "y": 0}

                    def mixk(kc, tok):
                        if kc < 4:
                            return mix_attn[:, kc, tok]
                        if kc < 6:
                            return mix_lru[:, kc - 4, tok]
                        return mix_four[:, kc - 6, tok]

                    def post_norm_update(gidx, c_, osb=osb, b_o=b_o):
                        OP("act", "activation", sq[:, 0:4, :], osb[:, 0:4, :], AF.Square, rd=[b_o], wr=[b_sq])
                        OP("pool", "tensor_tensor", sq[:, 4:8, :], osb[:, 4:8, :], osb[:, 4:8, :], ALU.mult, rd=[b_o], wr=[b_sq])
                        for c in range(8):
                            MM(PS[0], ones_b, sq[:, c, :], c == 0, c == 7, rd=[b_sq, b_const], wr=[PB[0]])
                        rstd_from(PS[0], PB[0], rstd, b_rstd, 1.0 / D, tl, b_tl)
                        for c in range(8):
                            i = cnt["tmpf"] % 2
                            cnt["tmpf"] += 1
                            OP("dve", "tensor_tensor", tmpf[i], osb[:, c, :], rstd, ALU.mult, rd=[b_o, b_rstd], wr=[b_tmpf[i]])
                            OP("dve", "scalar_tensor_tensor", xblk[:, c, :], tmpf[i], c_[:, gidx, c:c + 1], xblk[:, c, :], ALU.mult, ALU.add,
                               rd=[b_tmpf[i], b_cst, b_x], wr=[b_x])

                    for tb in range(5 if k == 0 else 4):
                        w = 0 if tb < 4 else 1
                        c_ = cst[l][w]
                        tok = slice(tb * 512, (tb + 1) * 512)
                        DMA("sp", xblk, XS[l][k][:, :, tok], rd=[b_xs[l][k][tb]], wr=[b_x])
                        if own and tb < 4:
                            OP("act", "activation", xblk, xblk, AF.Identity, scale=m0, rd=[b_x, b_const], wr=[b_x])
                            for c in range(8):
                                i = cnt["tmpf"] % 2
                                cnt["tmpf"] += 1
                                DMA("sp", tmpf[i], XS[l][1][:, c, tok], rd=[b_xs[l][1][tb]], wr=[b_tmpf[i]])
                                OP("dve", "scalar_tensor_tensor", xblk[:, c, :], tmpf[i], m1, xblk[:, c, :], ALU.mult, ALU.add,
                                   rd=[b_tmpf[i], b_const, b_x], wr=[b_x])
                        for c in range(8):
                            pb = 1 + cnt["o"] % 2
                            cnt["o"] += 1
                            for kc in range(8):
                                MM(PS[pb], wo[:, kc, c * 128:(c + 1) * 128], mixk(kc, tok), kc == 0, kc == 7,
                                   rd=[b_wo, b_mixa[tb], b_mixl, b_mixf], wr=[PB[pb]])
                            OP("act", "copy", osbW[:, c, :], PS[pb], rd=[PB[pb]], wr=[b_oW])
                        post_norm_update(2, c_, osbW, b_oW)
                        OP("act", "activation", sq[:, 0:4, :], xblk[:, 0:4, :], AF.Square, rd=[b_x], wr=[b_sq])
                        OP("pool", "tensor_tensor", sq[:, 4:8, :], xblk[:, 4:8, :], xblk[:, 4:8, :], ALU.mult, rd=[b_x], wr=[b_sq])
                        for c in range(8):
                            MM(PS[0], ones_b, sq[:, c, :], c == 0, c == 7, rd=[b_sq, b_const], wr=[PB[0]])
                        rstd_from(PS[0], PB[0], rstd, b_rstd, 1.0 / D, tl, b_tl)
                        for c in range(8):
                            i = cnt["tmpf"] % 2
                            cnt["tmpf"] += 1
                            OP("dve", "scalar_tensor_tensor", tmpf[i], xblk[:, c, :], c_[:, 3, c:c + 1], rstd, ALU.mult, ALU.mult,
                               rd=[b_x, b_rstd, b_cst], wr=[b_tmpf[i]])
                            OP("act", "activation", u[:, c, :], tmpf[i], AF.Identity, bias=c_[:, 4, c:c + 1], scale=1.0,
                               rd=[b_tmpf[i], b_cst], wr=[b_u])
                        for js in range(11):
                            i = cnt["gu"] % 2
                            cnt["gu"] += 1
                            DMA("sp", gu[i], WGU[l][js], rd=[b_wgb[l]], wr=[b_gu[i]])
                            for jj in range(2):
                                j = js * 2 + jj
                                pg, pu = 3 + (j % 2), 5 + (j % 2)
                                for kc in range(8):
                                    MM(PS[pg], gu[i][:, kc, jj * 128:(jj + 1) * 128], u[:, kc, :], kc == 0, kc == 7, rd=[b_gu[i], b_u], wr=[PB[pg]])
                                for kc in range(8):
                                    MM(PS[pu], gu[i][:, kc, 256 + jj * 128: 256 + (jj + 1) * 128], u[:, kc, :], kc == 0, kc == 7,
                                       rd=[b_gu[i], b_u], wr=[PB[pu]])
                                si = cnt["sg"] % 2
                                cnt["sg"] += 1
                                OP("act", "activation", sg[si], PS[pg], AF.Silu, rd=[PB[pg]], wr=[b_sg[si]])
                                OP("dve", "tensor_tensor", hid.rearrange("p a b -> p (a b)")[:, j * 512:(j + 1) * 512], PS[pu], sg[si], ALU.mult, rd=[PB[pu], b_sg[si]], wr=[b_hid])
                        for c in range(8):
                            i = cnt["dn"] % 2
                            cnt["dn"] += 1
                            DMA("sp", dn[i], WGB[l][c * 128:(c + 1) * 128, OFF_DN:OFF_DN + 2816].rearrange("p (k j) -> p k j", j=128),
                                rd=[b_wgb[l]], wr=[b_dn[i]])
                            pb = 1 + cnt["o"] % 2
                            cnt["o"] += 1
                            for kc in range(22):
                                MM(PS[pb], dn[i][:, kc, :], hid[:, kc, :], kc == 0, kc == 21, rd=[b_dn[i], b_hid], wr=[PB[pb]])
                            OP("act", "copy", osb[:, c, :], PS[pb], rd=[PB[pb]], wr=[b_o])
                        post_norm_update(5, c_)
                        if l == 0:
                            DMA("pool", XS[1][k][:, :, tok], xblk, rd=[b_x], wr=[b_xs[1][k][tb]])
                        else:
                            dst = ys_d if tb < 4 else yp_d
                            r0 = tb * 512 if tb < 4 else 0
                            for tt in range(4):
                                yi = cnt["y"] % 2
                                cnt["y"] += 1
                                for g in range(2):
                                    pb = 7 if g == 0 else 0
                                    for cc in range(4):
                                        c = g * 4 + cc
                                        OP("pe", "transpose", PS[pb][:, cc * 128:(cc + 1) * 128], xblk[:, c, tt * 128:(tt + 1) * 128], ident,
                                           rd=[b_x, b_const], wr=[PB[pb]])
                                    OP("act" if g == 0 else "dve", "copy" if g == 0 else "tensor_copy",
                                       ytm[yi][:, g * 512:(g + 1) * 512], PS[pb], rd=[PB[pb]], wr=[b_ytm[yi]])
                                DMA("pool", dst[r0 + tt * 128: r0 + (tt + 1) * 128, :], ytm[yi], rd=[b_ytm[yi]])
                    S.barrier()
                    check_stop("D%d_%d" % (l, k), [("xs1", XS[1][k])])
                phaseA(0)
                phaseA(1)
                if l == 0:
                    for k in (0, 1):
                        phaseB1(k)
                        phaseB2(k)
                        phaseC(k)
                        phaseD(k)
                else:
                    phaseB1(0, True)
                    phaseB2(0, True)
                    phaseC(0, True)
                    phaseD(0, True)
        except _Stop:
            pass

        S.final_wait("sp")
        with nc.Block() as block:
            S.emit(block)
    nc._taps = TAPS
    return nc


def _fm(v):
    v = np.asarray(v, np.float32)
    return np.ascontiguousarray(v.reshape(-1, 128).T)


_CONST_CACHE = {}


def _constants():
    if _CONST_CACHE:
        return _CONST_CACHE
    t = np.arange(4096, dtype=np.float64)[:, None]
    j = np.arange(2048, dtype=np.float64)[None, :]
    ang = 2.0 * np.pi * ((t * j) % 4096) / 4096.0
    s = 1.0 / math.sqrt(4096 * 64)
    dft = np.stack([np.cos(ang) * s, -np.sin(ang) * s]).astype(np.float32).astype(ml_dtypes.bfloat16)
    t = np.arange(256, dtype=np.float64)[:, None]
    j = np.arange(256, dtype=np.float64)[None, :]
    ang = 2.0 * np.pi * ((t * j) % 256) / 256.0
    s = 1.0 / math.sqrt(256 * 64)
    dftp = np.stack([np.cos(ang) * s, -np.sin(ang) * s]).astype(np.float32).astype(ml_dtypes.bfloat16)
    c = np.arange(64, dtype=np.float64)
    a64 = 2.0 * np.pi * np.outer(c, c) / 64.0
    cs64 = np.zeros((128, 256), np.float32)
    for g in range(2):
        cs64[g * 64:(g + 1) * 64, g * 64:(g + 1) * 64] = np.cos(a64)
        cs64[g * 64:(g + 1) * 64, 128 + g * 64:128 + (g + 1) * 64] = np.sin(a64)
    rmat = np.zeros((128, 128), np.float32)
    for dd in range(128):
        partner = dd + 16 if (dd % 32) < 16 else dd - 16
        rmat[partner, dd] = 1.0
    _CONST_CACHE.update(dft=dft, dftp=dftp, cs64=cs64, rmat=rmat)
    return _CONST_CACHE


def _rope_table(hf):
    n = 16
    inv = (10000.0 ** (-np.arange(n, dtype=np.float32) / n)).astype(np.float32)
    tpos = np.arange(hf * 2048, (hf + 1) * 2048)
    row = (tpos // 64).astype(np.float32)
    col = (tpos % 64).astype(np.float32)
    tab = np.zeros((128, 2, 2048), np.float32)
    for p in range(128):
        dd = p % 64
        pos = row if dd < 32 else col
        i = dd % 16
        ang = (pos * inv[i]).astype(np.float32)
        sign = -1.0 if (dd % 32) < 16 else 1.0
        tab[p, 0] = np.cos(ang)
        tab[p, 1] = sign * np.sin(ang)
    return tab


_NC = None
_DEBUG = {}


def kernel(x_prompt, x_sample, cache_k, cache_v, state_lru, c, c_ctx,
           w_mod, b_mod, g_pre_mix, g_post_mix, g_pre_ffn, g_post_ffn,
           w_in, w_out, w_lambda, g_subln, conv_w, conv_b,
           lru_wa, lru_ba, lru_wx, lru_bx, lru_lambda, w_gate_up, w_down):
    global _NC
    f = lambda a: np.asarray(a, np.float32)
    x_prompt, x_sample, cache_k, cache_v, state_lru, c, c_ctx = map(f, (x_prompt, x_sample, cache_k, cache_v, state_lru, c, c_ctx))
    w_mod, b_mod, w_in, w_out, w_gate_up, w_down = map(f, (w_mod, b_mod, w_in, w_out, w_gate_up, w_down))
    lru_wa, lru_wx = f(lru_wa), f(lru_wx)
    K = _constants()
    if _NC is None:
        _NC = build_program()
    nc = _NC
    lw = np.zeros((128, 16, 128), np.float32)
    for l in range(2):
        for d in range(2):
            for g, W in enumerate((lru_wa, lru_wx)):
                for c2 in range(2):
                    idx = ((l * 2 + d) * 2 + g) * 2 + c2
                    for bb in range(2):
                        lw[bb * 64:(bb + 1) * 64, idx, bb * 64:(bb + 1) * 64] = W[l, d, c2 * 2 + bb]
    lw = lw.reshape(128, 2048)
    wsl = np.empty((2, 8, 128, WA_W + WB_W), np.float32)
    for l in range(2):
        for r in range(8):
            rows = slice(r * 128, (r + 1) * 128)
            wsl[l, r, :, 0:2304] = w_in[l, rows, :]
            wsl[l, r, :, 2304:8448] = w_mod[l, rows, :]
            wsl[l, r, :, 8448:9472] = w_out[l, rows, :]
            wsl[l, r, :, 9472:15104] = w_gate_up[l, rows, :]
            wd = w_down[l][:, r * 128:(r + 1) * 128]
            wsl[l, r, :, 15104:17920] = wd.reshape(22, 128, 128).transpose(1, 0, 2).reshape(128, 2816)
    in_maps = []
    cores = _DEBUG.get("cores", list(range(8)))
    for core in cores:
        p, hf = core // 2, core % 2
        sv = np.zeros((128, NSV), np.float32)
        for l in range(2):
            def put(name, arr):
                o, w = SVL[name]
                sv[:, l * SV_PER_L + o: l * SV_PER_L + o + w] = arr
            put("gpm", _fm(g_pre_mix[l]))
            put("gqm", _fm(g_post_mix[l]))
            put("gpf", _fm(g_pre_ffn[l]))
            put("gqf", _fm(g_post_ffn[l]))
            put("bmod", _fm(b_mod[l]))
            put("gsub", f(g_subln[l]).reshape(128, 1))
            cw = f(conv_w[l])
            put("convw", np.stack([_fm(cw[t_]) for t_ in range(4)], axis=1).reshape(128, 8))
            put("convb", _fm(conv_b[l]))
            put("ba", np.concatenate([_fm(f(lru_ba)[l, d]) for d in range(2)], axis=1))
            put("bx", np.concatenate([_fm(f(lru_bx)[l, d]) for d in range(2)], axis=1))
            put("lam", np.concatenate([_fm(f(lru_lambda)[l, d]) for d in range(2)], axis=1))
            put("wl", np.broadcast_to(f(w_lambda[l]).reshape(1, 256), (128, 256)))
            put("h0", np.concatenate([_fm(state_lru[p, l, d]) for d in range(2)], axis=1))
        par = (np.arange(128) % 2 == 1)
        sv[:, SV_MISC + 0] = 1.0 - hf
        sv[:, SV_MISC + 1] = float(hf)
        for k in range(2):
            o = k
            sv[:, SV_MISC + 2 + 2 * k] = 1.0 - o
            sv[:, SV_MISC + 3 + 2 * k] = float(o)
            sv[:, SV_MISC + 6 + k] = np.where(par & (o == 1), -1.0, 1.0)
        sv[:, SV_MISC + 2] = np.where(par & (hf == 1), -1.0, 1.0)
        cond = np.stack([_fm(c[p]), _fm(c_ctx)], axis=2)
        sv[:, SV_MISC + 8: SV_MISC + 24] = cond.reshape(128, 16)
        csign = np.ones((128, 512), np.float32)
        if hf == 1:
            csign[:, 1::2] = -1.0
        xs2 = np.stack([x_sample[p, 0:2048, :], x_sample[p, 2048:4096, :]])
        in_maps.append({
            "xs": np.ascontiguousarray(xs2),
            "xp": np.ascontiguousarray(x_prompt[2 * core:2 * core + 2].reshape(512, 1024)),
            "ck": np.ascontiguousarray(cache_k[p].reshape(2, 256, 512)),
            "cv": np.ascontiguousarray(cache_v[p].reshape(2, 256, 512)),
            "sv": sv, "lw": lw, "wsl": wsl,
            "rope": np.stack([_rope_table(0), _rope_table(1)]), "csign": np.ones((128, 512), np.float32),
            "rmat": K["rmat"], "cs64": K["cs64"],
            "dft": K["dft"], "dftp": K["dftp"],
        })
    if _DEBUG.get("stop") is not None:
        nc = build_program(stop=_DEBUG["stop"])
        res = run_bass_kernel_spmd(nc, in_maps, core_ids=list(range(len(cores))), trace=bool(_DEBUG.get("trace")))
        _DEBUG["exec_ns"] = getattr(res, "exec_time_ns", None)
        _DEBUG["results"] = res.results
        _DEBUG["taps"] = nc._taps
    else:
        res = run_bass_kernel_spmd(nc, in_maps, core_ids=list(range(8)))
    R = res.results
    y_prompt = np.empty((16, 256, 1024), np.float32)
    y_sample = np.empty((4, 4096, 1024), np.float32)
    new_k = np.empty((16, 2, 256, 4, 2, 64), np.float32)
    new_v = np.empty((16, 2, 256, 4, 128), np.float32)
    new_h = np.empty((16, 2, 2, 256), np.float32)
    for core in range(8):
        p, hf = core // 2, core % 2
        r = R[core]
        y_sample[p, hf * 2048:(hf + 1) * 2048] = np.asarray(r["ys"])
        y_prompt[2 * core:2 * core + 2] = np.asarray(r["yp"]).reshape(2, 256, 1024)
        new_k[2 * core:2 * core + 2] = np.asarray(r["nk"]).reshape(2, 2, 256, 4, 2, 64)
        new_v[2 * core:2 * core + 2] = np.asarray(r["nv"]).reshape(2, 2, 256, 4, 128)
        new_h[2 * core:2 * core + 2] = np.asarray(r["nh"]).reshape(2, 2, 2, 256)
    return (y_prompt, y_sample, new_k, new_v, new_h)
```

```python
import math
from contextlib import ExitStack
import numpy as np
import ml_dtypes
import concourse.bass as bass
import concourse.mybir as mybir
from concourse.bass_utils import run_bass_kernel_spmd

F32 = mybir.dt.float32
BF16 = mybir.dt.bfloat16
ALU = mybir.AluOpType
AF = mybir.ActivationFunctionType

D = 1024
TS = 2048
TP = 512
T = 2560
EPS = 1e-6
WA_W = 8448
WB_W = 9472
OFF_WOUT, OFF_GU, OFF_DN = 0, 1024, 6656

SVL = {}
_o = 0
for _n, _w in (("gpm", 8), ("gqm", 8), ("gpf", 8), ("gqf", 8), ("bmod", 48), ("gsub", 1), ("convw", 8),
               ("convb", 2), ("ba", 4), ("bx", 4), ("lam", 4), ("wl", 256), ("h0", 4)):
    SVL[_n] = (_o, _w)
    _o += _w
SV_PER_L = _o
SV_MISC = 2 * SV_PER_L
NSV = SV_MISC + 8 + 16


def lambda_init(l):
    return 0.8 - 0.6 * math.exp(-0.3 * l)


class Buf:
    __slots__ = ("name", "w", "r")

    def __init__(self, name):
        self.name = name
        self.w = None
        self.r = []


class Sched:
    ENGS = ("pe", "act", "dve", "pool", "sp")
    NDMA = 16

    def __init__(self, nc, es):
        self.nc = nc
        self.q = {e: [] for e in self.ENGS}
        self.sem = {e: es.enter_context(nc.semaphore("c_" + e)) for e in self.ENGS}
        self.cnt = {e: 0 for e in self.ENGS}
        self.known = {e: {} for e in self.ENGS}
        self.dsem, self.dcnt, self.dnext, self.dlast = {}, {}, {}, {}
        for e in ("sp", "pool", "act"):
            self.dsem[e] = [es.enter_context(nc.semaphore("d_%s%d" % (e, i))) for i in range(self.NDMA)]
            self.dcnt[e] = [0] * self.NDMA
            self.dlast[e] = [None] * self.NDMA
            self.dnext[e] = 0

    def _deps(self, reads, writes):
        deps = []
        for b in reads:
            if b.w is not None:
                deps.append(b.w)
        for b in writes:
            if b.w is not None:
                deps.append(b.w)
            deps.extend(b.r)
        return deps

    def _emit_waits(self, eng, deps, skip_own=False):
        kn = self.known[eng]
        best = {}
        for (s, v, key) in deps:
            if skip_own and key == ("c", eng):
                continue
            if kn.get(key, 0) >= v:
                continue
            if best.get(key, (None, 0))[1] < v:
                best[key] = (s, v)
        for key, (s, v) in best.items():
            kn[key] = v
            self.q[eng].append(lambda e, s=s, v=v: e.wait_ge(s, v))

    def _commit(self, ev, reads, writes):
        for b in reads:
            b.r.append(ev)
            if len(b.r) > 64:
                last = {}
                for x in b.r:
                    if last.get(x[2], (None, 0, None))[1] < x[1]:
                        last[x[2]] = x
                b.r = list(last.values())
        for b in writes:
            b.w = ev
            b.r = []

    def op(self, eng, fn, reads=(), writes=()):
        writes = list(writes) + [b for b in reads if b.name.startswith("ps") and b not in writes]
        deps = self._deps(reads, writes)
        self._emit_waits(eng, deps, skip_own=(eng == "pe"))
        self.cnt[eng] += 1
        s = self.sem[eng]
        ev = (s, self.cnt[eng], ("c", eng))
        self.q[eng].append(lambda e, fn=fn, s=s: fn(e).then_inc(s, 1))
        self._commit(ev, reads, writes)
        return ev

    def dma(self, eng, fn, reads=(), writes=()):
        i = self.dnext[eng]
        self.dnext[eng] = (i + 1) % self.NDMA
        deps = self._deps(reads, writes)
        if self.dlast[eng][i] is not None:
            deps.append(self.dlast[eng][i])
        self._emit_waits(eng, deps)
        self.dcnt[eng][i] += 16
        s = self.dsem[eng][i]
        ev = (s, self.dcnt[eng][i], ("d", eng, i))
        self.dlast[eng][i] = ev
        self.q[eng].append(lambda e, fn=fn, s=s: fn(e).then_inc(s, 16))
        self._commit(ev, reads, writes)
        return ev

    def _all_events(self):
        evs = []
        for e in self.ENGS:
            if self.cnt[e] > 0:
                evs.append((self.sem[e], self.cnt[e], ("c", e)))
        for e in self.dsem:
            for i in range(self.NDMA):
                if self.dlast[e][i] is not None:
                    evs.append(self.dlast[e][i])
        ex = getattr(self, "extra_events", [])
        if ex:
            evs.append(ex[-1])
        return evs

    def barrier(self):
        evs = self._all_events()
        for e in self.ENGS:
            self._emit_waits(e, evs)

    def final_wait(self, eng="sp"):
        self._emit_waits(eng, self._all_events())

    def emit(self, block):
        q = self.q

        @block.tensor
        def _(e):
            for t in q["pe"]:
                t(e)

        @block.scalar
        def _(e):
            for t in q["act"]:
                t(e)

        @block.vector
        def _(e):
            for t in q["dve"]:
                t(e)

        @block.gpsimd
        def _(e):
            for t in q["pool"]:
                t(e)

        @block.sync
        def _(e):
            for t in q["sp"]:
                t(e)


def build_program(stop=None):
    nc = bass.Bass("TRN2", target_bir_lowering=False)
    TAPS = []

    def din(name, shape, dt=F32):
        return nc.dram_tensor(name, shape, dt, kind="ExternalInput").ap()

    def dout(name, shape, dt=F32):
        return nc.dram_tensor(name, shape, dt, kind="ExternalOutput").ap()

    def dint(name, shape, dt):
        return nc.dram_tensor(name, shape, dt).ap()

    xs_d = din("xs", [2, TS, D])
    xp_d = din("xp", [TP, D])
    ck_d = din("ck", [2, 256, 512])
    cv_d = din("cv", [2, 256, 512])
    sv_d = din("sv", [128, NSV])
    lw_d = din("lw", [128, 2048])
    wsl_d = din("wsl", [2, 8, 128, WA_W + WB_W])
    rope_d = din("rope", [2, 128, 2, TS])
    csign_d = din("csign", [128, 512])
    rmat_d = din("rmat", [128, 128])
    cs64_d = din("cs64", [128, 256])
    dft_d = din("dft", [2, 4096, 2048], BF16)
    dftp_d = din("dftp", [2, 256, 256], BF16)
    ys_d = dout("ys", [TS, D])
    yp_d = dout("yp", [TP, D])
    nk_d = dout("nk", [2, 2, 256, 512])
    nv_d = dout("nv", [2, 2, 256, 512])
    nh_d = dout("nh", [2, 2, 2, 256])

    WGA = [dint("WGA%d" % l, [1024, WA_W], BF16) for l in range(2)]
    WGB = [dint("WGB%d" % l, [1024, WB_W], BF16) for l in range(2)]
    WGU = [dint("WGU%d" % l, [11, 128, 8, 512], BF16) for l in range(2)]
    LS = dint("LS0", [128, 2, TS], BF16)
    EXG = [dint("EXG%d" % l, [3072, 2048], BF16) for l in range(2)]
    QS = [[dint("QS%d_%d" % (l, k), [128, 4, TS], BF16) for k in range(2)] for l in range(2)]
    GS = [[dint("GS%d_%d" % (l, k), [128, 2, TS], BF16) for k in range(2)] for l in range(2)]
    XS = [[dint("XS%d_%d" % (l, k), [128, 8, T], F32) for k in range(2)] for l in range(2)]

    es = ExitStack()
    with es:
        S = Sched(nc, es)

        def OP(eng, method, *args, rd=(), wr=(), **kw):
            return S.op(eng, lambda e: getattr(e, method)(*args, **kw), rd, wr)

        def DMA(eng, out, in_, rd=(), wr=()):
            return S.dma(eng, lambda e: e.dma_start(out=out, in_=in_), rd, wr)

        def MM(out, lhsT, rhs, start, stop, rd=(), wr=()):
            return S.op("pe", lambda e: e.matmul(out, lhsT, rhs, start=start, stop=stop), rd, wr)

        class _Stop(Exception):
            pass

        def tap(name, src):
            o = nc.dram_tensor("dbg_" + name, list(src.shape), src.dtype, kind="ExternalOutput").ap()
            S.dma("sp", lambda e: e.dma_start(out=o, in_=src), (), ())
            TAPS.append("dbg_" + name)

        def check_stop(tag, taps=()):
            if stop == tag:
                S.barrier()
                for (n_, a_) in taps:
                    tap(n_, a_)
                raise _Stop()

        cc_sem = es.enter_context(nc.semaphore("cc_sem"))
        cc_state = {"n": 0}

        def CC(groups, src, dst, rd, wr):
            deps = S._deps(rd, wr)
            S._emit_waits("pool", deps)
            cc_state["n"] += 1
            v = cc_state["n"]
            ev = (cc_sem, v, ("cc",))
            S.q["pool"].append(lambda e: e.collective_compute(
                "AllGather", ALU.bypass, replica_groups=groups, ins=[src], outs=[dst]).then_inc(cc_sem, 1))
            S._commit(ev, rd, wr)
            S.extra_events = getattr(S, "extra_events", [])
            S.extra_events.append(ev)

        NBIG = 188 * 1024 // 4
        BIG = es.enter_context(nc.sbuf_tensor("big", [128, NBIG], F32))
        PSP = [es.enter_context(nc.psum_tensor("psp%d" % i, [128, 1024], F32))[:, :] for i in range(2)]
        PS = [PSP[0][:, 0:512], PSP[0][:, 512:1024], PSP[1][:, 0:512], PSP[1][:, 512:1024]] + \
             [es.enter_context(nc.psum_tensor("ps%d" % i, [128, 512], F32))[:, :] for i in range(4, 8)]
        PB = [Buf("ps%d" % i) for i in range(8)]

        class Arena:
            def __init__(self, start, end):
                self.start, self.cur, self.end = start, start, end

            def alloc(self, free_shape, dt):
                n = 1
                for s_ in free_shape:
                    n *= s_
                nbytes = n * (4 if dt == F32 else 2)
                nbytes = (nbytes + 63) // 64 * 64
                off = self.cur
                self.cur += nbytes
                assert self.cur <= self.end, ("arena overflow", self.cur, self.end)
                ap = BIG[:, off // 4:(off + nbytes) // 4]
                if dt != F32:
                    ap = ap.bitcast(dt)
                ap = ap[:, 0:n]
                if len(free_shape) == 2:
                    ap = ap.rearrange("p (a b) -> p a b", a=free_shape[0], b=free_shape[1])
                elif len(free_shape) == 3:
                    ap = ap.rearrange("p (a b c) -> p a b c", a=free_shape[0], b=free_shape[1], c=free_shape[2])
                return ap

            def reset(self, to=None):
                self.cur = self.start if to is None else to

        TOTAL = NBIG * 4
        AR = Arena(0, TOTAL)
        sv = AR.alloc([NSV], F32)
        ident = AR.alloc([128], F32)
        ones_b = AR.alloc([128], BF16)
        rmat_b = AR.alloc([128], BF16)
        cs64_b = AR.alloc([256], BF16)
        lw_b = AR.alloc([16, 128], BF16)
        P_ROPE = AR.cur
        rope_c = AR.alloc([TS], F32)
        rope_s = AR.alloc([TS], F32)
        csign = AR.alloc([512], F32)
        cst = [[AR.alloc([6, 8], F32) for w in range(2)] for l in range(2)]
        scond = AR.alloc([8, 2], BF16)
        small = AR.alloc([64], F32)
        nhst = AR.alloc([16], F32)
        dftp_b = AR.alloc([2, 2, 256], BF16)
        P_MLRU = AR.cur
        mix_lru = AR.alloc([2, T], BF16)
        P_MFOUR = AR.cur
        mix_four = AR.alloc([2, T], BF16)
        P_ATTN = AR.cur
        mix_attn = AR.alloc([4, T], BF16)
        P_Q = AR.cur
        kTp = AR.alloc([4, TP], BF16)
        Vp = AR.alloc([4, 512], BF16)
        xr_p = AR.alloc([2, TP], BF16)
        xf_p = AR.alloc([2, TP], BF16)
        q_all = AR.alloc([4, T], BF16)
        gr_all = AR.alloc([2, T], BF16)
        P_SCR = AR.cur
        print("persistent bytes", P_MLRU, P_Q, P_SCR, TOTAL)
        b_const = Buf("const")
        b_cst = Buf("cst")
        b_q = [Buf("q%d" % i) for i in range(5)]
        b_gr = [Buf("gr%d" % i) for i in range(5)]
        b_prm = Buf("prm")
        b_mixa = [Buf("mixa%d" % i) for i in range(5)]
        b_mixl = Buf("mixl")
        b_mixf = Buf("mixf")
        b_exg = [Buf("exg0"), Buf("exg1")]
        b_wga = [Buf("wga0"), Buf("wga1")]
        b_wgb = [Buf("wgb0"), Buf("wgb1")]
        b_xs = [[[Buf("xs%d_%d_%d" % (l, k, i)) for i in range(5)] for k in range(2)] for l in range(2)]
        b_qs = [[Buf("qs%d_%d" % (l, k)) for k in range(2)] for l in range(2)]
        b_rope = Buf("rope")
        b_ls = Buf("ls")
        b_nh = Buf("nh")

        def svc(l, name, i=0, n=1):
            o, w = SVL[name]
            return sv[:, l * SV_PER_L + o + i: l * SV_PER_L + o + i + n]

        try:
            DMA("sp", sv, sv_d, wr=[b_const])
            DMA("sp", csign, csign_d, wr=[b_const])
            DMA("sp", dftp_b, dftp_d.rearrange("c (t p) j -> p c t j", p=128), wr=[b_const])
            OP("pool", "memset", ident, 0.0, wr=[b_const])
            OP("pool", "affine_select", ident, ident, [[-1, 128]], ALU.not_equal, 1.0, base=0, channel_multiplier=1,
               rd=[b_const], wr=[b_const])
            OP("pool", "memset", ones_b, 1.0, wr=[b_const])
            OP("pool", "memset", nhst, 0.0, wr=[b_nh])
            AR.reset(P_SCR)
            st_a = AR.alloc([2048], F32)
            st_b = AR.alloc([128], F32)
            st_c = AR.alloc([256], F32)
            b_st = Buf("st")
            DMA("sp", st_a, lw_d, wr=[b_st])
            DMA("sp", st_b, rmat_d, wr=[b_st])
            DMA("sp", st_c, cs64_d, wr=[b_st])
            OP("dve", "tensor_copy", lw_b.rearrange("p a b -> p (a b)"), st_a, rd=[b_st], wr=[b_const])
            OP("dve", "tensor_copy", rmat_b, st_b, rd=[b_st], wr=[b_const])
            OP("dve", "tensor_copy", cs64_b, st_c, rd=[b_st], wr=[b_const])
            OP("act", "activation", scond.rearrange("p a b -> p (a b)"), sv[:, SV_MISC + 8: SV_MISC + 24], AF.Silu,
               rd=[b_const], wr=[b_const])

            def convert_weights(l, cast_engs, pwmax=2368, reset_to=None, kinds=None):
                if reset_to is not None:
                    AR.reset(reset_to)
                stf = [AR.alloc([pwmax], F32) for _ in range(2)]
                stb = [AR.alloc([pwmax], BF16) for _ in range(2)]
                bf = [Buf("stf0"), Buf("stf1")]
                bb = [Buf("stb0"), Buf("stb1")]
                jobs = []
                def split(c0, wtot, kind, base, align=1):
                    step = (pwmax // align) * align
                    o = 0
                    while o < wtot:
                        w_ = min(step, wtot - o)
                        jobs.append((c0 + o, w_, kind, base + o))
                        o += w_
                split(0, WA_W, "A", 0)
                split(WA_W + OFF_WOUT, 1024, "B", OFF_WOUT)
                split(WA_W + OFF_GU, 2816, "G0", 0, 256)
                split(WA_W + OFF_GU + 2816, 2816, "G1", 0, 256)
                split(WA_W + OFF_DN, 2816, "B", OFF_DN)
                if kinds is not None:
                    jobs = [j_ for j_ in jobs if j_[2] in kinds]
                k_ = 0
                for r in range(8):
                    for (c0, w_, kind, base) in jobs:
                        i = k_ % 2
                        eng = cast_engs[k_ % len(cast_engs)]
                        k_ += 1
                        DMA("sp", stf[i][:, 0:w_], wsl_d[l, r, :, c0:c0 + w_], wr=[bf[i]])
                        OP(eng, "tensor_copy", stb[i][:, 0:w_], stf[i][:, 0:w_], rd=[bf[i]], wr=[bb[i]])
                        if kind == "A":
                            DMA("pool", WGA[l][r * 128:(r + 1) * 128, base:base + w_], stb[i][:, 0:w_], rd=[bb[i]], wr=[b_wga[l]])
                        elif kind == "B":
                            DMA("pool", WGB[l][r * 128:(r + 1) * 128, base:base + w_], stb[i][:, 0:w_], rd=[bb[i]], wr=[b_wgb[l]])
                        else:
                            part = int(kind[1])
                            ns_ = w_ // 256
                            js0 = base // 256
                            DMA("pool", WGU[l][js0:js0 + ns_, :, r, part * 256:(part + 1) * 256].rearrange("s p n -> p s n"),
                                stb[i][:, 0:w_].rearrange("p (s n) -> p s n", n=256), rd=[bb[i]], wr=[b_wgb[l]])

            convert_weights(0, ["dve", "pool"], reset_to=P_SCR + 12 * 1024, kinds=("A",))
            S.barrier()
            check_stop("setup", [("wga", WGA[0][:, 0:2304])])

            def rstd_from(ps_ap, pbuf, out_ap, obuf, scale, tmp_ap, tbuf):
                OP("act", "activation", tmp_ap, ps_ap, AF.Ln, bias=small[:, 0:1], scale=scale, rd=[pbuf, b_const], wr=[tbuf])
                OP("act", "activation", out_ap, tmp_ap, AF.Exp, scale=-0.5, rd=[tbuf], wr=[obuf])

            OP("pool", "memset", small[:, 0:1], EPS, wr=[b_const])
            OP("pool", "memset", small[:, 1:2], 1.0, wr=[b_const])

            m0 = sv[:, SV_MISC + 0:SV_MISC + 1]
            m1 = sv[:, SV_MISC + 1:SV_MISC + 2]

            def blend(dst, other, rd_dst, rd_other, wr_dst):
                OP("act", "activation", dst, dst, AF.Identity, scale=m0, rd=list(rd_dst) + [b_const], wr=wr_dst)
                OP("dve", "scalar_tensor_tensor", dst, other, m1, dst, ALU.mult, ALU.add,
                   rd=list(rd_other) + [b_const] + list(wr_dst), wr=wr_dst)

            for l in range(2):
                li = lambda_init(l)
                AR.reset(P_SCR)
                wm = [AR.alloc([8, 1536], BF16) for _ in range(2)]
                bwm = [Buf("wm0"), Buf("wm1")]
                modsb = AR.alloc([48, 2], F32)
                bmod_ = Buf("modsb")
                for sl in range(4):
                    i = sl % 2
                    DMA("sp", wm[i], WGA[l][:, 2304 + sl * 1536: 2304 + (sl + 1) * 1536].rearrange("(k p) n -> p k n", p=128),
                        rd=[b_wga[l]], wr=[bwm[i]])
                    for jj in range(12):
                        j = sl * 12 + jj
                        for kc in range(8):
                            MM(PS[0][:, 2 * j:2 * j + 2], wm[i][:, kc, jj * 128:(jj + 1) * 128], scond[:, kc, :],
                               kc == 0, kc == 7, rd=[bwm[i], b_const], wr=[PB[0]])
                check_stop("Ma%d" % l, [("wm0", wm[0]), ("wm1", wm[1])])
                psm = PS[0][:, 0:96].rearrange("p (a b) -> p a b", b=2)
                for w in range(2):
                    OP("dve", "tensor_tensor", modsb[:, :, w], psm[:, :, w], svc(l, "bmod", 0, 48), ALU.add,
                       rd=[PB[0], b_const], wr=[bmod_])
                for w in range(2):
                    c_ = cst[l][w]
                    OP("dve", "scalar_tensor_tensor", c_[:, 0, :], modsb[:, 8:16, w], 1.0, svc(l, "gpm", 0, 8), ALU.add, ALU.mult,
                       rd=[bmod_, b_const], wr=[b_cst])
                    OP("dve", "tensor_copy", c_[:, 1, :], modsb[:, 0:8, w], rd=[bmod_], wr=[b_cst])
                    OP("dve", "tensor_tensor", c_[:, 2, :], modsb[:, 16:24, w], svc(l, "gqm", 0, 8), ALU.mult,
                       rd=[bmod_, b_const], wr=[b_cst])
                    OP("dve", "scalar_tensor_tensor", c_[:, 3, :], modsb[:, 32:40, w], 1.0, svc(l, "gpf", 0, 8), ALU.add, ALU.mult,
                       rd=[bmod_, b_const], wr=[b_cst])
                    OP("dve", "tensor_copy", c_[:, 4, :], modsb[:, 24:32, w], rd=[bmod_], wr=[b_cst])
                    OP("dve", "tensor_tensor", c_[:, 5, :], modsb[:, 40:48, w], svc(l, "gqf", 0, 8), ALU.mult,
                       rd=[bmod_, b_const], wr=[b_cst])
                S.barrier()
                check_stop("M%d" % l, [("cst0", cst[l][0]), ("cst1", cst[l][1]), ("modsb", modsb)])

                def phaseA(k):
                    AR.reset(P_MLRU)
                    win = AR.alloc([8, 2304], BF16)
                    rc_t = AR.alloc([512], F32)
                    rs_t = AR.alloc([512], F32)
                    assert AR.cur <= P_Q
                    AR.reset(P_SCR)
                    b_win = Buf("win")
                    xblk = AR.alloc([8, 512], F32)
                    b_x = Buf("xblk")
                    xtm = [AR.alloc([1024], F32) for _ in range(4)]
                    b_xtm = [Buf("xtm%d" % i) for i in range(4)]
                    sq = AR.alloc([8, 512], BF16)
                    b_sq = Buf("sq")
                    u = AR.alloc([8, 512], BF16)
                    b_u = Buf("u")
                    rstd = AR.alloc([512], F32)
                    b_rstd = Buf("rstd")
                    tl = AR.alloc([512], F32)
                    b_tl = Buf("tl")
                    tmpf = [AR.alloc([512], F32) for _ in range(2)]
                    b_tmpf = [Buf("tmpf0"), Buf("tmpf1")]
                    qb_ = [AR.alloc([512], BF16) for _ in range(2)]
                    b_qb = [Buf("qb0"), Buf("qb1")]
                    t1_ = [AR.alloc([512], F32) for _ in range(2)]
                    t2_ = [AR.alloc([512], F32) for _ in range(2)]
                    b_t1 = [Buf("t1_0"), Buf("t1_1")]
                    b_t2 = [Buf("t2_0"), Buf("t2_1")]
                    stg = [AR.alloc([512], BF16) for _ in range(3)]
                    b_stg = [Buf("stg%d" % i) for i in range(3)]
                    stgf = [AR.alloc([512], F32) for _ in range(2)]
                    b_stgf = [Buf("stgf0"), Buf("stgf1")]
                    b_rt = Buf("rt")
                    for h_ in range(2):
                        DMA("sp", win[:, h_ * 4:(h_ + 1) * 4, :],
                            WGA[l][h_ * 512:(h_ + 1) * 512, 0:2304].rearrange("(k p) n -> p k n", p=128),
                            rd=[b_wga[l]], wr=[b_win])
                    cnt = {"pj": 0, "rot": 0, "tm": 0, "stg": 0, "stgf": 0, "tmpf": 0, "qb": 0}
                    DMA("sp", rope_c, rope_d[k, :, 0, :], wr=[b_rope])
                    DMA("sp", rope_s, rope_d[k, :, 1, :], wr=[b_rope])
                    b_exb_k = Buf("exbk")
                    exbk = EXG[l][k * 1536:(k + 1) * 1536, :]
                    for tb in range(5 if k == 0 else 4):
                        w = 0 if tb < 4 else 1
                        c_ = cst[l][w]
                        tok = slice(tb * 512, (tb + 1) * 512)
                        if l == 0:
                            src = xs_d[k] if tb < 4 else xp_d
                            r0 = tb * 512 if tb < 4 else 0
                            for tt in range(4):
                                DMA("sp", xtm[tt], src[r0 + tt * 128: r0 + (tt + 1) * 128, :], wr=[b_xtm[tt]])
                            for c in range(8):
                                pb = 1 + (c % 2)
                                for tt in range(4):
                                    OP("pe", "transpose", PS[pb][:, tt * 128:(tt + 1) * 128], xtm[tt][:, c * 128:(c + 1) * 128], ident,
                                       rd=[b_xtm[tt], b_const], wr=[PB[pb]])
                                OP("act" if c % 2 == 0 else "dve", "copy" if c % 2 == 0 else "tensor_copy", xblk[:, c, :], PS[pb],
                                   rd=[PB[pb]], wr=[b_x])
                            DMA("pool", XS[0][k][:, :, tok], xblk, rd=[b_x], wr=[b_xs[0][k][tb]])
                        else:
                            DMA("sp", xblk, XS[1][k][:, :, tok], rd=[b_xs[1][k][tb]], wr=[b_x])
                        if tb == 0 and k == 0:
                            check_stop("Ax%d" % l, [("xblk", xblk)])
                        OP("act", "activation", sq[:, 0:6, :], xblk[:, 0:6, :], AF.Square, rd=[b_x], wr=[b_sq])
                        OP("pool", "tensor_tensor", sq[:, 6:8, :], xblk[:, 6:8, :], xblk[:, 6:8, :], ALU.mult, rd=[b_x], wr=[b_sq])
                        for c in range(8):
                            MM(PS[0], ones_b, sq[:, c, :], c == 0, c == 7, rd=[b_sq, b_const], wr=[PB[0]])
                        rstd_from(PS[0], PB[0], rstd, b_rstd, 1.0 / D, tl, b_tl)
                        for c in range(8):
                            i = cnt["tmpf"] % 2
                            cnt["tmpf"] += 1
                            OP("dve", "scalar_tensor_tensor", tmpf[i], xblk[:, c, :], c_[:, 0, c:c + 1], rstd, ALU.mult, ALU.mult,
                               rd=[b_x, b_rstd, b_cst], wr=[b_tmpf[i]])
                            OP("act", "activation", u[:, c, :], tmpf[i], AF.Identity, bias=c_[:, 1, c:c + 1], scale=1.0,
                               rd=[b_tmpf[i], b_cst], wr=[b_u])

                        if tb == 0 and k == 0:
                            check_stop("An%d" % l, [("u", u), ("rstd", rstd)])

                        if tb < 4:
                            OP("act", "copy", rc_t, rope_c[:, tok], rd=[b_rope], wr=[b_rt])
                            OP("act", "copy", rs_t, rope_s[:, tok], rd=[b_rope], wr=[b_rt])

                        def proj_fm(j):
                            pb = 3 + (cnt["pj"] % 3)
                            cnt["pj"] += 1
                            for kc in range(8):
                                MM(PS[pb], win[:, kc, j * 128:(j + 1) * 128], u[:, kc, :], kc == 0, kc == 7,
                                   rd=[b_win, b_u], wr=[PB[pb]])
                            return pb

                        def proj_tm(tt, c0):
                            pb = 7
                            for kc in range(8):
                                MM(PS[pb], u[:, kc, tt * 128:(tt + 1) * 128], win[:, kc, c0:c0 + 512], kc == 0, kc == 7,
                                   rd=[b_win, b_u], wr=[PB[pb]])
                            return pb

                        def nstg():
                            i = cnt["stg"] % 3
                            cnt["stg"] += 1
                            return i

                        for j in range(18):
                            kind = ("q", "k", "v", "xr", "gr", "xf")[[0, 0, 0, 0, 1, 1, 1, 1, 2, 2, 2, 2, 3, 3, 4, 4, 5, 5][j]]
                            if kind == "v":
                                continue
                            pb = proj_fm(j)
                            dbg0 = (tb == 0 and k == 0 and l == 0 and j == 0)
                            if dbg0:
                                check_stop("As1", [("u", u)])
                            if kind in ("q", "k"):
                                h = j % 4
                                if tb < 4:
                                    i = cnt["qb"] % 2
                                    cnt["qb"] += 1
                                    OP("act", "copy", qb_[i], PS[pb], rd=[PB[pb]], wr=[b_qb[i]])
                                    if dbg0:
                                        check_stop("As2", [("qb", qb_[i])])
                                    MM(PS[6], rmat_b, qb_[i], True, True, rd=[b_qb[i], b_const], wr=[PB[6]])
                                    if dbg0:
                                        check_stop("As3", [("qb", qb_[i])])
                                    OP("dve", "tensor_tensor", t1_[i], PS[pb], rc_t, ALU.mult, rd=[PB[pb], b_rt], wr=[b_t1[i]])
                                    OP("dve", "tensor_tensor", t2_[i], PS[6], rs_t, ALU.mult, rd=[PB[6], b_rt], wr=[b_t2[i]])
                                    if dbg0:
                                        check_stop("As4", [("t1", t1_[i]), ("t2", t2_[i])])
                                    if kind == "q":
                                        OP("pool", "tensor_tensor", q_all[:, h, tok], t1_[i], t2_[i], ALU.add,
                                           rd=[b_t1[i], b_t2[i]], wr=[b_q[tb]])
                                    else:
                                        si = nstg()
                                        OP("pool", "tensor_tensor", stg[si], t1_[i], t2_[i], ALU.add,
                                           rd=[b_t1[i], b_t2[i]], wr=[b_stg[si]])
                                        DMA("pool", exbk[h * 128:(h + 1) * 128, tok], stg[si], rd=[b_stg[si]], wr=[b_exb_k])
                                else:
                                    if kind == "q":
                                        OP("act", "copy", q_all[:, h, tok], PS[pb], rd=[PB[pb]], wr=[b_q[tb]])
                                    else:
                                        OP("act", "copy", kTp[:, h, :], PS[pb], rd=[PB[pb]], wr=[b_prm])
                            elif kind == "gr":
                                OP("act", "activation", gr_all[:, j - 14, tok], PS[pb], AF.Gelu_apprx_tanh, rd=[PB[pb]], wr=[b_gr[tb]])
                            else:
                                c2 = j % 2
                                if tb < 4:
                                    si = nstg()
                                    OP("act", "copy", stg[si], PS[pb], rd=[PB[pb]], wr=[b_stg[si]])
                                    base = 512 if kind == "xr" else 768
                                    DMA("pool", exbk[base + c2 * 128: base + (c2 + 1) * 128, tok], stg[si], rd=[b_stg[si]], wr=[b_exb_k])
                                else:
                                    dst = xr_p if kind == "xr" else xf_p
                                    OP("act", "copy", dst[:, c2, :], PS[pb], rd=[PB[pb]], wr=[b_prm])
                            if tb == 0 and k == 0 and l == 0:
                                check_stop("Aj%d" % j, [("q", q_all[:, :, 0:512])])
                        for tt in range(4):
                            pb = proj_tm(tt, 1024)
                            if tb < 4:
                                si = nstg()
                                OP("dve", "tensor_copy", stg[si], PS[pb], rd=[PB[pb]], wr=[b_stg[si]])
                                vview = exbk[1024:1536, :].rearrange("r (t f) -> (r t) f", f=512)
                                vdst = vview[tb * 512 + tt * 128: tb * 512 + (tt + 1) * 128, :]
                                DMA("pool", vdst, stg[si], rd=[b_stg[si]], wr=[b_exb_k])
                            else:
                                seq, hh = tt // 2, tt % 2
                                fi = cnt["stgf"] % 2
                                cnt["stgf"] += 1
                                OP("dve", "tensor_copy", stgf[fi], PS[pb], rd=[PB[pb]], wr=[b_stgf[fi]])
                                OP("act", "copy", Vp[:, tt, :], PS[pb], rd=[PB[pb]], wr=[b_prm])
                                DMA("pool", nv_d[seq, l, hh * 128:(hh + 1) * 128, :], stgf[fi], rd=[b_stgf[fi]])
                                pb = proj_tm(tt, 512)
                                fi = cnt["stgf"] % 2
                                cnt["stgf"] += 1
                                OP("dve", "tensor_copy", stgf[fi], PS[pb], rd=[PB[pb]], wr=[b_stgf[fi]])
                                DMA("pool", nk_d[seq, l, hh * 128:(hh + 1) * 128, :], stgf[fi], rd=[b_stgf[fi]])
                        if tb == 0 and k == 0:
                            check_stop("Ap%d" % l, [("q", q_all), ("gr", gr_all)])
                    DMA("pool", QS[l][k], q_all[:, :, 0:TS], rd=b_q[0:4], wr=[b_qs[l][k]])
                    DMA("pool", GS[l][k], gr_all[:, :, 0:TS], rd=b_gr[0:4], wr=[b_qs[l][k]])
                    b_exg[l].w = None
                    S.barrier()
                    check_stop("A%d_%d" % (l, k), [("q", q_all), ("gr", gr_all), ("exg", EXG[l]), ("ktp", kTp), ("vp", Vp), ("xrp", xr_p), ("xs", XS[l][k])])

                def phaseB1(k, own=False):
                    if l == 0 and k == 1:
                        DMA("sp", mix_lru[:, :, 0:TS], LS, rd=[b_ls], wr=[b_mixl])
                        S.barrier()
                        return
                    AR.reset(P_SCR)
                    NT = 4608
                    xr_c = AR.alloc([NT], BF16)
                    xc = AR.alloc([NT], F32)
                    xcb = AR.alloc([NT], BF16)
                    HS = AR.alloc([NT], F32)
                    _sv = AR.cur
                    AR.reset(P_MFOUR)
                    HB = AR.alloc([2560], F32)
                    GR = AR.alloc([2560], F32)
                    assert AR.cur <= P_Q
                    AR.reset(_sv)
                    GI = AR.alloc([2560], F32)
                    GC = AR.alloc([2560], F32)
                    sp_ = AR.alloc([8], F32)
                    b_xr, b_xc, b_xcb, b_hs, b_hb = Buf("xr"), Buf("xc"), Buf("xcb"), Buf("hs"), Buf("hb")
                    b_g = Buf("gates")
                    b_sp = Buf("sp")
                    NTK = 4608 if k == 0 else 4096
                    seqs = [(0, 4096), (4096, 4352), (4352, 4608)] if k == 0 else [(0, 4096)]
                    _sv2 = AR.cur
                    AR.reset(P_MFOUR + 20480)
                    tmpA = AR.alloc([2048], BF16)
                    tmpB = AR.alloc([2048], BF16)
                    assert AR.cur <= P_Q
                    AR.reset(_sv2)
                    b_tA, b_tB = Buf("tA"), Buf("tB")
                    if own:
                        _sv3 = AR.cur
                        AR.reset(P_MFOUR + 20480)
                        tmpG = AR.alloc([2, TS], BF16)
                        assert AR.cur <= P_Q
                        AR.reset(_sv3)
                        b_tg = Buf("tmpG")
                        DMA("sp", gr_all[:, :, 0:TS], GS[l][0], rd=[b_qs[l][0]], wr=b_gr[0:4])
                        DMA("sp", tmpG, GS[l][1], rd=[b_qs[l][1]], wr=[b_tg])
                        blend(gr_all[:, :, 0:TS], tmpG, b_gr[0:4], [b_tg], b_gr[0:4])
                    else:
                        DMA("sp", gr_all[:, :, 0:TS], GS[l][k], rd=[b_qs[l][k]], wr=b_gr[0:4])
                        if l == 0:
                            _sv4 = AR.cur
                            AR.reset(P_MFOUR + 20480)
                            grO = AR.alloc([2, TS], BF16)
                            assert AR.cur <= P_Q
                            AR.reset(_sv4)
                            b_gro = Buf("grO")
                            DMA("sp", grO, GS[l][1], rd=[b_qs[l][1]], wr=[b_gro])
                    a0 = sv[:, SV_MISC + 0:SV_MISC + 1]
                    a1 = sv[:, SV_MISC + 1:SV_MISC + 2]
                    OP("act", "activation", sp_[:, 0:4], svc(l, "lam", 0, 4), AF.Exp, scale=-1.0, rd=[b_const], wr=[b_sp])
                    OP("act", "activation", sp_[:, 4:8], sp_[:, 0:4], AF.Ln, bias=small[:, 1:2], scale=1.0, rd=[b_sp, b_const], wr=[b_sp])
                    OP("dve", "tensor_scalar", sp_[:, 0:4], sp_[:, 4:8], -8.0, None, ALU.mult, rd=[b_sp], wr=[b_sp])
                    OP("dve", "tensor_scalar", sp_[:, 4:8], sp_[:, 4:8], -16.0, None, ALU.mult, rd=[b_sp], wr=[b_sp])
                    for c2 in range(2):
                        for r in range(2):
                            DMA("sp", xr_c[:, r * 2048:(r + 1) * 2048],
                                EXG[l][r * 1536 + 512 + c2 * 128: r * 1536 + 512 + (c2 + 1) * 128, :], wr=[b_xr])
                        if k == 0:
                            OP("pool", "tensor_copy", xr_c[:, 4096:4608], xr_p[:, c2, :], rd=[b_prm], wr=[b_xr])
                        cw = lambda t_: svc(l, "convw", t_ * 2 + c2)
                        OP("dve", "tensor_scalar", xc[:, 0:NTK], xr_c[:, 0:NTK], cw(2), svc(l, "convb", c2), ALU.mult, ALU.add,
                           rd=[b_xr, b_const], wr=[b_xc])
                        for (s0, s1) in seqs:
                            OP("dve", "scalar_tensor_tensor", xc[:, s0 + 2:s1], xr_c[:, s0:s1 - 2], cw(0), xc[:, s0 + 2:s1],
                               ALU.mult, ALU.add, rd=[b_xr, b_const, b_xc], wr=[b_xc])
                            OP("dve", "scalar_tensor_tensor", xc[:, s0 + 1:s1], xr_c[:, s0:s1 - 1], cw(1), xc[:, s0 + 1:s1],
                               ALU.mult, ALU.add, rd=[b_xr, b_const, b_xc], wr=[b_xc])
                            OP("dve", "scalar_tensor_tensor", xc[:, s0:s1 - 1], xr_c[:, s0 + 1:s1], cw(3), xc[:, s0:s1 - 1],
                               ALU.mult, ALU.add, rd=[b_xr, b_const, b_xc], wr=[b_xc])
                        OP("pool", "tensor_copy", xcb[:, 0:NTK], xc[:, 0:NTK], rd=[b_xc], wr=[b_xcb])
                        for d in range(2):
                            halves = [(0, 2048), (2048, NTK)]
                            if d == 1:
                                halves = halves[::-1]
                            col = d * 2 + c2
                            wa = lw_b[:, ((l * 2 + d) * 2 + 0) * 2 + c2, :]
                            wx = lw_b[:, ((l * 2 + d) * 2 + 1) * 2 + c2, :]
                            for (t0, t1) in halves:
                                n = t1 - t0
                                for s_ in range(n // 512):
                                    a0 = t0 + s_ * 512
                                    pa, pi = 1 + 2 * (s_ % 2), 2 + 2 * (s_ % 2)
                                    MM(PS[pa], wa, xcb[:, a0:a0 + 512], True, True, rd=[b_xcb, b_const], wr=[PB[pa]])
                                    MM(PS[pi], wx, xcb[:, a0:a0 + 512], True, True, rd=[b_xcb, b_const], wr=[PB[pi]])
                                    OP("act", "activation", GR[:, s_ * 512:(s_ + 1) * 512], PS[pa], AF.Sigmoid,
                                       bias=svc(l, "ba", col), scale=1.0, rd=[PB[pa], b_const], wr=[b_g])
                                    OP("act", "activation", GI[:, s_ * 512:(s_ + 1) * 512], PS[pi], AF.Sigmoid,
                                       bias=svc(l, "bx", col), scale=1.0, rd=[PB[pi], b_const], wr=[b_g])
                                OP("act", "activation", GC[:, 0:n], GR[:, 0:n], AF.Exp, scale=sp_[:, 4 + col:5 + col], rd=[b_g, b_sp], wr=[b_g])
                                OP("act", "activation", GR[:, 0:n], GR[:, 0:n], AF.Exp, scale=sp_[:, col:col + 1], rd=[b_g, b_sp], wr=[b_g])
                                OP("act", "activation", GC[:, 0:n], GC[:, 0:n], AF.Ln, bias=small[:, 1:2], scale=-1.0, rd=[b_g, b_const], wr=[b_g])
                                OP("act", "activation", GC[:, 0:n], GC[:, 0:n], AF.Exp, scale=0.5, rd=[b_g], wr=[b_g])
                                OP("dve", "tensor_tensor", GI[:, 0:n], GI[:, 0:n], xc[:, t0:t1], ALU.mult, rd=[b_g, b_xc], wr=[b_g])
                                OP("dve", "tensor_tensor", GI[:, 0:n], GI[:, 0:n], GC[:, 0:n], ALU.mult, rd=[b_g], wr=[b_g])
                                for (s0, s1) in seqs:
                                    lo, hi = max(s0, t0), min(s1, t1)
                                    if lo >= hi:
                                        continue
                                    if d == 0:
                                        if lo == s0:
                                            init = svc(l, "h0", col) if s0 == 0 else 0.0
                                        else:
                                            init = HS[:, lo - 1:lo]
                                        OP("dve", "tensor_tensor_scan", HS[:, lo:hi], GR[:, lo - t0:hi - t0], GI[:, lo - t0:hi - t0],
                                           init, ALU.mult, ALU.add, rd=[b_g, b_hs, b_const], wr=[b_hs])
                                    else:
                                        if hi == s1:
                                            init = svc(l, "h0", col) if s0 == 0 else 0.0
                                        else:
                                            init = small[:, 8:9]
                                        OP("dve", "tensor_tensor_scan", HB[:, lo - t0:hi - t0][:, ::-1], GR[:, lo - t0:hi - t0][:, ::-1],
                                           GI[:, lo - t0:hi - t0][:, ::-1], init, ALU.mult, ALU.add,
                                           rd=[b_g, b_hb, b_const], wr=[b_hb])
                                if d == 0 and k == 0 and t1 == 4608:
                                    for sq_ in range(2):
                                        e_ = 4096 + sq_ * 256 + 255
                                        OP("dve", "tensor_copy", nhst[:, (sq_ * 2 + 0) * 2 + c2:(sq_ * 2 + 0) * 2 + c2 + 1], HS[:, e_:e_ + 1],
                                           rd=[b_hs], wr=[b_nh])
                                if d == 1:
                                    if t1 == NTK:
                                        for sq_ in (range(2) if k == 0 else ()):
                                            e_ = 4096 + sq_ * 256 - t0
                                            OP("dve", "tensor_copy", nhst[:, (sq_ * 2 + 1) * 2 + c2:(sq_ * 2 + 1) * 2 + c2 + 1], HB[:, e_:e_ + 1],
                                               rd=[b_hb], wr=[b_nh])
                                        OP("dve", "tensor_copy", small[:, 8:9], HB[:, 0:1], rd=[b_hb], wr=[b_const])
                                    OP("dve", "tensor_tensor", HS[:, t0:t1], HS[:, t0:t1], HB[:, 0:n], ALU.add, rd=[b_hs, b_hb], wr=[b_hs])
                        if own:
                            OP("act", "activation", HB[:, 0:2048], HS[:, 0:2048], AF.Identity, scale=m0, rd=[b_hs, b_const, b_hb], wr=[b_hb])
                            OP("dve", "scalar_tensor_tensor", HB[:, 0:2048], HS[:, 2048:4096], m1, HB[:, 0:2048], ALU.mult, ALU.add,
                               rd=[b_hs, b_const, b_hb], wr=[b_hb])
                            OP("dve", "tensor_tensor", mix_lru[:, c2, 0:2048], HB[:, 0:2048], gr_all[:, c2, 0:2048], ALU.mult,
                               rd=[b_hb] + b_gr, wr=[b_mixl])
                        else:
                            OP("dve", "tensor_tensor", mix_lru[:, c2, 0:2048], HS[:, k * 2048:(k + 1) * 2048], gr_all[:, c2, 0:2048], ALU.mult,
                               rd=[b_hs] + b_gr, wr=[b_mixl])
                            if l == 0:
                                OP("dve", "tensor_tensor", xcb[:, 0:2048], HS[:, 2048:4096], grO[:, c2, :], ALU.mult,
                                   rd=[b_hs, b_gro, b_xcb], wr=[b_xcb])
                                DMA("pool", LS[:, c2, :], xcb[:, 0:2048], rd=[b_xcb], wr=[b_ls])
                        if k == 0:
                            OP("dve", "tensor_tensor", mix_lru[:, c2, 2048:2560], HS[:, 4096:4608], gr_all[:, c2, 2048:2560], ALU.mult,
                               rd=[b_hs] + b_gr, wr=[b_mixl])
                    for sq_ in (range(2) if k == 0 else ()):
                        for d in range(2):
                            for c2 in range(2):
                                k_ = (sq_ * 2 + d) * 2 + c2
                                dst = bass.AP(nh_d.tensor, ((sq_ * 2 + l) * 2 + d) * 256 + c2 * 128, [[1, 128], [1, 1]])
                                DMA("pool", dst, nhst[:, k_:k_ + 1], rd=[b_nh])
                    S.barrier()
                    check_stop("B1_%d_%d" % (l, k), [("mixl", mix_lru), ("nhst", nhst), ("hs", HS), ("xc", xc), ("xrc", xr_c), ("hb", HB), ("svh0", sv[:, 359:363])])

                def phaseB2(k, own=False):
                    AR.reset(P_ATTN)
                    xf_c = AR.alloc([2, 4096], BF16)
                    assert AR.cur <= P_Q
                    AR.reset(P_SCR)
                    Ytm = AR.alloc([32, 512], BF16)
                    Ytp = AR.alloc([4, 512], BF16)
                    tabs = [[AR.alloc([4, 512], BF16) for _ in range(3)] for cs in range(2)]
                    b_xf, b_y, b_yp = Buf("xf"), Buf("ytm"), Buf("ytp")
                    b_tab = [[Buf("tab%d_%d" % (cs, i)) for i in range(3)] for cs in range(2)]
                    for c2 in range(2):
                        for r in range(2):
                            DMA("sp", xf_c[:, c2, r * 2048:(r + 1) * 2048],
                                EXG[l][r * 1536 + 768 + c2 * 128: r * 1536 + 768 + (c2 + 1) * 128, :], rd=[b_exg[l]], wr=[b_xf])
                    for tc in range(32):
                        pb = 1 + tc % 2
                        for c2 in range(2):
                            MM(PS[pb][:, c2 * 256:(c2 + 1) * 256], xf_c[:, c2, tc * 128:(tc + 1) * 128], cs64_b, True, True,
                               rd=[b_xf, b_const], wr=[PB[pb]])
                        OP("act", "activation", Ytm[:, tc, :], PS[pb], AF.Identity, scale=(sv[:, SV_MISC + 2:SV_MISC + 3] if own else sv[:, SV_MISC + 6 + k:SV_MISC + 7 + k]),
                           rd=[PB[pb], b_const], wr=[b_y])
                    for tc in (range(4) if k == 0 else ()):
                        pb = 1 + tc % 2
                        for c2 in range(2):
                            MM(PS[pb][:, c2 * 256:(c2 + 1) * 256], xf_p[:, c2, tc * 128:(tc + 1) * 128], cs64_b, True, True,
                               rd=[b_prm, b_const], wr=[PB[pb]])
                        OP("act", "copy", Ytp[:, tc, :], PS[pb], rd=[PB[pb]], wr=[b_yp])
                    kk = 0
                    for jb in range(4):
                        for tg in range(8):
                            i = kk % 3
                            kk += 1
                            for cs in range(2):
                                DMA("sp", tabs[cs][i],
                                    dft_d[cs, tg * 512:(tg + 1) * 512, jb * 512:(jb + 1) * 512].rearrange("(t p) j -> p t j", p=128),
                                    wr=[b_tab[cs][i]])
                            for c2 in range(2):
                                pb = 3 + c2
                                for t_ in range(4):
                                    for cs in range(2):
                                        tc = tg * 4 + t_
                                        MM(PS[pb], Ytm[:, tc, c2 * 256 + cs * 128: c2 * 256 + (cs + 1) * 128], tabs[cs][i][:, t_, :],
                                           tg == 0 and t_ == 0 and cs == 0, tg == 7 and t_ == 3 and cs == 1,
                                           rd=[b_y, b_tab[cs][i]], wr=[PB[pb]])
                        for c2 in range(2):
                            OP("dve", "tensor_tensor", mix_four.rearrange("p a b -> p (a b)")[:, c2 * T + jb * 512: c2 * T + (jb + 1) * 512], PS[3 + c2], csign, ALU.mult,
                               rd=[PB[3 + c2], b_const], wr=[b_mixf])
                    for sq_ in (range(2) if k == 0 else ()):
                        for c2 in range(2):
                            pb = 5 + c2
                            for t_ in range(2):
                                for cs in range(2):
                                    MM(PS[pb][:, 0:256], Ytp[:, sq_ * 2 + t_, c2 * 256 + cs * 128: c2 * 256 + (cs + 1) * 128],
                                       dftp_b[:, cs, t_, :], t_ == 0 and cs == 0, t_ == 1 and cs == 1,
                                       rd=[b_yp, b_const], wr=[PB[pb]])
                            OP("act", "copy", mix_four[:, c2, 2048 + sq_ * 256: 2048 + (sq_ + 1) * 256], PS[pb][:, 0:256],
                               rd=[PB[pb]], wr=[b_mixf])
                    S.barrier()
                    check_stop("B2_%d_%d" % (l, k), [("mixf", mix_four)])

                def phaseC(k, own=False):
                    AR.reset(P_SCR)
                    KT = [AR.alloc([4352], BF16) for _ in range(2)]
                    VH = [AR.alloc([34, 128], BF16) for _ in range(2)]
                    b_kt = [Buf("kt0"), Buf("kt1")]
                    b_vh = [Buf("vh0"), Buf("vh1")]
                    ckf = AR.alloc([2, 512], F32)
                    cvf = AR.alloc([2, 512], F32)
                    b_ckf, b_cvf = Buf("ckf"), Buf("cvf")
                    pTP = [AR.alloc([1024], BF16) for _ in range(2)]
                    pT = [pTP[0][:, 0:512], pTP[0][:, 512:1024], pTP[1][:, 0:512], pTP[1][:, 512:1024]]
                    b_pT = [Buf("pT%d" % i) for i in range(4)]
                    fR = [AR.alloc([512], F32) for _ in range(2)]
                    fT = [AR.alloc([512], F32) for _ in range(2)]
                    fo = AR.alloc([512], F32)
                    fsq = AR.alloc([512], BF16)
                    frs = AR.alloc([512], F32)
                    ftl = AR.alloc([512], F32)
                    b_fR = [Buf("fR0"), Buf("fR1")]
                    b_fT = [Buf("fT0"), Buf("fT1")]
                    b_fo, b_fsq, b_frs, b_ftl = Buf("fo"), Buf("fsq"), Buf("frs"), Buf("ftl")
                    lamv = AR.alloc([8], F32)
                    lprod = AR.alloc([128], F32)
                    b_lam = Buf("lam")
                    OP("dve", "tensor_tensor", lprod[:, 0:64], svc(l, "wl", 0, 64), svc(l, "wl", 64, 64), ALU.mult, rd=[b_const], wr=[b_lam])
                    OP("dve", "tensor_tensor", lprod[:, 64:128], svc(l, "wl", 128, 64), svc(l, "wl", 192, 64), ALU.mult, rd=[b_const], wr=[b_lam])
                    OP("dve", "reduce_sum", lamv[:, 0:1], lprod[:, 0:64], mybir.AxisListType.X, rd=[b_lam], wr=[b_lam])
                    OP("dve", "reduce_sum", lamv[:, 1:2], lprod[:, 64:128], mybir.AxisListType.X, rd=[b_lam], wr=[b_lam])
                    OP("act", "activation", lamv[:, 2:4], lamv[:, 0:2], AF.Exp, rd=[b_lam], wr=[b_lam])
                    OP("dve", "tensor_tensor", lamv[:, 4:5], lamv[:, 3:4], lamv[:, 2:3], ALU.subtract, rd=[b_lam], wr=[b_lam])
                    OP("dve", "tensor_scalar", lamv[:, 4:5], lamv[:, 4:5], -li, None, ALU.add, rd=[b_lam], wr=[b_lam])
                    OP("dve", "tensor_scalar", lamv[:, 5:6], svc(l, "gsub"), 1.0 - li, None, ALU.mult, rd=[b_const], wr=[b_lam])
                    if own:
                        tmpQ = AR.alloc([2, TS], BF16)
                        b_tq = Buf("tmpQ")
                        DMA("act", q_all[:, :, 0:TS], QS[l][0], rd=[b_qs[l][0]], wr=b_q[0:4])
                        for hp in range(2):
                            DMA("act", tmpQ, QS[l][1][:, hp * 2:(hp + 1) * 2, :], rd=[b_qs[l][1]], wr=[b_tq])
                            blend(q_all[:, hp * 2:(hp + 1) * 2, 0:TS], tmpQ, b_q[0:4], [b_tq], b_q[0:4])
                    else:
                        DMA("act", q_all[:, :, 0:TS], QS[l][k], rd=[b_qs[l][k]], wr=b_q[0:4])
                    DMA("act", ckf, ck_d[l].rearrange("(t p) f -> p t f", p=128), wr=[b_ckf])
                    DMA("act", cvf, cv_d[l].rearrange("(t p) f -> p t f", p=128), wr=[b_cvf])
                    if l == 0 and k == 0:
                        convert_weights(0, ["pool"], pwmax=1024, kinds=("B", "G0", "G1"))
                    if l == 0 and k == 1:
                        convert_weights(1, ["pool"], pwmax=1024)
                    scale = 64 ** -0.5
                    state = {"pt": 0}

                    def attn_block(q_ap, nq, kt_ap, v_fn, nkc, rd_q, rd_k, rd_v, out_ap, wr_out):
                        def qk(kc):
                            par = kc % 2
                            for m in range(2):
                                sb = par * 2 + m
                                MM(PS[sb][:, 0:nq], kt_ap[m * 64:(m + 1) * 64, kc * 128:(kc + 1) * 128], q_ap[m * 64:(m + 1) * 64, :],
                                   True, True, rd=rd_q + rd_k, wr=[PB[sb]])
                                if nq != 512:
                                    OP("act", "activation", pT[sb][:, 0:nq], PS[sb][:, 0:nq], AF.Exp, scale=scale, rd=[PB[sb]], wr=[b_pT[sb]])
                            if nq == 512:
                                OP("act", "activation", pTP[par], PSP[par], AF.Exp, scale=scale,
                                   rd=[PB[par * 2], PB[par * 2 + 1]], wr=[b_pT[par * 2], b_pT[par * 2 + 1]])

                        def pv(kc):
                            par = kc % 2
                            for m in range(2):
                                sb = par * 2 + m
                                MM(PS[4 + m][:, 0:nq], v_fn(kc), pT[sb][:, 0:nq], kc == 0, kc == nkc - 1, rd=rd_v + [b_pT[sb]], wr=[PB[4 + m]])
                                MM(PS[6 + m][:, 0:nq], ones_b, pT[sb][:, 0:nq], kc == 0, kc == nkc - 1, rd=[b_pT[sb], b_const], wr=[PB[6 + m]])

                        qk(0)
                        if state.get("fin") is not None:
                            state["fin"]()
                            state["fin"] = None
                        for kc in range(nkc):
                            if kc + 1 < nkc:
                                qk(kc + 1)
                            pv(kc)
                        state["fin"] = lambda: finalize(nq, out_ap, wr_out)

                    def finalize(nq, out_ap, wr_out):
                        for m in range(2):
                            OP("dve", "reciprocal", fR[m][:, 0:nq], PS[6 + m][:, 0:nq], rd=[PB[6 + m]], wr=[b_fR[m]])
                            OP("dve", "tensor_tensor", fT[m][:, 0:nq], PS[4 + m][:, 0:nq], fR[m][:, 0:nq], ALU.mult, rd=[PB[4 + m], b_fR[m]], wr=[b_fT[m]])
                        OP("dve", "scalar_tensor_tensor", fo[:, 0:nq], fT[1][:, 0:nq], lamv[:, 4:5], fT[0][:, 0:nq], ALU.mult, ALU.add,
                           rd=[b_fT[0], b_fT[1], b_lam], wr=[b_fo])
                        OP("act", "activation", fsq[:, 0:nq], fo[:, 0:nq], AF.Square, rd=[b_fo], wr=[b_fsq])
                        MM(PS[6][:, 0:nq], ones_b, fsq[:, 0:nq], True, True, rd=[b_fsq, b_const], wr=[PB[6]])
                        rstd_from(PS[6][:, 0:nq], PB[6], frs[:, 0:nq], b_frs, 1.0 / 128, ftl[:, 0:nq], b_ftl)
                        OP("dve", "scalar_tensor_tensor", out_ap, fo[:, 0:nq], lamv[:, 5:6], frs[:, 0:nq], ALU.mult, ALU.mult,
                           rd=[b_fo, b_frs, b_lam], wr=wr_out)

                    for h in range(4):
                        i = h % 2
                        for r in range(2):
                            DMA("act", KT[i][:, r * 2048:(r + 1) * 2048], EXG[l][r * 1536 + h * 128: r * 1536 + (h + 1) * 128, :],
                                rd=[b_exg[l]], wr=[b_kt[i]])
                            vview = EXG[l][r * 1536 + 1024: r * 1536 + 1536, :].rearrange("r (t f) -> (r t) f", f=512)
                            vsrc = vview[:, h * 128:(h + 1) * 128].rearrange("(c p) e -> p c e", p=128)
                            DMA("act", VH[i][:, r * 16:(r + 1) * 16, :], vsrc, rd=[b_exg[l]], wr=[b_vh[i]])
                        for t_ in range(2):
                            OP("pe", "transpose", PS[0][:, t_ * 128:(t_ + 1) * 128], ckf[:, t_, h * 128:(h + 1) * 128], ident,
                               rd=[b_ckf, b_const], wr=[PB[0]])
                        OP("act", "copy", KT[i][:, 4096:4352], PS[0][:, 0:256], rd=[PB[0]], wr=[b_kt[i]])
                        OP("dve", "tensor_copy", VH[i][:, 32:34, :], cvf[:, :, h * 128:(h + 1) * 128], rd=[b_cvf], wr=[b_vh[i]])
                        for qb in range(4):
                            attn_block(q_all[:, h, qb * 512:(qb + 1) * 512], 512, KT[i], lambda kc, i=i: VH[i][:, kc, :], 34,
                                       [b_q[qb]], [b_kt[i]], [b_vh[i]], mix_attn[:, h, qb * 512:(qb + 1) * 512], [b_mixa[qb]])
                        for sq_ in (range(2) if k == 0 else ()):
                            attn_block(q_all[:, h, 2048 + sq_ * 256: 2048 + (sq_ + 1) * 256], 256, kTp[:, h, sq_ * 256:(sq_ + 1) * 256],
                                       lambda kc, sq_=sq_, h=h: Vp[:, sq_ * 2 + kc, h * 128:(h + 1) * 128], 2,
                                       [b_q[4]], [b_prm], [b_prm], mix_attn[:, h, 2048 + sq_ * 256: 2048 + (sq_ + 1) * 256], [b_mixa[4]])
                    if state.get("fin") is not None:
                        state["fin"]()
                        state["fin"] = None
                    S.barrier()
                    check_stop("C%d_%d" % (l, k), [("mixf", mix_four), ("xfp", xf_p), ("dftp", dftp_b), ("cs64", cs64_b)])

                def phaseD(k, own=False):
                    AR.reset(P_Q)
                    wo = AR.alloc([8, 1024], BF16)
                    b_wo = Buf("wo")
                    xblk = AR.alloc([8, 512], F32)
                    b_x = Buf("xblk")
                    osb = AR.alloc([8, 512], F32)
                    b_o = Buf("osb")
                    _svd = AR.cur
                    AR.reset(P_ROPE)
                    osbW = AR.alloc([8, 512], F32)
                    AR.reset(_svd)
                    b_oW = Buf("osbW")
                    sq = AR.alloc([8, 512], BF16)
                    b_sq = Buf("sq")
                    u = sq
                    b_u = b_sq
                    hid = AR.alloc([22, 512], BF16)
                    b_hid = Buf("hid")
                    gu = [AR.alloc([8, 512], BF16) for _ in range(2)]
                    b_gu = [Buf("gu0"), Buf("gu1")]
                    dn = [AR.alloc([22, 128], BF16) for _ in range(2)]
                    b_dn = [Buf("dn0"), Buf("dn1")]
                    rstd = AR.alloc([512], F32)
                    tl = AR.alloc([512], F32)
                    b_rstd, b_tl = Buf("rstd"), Buf("tl")
                    tmpf = [AR.alloc([512], F32) for _ in range(2)]
                    b_tmpf = [Buf("tmpf0"), Buf("tmpf1")]
                    sg = [AR.alloc([512], F32) for _ in range(2)]
                    b_sg = [Buf("sg0"), Buf("sg1")]
                    ytm = [osb[:, 0:2, :].rearrange("p a b -> p (a b)"), osb[:, 2:4, :].rearrange("p a b -> p (a b)")]
                    b_ytm = [b_o, b_o]
                    for h_ in range(2):
                        DMA("sp", wo[:, h_ * 4:(h_ + 1) * 4, :],
                            WGB[l][h_ * 512:(h_ + 1) * 512, OFF_WOUT:OFF_WOUT + 1024].rearrange("(k p) n -> p k n", p=128),
                            rd=[b_wgb[l]], wr=[b_wo])
                    cnt = {"o": 0, "tmpf": 0, "gu": 0, "dn": 0, "sg": 0, "y": 0}

                    def mixk(kc, tok):
                        if kc < 4:
                            return mix_attn[:, kc, tok]
                        if kc < 6:
                            return mix_lru[:, kc - 4, tok]
                        return mix_four[:, kc - 6, tok]

                    def post_norm_update(gidx, c_, osb=osb, b_o=b_o):
                        OP("act", "activation", sq[:, 0:6, :], osb[:, 0:6, :], AF.Square, rd=[b_o], wr=[b_sq])
                        OP("pool", "tensor_tensor", sq[:, 6:8, :], osb[:, 6:8, :], osb[:, 6:8, :], ALU.mult, rd=[b_o], wr=[b_sq])
                        for c in range(8):
                            MM(PS[0], ones_b, sq[:, c, :], c == 0, c == 7, rd=[b_sq, b_const], wr=[PB[0]])
                        rstd_from(PS[0], PB[0], rstd, b_rstd, 1.0 / D, tl, b_tl)
                        for c in range(8):
                            i = cnt["tmpf"] % 2
                            cnt["tmpf"] += 1
                            OP("dve", "tensor_tensor", tmpf[i], osb[:, c, :], rstd, ALU.mult, rd=[b_o, b_rstd], wr=[b_tmpf[i]])
                            OP("dve", "scalar_tensor_tensor", xblk[:, c, :], tmpf[i], c_[:, gidx, c:c + 1], xblk[:, c, :], ALU.mult, ALU.add,
                               rd=[b_tmpf[i], b_cst, b_x], wr=[b_x])

                    for tb in range(5 if k == 0 else 4):
                        w = 0 if tb < 4 else 1
                        c_ = cst[l][w]
                        tok = slice(tb * 512, (tb + 1) * 512)
                        DMA("sp", xblk, XS[l][k][:, :, tok], rd=[b_xs[l][k][tb]], wr=[b_x])
                        if own and tb < 4:
                            OP("act", "activation", xblk, xblk, AF.Identity, scale=m0, rd=[b_x, b_const], wr=[b_x])
                            for c in range(8):
                                i = cnt["tmpf"] % 2
                                cnt["tmpf"] += 1
                                DMA("sp", tmpf[i], XS[l][1][:, c, tok], rd=[b_xs[l][1][tb]], wr=[b_tmpf[i]])
                                OP("dve", "scalar_tensor_tensor", xblk[:, c, :], tmpf[i], m1, xblk[:, c, :], ALU.mult, ALU.add,
                                   rd=[b_tmpf[i], b_const, b_x], wr=[b_x])
                        for c in range(8):
                            pb = 1 + cnt["o"] % 2
                            cnt["o"] += 1
                            for kc in range(8):
                                MM(PS[pb], wo[:, kc, c * 128:(c + 1) * 128], mixk(kc, tok), kc == 0, kc == 7,
                                   rd=[b_wo, b_mixa[tb], b_mixl, b_mixf], wr=[PB[pb]])
                            OP("act", "copy", osbW[:, c, :], PS[pb], rd=[PB[pb]], wr=[b_oW])
                        post_norm_update(2, c_, osbW, b_oW)
                        OP("act", "activation", sq[:, 0:6, :], xblk[:, 0:6, :], AF.Square, rd=[b_x], wr=[b_sq])
                        OP("pool", "tensor_tensor", sq[:, 6:8, :], xblk[:, 6:8, :], xblk[:, 6:8, :], ALU.mult, rd=[b_x], wr=[b_sq])
                        for c in range(8):
                            MM(PS[0], ones_b, sq[:, c, :], c == 0, c == 7, rd=[b_sq, b_const], wr=[PB[0]])
                        rstd_from(PS[0], PB[0], rstd, b_rstd, 1.0 / D, tl, b_tl)
                        for c in range(8):
                            i = cnt["tmpf"] % 2
                            cnt["tmpf"] += 1
                            OP("dve", "scalar_tensor_tensor", tmpf[i], xblk[:, c, :], c_[:, 3, c:c + 1], rstd, ALU.mult, ALU.mult,
                               rd=[b_x, b_rstd, b_cst], wr=[b_tmpf[i]])
                            OP("act", "activation", u[:, c, :], tmpf[i], AF.Identity, bias=c_[:, 4, c:c + 1], scale=1.0,
                               rd=[b_tmpf[i], b_cst], wr=[b_u])
                        for js in range(11):
                            i = cnt["gu"] % 2
                            cnt["gu"] += 1
                            DMA("sp", gu[i], WGU[l][js], rd=[b_wgb[l]], wr=[b_gu[i]])
                            for jj in range(2):
                                j = js * 2 + jj
                                pg, pu = 3 + (j % 2), 5 + (j % 2)
                                for kc in range(8):
                                    MM(PS[pg], gu[i][:, kc, jj * 128:(jj + 1) * 128], u[:, kc, :], kc == 0, kc == 7, rd=[b_gu[i], b_u], wr=[PB[pg]])
                                for kc in range(8):
                                    MM(PS[pu], gu[i][:, kc, 256 + jj * 128: 256 + (jj + 1) * 128], u[:, kc, :], kc == 0, kc == 7,
                                       rd=[b_gu[i], b_u], wr=[PB[pu]])
                                si = cnt["sg"] % 2
                                cnt["sg"] += 1
                                OP("act", "activation", sg[si], PS[pg], AF.Silu, rd=[PB[pg]], wr=[b_sg[si]])
                                OP("dve", "tensor_tensor", hid.rearrange("p a b -> p (a b)")[:, j * 512:(j + 1) * 512], PS[pu], sg[si], ALU.mult, rd=[PB[pu], b_sg[si]], wr=[b_hid])
                        for c in range(8):
                            i = cnt["dn"] % 2
                            cnt["dn"] += 1
                            DMA("sp", dn[i], WGB[l][c * 128:(c + 1) * 128, OFF_DN:OFF_DN + 2816].rearrange("p (k j) -> p k j", j=128),
                                rd=[b_wgb[l]], wr=[b_dn[i]])
                            pb = 1 + cnt["o"] % 2
                            cnt["o"] += 1
                            for kc in range(22):
                                MM(PS[pb], dn[i][:, kc, :], hid[:, kc, :], kc == 0, kc == 21, rd=[b_dn[i], b_hid], wr=[PB[pb]])
                            OP("act", "copy", osb[:, c, :], PS[pb], rd=[PB[pb]], wr=[b_o])
                        post_norm_update(5, c_)
                        if l == 0:
                            DMA("pool", XS[1][k][:, :, tok], xblk, rd=[b_x], wr=[b_xs[1][k][tb]])
                        else:
                            dst = ys_d if tb < 4 else yp_d
                            r0 = tb * 512 if tb < 4 else 0
                            for tt in range(4):
                                yi = cnt["y"] % 2
                                cnt["y"] += 1
                                for g in range(2):
                                    pb = 7 if g == 0 else 0
                                    for cc in range(4):
                                        c = g * 4 + cc
                                        OP("pe", "transpose", PS[pb][:, cc * 128:(cc + 1) * 128], xblk[:, c, tt * 128:(tt + 1) * 128], ident,
                                           rd=[b_x, b_const], wr=[PB[pb]])
                                    OP("act" if g == 0 else "dve", "copy" if g == 0 else "tensor_copy",
                                       ytm[yi][:, g * 512:(g + 1) * 512], PS[pb], rd=[PB[pb]], wr=[b_ytm[yi]])
                                DMA("pool", dst[r0 + tt * 128: r0 + (tt + 1) * 128, :], ytm[yi], rd=[b_ytm[yi]])
                    S.barrier()
                    check_stop("D%d_%d" % (l, k), [("xs1", XS[1][k])])
                phaseA(0)
                phaseA(1)
                if l == 0:
                    for k in (0, 1):
                        phaseB1(k)
                        phaseB2(k)
                        phaseC(k)
                        phaseD(k)
                else:
                    phaseB1(0, True)
                    phaseB2(0, True)
                    phaseC(0, True)
                    phaseD(0, True)
        except _Stop:
            pass

        S.final_wait("sp")
        with nc.Block() as block:
            S.emit(block)
    nc._taps = TAPS
    return nc


def _fm(v):
    v = np.asarray(v, np.float32)
    return np.ascontiguousarray(v.reshape(-1, 128).T)


_CONST_CACHE = {}


def _constants():
    if _CONST_CACHE:
        return _CONST_CACHE
    t = np.arange(4096, dtype=np.float64)[:, None]
    j = np.arange(2048, dtype=np.float64)[None, :]
    ang = 2.0 * np.pi * ((t * j) % 4096) / 4096.0
    s = 1.0 / math.sqrt(4096 * 64)
    dft = np.stack([np.cos(ang) * s, -np.sin(ang) * s]).astype(np.float32).astype(ml_dtypes.bfloat16)
    t = np.arange(256, dtype=np.float64)[:, None]
    j = np.arange(256, dtype=np.float64)[None, :]
    ang = 2.0 * np.pi * ((t * j) % 256) / 256.0
    s = 1.0 / math.sqrt(256 * 64)
    dftp = np.stack([np.cos(ang) * s, -np.sin(ang) * s]).astype(np.float32).astype(ml_dtypes.bfloat16)
    c = np.arange(64, dtype=np.float64)
    a64 = 2.0 * np.pi * np.outer(c, c) / 64.0
    cs64 = np.zeros((128, 256), np.float32)
    for g in range(2):
        cs64[g * 64:(g + 1) * 64, g * 64:(g + 1) * 64] = np.cos(a64)
        cs64[g * 64:(g + 1) * 64, 128 + g * 64:128 + (g + 1) * 64] = np.sin(a64)
    rmat = np.zeros((128, 128), np.float32)
    for dd in range(128):
        partner = dd + 16 if (dd % 32) < 16 else dd - 16
        rmat[partner, dd] = 1.0
    _CONST_CACHE.update(dft=dft, dftp=dftp, cs64=cs64, rmat=rmat)
    return _CONST_CACHE


def _rope_table(hf):
    n = 16
    inv = (10000.0 ** (-np.arange(n, dtype=np.float32) / n)).astype(np.float32)
    tpos = np.arange(hf * 2048, (hf + 1) * 2048)
    row = (tpos // 64).astype(np.float32)
    col = (tpos % 64).astype(np.float32)
    tab = np.zeros((128, 2, 2048), np.float32)
    for p in range(128):
        dd = p % 64
        pos = row if dd < 32 else col
        i = dd % 16
        ang = (pos * inv[i]).astype(np.float32)
        sign = -1.0 if (dd % 32) < 16 else 1.0
        tab[p, 0] = np.cos(ang)
        tab[p, 1] = sign * np.sin(ang)
    return tab


_NC = None
_DEBUG = {}


def kernel(x_prompt, x_sample, cache_k, cache_v, state_lru, c, c_ctx,
           w_mod, b_mod, g_pre_mix, g_post_mix, g_pre_ffn, g_post_ffn,
           w_in, w_out, w_lambda, g_subln, conv_w, conv_b,
           lru_wa, lru_ba, lru_wx, lru_bx, lru_lambda, w_gate_up, w_down):
    global _NC
    f = lambda a: np.asarray(a, np.float32)
    x_prompt, x_sample, cache_k, cache_v, state_lru, c, c_ctx = map(f, (x_prompt, x_sample, cache_k, cache_v, state_lru, c, c_ctx))
    w_mod, b_mod, w_in, w_out, w_gate_up, w_down = map(f, (w_mod, b_mod, w_in, w_out, w_gate_up, w_down))
    lru_wa, lru_wx = f(lru_wa), f(lru_wx)
    K = _constants()
    if _NC is None:
        _NC = build_program()
    nc = _NC
    lw = np.zeros((128, 16, 128), np.float32)
    for l in range(2):
        for d in range(2):
            for g, W in enumerate((lru_wa, lru_wx)):
                for c2 in range(2):
                    idx = ((l * 2 + d) * 2 + g) * 2 + c2
                    for bb in range(2):
                        lw[bb * 64:(bb + 1) * 64, idx, bb * 64:(bb + 1) * 64] = W[l, d, c2 * 2 + bb]
    lw = lw.reshape(128, 2048)
    wsl = np.empty((2, 8, 128, WA_W + WB_W), np.float32)
    for l in range(2):
        for r in range(8):
            rows = slice(r * 128, (r + 1) * 128)
            wsl[l, r, :, 0:2304] = w_in[l, rows, :]
            wsl[l, r, :, 2304:8448] = w_mod[l, rows, :]
            wsl[l, r, :, 8448:9472] = w_out[l, rows, :]
            wsl[l, r, :, 9472:15104] = w_gate_up[l, rows, :]
            wd = w_down[l][:, r * 128:(r + 1) * 128]
            wsl[l, r, :, 15104:17920] = wd.reshape(22, 128, 128).transpose(1, 0, 2).reshape(128, 2816)
    in_maps = []
    cores = _DEBUG.get("cores", list(range(8)))
    for core in cores:
        p, hf = core // 2, core % 2
        sv = np.zeros((128, NSV), np.float32)
        for l in range(2):
            def put(name, arr):
                o, w = SVL[name]
                sv[:, l * SV_PER_L + o: l * SV_PER_L + o + w] = arr
            put("gpm", _fm(g_pre_mix[l]))
            put("gqm", _fm(g_post_mix[l]))
            put("gpf", _fm(g_pre_ffn[l]))
            put("gqf", _fm(g_post_ffn[l]))
            put("bmod", _fm(b_mod[l]))
            put("gsub", f(g_subln[l]).reshape(128, 1))
            cw = f(conv_w[l])
            put("convw", np.stack([_fm(cw[t_]) for t_ in range(4)], axis=1).reshape(128, 8))
            put("convb", _fm(conv_b[l]))
            put("ba", np.concatenate([_fm(f(lru_ba)[l, d]) for d in range(2)], axis=1))
            put("bx", np.concatenate([_fm(f(lru_bx)[l, d]) for d in range(2)], axis=1))
            put("lam", np.concatenate([_fm(f(lru_lambda)[l, d]) for d in range(2)], axis=1))
            put("wl", np.broadcast_to(f(w_lambda[l]).reshape(1, 256), (128, 256)))
            put("h0", np.concatenate([_fm(state_lru[p, l, d]) for d in range(2)], axis=1))
        par = (np.arange(128) % 2 == 1)
        sv[:, SV_MISC + 0] = 1.0 - hf
        sv[:, SV_MISC + 1] = float(hf)
        for k in range(2):
            o = k
            sv[:, SV_MISC + 2 + 2 * k] = 1.0 - o
            sv[:, SV_MISC + 3 + 2 * k] = float(o)
            sv[:, SV_MISC + 6 + k] = np.where(par & (o == 1), -1.0, 1.0)
        sv[:, SV_MISC + 2] = np.where(par & (hf == 1), -1.0, 1.0)
        cond = np.stack([_fm(c[p]), _fm(c_ctx)], axis=2)
        sv[:, SV_MISC + 8: SV_MISC + 24] = cond.reshape(128, 16)
        csign = np.ones((128, 512), np.float32)
        if hf == 1:
            csign[:, 1::2] = -1.0
        xs2 = np.stack([x_sample[p, 0:2048, :], x_sample[p, 2048:4096, :]])
        in_maps.append({
            "xs": np.ascontiguousarray(xs2),
            "xp": np.ascontiguousarray(x_prompt[2 * core:2 * core + 2].reshape(512, 1024)),
            "ck": np.ascontiguousarray(cache_k[p].reshape(2, 256, 512)),
            "cv": np.ascontiguousarray(cache_v[p].reshape(2, 256, 512)),
            "sv": sv, "lw": lw, "wsl": wsl,
            "rope": np.stack([_rope_table(0), _rope_table(1)]), "csign": np.ones((128, 512), np.float32),
            "rmat": K["rmat"], "cs64": K["cs64"],
            "dft": K["dft"], "dftp": K["dftp"],
        })
    if _DEBUG.get("stop") is not None:
        nc = build_program(stop=_DEBUG["stop"])
        res = run_bass_kernel_spmd(nc, in_maps, core_ids=list(range(len(cores))), trace=bool(_DEBUG.get("trace")))
        _DEBUG["exec_ns"] = getattr(res, "exec_time_ns", None)
        _DEBUG["results"] = res.results
        _DEBUG["taps"] = nc._taps
    else:
        res = run_bass_kernel_spmd(nc, in_maps, core_ids=list(range(8)))
    R = res.results
    y_prompt = np.empty((16, 256, 1024), np.float32)
    y_sample = np.empty((4, 4096, 1024), np.float32)
    new_k = np.empty((16, 2, 256, 4, 2, 64), np.float32)
    new_v = np.empty((16, 2, 256, 4, 128), np.float32)
    new_h = np.empty((16, 2, 2, 256), np.float32)
    for core in range(8):
        p, hf = core // 2, core % 2
        r = R[core]
        y_sample[p, hf * 2048:(hf + 1) * 2048] = np.asarray(r["ys"])
        y_prompt[2 * core:2 * core + 2] = np.asarray(r["yp"]).reshape(2, 256, 1024)
        new_k[2 * core:2 * core + 2] = np.asarray(r["nk"]).reshape(2, 2, 256, 4, 2, 64)
        new_v[2 * core:2 * core + 2] = np.asarray(r["nv"]).reshape(2, 2, 256, 4, 128)
        new_h[2 * core:2 * core + 2] = np.asarray(r["nh"]).reshape(2, 2, 2, 256)
    return (y_prompt, y_sample, new_k, new_v, new_h)
```
